# Optimizing a Trainium2 kernel written in Bass

```python
import jax, jax.numpy as jnp
from jax import lax
import numpy as np

D_MODEL = 2048
BATCH = 1
SEQ = 8192
DEPTH = 1
DEC_BATCH = 8
DEC_SEQ = 2048
PAST_LEN = 128

N_META = 16
GRID_W = 64
WIN_H = 8
WIN_W = 16
D_ATTN = D_MODEL // 2
HD_ATTN = 128
H_ATTN = D_ATTN // HD_ATTN
D_RWKV = D_MODEL - D_ATTN
HD_RWKV = 64
H_RWKV = D_RWKV // HD_RWKV
R_DECAY = 96
R_ICLR = 96
R_GATE = 256
N_SHIFT = 3 * D_RWKV + R_DECAY + R_ICLR
N_IN = 3 * D_ATTN + N_SHIFT + R_GATE
D_FF = ((8 * D_MODEL // 3 + 255) // 256) * 256
DEEPNORM_ALPHA = (2 * DEPTH) ** 0.25
DEEPNORM_BETA = (8 * DEPTH) ** -0.25
LN_EPS = 1e-5
GN_EPS = 64e-5
NEG_INF = -1e30

kernel_name = "hymba_natten2d_rwkv7bi_deepnorm_encoder"


def _layer_norm(x, g, b):
    xf = x.astype(jnp.float32)
    mu = jnp.mean(xf, axis=-1, keepdims=True)
    xc = xf - mu
    var = jnp.mean(xc * xc, axis=-1, keepdims=True)
    return (xc * lax.rsqrt(var + LN_EPS) * g + b).astype(x.dtype)


def _neighbourhood_attention(q, k, v, rpb):
    B, L, _ = q.shape
    T = L - N_META
    rows = T // GRID_W
    kh = min(WIN_H, rows)

    def heads(t):
        return t.reshape(B, L, H_ATTN, HD_ATTN).transpose(0, 2, 1, 3)

    qh = heads(q) * (HD_ATTN ** -0.5)
    kh_, vh = heads(k), heads(v)
    qm, km, vm = qh[:, :, :N_META], kh_[:, :, :N_META], vh[:, :, :N_META]
    qg = qh[:, :, N_META:].reshape(B, H_ATTN, rows, GRID_W, HD_ATTN)
    kg = kh_[:, :, N_META:].reshape(B, H_ATTN, rows, GRID_W, HD_ATTN)
    vg = vh[:, :, N_META:].reshape(B, H_ATTN, rows, GRID_W, HD_ATTN)

    s_mm = jnp.einsum('bhmd,bhnd->bhmn', qm, km).astype(jnp.float32)
    o_meta = jnp.einsum('bhmn,bhnd->bhmd', jax.nn.softmax(s_mm, axis=-1).astype(vm.dtype), vm)

    qc = np.arange(GRID_W)
    c0 = np.clip(qc - WIN_W // 2, 0, GRID_W - WIN_W)
    kc = np.arange(GRID_W)
    colmask = (kc[None, :] >= c0[:, None]) & (kc[None, :] < c0[:, None] + WIN_W)
    dc = np.clip(kc[None, :] - qc[:, None], -(WIN_W - 1), WIN_W - 1) + (WIN_W - 1)
    rpb_c = rpb[:, :, dc]
    colmask_j = jnp.asarray(colmask)[:, None, :]

    def row_block(r):
        r0 = jnp.clip(r - kh // 2, 0, rows - kh)
        kb = lax.dynamic_slice_in_dim(kg, r0, kh, axis=2)
        vb = lax.dynamic_slice_in_dim(vg, r0, kh, axis=2)
        qr = lax.dynamic_index_in_dim(qg, r, axis=2, keepdims=False)
        s = jnp.einsum('bhqd,bhjkd->bhqjk', qr, kb).astype(jnp.float32)
        dr = r0 + jnp.arange(kh) - r + (WIN_H - 1)
        bias = jnp.take(rpb_c, dr, axis=1).transpose(0, 2, 1, 3)
        s = jnp.where(colmask_j, s + bias[None].astype(jnp.float32), NEG_INF)
        s_m = jnp.einsum('bhqd,bhmd->bhqm', qr, km).astype(jnp.float32)
        p = jax.nn.softmax(jnp.concatenate([s.reshape(B, H_ATTN, GRID_W, kh * GRID_W), s_m], axis=-1), axis=-1)
        p = p.astype(vb.dtype)
        pb = p[..., :kh * GRID_W].reshape(B, H_ATTN, GRID_W, kh, GRID_W)
        pm = p[..., kh * GRID_W:]
        return jnp.einsum('bhqjk,bhjkd->bhqd', pb, vb) + jnp.einsum('bhqm,bhmd->bhqd', pm, vm)

    og = lax.map(row_block, jnp.arange(rows))
    og = og.transpose(1, 2, 0, 3, 4).reshape(B, H_ATTN, T, HD_ATTN)
    o = jnp.concatenate([o_meta, og], axis=2)
    return o.transpose(0, 2, 1, 3).reshape(B, L, D_ATTN)


def _wkv7_scan(r, decay, k, v, kk, a, reverse):
    B, L, H, N = r.shape

    def step(S, inp):
        r_t, w_t, k_t, v_t, kk_t, a_t = inp
        sa = jnp.einsum('bhvk,bhk->bhv', S, -kk_t)
        S = (S * w_t[:, :, None, :] + sa[..., None] * (kk_t * a_t)[:, :, None, :]
             + v_t[..., None] * k_t[:, :, None, :])
        return S, jnp.einsum('bhvk,bhk->bhv', S, r_t)

    xs = tuple(jnp.moveaxis(t, 1, 0) for t in (r, decay, k, v, kk, a))
    S0 = jnp.zeros((B, H, N, N), jnp.float32)
    _, ys = lax.scan(step, S0, xs, reverse=reverse)
    return jnp.moveaxis(ys, 0, 1)


def _rwkv7_direction(p, mu, w0, w2, a0, a2, k_k, k_a, r_k, reverse):
    B, L, _ = p.shape
    if reverse:
        nb = jnp.pad(p[:, 1:], ((0, 0), (0, 1), (0, 0)))
    else:
        nb = jnp.pad(p[:, :-1], ((0, 0), (1, 0), (0, 0)))
    f = (p + (nb - p) * mu).astype(jnp.float32)
    r, k, v, wd, ad = jnp.split(f, [D_RWKV, 2 * D_RWKV, 3 * D_RWKV, 3 * D_RWKV + R_DECAY], axis=-1)
    w = -jax.nn.softplus(-(w0 + jnp.tanh(wd) @ w2)) - 0.5
    decay = jnp.exp(-jnp.exp(w))
    a = jax.nn.sigmoid(a0 + ad @ a2)
    hs = lambda t: t.reshape(B, L, H_RWKV, HD_RWKV)
    kk = hs(k * k_k)
    kk = kk / jnp.maximum(jnp.sqrt(jnp.sum(kk * kk, axis=-1, keepdims=True)), 1e-12)
    k = k * (1.0 + (a - 1.0) * k_a)
    r, k, v, decay, a = hs(r), hs(k), hs(v), hs(decay), hs(a)
    y = _wkv7_scan(r, decay, k, v, kk, a, reverse)
    bonus = jnp.sum(r * k * r_k, axis=-1, keepdims=True) * v
    return y, bonus


def _rwkv7_bidirectional(p, gd, mu, w0, w2, a0, a2, g2, k_k, k_a, r_k, lnx_g, lnx_b):
    B, L, _ = p.shape
    y_f, b_f = _rwkv7_direction(p, mu[0], w0[0], w2[0], a0[0], a2[0], k_k, k_a, r_k, False)
    y_b, b_b = _rwkv7_direction(p, mu[1], w0[1], w2[1], a0[1], a2[1], k_k, k_a, r_k, True)
    y = y_f + y_b
    m = jnp.mean(y, axis=-1, keepdims=True)
    yc = y - m
    var = jnp.mean(yc * yc, axis=-1, keepdims=True)
    y = (yc * lax.rsqrt(var + GN_EPS) * lnx_g.reshape(H_RWKV, HD_RWKV)
         + lnx_b.reshape(H_RWKV, HD_RWKV))
    y = (y + b_f + b_b).reshape(B, L, D_RWKV)
    g = jax.nn.sigmoid(gd.astype(jnp.float32)) @ g2
    return (y * g).astype(p.dtype)


def _conv_glu_ffn(h, w_in, conv_w, conv_b, w_out):
    u = h @ w_in
    gate, up = u[..., :D_FF], u[..., D_FF:]
    prev = jnp.pad(gate[:, :-1], ((0, 0), (1, 0), (0, 0)))
    nxt = jnp.pad(gate[:, 1:], ((0, 0), (0, 1), (0, 0)))
    gate = prev * conv_w[0] + gate * conv_w[1] + nxt * conv_w[2] + conv_b
    return (jax.nn.gelu(gate, approximate=False) * up) @ w_out


def _encode(x, meta_tokens, emb_ln_g, emb_ln_b, w_in, attn_rpb, rwkv_mu, rwkv_w0, rwkv_w2,
            rwkv_a0, rwkv_a2, rwkv_g2, rwkv_k_k, rwkv_k_a, rwkv_r_k, rwkv_lnx_g, rwkv_lnx_b,
            w_out, ln1_g, ln1_b, ffn_w_in, ffn_conv_w, ffn_conv_b, ffn_w_out, ln2_g, ln2_b):
    B = x.shape[0]
    meta = jnp.broadcast_to(meta_tokens.astype(x.dtype)[None], (B, N_META, D_MODEL))
    h = _layer_norm(jnp.concatenate([meta, x], axis=1), emb_ln_g, emb_ln_b)
    for l in range(DEPTH):
        proj = h @ w_in[l]
        qa = proj[..., :D_ATTN]
        ka = proj[..., D_ATTN:2 * D_ATTN]
        va = proj[..., 2 * D_ATTN:3 * D_ATTN]
        pr = proj[..., 3 * D_ATTN:3 * D_ATTN + N_SHIFT]
        gd = proj[..., 3 * D_ATTN + N_SHIFT:]
        o_attn = _neighbourhood_attention(qa, ka, va, attn_rpb[l]).astype(h.dtype)
        o_rwkv = _rwkv7_bidirectional(pr, gd, rwkv_mu[l], rwkv_w0[l], rwkv_w2[l], rwkv_a0[l],
                                      rwkv_a2[l], rwkv_g2[l], rwkv_k_k[l], rwkv_k_a[l],
                                      rwkv_r_k[l], rwkv_lnx_g[l], rwkv_lnx_b[l]).astype(h.dtype)
        mix = jnp.concatenate([o_attn, o_rwkv], axis=-1) @ w_out[l]
        h = _layer_norm(DEEPNORM_ALPHA * h + mix, ln1_g[l], ln1_b[l])
        ffn = _conv_glu_ffn(h, ffn_w_in[l], ffn_conv_w[l], ffn_conv_b[l], ffn_w_out[l])
        h = _layer_norm(DEEPNORM_ALPHA * h + ffn, ln2_g[l], ln2_b[l])
    return h[:, N_META:]


def setup_inputs(seed: int = 0) -> dict:
    key = jax.random.key(seed)
    ks = jax.random.split(key, 32)
    nrm = lambda k, shape, s: jax.random.normal(k, shape, jnp.float32) * s
    L = DEPTH
    return {
        "x_prompt": nrm(ks[0], (BATCH, SEQ, D_MODEL), 1.0),
        "x_sample": nrm(ks[1], (DEC_BATCH, DEC_SEQ, D_MODEL), 1.0),
        "meta_tokens": nrm(ks[2], (N_META, D_MODEL), 1.0),
        "emb_ln_g": 1.0 + nrm(ks[3], (D_MODEL,), 0.05),
        "emb_ln_b": nrm(ks[4], (D_MODEL,), 0.01),
        "w_in": nrm(ks[5], (L, D_MODEL, N_IN), D_MODEL ** -0.5),
        "attn_rpb": nrm(ks[6], (L, H_ATTN, 2 * WIN_H - 1, 2 * WIN_W - 1), 0.1),
        "rwkv_mu": jax.random.uniform(ks[7], (L, 2, N_SHIFT), jnp.float32),
        "rwkv_w0": jax.random.uniform(ks[8], (L, 2, D_RWKV), jnp.float32, minval=-6.0, maxval=-1.0),
        "rwkv_w2": nrm(ks[9], (L, 2, R_DECAY, D_RWKV), 0.5 * R_DECAY ** -0.5),
        "rwkv_a0": nrm(ks[10], (L, 2, D_RWKV), 0.1),
        "rwkv_a2": nrm(ks[11], (L, 2, R_ICLR, D_RWKV), R_ICLR ** -0.5),
        "rwkv_g2": nrm(ks[12], (L, R_GATE, D_RWKV), R_GATE ** -0.5),
        "rwkv_k_k": 0.85 + nrm(ks[13], (L, D_RWKV), 0.05),
        "rwkv_k_a": 1.0 + nrm(ks[14], (L, D_RWKV), 0.05),
        "rwkv_r_k": nrm(ks[15], (L, H_RWKV, HD_RWKV), 0.1),
        "rwkv_lnx_g": 1.0 + nrm(ks[16], (L, D_RWKV), 0.05),
        "rwkv_lnx_b": nrm(ks[17], (L, D_RWKV), 0.01),
        "w_out": nrm(ks[18], (L, D_MODEL, D_MODEL), D_MODEL ** -0.5 * DEEPNORM_BETA),
        "ln1_g": 1.0 + nrm(ks[19], (L, D_MODEL), 0.05),
        "ln1_b": nrm(ks[20], (L, D_MODEL), 0.01),
        "ffn_w_in": nrm(ks[21], (L, D_MODEL, 2 * D_FF), D_MODEL ** -0.5),
        "ffn_conv_w": nrm(ks[22], (L, 3, D_FF), 3.0 ** -0.5),
        "ffn_conv_b": nrm(ks[23], (L, D_FF), 0.01),
        "ffn_w_out": nrm(ks[24], (L, D_FF, D_MODEL), D_FF ** -0.5 * DEEPNORM_BETA),
        "ln2_g": 1.0 + nrm(ks[25], (L, D_MODEL), 0.05),
        "ln2_b": nrm(ks[26], (L, D_MODEL), 0.01),
    }


def reference(x_prompt, x_sample, meta_tokens, emb_ln_g, emb_ln_b, w_in, attn_rpb, rwkv_mu,
              rwkv_w0, rwkv_w2, rwkv_a0, rwkv_a2, rwkv_g2, rwkv_k_k, rwkv_k_a, rwkv_r_k,
              rwkv_lnx_g, rwkv_lnx_b, w_out, ln1_g, ln1_b, ffn_w_in, ffn_conv_w, ffn_conv_b,
              ffn_w_out, ln2_g, ln2_b):
    params = (meta_tokens, emb_ln_g, emb_ln_b, w_in, attn_rpb, rwkv_mu, rwkv_w0, rwkv_w2,
              rwkv_a0, rwkv_a2, rwkv_g2, rwkv_k_k, rwkv_k_a, rwkv_r_k, rwkv_lnx_g, rwkv_lnx_b,
              w_out, ln1_g, ln1_b, ffn_w_in, ffn_conv_w, ffn_conv_b, ffn_w_out, ln2_g, ln2_b)
    y_prompt = _encode(x_prompt, *params)
    y_sample = _encode(x_sample, *params)
    return (y_prompt, y_sample)
```

```python
import numpy as np
import ml_dtypes
import concourse.bass as bass
import concourse.mybir as mybir
from concourse.bass_utils import run_bass_kernel_spmd

F32 = mybir.dt.float32
BF16 = mybir.dt.bfloat16
U8 = mybir.dt.uint8
ALU = mybir.AluOpType
AF = mybir.ActivationFunctionType
AX = mybir.AxisListType

D = 2048
NM = 16
DA = 1024
HA = 8
DR = 1024
HR = 16
NSH = 3 * DR + 192
NIN = 3 * DA + NSH + 256
DFF = 5632
ALPHA = 2.0 ** 0.25
LN_EPS = 1e-5
GN_EPS = 64e-5
CDEC = -float(np.exp(-0.5))
NEG = -30000.0
WINR = 10
NKEY = WINR * 64 + NM


class Prog:
    ENG = ("sp", "act", "dve", "pool", "pe")

    def __init__(self, nc, n_dma_sems=40):
        self.nc = nc
        self.q = {e: [] for e in self.ENG}
        self.cnt = {e: 0 for e in self.ENG}
        self.known = {e: {} for e in self.ENG}
        self.last_w = {}
        self.readers = {}
        self.semh = {}
        for e in ("act", "dve", "pool", "pe"):
            self.semh[e] = nc.alloc_semaphore("sem_" + e)
        self.ndma = n_dma_sems
        for i in range(n_dma_sems):
            self.semh[("dma", i)] = nc.alloc_semaphore("sem_dma%d" % i)
        self.dma_tot = [0] * n_dma_sems
        self.rr = 0
        self.arena = nc.alloc_sbuf_tensor("arena", [128, 204 * 1024], U8)
        self.off = 0
        self.uid = 0
        self.psum = [nc.alloc_psum_tensor("psb%d" % i, [128, 512], F32) for i in range(8)]

    def reset_arena(self, off=0):
        self.off = off

    def tile(self, shape, dtype, name="t"):
        esz = 4 if dtype == F32 else 2
        n = int(np.prod(shape[1:]))
        nbytes = (n * esz + 31) // 32 * 32
        assert self.off + nbytes <= 204 * 1024, ("sbuf overflow", name, self.off, nbytes)
        ap = self.arena[:, self.off:self.off + n * esz].bitcast(dtype)
        self.off += nbytes
        if len(shape) == 3:
            ap = ap.rearrange("p (a b) -> p a b", a=shape[1])
        elif len(shape) == 4:
            ap = ap.rearrange("p (a b c) -> p a b c", a=shape[1], b=shape[2])
        self.uid += 1
        return ap, "%s#%d" % (name, self.uid)

    def bank(self, i, dtype=F32):
        ap = self.psum[i][:, :]
        if dtype != F32:
            ap = ap.bitcast(dtype)
        return ap

    def _waits(self, eng, reads, writes):
        need = {}

        def add(ev):
            if ev is None:
                return
            s, v = ev
            if need.get(s, 0) < v:
                need[s] = v
        for k in reads:
            add(self.last_w.get(k))
        for k in writes:
            add(self.last_w.get(k))
            for s, v in self.readers.get(k, {}).items():
                add((s, v))
        out = []
        for s, v in need.items():
            if s == "pe" and eng == "pe":
                continue
            if self.known[eng].get(s, 0) >= v:
                continue
            self.known[eng][s] = v
            out.append((s, v))
        return out

    def _record(self, ev, reads, writes):
        s, v = ev
        for k in reads:
            d = self.readers.setdefault(k, {})
            if d.get(s, 0) < v:
                d[s] = v
        for k in writes:
            self.last_w[k] = ev
            self.readers[k] = {}

    @staticmethod
    def _px(reads, writes):
        pr = tuple(k for k in reads if k.startswith("psb"))
        if pr:
            reads = tuple(k for k in reads if not k.startswith("psb"))
            writes = tuple(writes) + pr
        return reads, writes

    def op(self, eng, fn, reads=(), writes=()):
        reads, writes = self._px(reads, writes)
        waits = self._waits(eng, reads, writes)
        self.cnt[eng] += 1
        ev = (eng, self.cnt[eng])
        self.q[eng].append((waits, fn, eng, 1))
        self._record(ev, reads, writes)

    def dma(self, eng, out, in_, reads=(), writes=()):
        i = self.rr
        self.rr = (self.rr + 1) % self.ndma
        s = ("dma", i)
        waits = self._waits(eng, reads, writes)
        if self.dma_tot[i] > 0 and self.known[eng].get(s, 0) < self.dma_tot[i]:
            self.known[eng][s] = self.dma_tot[i]
            waits.append((s, self.dma_tot[i]))
        self.dma_tot[i] += 16
        ev = (s, self.dma_tot[i])
        self.q[eng].append((waits, lambda e, o=out, a=in_: e.dma_start(out=o, in_=a), s, 16))
        self._record(ev, reads, writes)

    def barrier(self):
        for e in self.ENG:
            waits = []
            for s in ("act", "dve", "pool", "pe"):
                if s != e and self.cnt[s] > self.known[e].get(s, 0):
                    self.known[e][s] = self.cnt[s]
                    waits.append((s, self.cnt[s]))
            for i in range(self.ndma):
                s = ("dma", i)
                if self.dma_tot[i] > self.known[e].get(s, 0):
                    self.known[e][s] = self.dma_tot[i]
                    waits.append((s, self.dma_tot[i]))
            if waits:
                self.q[e].append((waits, None, None, 0))

    def finish(self):
        self.barrier()
        nc = self.nc
        semh = self.semh
        q = self.q
        with nc.Block() as block:
            def mk(name):
                def body(e):
                    for waits, fn, s, amt in q[name]:
                        for ws, wv in waits:
                            e.wait_ge(semh[ws], wv)
                        if fn is not None:
                            fn(e).then_inc(semh[s], amt)
                return body
            block.sync(mk("sp"))
            block.scalar(mk("act"))
            block.vector(mk("dve"))
            block.gpsimd(mk("pool"))
            block.tensor(mk("pe"))

    def mm(self, out, lhsT, rhs, start, stop, reads, writes):
        self.op("pe", lambda e: e.matmul(out, lhsT, rhs, start=start, stop=stop), reads, writes)

    def tr(self, out, in_, ident, reads, writes):
        self.op("pe", lambda e: e.transpose(out, in_, ident), reads, writes)

    def act(self, out, in_, func, reads, writes, bias=None, scale=None, accum=None):
        kw = {}
        if bias is not None:
            kw["bias"] = bias
        if scale is not None:
            kw["scale"] = scale
        if accum is not None:
            kw["accum_out"] = accum
        self.op("act", lambda e: e.activation(out, in_, func, **kw), reads, writes)

    def tt(self, eng, out, a, b, op, reads, writes):
        self.op(eng, lambda e: e.tensor_tensor(out, a, b, op), reads, writes)

    def ts(self, eng, out, a, s1, s2, op0, op1, reads, writes):
        if s2 is None:
            self.op(eng, lambda e: e.tensor_scalar(out, a, s1, None, op0), reads, writes)
        else:
            self.op(eng, lambda e: e.tensor_scalar(out, a, s1, s2, op0, op1), reads, writes)

    def stt(self, eng, out, a, sc, b, op0, op1, reads, writes):
        self.op(eng, lambda e: e.scalar_tensor_tensor(out, a, sc, b, op0, op1), reads, writes)

    def copy(self, eng, out, in_, reads, writes):
        if eng == "act":
            self.op("act", lambda e: e.copy(out, in_), reads, writes)
        else:
            self.op(eng, lambda e: e.tensor_copy(out, in_), reads, writes)

    def reduce(self, eng, out, in_, op, reads, writes):
        self.op(eng, lambda e: e.tensor_reduce(out, in_, AX.X, op), reads, writes)

    def recip(self, eng, out, in_, reads, writes):
        self.op(eng, lambda e: e.reciprocal(out, in_), reads, writes)

    def memset(self, eng, ap, val, writes):
        self.op(eng, lambda e: e.memset(ap, val), (), writes)


class Seq:
    def __init__(self, name, T):
        self.name = name
        self.T = T
        self.L = T + NM
        self.rows = T // 64
        self.ntile = 1 + T // 128

    def tile_range(self, i):
        if i == 0:
            return 0, NM
        return NM + (i - 1) * 128, 128


class Builder:
    def __init__(self, seq_T, debug=False, stages=5, run=(1, 2, 3, 4, 5)):
        self.debug = debug
        self.run = run
        self.stages = stages
        nc = bass.Bass("TRN2", target_bir_lowering=False)
        self.nc = nc
        self.P = Prog(nc)
        self.seqs = [Seq("s%d" % i, T) for i, T in enumerate(seq_T)]
        self.dram = {}
        self._declare_io()

    def din(self, name, shape, dtype=F32):
        ap = self.nc.dram_tensor(name, list(shape), dtype, kind="ExternalInput").ap()
        self.dram[name] = ap
        return ap

    def dscr(self, name, shape, dtype=F32):
        kind = "ExternalOutput" if self.debug else "Internal"
        ap = self.nc.dram_tensor(name, list(shape), dtype, kind=kind).ap()
        self.dram[name] = ap
        return ap

    def _declare_io(self):
        for s in self.seqs:
            self.din("x_" + s.name, [s.T, D])
            self.nc_out = None
        for s in self.seqs:
            ap = self.nc.dram_tensor("y_" + s.name, [s.T, D], F32, kind="ExternalOutput").ap()
            self.dram["y_" + s.name] = ap
        self.din("meta_tokens", [NM, D])
        for nm in ("emb_ln_g", "emb_ln_b", "ln1_g", "ln1_b", "ln2_g", "ln2_b"):
            self.din(nm, [D])
        self.din("w_in", [D, NIN])
        self.din("bias_tab", [HA, 5, 128, NKEY])
        self.din("rwkv_mu", [2, NSH])
        self.din("rwkv_w0", [2, DR])
        self.din("rwkv_w2", [2, 96, DR])
        self.din("rwkv_a0", [2, DR])
        self.din("rwkv_a2", [2, 96, DR])
        self.din("rwkv_g2", [256, DR])
        for nm in ("rwkv_k_k", "rwkv_k_a", "rwkv_r_k", "rwkv_lnx_g", "rwkv_lnx_b"):
            self.din(nm, [DR])
        self.din("w_out", [D, D])
        self.din("ffn_w_in", [D, 2 * DFF])
        self.din("ffn_conv_l", [128, 44, 4])
        self.din("ffn_w_out", [DFF, D])
        self.din("c_ident", [128, 128])
        self.din("c_masks", [6, 128, 128])
        self.din("c_zero", [1, NSH])
        for s in self.seqs:
            n = s.name
            self.dscr("H0_" + n, [s.L, D])
            self.dscr("QT_" + n, [HA, 128, s.L], BF16)
            self.dscr("KT_" + n, [HA, 128, s.L], BF16)
            self.dscr("V_" + n, [s.L, DA], BF16)
            self.dscr("P_" + n, [s.L, NSH])
            self.dscr("G_" + n, [s.L, DR])
            self.dscr("YF_" + n, [s.L, DR])
            self.dscr("BF_" + n, [s.L, DR])
            if self.debug:
                self.dscr("YB_" + n, [s.L, DR])
                self.dscr("BB_" + n, [s.L, DR])
                self.dscr("DF_" + n, [s.L, NSH])
                self.dscr("DA_" + n, [s.L, DR])
                self.dscr("DK_" + n, [s.L, DR])
                self.dscr("DT_" + n, [s.L, DR])
                self.dscr("DP_" + n, [3, 128, DR])
            self.dscr("OT_" + n, [16, 128, s.L], BF16)
            self.dscr("H1_" + n, [s.L, D])
            self.dscr("H1T_" + n, [16, 128, s.L], BF16)

    def setup_consts(self):
        P = self.P
        d = self.dram
        self.ident_f, k1 = P.tile([128, 128], F32, "identf")
        P.dma("sp", self.ident_f, d["c_ident"], (), (k1,))
        self.ident_b, k2 = P.tile([128, 128], BF16, "identb")
        P.copy("dve", self.ident_b, self.ident_f, (k1,), (k2,))
        self.kid = (k1, k2)
        self.const_end = P.off

    def bcast_load(self, name_ap, n, nm):
        P = self.P
        t, k = P.tile([128, n], F32, nm)
        P.dma("sp", t, name_ap.partition_broadcast(128), (), (k,))
        return t, k

    def layer_norm(self, x, kx, n, gb, kgb, out, kout, eps, tmp):
        P = self.P
        (st, kst), (mv, kmv), (rs, krs) = tmp
        g, b = gb
        for c in range(4):
            P.op("dve", lambda e, c=c: e.bn_stats(st[:n, c, :], x[:n, c * 512:(c + 1) * 512]), (kx,), (kst,))
        P.op("dve", lambda e: e.bn_aggr(mv[:n, :], st[:n].rearrange("p a b -> p (a b)")), (kst,), (kmv,))
        P.act(rs[:n, :], mv[:n, 1:2], AF.Sqrt, (kmv,), (krs,), bias=eps)
        P.op("dve", lambda e: e.reciprocal(rs[:n, :], rs[:n, :]), (krs,), (krs,))
        P.ts("dve", x[:n, :], x[:n, :], mv[:n, 0:1], rs[:n, 0:1], ALU.subtract, ALU.mult, (kx, kmv, krs), (kx,))
        P.tt("pool", x[:n, :], x[:n, :], g[:n, :], ALU.mult, (kx,) + kgb, (kx,))
        P.tt("dve", out[:n, :], x[:n, :], b[:n, :], ALU.add, (kx,) + kgb, (kout,))

    def ln_tmp(self):
        P = self.P
        return (P.tile([128, 4, 6], F32, "bnst"), P.tile([128, 2], F32, "bnmv"), P.tile([128, 1], F32, "rstd"))

    def transpose_to_fm(self, src, ksrc, n, dst, kdst, col0, banks, ctr):
        P = self.P
        for g4 in range(4):
            bi = banks[(ctr[0]) % len(banks)]
            ctr[0] += 1
            pb = P.bank(bi)
            kb = "psb%d" % bi
            for j in range(4):
                kc = g4 * 4 + j
                P.tr(pb[:, j * 128:j * 128 + n], src[:n, kc * 128:(kc + 1) * 128], self.ident_f[:n, :n],
                     (ksrc, self.kid[0]), (kb,))
            eng = "act" if g4 % 2 == 0 else "dve"
            P.copy(eng, dst[:, g4 * 4:g4 * 4 + 4, col0:col0 + n],
                   pb.rearrange("p (a b) -> p a b", a=4)[:, :, :n], (kb,), (kdst,))

    def load_w(self, wt, kw, src, rows, col0, ncols):
        P = self.P
        nk = rows // 128
        for k0 in range(0, nk, 4):
            k1 = min(nk, k0 + 4)
            P.dma("pool", wt[:, k0:k1, :ncols],
                  src[k0 * 128:k1 * 128, col0:col0 + ncols].rearrange("(kc p) n -> p kc n", p=128), (), (kw,))

    def stage1(self, s):
        P = self.P
        d = self.dram
        n_ = s.name
        P.barrier()
        P.reset_arena(self.const_end)
        g0, kg = self.bcast_load(d["emb_ln_g"], D, "g0")
        b0, kb = self.bcast_load(d["emb_ln_b"], D, "b0")
        tmp = self.ln_tmp()
        g2, kg2 = P.tile([128, 2, DR], BF16, "g2")
        self.load_w(g2, kg2, d["rwkv_g2"], 256, 0, DR)
        xt = [P.tile([128, D], F32, "xt%d" % i) for i in range(2)]
        ht = [P.tile([128, D], F32, "ht%d" % i) for i in range(2)]
        wbuf = [P.tile([128, 16, 512], BF16, "w%d" % i) for i in range(2)]
        stg = [P.tile([128, 512], F32, "stg%d" % i) for i in range(3)]
        stgb = [P.tile([128, 512], BF16, "stgb%d" % i) for i in range(2)]
        qkT = [P.tile([128, 4, 128], BF16, "qkT%d" % i) for i in range(2)]
        sgT = [P.tile([128, 2, 128], BF16, "sgT%d" % i) for i in range(2)]
        gst = [P.tile([128, DR], F32, "gst%d" % i) for i in range(2)]
        TB = 16
        h0T, kh0T = P.tile([128, 16, TB * 128 + NM], BF16, "h0T")
        groups = []
        for c0 in range(0, NIN, 512):
            groups.append((c0, min(512, NIN - c0)))
        tiles = list(range(s.ntile))
        blocks = [tiles[0:TB + 1]] + [tiles[i:i + TB] for i in range(TB + 1, s.ntile, TB)]
        ctr = [0]
        wi = 0
        ci = 0
        for blk in blocks:
            base = s.tile_range(blk[0])[0]
            for ti in blk:
                t0, n = s.tile_range(ti)
                x, kx = xt[ci % 2]
                h, kh = ht[ci % 2]
                ci += 1
                if ti == 0:
                    P.dma("sp", x[:n, :], d["meta_tokens"], (), (kx,))
                else:
                    P.dma("sp", x[:n, :], d["x_" + n_][t0 - NM:t0 - NM + n, :], (), (kx,))
                self.layer_norm(x, kx, n, (g0, b0), (kg, kb), h, kh, LN_EPS, tmp)
                P.dma("sp", d["H0_" + n_][t0:t0 + n, :], h[:n, :], (kh,), ())
                self.transpose_to_fm(h, kh, n, h0T, kh0T, t0 - base, (0, 1), ctr)
            if self.stages == 0:
                continue
            for gi, (c0, nc_) in enumerate(groups):
                w, kw = wbuf[wi % 2]
                wi += 1
                self.load_w(w, kw, d["w_in"], D, c0, nc_)
                for ti in blk:
                    t0, n = s.tile_range(ti)
                    bi = 2 + (ctr[0] % 2)
                    ctr[0] += 1
                    pb = P.bank(bi)
                    kpb = "psb%d" % bi
                    for kc in range(16):
                        P.mm(pb[:n, :nc_], h0T[:, kc, t0 - base:t0 - base + n], w[:, kc, :nc_],
                             kc == 0, kc == 15, (kh0T, kw), (kpb,))
                    ev = "act" if ctr[0] % 2 == 0 else "dve"
                    if self.stages == -1 or (self.stages == -2 and gi < 4) or (self.stages == -3 and gi >= 4) or (self.stages == -4 and gi < 12):
                        sf, ksf = stg[ctr[0] % 3]
                        P.copy(ev, sf[:n, :], pb[:n, :], (kpb,), (ksf,))
                        continue
                    if gi < 4:
                        sb, ksb = stgb[ctr[0] % 2]
                        if gi < 2:
                            P.act(sb[:n, :], pb[:n, :], AF.Copy, (kpb,), (ksb,), scale=128.0 ** -0.5)
                        else:
                            P.copy(ev, sb[:n, :], pb[:n, :], (kpb,), (ksb,))
                        tb = 4
                        ptb = P.bank(tb, BF16)
                        for j in range(4):
                            P.tr(ptb[:, j * 128:j * 128 + n], sb[:n, j * 128:(j + 1) * 128], self.ident_b[:n, :n],
                                 (ksb, self.kid[1]), ("psb4",))
                        qt, kqt = qkT[ctr[0] % 2]
                        P.copy("dve" if ev == "act" else "act", qt[:, :, :n],
                               ptb[:, 0:512].rearrange("p (a b) -> p a b", a=4)[:, :, :n], ("psb4",), (kqt,))
                        dst = d[("QT_" if gi < 2 else "KT_") + n_]
                        h0 = (gi % 2) * 4
                        P.dma("sp", dst[h0:h0 + 4, :, t0:t0 + n].rearrange("h p t -> p h t"), qt[:, :, :n], (kqt,), ())
                    elif gi < 6:
                        sb, ksb = stgb[ctr[0] % 2]
                        P.copy(ev, sb[:n, :], pb[:n, :], (kpb,), (ksb,))
                        P.dma("sp", d["V_" + n_][t0:t0 + n, (gi - 4) * 512:(gi - 3) * 512], sb[:n, :], (ksb,), ())
                    elif gi < 12:
                        sf, ksf = stg[ctr[0] % 3]
                        P.copy(ev, sf[:n, :], pb[:n, :], (kpb,), (ksf,))
                        P.dma("sp", d["P_" + n_][t0:t0 + n, (gi - 6) * 512:(gi - 5) * 512], sf[:n, :], (ksf,), ())
                    else:
                        sf, ksf = stg[ctr[0] % 3]
                        P.copy("dve", sf[:n, :192], pb[:n, :192], (kpb,), (ksf,))
                        P.dma("sp", d["P_" + n_][t0:t0 + n, 3072:3264], sf[:n, :192], (ksf,), ())
                        sb, ksb = stgb[ctr[0] % 2]
                        P.act(sb[:n, :256], pb[:n, 192:448], AF.Sigmoid, (kpb,), (ksb,))
                        ptb = P.bank(4, BF16)
                        for j in range(2):
                            P.tr(ptb[:, j * 128:j * 128 + n], sb[:n, j * 128:(j + 1) * 128], self.ident_b[:n, :n],
                                 (ksb, self.kid[1]), ("psb4",))
                        sg, ksg = sgT[ctr[0] % 2]
                        P.copy("dve", sg[:, :, :n], ptb[:, 0:256].rearrange("p (a b) -> p a b", a=2)[:, :, :n],
                               ("psb4",), (ksg,))
                        go, kgo = gst[ctr[0] % 2]
                        for hf in range(2):
                            pg = P.bank(5 + hf)
                            kpg = "psb%d" % (5 + hf)
                            for j in range(2):
                                P.mm(pg[:n, :], sg[:, j, :n], g2[:, j, hf * 512:(hf + 1) * 512], j == 0, j == 1,
                                     (ksg, kg2), (kpg,))
                            P.copy("act" if hf == 0 else "dve", go[:n, hf * 512:(hf + 1) * 512], pg[:n, :], (kpg,), (kgo,))
                        P.dma("sp", d["G_" + n_][t0:t0 + n, :], go[:n, :], (kgo,), ())

    def stage2(self, s):
        P = self.P
        d = self.dram
        n_ = s.name
        for dr in (0, 1):
            P.barrier()
            P.reset_arena(self.const_end)
            self._rwkv_dir(s, dr)

    def _rwkv_dir(self, s, dr):
        P = self.P
        d = self.dram
        n_ = s.name
        Pd = d["P_" + n_]
        L = s.L
        mu, kmu = self.bcast_load(d["rwkv_mu"][dr], NSH, "mu")
        w0, kw0 = self.bcast_load(d["rwkv_w0"][dr], DR, "w0")
        a0, ka0 = self.bcast_load(d["rwkv_a0"][dr], DR, "a0")
        kkp, kkkp = self.bcast_load(d["rwkv_k_k"], DR, "k_k")
        kap, kkap = self.bcast_load(d["rwkv_k_a"], DR, "k_a")
        rkp, krkp = self.bcast_load(d["rwkv_r_k"], DR, "r_k")
        w2, kw2 = P.tile([128, DR], F32, "w2")
        a2, ka2 = P.tile([128, DR], F32, "a2")
        P.dma("sp", w2[:96, :], d["rwkv_w2"][dr], (), (kw2,))
        P.dma("sp", a2[:96, :], d["rwkv_a2"][dr], (), (ka2,))
        tri, ktri = P.tile([128, 128], F32, "tri")
        P.dma("sp", tri, d["c_masks"][dr], (), (ktri,))
        ones, kones = P.tile([128, 128], F32, "ones")
        P.memset("pool", ones, 1.0, (kones,))
        m4, km4 = P.tile([128, 4, 128], F32, "mask4")
        for q in range(4):
            P.dma("sp", m4[:, q, :], d["c_masks"][2 + 2 * dr + (q % 2)], (), (km4,))
        suT, ksuT = P.tile([128, 128], F32, "suT")
        P.dma("sp", suT, d["c_masks"][2 + 2 * (1 - dr)], (), (ksuT,))
        if dr == 1:
            lg, klg = self.bcast_load(d["rwkv_lnx_g"], DR, "lnxg")
            lb, klb = self.bcast_load(d["rwkv_lnx_b"], DR, "lnxb")
        Sf, kSf = P.tile([128, 8, 64], F32, "Sf")
        Sb, kSb = P.tile([128, 8, 64], BF16, "Sb")
        P.memset("pool", Sf, 0.0, (kSf,))
        P.memset("pool", Sb, 0.0, (kSb,))
        kSfh = [[kSf + "/%d/%d" % (j, h) for h in range(2)] for j in range(8)]
        kSbh = [[kSb + "/%d/%d" % (j, h) for h in range(2)] for j in range(8)]
        for j in range(8):
            for h in range(2):
                P.last_w[kSfh[j][h]] = P.last_w[kSf]
                P.last_w[kSbh[j][h]] = P.last_w[kSb]
        pt = [P.tile([128, NSH], F32, "p%d" % i) for i in range(1)]
        pn = [P.tile([128, NSH], F32, "pn%d" % i) for i in range(1)]
        f, kf = P.tile([128, NSH], F32, "f")
        T = lambda nm, dt=F32: P.tile([128, DR], dt, nm)
        sg, ksg = T("sg"); av, kav = T("a"); cum, kcum = T("cum"); e1, ke1 = T("e1"); e2, ke2 = T("e2")
        kk, kkk = T("kk"); kq, kkq = T("kq"); ka_, kka_ = T("ka"); tmp, ktmp = T("tmp"); bon, kbon = T("bon")
        Y, kY = T("Y")
        At, kAt = T("At", BF16); Bt, kBt = T("Bt", BF16); Kt, kKt = T("Kt", BF16); Rt, kRt = T("Rt", BF16)
        BW, kBW = T("BW", BF16); KW, kKW = T("KW", BF16); Vb, kVb = T("Vb", BF16)
        ART, kART = P.tile([128, 8, 256], BF16, "ART")
        BTt, kBTt = P.tile([128, 8, 128], BF16, "BT")
        KTt, kKTt = P.tile([128, 8, 128], BF16, "KT")
        thT, kthT = P.tile([128, 2, 128], F32, "thT")
        th, kth = P.tile([128, 192], F32, "th")
        sm, ksm = P.tile([128, 4, 16], F32, "small")
        wc, kwc = P.tile([128, 8], F32, "wc")
        if dr == 1:
            yf, kyf = T("yf"); gt, kgt = T("gt"); ob, kob = T("ob", BF16)
            oT, koT = P.tile([128, 8, 128], BF16, "oT")
        U_ = []
        for h in range(2):
            U_.append(dict(
                M4=P.tile([128, 4, 128], BF16, "M4"), NT=P.tile([128, 128], BF16, "NT"),
                T=[P.tile([128, 128], BF16, "T%d" % i) for i in range(2)],
                PP=[P.tile([128, 2, 128], BF16, "PP%d" % i) for i in range(2)],
                X=P.tile([128, 64], BF16, "X"), U=P.tile([128, 64], BF16, "U")))
        order = list(range(s.ntile)) if dr == 0 else list(range(s.ntile - 1, -1, -1))
        ci = 0
        tctr = [0]
        for ti in order:
            t0, n = s.tile_range(ti)
            p, kp = pt[0]
            q_, kq_ = pn[0]
            ci += 1
            P.dma("sp", p[:n, :], Pd[t0:t0 + n, :], (), (kp,))
            if dr == 0:
                if t0 == 0:
                    P.dma("sp", q_[0:1, :], d["c_zero"], (), (kq_,))
                    P.dma("sp", q_[1:n, :], Pd[0:n - 1, :], (), (kq_,))
                else:
                    P.dma("sp", q_[:n, :], Pd[t0 - 1:t0 - 1 + n, :], (), (kq_,))
            else:
                if t0 + n == L:
                    P.dma("sp", q_[n - 1:n, :], d["c_zero"], (), (kq_,))
                    P.dma("sp", q_[0:n - 1, :], Pd[t0 + 1:t0 + n, :], (), (kq_,))
                else:
                    P.dma("sp", q_[:n, :], Pd[t0 + 1:t0 + 1 + n, :], (), (kq_,))
            P.tt("dve", q_[:n, :], q_[:n, :], p[:n, :], ALU.subtract, (kq_, kp), (kq_,))
            P.tt("pool", q_[:n, :], q_[:n, :], mu[:n, :], ALU.mult, (kq_, kmu), (kq_,))
            P.tt("dve", f[:n, :], q_[:n, :], p[:n, :], ALU.add, (kq_, kp), (kf,))
            r_ = f[:, 0:DR]; k_ = f[:, DR:2 * DR]; v_ = f[:, 2 * DR:3 * DR]
            P.act(th[:n, 0:96], f[:n, 3 * DR:3 * DR + 96], AF.Tanh, (kf,), (kth,))
            P.copy("pool", th[:n, 96:192], f[:n, 3 * DR + 96:3 * DR + 192], (kf,), (kth,))
            pb0 = P.bank(0)
            for q in range(2):
                P.tr(pb0[:96, q * 128:q * 128 + n], th[:n, q * 96:(q + 1) * 96], self.ident_f[:n, :n],
                     (kth, self.kid[0]), ("psb0",))
            P.copy("act", thT[:96, :, :n], pb0[:96, 0:256].rearrange("p (a b) -> p a b", a=2)[:, :, :n], ("psb0",), (kthT,))
            for which, (wm, kwm, bias, kbias, dst, kdst) in enumerate(((w2, kw2, w0, kw0, sg, ksg), (a2, ka2, a0, ka0, av, kav))):
                for hf in range(2):
                    pb = P.bank(hf); kpb = "psb%d" % hf
                    P.mm(pb[:n, :], thT[:96, which, :n], wm[:96, hf * 512:(hf + 1) * 512], True, True, (kthT, kwm), (kpb,))
                    P.tt("dve", dst[:n, hf * 512:(hf + 1) * 512], pb[:n, :], bias[:n, hf * 512:(hf + 1) * 512], ALU.add,
                         (kpb, kbias), (kdst,))
                P.act(dst[:n, :], dst[:n, :], AF.Sigmoid, (kdst,), (kdst,))
            if self.debug and dr == 1:
                P.dma("sp", d["DF_" + n_][t0:t0 + n, :], f[:n, :], (kf,), ())
                P.dma("sp", d["DA_" + n_][t0:t0 + n, :], av[:n, :], (kav,), ())
            for hf in range(2):
                pb = P.bank(hf); kpb = "psb%d" % hf
                P.mm(pb[:n, :], tri[:n, :n], sg[:n, hf * 512:(hf + 1) * 512], True, True, (ktri, ksg), (kpb,))
                P.copy("act", cum[:n, hf * 512:(hf + 1) * 512], pb[:n, :], (kpb,), (kcum,))
            for hf in range(2):
                pb = P.bank(2 + hf); kpb = "psb%d" % (2 + hf)
                P.mm(pb[:n, :], ones[:n, :n], sg[:n, hf * 512:(hf + 1) * 512], True, True, (kones, ksg), (kpb,))
                P.tt("dve", e2[:n, hf * 512:(hf + 1) * 512], pb[:n, :], cum[:n, hf * 512:(hf + 1) * 512], ALU.subtract,
                     (kpb, kcum), (ke2,))
            pb0 = P.bank(0)
            for j in range(8):
                P.mm(pb0[:, j:j + 1], sg[:n, j * 128:(j + 1) * 128], ones[:n, 0:1], True, True, (ksg, kones), ("psb0",))
            P.act(wc[:, :], pb0[:, 0:8], AF.Exp, ("psb0",), (kwc,), scale=CDEC)
            P.tt("pool", kk[:n, :], k_[:n, :], kkp[:n, :], ALU.mult, (kf, kkkp), (kkk,))
            P.tt("pool", tmp[:n, :], kk[:n, :], kk[:n, :], ALU.mult, (kkk,), (ktmp,))
            v3 = lambda ap: ap.rearrange("p (h c) -> p h c", h=16)
            P.reduce("dve", sm[:n, 0, :], v3(tmp[:n, :]), ALU.add, (ktmp,), (ksm,))
            P.act(sm[:n, 0, :], sm[:n, 0, :], AF.Sqrt, (ksm,), (ksm,))
            P.ts("dve", sm[:n, 0, :], sm[:n, 0, :], 1e-12, None, ALU.max, None, (ksm,), (ksm,))
            P.recip("dve", sm[:n, 0, :], sm[:n, 0, :], (ksm,), (ksm,))
            P.tt("dve", v3(kk[:n, :]), v3(kk[:n, :]), sm[:n, 0, :].unsqueeze(2).to_broadcast([n, 16, 64]), ALU.mult,
                 (kkk, ksm), (kkk,))
            P.stt("dve", kq[:n, :], av[:n, :], -1.0, kap[:n, :], ALU.add, ALU.mult, (kav, kkap), (kkq,))
            P.stt("dve", kq[:n, :], kq[:n, :], 1.0, k_[:n, :], ALU.add, ALU.mult, (kkq, kf), (kkq,))
            P.tt("pool", ka_[:n, :], kk[:n, :], av[:n, :], ALU.mult, (kkk, kav), (kka_,))
            P.tt("pool", tmp[:n, :], r_[:n, :], kq[:n, :], ALU.mult, (kf, kkq), (ktmp,))
            P.tt("pool", tmp[:n, :], tmp[:n, :], rkp[:n, :], ALU.mult, (ktmp, krkp), (ktmp,))
            P.reduce("dve", sm[:n, 1, :], v3(tmp[:n, :]), ALU.add, (ktmp,), (ksm,))
            P.tt("dve", v3(bon[:n, :]), v3(v_[:n, :]), sm[:n, 1, :].unsqueeze(2).to_broadcast([n, 16, 64]), ALU.mult,
                 (kf, ksm), (kbon,))
            if self.debug and dr == 1:
                P.dma("sp", d["DK_" + n_][t0:t0 + n, :], kq[:n, :], (kkq,), ())
                P.dma("sp", d["DT_" + n_][t0:t0 + n, :], tmp[:n, :], (ktmp,), ())
                if ti == 3:
                    P.dma("sp", d["DP_" + n_][0], kap, (kkap,), ())
                    P.dma("sp", d["DP_" + n_][1], rkp, (krkp,), ())
                    P.dma("sp", d["DP_" + n_][2], kkp, (kkkp,), ())
            if dr == 0:
                P.dma("sp", d["BF_" + n_][t0:t0 + n, :], bon[:n, :], (kbon,), ())
            P.act(e1[:n, :], cum[:n, :], AF.Exp, (kcum,), (ke1,), scale=CDEC)
            P.tt("dve", Rt[:n, :], r_[:n, :], e1[:n, :], ALU.mult, (kf, ke1), (kRt,))
            P.act(e1[:n, :], cum[:n, :], AF.Exp, (kcum, kRt), (ke1,), scale=-CDEC)
            P.tt("pool", Bt[:n, :], ka_[:n, :], e1[:n, :], ALU.mult, (kka_, ke1), (kBt,))
            P.tt("dve", Kt[:n, :], kq[:n, :], e1[:n, :], ALU.mult, (kkq, ke1), (kKt,))
            P.act(e2[:n, :], e2[:n, :], AF.Exp, (ke2,), (ke2,), scale=CDEC)
            P.tt("pool", BW[:n, :], ka_[:n, :], e2[:n, :], ALU.mult, (kka_, ke2), (kBW,))
            P.tt("dve", KW[:n, :], kq[:n, :], e2[:n, :], ALU.mult, (kkq, ke2), (kKW,))
            P.tt("dve", cum[:n, :], cum[:n, :], sg[:n, :], ALU.subtract, (kcum, ksg, kBt, kKt), (kcum,))
            P.act(e1[:n, :], cum[:n, :], AF.Exp, (kcum, kBt, kKt), (ke1,), scale=CDEC)
            P.stt("dve", At[:n, :], kk[:n, :], -1.0, e1[:n, :], ALU.mult, ALU.mult, (kkk, ke1), (kAt,))
            P.copy("pool", Vb[:n, :], v_[:n, :], (kf,), (kVb,))
            for xi, (src, ksrc) in enumerate(((At, kAt), (Rt, kRt), (Bt, kBt), (Kt, kKt))):
                bi = tctr[0] % 4
                tctr[0] += 1
                ptb = P.bank(bi, BF16)
                kptb = "psb%d" % bi
                for j in range(8):
                    P.tr(ptb[:, j * 128:j * 128 + n], src[:n, j * 128:(j + 1) * 128], self.ident_b[:n, :n],
                         (ksrc, self.kid[1]), (kptb,))
                srcv = ptb.rearrange("p (a b) -> p a b", a=8)[:, :, :n]
                if xi == 0:
                    P.copy("act", ART[:, :, 0:n], srcv, (kptb,), (kART,))
                elif xi == 1:
                    P.copy("dve", ART[:, :, 128:128 + n], srcv, (kptb,), (kART,))
                elif xi == 2:
                    P.copy("act", BTt[:, :, :n], srcv, (kptb,), (kBTt,))
                else:
                    P.copy("dve", KTt[:, :, :n], srcv, (kptb,), (kKTt,))
            nsq = int(np.log2(n)) - 1
            for j in range(8):
                hs = []
                for h in range(2):
                    hs.append(dict(hb=64 * h, col=(2 * j + h) * 64, u=U_[h], bx=4 + 2 * h, by=5 + 2 * h,
                                   kS=kSfh[j][h], kSb=kSbh[j][h]))
                kx = lambda H: "psb%d" % H["bx"]
                ky = lambda H: "psb%d" % H["by"]
                for H in hs:
                    hb = H["hb"]; u = H["u"]
                    by_ = P.bank(H["by"]); bx_ = P.bank(H["bx"])
                    P.mm(by_[:n, 0:256], BTt[hb:hb + 64, j, :n], ART[hb:hb + 64, j, :], True, True, (kBTt, kART), (ky(H),))
                    P.mm(by_[:n, 256:512], KTt[hb:hb + 64, j, :n], ART[hb:hb + 64, j, :], True, True, (kKTt, kART), (ky(H),))
                    P.mm(bx_[:n, 0:n], ART[hb:hb + 64, j, 0:n], BTt[hb:hb + 64, j, :n], True, True, (kART, kBTt), (kx(H),))
                for H in hs:
                    u = H["u"]
                    by_ = P.bank(H["by"]); bx_ = P.bank(H["bx"])
                    M4, kM4 = u["M4"]
                    P.tt("dve", M4[:n, :, :n], by_[:n, :].rearrange("p (a b) -> p a b", a=4)[:, :, :n], m4[:n, :, :n], ALU.mult,
                         (ky(H), km4), (kM4,))
                    NT, kNT = u["NT"]
                    P.tt("dve", NT[:n, :n], bx_[:n, 0:n], suT[:n, :n], ALU.mult, (kx(H), ksuT), (kNT,))
                    T0, kT0 = u["T"][0]
                    P.tt("pool", T0[:n, :n], M4[:n, 0, :n], self.ident_b[:n, :n], ALU.add, (kM4, self.kid[1]), (kT0,))
                for H in hs:
                    H["Pc"] = (H["u"]["M4"][0][:, 0, :], H["u"]["M4"][1])
                    H["PTc"] = H["u"]["NT"]
                    H["Tc"] = 0
                for it in range(1, nsq + 1):
                    last = it == nsq
                    for H in hs:
                        bx_ = P.bank(H["bx"])
                        Pc, kPc = H["Pc"]; PTc, kPTc = H["PTc"]
                        P.mm(bx_[:n, 128:128 + n], Pc[:n, :n], PTc[:n, :n], True, True, (kPc, kPTc), (kx(H),))
                        if not last:
                            P.mm(bx_[:n, 0:n], PTc[:n, :n], Pc[:n, :n], True, True, (kPc, kPTc), (kx(H),))
                    for H in hs:
                        bx_ = P.bank(H["bx"])
                        PP, kPP = H["u"]["PP"][it % 2]
                        if not last:
                            P.copy("act", PP[:n, :, :n], bx_[:n, 0:256].rearrange("p (a b) -> p a b", a=2)[:, :, :n], (kx(H),), (kPP,))
                        else:
                            P.copy("act", PP[:n, 1, :n], bx_[:n, 128:128 + n], (kx(H),), (kPP,))
                        H["Pc"] = (PP[:, 0, :], kPP)
                        H["PTc"] = (PP[:, 1, :], kPP)
                    for H in hs:
                        by_ = P.bank(H["by"])
                        Tc, kTc = H["u"]["T"][H["Tc"]]
                        PTc, kPTc = H["PTc"]
                        P.mm(by_[:n, 0:n], PTc[:n, :n], Tc[:n, :n], True, True, (kPTc, kTc), (ky(H),))
                    for H in hs:
                        by_ = P.bank(H["by"])
                        Tc, kTc = H["u"]["T"][H["Tc"]]
                        Tn, kTn = H["u"]["T"][1 - H["Tc"]]
                        P.tt("dve", Tn[:n, :n], by_[:n, 0:n], Tc[:n, :n], ALU.add, (ky(H), kTc), (kTn,))
                        H["Tc"] = 1 - H["Tc"]
                for H in hs:
                    hb = H["hb"]; col = H["col"]; u = H["u"]
                    bx_ = P.bank(H["bx"])
                    M4, kM4 = u["M4"]
                    P.mm(bx_[:n, 256:320], ART[hb:hb + 64, j, 0:n], Sb[hb:hb + 64, j, :], True, False, (kART, H["kSb"]), (kx(H),))
                    P.mm(bx_[:n, 256:320], M4[:n, 2, :n], Vb[:n, col:col + 64], False, True, (kM4, kVb), (kx(H),))
                for H in hs:
                    X, kX = H["u"]["X"]
                    P.copy("act", X[:n, :], P.bank(H["bx"])[:n, 256:320], (kx(H),), (kX,))
                for H in hs:
                    X, kX = H["u"]["X"]
                    Tc, kTc = H["u"]["T"][H["Tc"]]
                    P.mm(P.bank(H["bx"])[:n, 320:384], Tc[:n, :n], X[:n, :], True, True, (kTc, kX), (kx(H),))
                for H in hs:
                    Uu, kU = H["u"]["U"]
                    P.copy("act", Uu[:n, :], P.bank(H["bx"])[:n, 320:384], (kx(H),), (kU,))
                for H in hs:
                    hb = H["hb"]; col = H["col"]; u = H["u"]
                    M4, kM4 = u["M4"]; Uu, kU = u["U"]
                    bx_ = P.bank(H["bx"]); by_ = P.bank(H["by"])
                    P.mm(bx_[:n, 384:448], ART[hb:hb + 64, j, 128:128 + n], Sb[hb:hb + 64, j, :], True, False, (kART, H["kSb"]), (kx(H),))
                    P.mm(bx_[:n, 384:448], M4[:n, 1, :n], Uu[:n, :], False, False, (kM4, kU), (kx(H),))
                    P.mm(bx_[:n, 384:448], M4[:n, 3, :n], Vb[:n, col:col + 64], False, True, (kM4, kVb), (kx(H),))
                    P.mm(by_[hb:hb + 64, 128:192], BW[:n, col:col + 64], Uu[:n, :], True, False, (kBW, kU), (ky(H),))
                    P.mm(by_[hb:hb + 64, 128:192], KW[:n, col:col + 64], Vb[:n, col:col + 64], False, True, (kKW, kVb), (ky(H),))
                for H in hs:
                    hb = H["hb"]; col = H["col"]
                    P.copy("act", Y[:n, col:col + 64], P.bank(H["bx"])[:n, 384:448], (kx(H),), (kY,))
                    P.stt("dve", Sf[hb:hb + 64, j, :], Sf[hb:hb + 64, j, :], wc[hb:hb + 64, j:j + 1],
                          P.bank(H["by"])[hb:hb + 64, 128:192], ALU.mult, ALU.add, (H["kS"], kwc, ky(H)), (H["kS"],))
                    P.copy("pool", Sb[hb:hb + 64, j, :], Sf[hb:hb + 64, j, :], (H["kS"],), (H["kSb"],))
            if dr == 0:
                P.dma("sp", d["YF_" + n_][t0:t0 + n, :], Y[:n, :], (kY,), ())
            else:
                if self.debug:
                    P.dma("sp", d["YB_" + n_][t0:t0 + n, :], Y[:n, :], (kY,), ())
                    P.dma("sp", d["BB_" + n_][t0:t0 + n, :], bon[:n, :], (kbon,), ())
                P.dma("sp", yf[:n, :], d["YF_" + n_][t0:t0 + n, :], (), (kyf,))
                P.dma("sp", gt[:n, :], d["G_" + n_][t0:t0 + n, :], (), (kgt,))
                P.tt("dve", Y[:n, :], Y[:n, :], yf[:n, :], ALU.add, (kY, kyf), (kY,))
                P.dma("sp", yf[:n, :], d["BF_" + n_][t0:t0 + n, :], (kY,), (kyf,))
                P.reduce("dve", sm[:n, 2, :], v3(Y[:n, :]), ALU.add, (kY,), (ksm,))
                P.tt("pool", tmp[:n, :], Y[:n, :], Y[:n, :], ALU.mult, (kY,), (ktmp,))
                P.reduce("dve", sm[:n, 3, :], v3(tmp[:n, :]), ALU.add, (ktmp,), (ksm,))
                P.ts("dve", sm[:n, 2, :], sm[:n, 2, :], 1.0 / 64, None, ALU.mult, None, (ksm,), (ksm,))
                P.ts("dve", sm[:n, 3, :], sm[:n, 3, :], 1.0 / 64, None, ALU.mult, None, (ksm,), (ksm,))
                P.tt("dve", sm[:n, 0, :], sm[:n, 2, :], sm[:n, 2, :], ALU.mult, (ksm,), (ksm,))
                P.tt("dve", sm[:n, 3, :], sm[:n, 3, :], sm[:n, 0, :], ALU.subtract, (ksm,), (ksm,))
                P.act(sm[:n, 3, :], sm[:n, 3, :], AF.Sqrt, (ksm,), (ksm,), bias=GN_EPS)
                P.recip("dve", sm[:n, 3, :], sm[:n, 3, :], (ksm,), (ksm,))
                bc = lambda q: sm[:n, q, :].unsqueeze(2).to_broadcast([n, 16, 64])
                P.tt("dve", v3(Y[:n, :]), v3(Y[:n, :]), bc(2), ALU.subtract, (kY, ksm), (kY,))
                P.tt("dve", v3(Y[:n, :]), v3(Y[:n, :]), bc(3), ALU.mult, (kY, ksm), (kY,))
                P.tt("pool", Y[:n, :], Y[:n, :], lg[:n, :], ALU.mult, (kY, klg), (kY,))
                P.tt("pool", Y[:n, :], Y[:n, :], lb[:n, :], ALU.add, (kY, klb), (kY,))
                P.tt("dve", Y[:n, :], Y[:n, :], bon[:n, :], ALU.add, (kY, kbon), (kY,))
                P.tt("dve", Y[:n, :], Y[:n, :], yf[:n, :], ALU.add, (kY, kyf), (kY,))
                P.tt("dve", ob[:n, :], Y[:n, :], gt[:n, :], ALU.mult, (kY, kgt), (kob,))
                bi = tctr[0] % 4
                tctr[0] += 1
                ptb = P.bank(bi, BF16)
                kptb = "psb%d" % bi
                for j in range(8):
                    P.tr(ptb[:, j * 128:j * 128 + n], ob[:n, j * 128:(j + 1) * 128], self.ident_b[:n, :n],
                         (kob, self.kid[1]), (kptb,))
                P.copy("act", oT[:, :, :n], ptb.rearrange("p (a b) -> p a b", a=8)[:, :, :n], (kptb,), (koT,))
                P.dma("sp", d["OT_" + n_][8:16, :, t0:t0 + n].rearrange("c p t -> p c t"), oT[:, :, :n], (koT,), ())

    def stage3(self, s):
        P = self.P
        d = self.dram
        n_ = s.name
        L = s.L
        P.barrier()
        P.reset_arena(self.const_end)
        kth = [P.tile([128, L], BF16, "kth%d" % i) for i in range(2)]
        qth = [P.tile([128, L], BF16, "qth%d" % i) for i in range(2)]
        vh = [P.tile([128, s.ntile, 128], BF16, "vh%d" % i) for i in range(2)]
        bt = [P.tile([128, 5, NKEY], F32, "bias%d" % i) for i in range(2)]
        oTh = [P.tile([128, L], BF16, "oTh%d" % i) for i in range(2)]
        sc = [P.tile([128, NKEY], F32, "sc%d" % i) for i in range(2)]
        pr = [P.tile([128, NKEY], BF16, "pr%d" % i) for i in range(2)]
        pT = [P.tile([128, 6, 128], BF16, "pT%d" % i) for i in range(2)]
        ob = [P.tile([128, 128], BF16, "ob%d" % i) for i in range(2)]
        st = [P.tile([128, 4], F32, "st%d" % i) for i in range(4)]
        uc = 0
        for h in range(HA):
            kt, kkt = kth[h % 2]; qt, kqt = qth[h % 2]; v, kv = vh[h % 2]; b, kb = bt[h % 2]; oh, koh = oTh[h % 2]
            P.dma("sp", kt, d["KT_" + n_][h], (), (kkt,))
            P.dma("sp", qt, d["QT_" + n_][h], (), (kqt,))
            P.dma("sp", v[:NM, 0, :], d["V_" + n_][0:NM, h * 128:(h + 1) * 128], (), (kv,))
            for i0 in range(1, s.ntile, 16):
                i1 = min(s.ntile, i0 + 16)
                P.dma("sp", v[:, i0:i1, :],
                      d["V_" + n_][NM + (i0 - 1) * 128:NM + (i1 - 1) * 128, h * 128:(h + 1) * 128].rearrange("(i p) c -> p i c", p=128),
                      (), (kv,))
            P.dma("sp", b, d["bias_tab"][h].rearrange("c q k -> q c k"), (), (kb,))
            units = [("meta", 0)] + [("grid", rp) for rp in range(s.rows // 2)]
            for kind, rp in units:
                u = uc
                uc += 1
                s_, ks_ = sc[u % 2]; p_, kp_ = pr[u % 2]; pt_, kpt_ = pT[u % 2]; o_, ko_ = ob[u % 2]; st_, kst_ = st[u % 4]
                ba = (u % 2) * 2
                pa = P.bank(ba); pbk = P.bank(ba + 1)
                kpa = "psb%d" % ba; kpb = "psb%d" % (ba + 1)
                if kind == "meta":
                    nq = NM
                    q0 = 0
                    P.mm(pbk[:nq, 128:144], qt[:, 0:NM], kt[:, 0:NM], True, True, (kqt, kkt), (kpb,))
                    P.copy("dve", s_[:nq, 640:656], pbk[:nq, 128:144], (kpb,), (ks_,))
                    lo = 640
                    blocks = [(5, NM, 0)]
                else:
                    nq = 128
                    r = 2 * rp
                    ws = min(max(r - 4, 0), s.rows - WINR)
                    assert ws % 2 == 0
                    cls = (r - ws) // 2
                    q0 = NM + rp * 128
                    k0 = NM + ws * 64
                    P.mm(pa[:, :], qt[:, q0:q0 + 128], kt[:, k0:k0 + 512], True, True, (kqt, kkt), (kpa,))
                    P.mm(pbk[:, 0:128], qt[:, q0:q0 + 128], kt[:, k0 + 512:k0 + 640], True, True, (kqt, kkt), (kpb,))
                    P.mm(pbk[:, 128:144], qt[:, q0:q0 + 128], kt[:, 0:NM], True, True, (kqt, kkt), (kpb,))
                    P.tt("dve", s_[:, 0:512], pa[:, :], b[:, cls, 0:512], ALU.add, (kpa, kb), (ks_,))
                    P.tt("dve", s_[:, 512:656], pbk[:, 0:144], b[:, cls, 512:656], ALU.add, (kpb, kb), (ks_,))
                    lo = 0
                    blocks = [(j, 128, ws // 2 + 1 + j) for j in range(5)] + [(5, NM, 0)]
                P.op("dve", lambda e, o=st_[:nq, 0:1], i=s_[:nq, lo:656]: e.reduce_max(o, i, AX.X), (ks_,), (kst_,))
                P.ts("dve", st_[:nq, 1:2], st_[:nq, 0:1], -1.0, None, ALU.mult, None, (kst_,), (kst_,))
                P.act(p_[:nq, lo:656], s_[:nq, lo:656], AF.Exp, (ks_, kst_), (kp_, kst_), bias=st_[:nq, 1:2], accum=st_[:nq, 2:3])
                P.recip("dve", st_[:nq, 3:4], st_[:nq, 2:3], (kst_,), (kst_,))
                tb = 4 + (u % 2)
                ptb = P.bank(tb, BF16)
                kptb = "psb%d" % tb
                for (j, nk, vt) in blocks:
                    P.tr(ptb[:nk, j * 128:j * 128 + nq], p_[:nq, j * 128:j * 128 + nk], self.ident_b[:nq, :nq],
                         (kp_, self.kid[1]), (kptb,))
                j0 = blocks[0][0]
                if kind == "meta":
                    P.copy("act", pt_[:NM, 5, :nq], ptb[:NM, 640:640 + nq], (kptb,), (kpt_,))
                else:
                    P.copy("act", pt_[:, :, :], ptb[:, 0:768].rearrange("p (a b) -> p a b", a=6), (kptb,), (kpt_,))
                po = P.bank(6)
                for bi, (j, nk, vt) in enumerate(blocks):
                    P.mm(po[:nq, 0:128], pt_[:nk, j, :nq], v[:nk, vt, :], bi == 0, bi == len(blocks) - 1, (kpt_, kv), ("psb6",))
                P.act(o_[:nq, :], po[:nq, 0:128], AF.Copy, ("psb6", kst_), (ko_,), scale=st_[:nq, 3:4])
                pot = P.bank(7, BF16)
                P.tr(pot[:, 0:nq], o_[:nq, :], self.ident_b[:nq, :nq], (ko_, self.kid[1]), ("psb7",))
                P.copy("dve", oh[:, q0:q0 + nq], pot[:, 0:nq], ("psb7",), (koh,))
            P.dma("sp", d["OT_" + n_][h], oh, (koh,), ())

    def stage4(self, s):
        P = self.P
        d = self.dram
        n_ = s.name
        P.barrier()
        P.reset_arena(self.const_end)
        g1, kg = self.bcast_load(d["ln1_g"], D, "g1")
        b1, kb = self.bcast_load(d["ln1_b"], D, "b1")
        tmp = self.ln_tmp()
        wo, kwo = P.tile([128, 16, D], BF16, "wo")
        for c0 in range(0, D, 512):
            self.load_w(wo[:, :, c0:c0 + 512], kwo, d["w_out"], D, c0, 512)
        oT = [P.tile([128, 16, 128], BF16, "oT%d" % i) for i in range(2)]
        h0 = [P.tile([128, D], F32, "h0%d" % i) for i in range(2)]
        h1 = [P.tile([128, D], F32, "h1%d" % i) for i in range(2)]
        h1T = [P.tile([128, 16, 128], BF16, "h1T%d" % i) for i in range(2)]
        ctr = [0]
        for ti in range(s.ntile):
            t0, n = s.tile_range(ti)
            o_, ko_ = oT[ti % 2]; x, kx = h0[ti % 2]; y, ky = h1[ti % 2]; yt, kyt = h1T[ti % 2]
            P.dma("sp", o_[:, :, :n], d["OT_" + n_][:, :, t0:t0 + n].rearrange("c p t -> p c t"), (), (ko_,))
            P.dma("sp", x[:n, :], d["H0_" + n_][t0:t0 + n, :], (), (kx,))
            for hf in range(4):
                pb = P.bank(hf); kpb = "psb%d" % hf
                for kc in range(16):
                    P.mm(pb[:n, :], o_[:, kc, :n], wo[:, kc, hf * 512:(hf + 1) * 512], kc == 0, kc == 15, (ko_, kwo), (kpb,))
                P.stt("dve", x[:n, hf * 512:(hf + 1) * 512], x[:n, hf * 512:(hf + 1) * 512], ALPHA, pb[:n, :], ALU.mult, ALU.add,
                      (kx, kpb), (kx,))
            self.layer_norm(x, kx, n, (g1, b1), (kg, kb), y, ky, LN_EPS, tmp)
            P.dma("sp", d["H1_" + n_][t0:t0 + n, :], y[:n, :], (ky,), ())
            self.transpose_to_fm(y, ky, n, yt, kyt, 0, (4, 5, 6, 7), ctr)
            P.dma("sp", d["H1T_" + n_][:, :, t0:t0 + n].rearrange("c p t -> p c t"), yt[:, :, :n], (kyt,), ())

    def stage5(self, s):
        P = self.P
        d = self.dram
        n_ = s.name
        P.barrier()
        P.reset_arena(self.const_end)
        g2_, kg = self.bcast_load(d["ln2_g"], D, "g2")
        b2_, kb = self.bcast_load(d["ln2_b"], D, "b2")
        tmp = self.ln_tmp()
        cw, kcw = P.tile([128, 44, 4], F32, "convw")
        P.dma("sp", cw, d["ffn_conv_l"], (), (kcw,))
        NB = 1024
        hT, khT = P.tile([128, 16, NB + 2], BF16, "hT")
        acc, kacc0 = P.tile([128, 8, D], F32, "acc")
        kacc = [kacc0 + "/%d" % i for i in range(8)]
        GC = 2
        w1 = [P.tile([128, 16, 2 * GC * 128], BF16, "w1_%d" % i) for i in range(2)]
        w2 = [P.tile([128, GC, D], BF16, "w2_%d" % i) for i in range(2)]
        gT = [P.tile([128, 514], F32, "gT%d" % i) for i in range(2)]
        t1 = [P.tile([128, 512], F32, "t1%d" % i) for i in range(2)]
        aT = [P.tile([128, GC, NB], BF16, "aT%d" % i) for i in range(2)]
        xr = [P.tile([128, D], F32, "xr%d" % i) for i in range(1)]
        yo = [P.tile([128, D], F32, "yo%d" % i) for i in range(1)]
        gi = 0
        hc = 0
        for b0 in range(0, s.T, NB):
            ts0 = NM + b0
            ntl = NB // 128
            lo = ts0 - 1
            hi = min(s.L, ts0 + NB + 1)
            P.dma("sp", hT[:, :, 0:hi - lo], d["H1T_" + n_][:, :, lo:hi].rearrange("c p t -> p c t"), (), (khT,))
            if hi - lo < NB + 2:
                P.memset("pool", hT[:, :, NB + 1:NB + 2], 0.0, (khT,))
            for g in range(44 // GC):
                w1_, kw1 = w1[gi % 2]; w2_, kw2 = w2[gi % 2]; a_, ka_ = aT[gi % 2]
                gi += 1
                c0 = g * GC * 128
                self.load_w(w1_[:, :, 0:GC * 128], kw1, d["ffn_w_in"], D, c0, GC * 128)
                self.load_w(w1_[:, :, GC * 128:2 * GC * 128], kw1, d["ffn_w_in"], D, DFF + c0, GC * 128)
                P.dma("pool", w2_, d["ffn_w_out"][c0:c0 + GC * 128, :].rearrange("(c p) n -> p c n", p=128), (), (kw2,))
                for cl in range(GC):
                    fc = g * GC + cl
                    for hh in range(2):
                        g_, kg_ = gT[hc % 2]; t_, kt_ = t1[hc % 2]
                        hc += 1
                        bg = 0 if hh == 0 else 3
                        pg = P.bank(bg); kpg = "psb%d" % bg
                        ph = P.bank(1); pu = P.bank(2)
                        cb = hh * 512
                        for kc in range(16):
                            P.mm(pg[:, :], w1_[:, kc, cl * 128:(cl + 1) * 128], hT[:, kc, cb:cb + 512], kc == 0, kc == 15,
                                 (kw1, khT), (kpg,))
                        for kc in range(16):
                            P.mm(ph[:, 0:2], w1_[:, kc, cl * 128:(cl + 1) * 128], hT[:, kc, cb + 512:cb + 514], kc == 0, kc == 15,
                                 (kw1, khT), ("psb1",))
                        for kc in range(16):
                            P.mm(pu[:, :], w1_[:, kc, (GC + cl) * 128:(GC + cl + 1) * 128], hT[:, kc, cb + 1:cb + 513],
                                 kc == 0, kc == 15, (kw1, khT), ("psb2",))
                        P.copy("act", g_[:, 0:512], pg[:, :], (kpg,), (kg_,))
                        P.copy("act", g_[:, 512:514], ph[:, 0:2], ("psb1",), (kg_,))
                        P.act(t_[:, :], g_[:, 1:513], AF.Identity, (kg_, kcw), (kt_,), bias=cw[:, fc, 3:4], scale=cw[:, fc, 1:2])
                        P.stt("dve", t_[:, :], g_[:, 0:512], cw[:, fc, 0:1], t_[:, :], ALU.mult, ALU.add, (kg_, kcw, kt_), (kt_,))
                        P.stt("dve", t_[:, :], g_[:, 2:514], cw[:, fc, 2:3], t_[:, :], ALU.mult, ALU.add, (kg_, kcw, kt_), (kt_,))
                        P.act(t_[:, :], t_[:, :], AF.Gelu, (kt_,), (kt_,))
                        P.tt("dve", a_[:, cl, cb:cb + 512], t_[:, :], pu[:, :], ALU.mult, (kt_, "psb2"), (ka_,))
                for tl in range(ntl):
                    for hf in range(4):
                        pb = P.bank(4 + hf); kpb = "psb%d" % (4 + hf)
                        for cl in range(GC):
                            P.mm(pb[:, :], a_[:, cl, tl * 128:(tl + 1) * 128], w2_[:, cl, hf * 512:(hf + 1) * 512],
                                 cl == 0, cl == GC - 1, (ka_, kw2), (kpb,))
                        dst = acc[:, tl, hf * 512:(hf + 1) * 512]
                        if g == 0:
                            P.copy("dve" if hf % 2 else "act", dst, pb[:, :], (kpb,), (kacc[tl],))
                        else:
                            P.tt("dve", dst, dst, pb[:, :], ALU.add, (kpb, kacc[tl]), (kacc[tl],))
            for tl in range(ntl):
                x, kx = xr[0]; y, ky = yo[0]
                tq = ts0 + tl * 128
                P.dma("sp", x, d["H1_" + n_][tq:tq + 128, :], (), (kx,))
                P.stt("dve", x, x, ALPHA, acc[:, tl, :], ALU.mult, ALU.add, (kx, kacc[tl]), (kx,))
                self.layer_norm(x, kx, 128, (g2_, b2_), (kg, kb), y, ky, LN_EPS, tmp)
                P.dma("sp", d["y_" + n_][tq - NM:tq - NM + 128, :], y, (ky,), ())

    def build(self):
        self.setup_consts()
        for s in self.seqs:
            if 1 in self.run:
                self.stage1(s)
            if 2 in self.run:
                self.stage2(s)
            if 3 in self.run:
                self.stage3(s)
            if 4 in self.run:
                self.stage4(s)
            if 5 in self.run:
                self.stage5(s)
        self.P.finish()
        return self.nc


def _bias_table(rpb):
    rpb = np.asarray(rpb, np.float32).reshape(HA, 15, 31)
    tab = np.full((HA, 5, 128, NKEY), NEG, np.float32)
    qc = np.arange(64)
    c0 = np.clip(qc - 8, 0, 48)
    kc = np.arange(64)
    colmask = (kc[None, :] >= c0[:, None]) & (kc[None, :] < c0[:, None] + 16)
    dc = np.clip(kc[None, :] - qc[:, None], -15, 15) + 15
    rel = {0: (0, 0), 1: (0, 0), 2: (0, 1), 3: (2, 2), 4: (2, 2)}
    for c in range(5):
        for qr in range(2):
            r_rel = 2 * c + qr
            r0 = rel[c][qr]
            for j in range(8):
                krow = r0 + j
                dr = krow - r_rel + 7
                blk = rpb[:, dr][:, dc]
                blk = np.where(colmask[None], blk, NEG)
                tab[:, c, qr * 64:(qr + 1) * 64, krow * 64:(krow + 1) * 64] = blk
        tab[:, c, :, WINR * 64:] = 0.0
    return tab


def _consts():
    ident = np.eye(128, dtype=np.float32)
    s = np.arange(128)[:, None]
    t = np.arange(128)[None, :]
    masks = np.stack([(s <= t), (s >= t), (s < t), (s <= t), (s > t), (s >= t)]).astype(np.float32)
    return ident, masks


def _common_inputs(inp):
    ident, masks = _consts()
    f = lambda a: np.ascontiguousarray(np.asarray(a, np.float32))
    m = {
        "meta_tokens": f(inp["meta_tokens"]),
        "emb_ln_g": f(inp["emb_ln_g"]), "emb_ln_b": f(inp["emb_ln_b"]),
        "ln1_g": f(inp["ln1_g"][0]), "ln1_b": f(inp["ln1_b"][0]),
        "ln2_g": f(inp["ln2_g"][0]), "ln2_b": f(inp["ln2_b"][0]),
        "w_in": f(inp["w_in"][0]),
        "bias_tab": _bias_table(inp["attn_rpb"][0]),
        "rwkv_mu": f(inp["rwkv_mu"][0]), "rwkv_w0": f(inp["rwkv_w0"][0]), "rwkv_w2": f(inp["rwkv_w2"][0]),
        "rwkv_a0": f(inp["rwkv_a0"][0]), "rwkv_a2": f(inp["rwkv_a2"][0]), "rwkv_g2": f(inp["rwkv_g2"][0]),
        "rwkv_k_k": f(inp["rwkv_k_k"][0]), "rwkv_k_a": f(inp["rwkv_k_a"][0]),
        "rwkv_r_k": f(inp["rwkv_r_k"][0]).reshape(DR),
        "rwkv_lnx_g": f(inp["rwkv_lnx_g"][0]), "rwkv_lnx_b": f(inp["rwkv_lnx_b"][0]),
        "w_out": f(inp["w_out"][0]), "ffn_w_in": f(inp["ffn_w_in"][0]),
        "ffn_conv_l": np.ascontiguousarray(np.concatenate([f(inp["ffn_conv_w"][0]), f(inp["ffn_conv_b"][0])[None]], 0)
                                           .reshape(4, 44, 128).transpose(2, 1, 0)),
        "ffn_w_out": f(inp["ffn_w_out"][0]),
        "c_ident": ident, "c_masks": masks, "c_zero": np.zeros((1, NSH), np.float32),
    }
    return m


def kernel(**inputs):
    xp = np.asarray(inputs["x_prompt"], np.float32)
    xs = np.asarray(inputs["x_sample"], np.float32)
    b = Builder([xs.shape[1], xp.shape[1]])
    nc = b.build()
    common = _common_inputs(inputs)
    in_maps = []
    for c in range(8):
        m = dict(common)
        m["x_s0"] = np.ascontiguousarray(xs[c])
        m["x_s1"] = np.ascontiguousarray(xp[0])
        in_maps.append(m)
    res = run_bass_kernel_spmd(nc, in_maps, core_ids=list(range(8)))
    y_s = np.stack([res.results[c]["y_s0"] for c in range(8)], axis=0)
    y_p = res.results[0]["y_s1"][None]
    return (y_p.astype(np.float32), y_s.astype(np.float32))
```

```python
import numpy as np
import ml_dtypes
import concourse.bass as bass
import concourse.mybir as mybir
from concourse.bass_utils import run_bass_kernel_spmd

F32 = mybir.dt.float32
BF16 = mybir.dt.bfloat16
U8 = mybir.dt.uint8
ALU = mybir.AluOpType
AF = mybir.ActivationFunctionType
AX = mybir.AxisListType

D = 2048
NM = 16
DA = 1024
HA = 8
DR = 1024
HR = 16
NSH = 3 * DR + 192
NIN = 3 * DA + NSH + 256
DFF = 5632
ALPHA = 2.0 ** 0.25
LN_EPS = 1e-5
GN_EPS = 64e-5
CDEC = -float(np.exp(-0.5))
NEG = -30000.0
WINR = 10
NKEY = WINR * 64 + NM


class Prog:
    ENG = ("sp", "act", "dve", "pool", "pe")

    def __init__(self, nc, n_dma_sems=40):
        self.nc = nc
        self.q = {e: [] for e in self.ENG}
        self.cnt = {e: 0 for e in self.ENG}
        self.known = {e: {} for e in self.ENG}
        self.last_w = {}
        self.readers = {}
        self.semh = {}
        for e in ("act", "dve", "pool", "pe"):
            self.semh[e] = nc.alloc_semaphore("sem_" + e)
        self.ndma = n_dma_sems
        for i in range(n_dma_sems):
            self.semh[("dma", i)] = nc.alloc_semaphore("sem_dma%d" % i)
        self.dma_tot = [0] * n_dma_sems
        self.rr = 0
        self.arena = nc.alloc_sbuf_tensor("arena", [128, 204 * 1024], U8)
        self.off = 0
        self.uid = 0
        self.psum = [nc.alloc_psum_tensor("psb%d" % i, [128, 512], F32) for i in range(8)]

    def reset_arena(self, off=0):
        self.off = off

    def tile(self, shape, dtype, name="t"):
        esz = 4 if dtype == F32 else 2
        n = int(np.prod(shape[1:]))
        nbytes = (n * esz + 31) // 32 * 32
        assert self.off + nbytes <= 204 * 1024, ("sbuf overflow", name, self.off, nbytes)
        ap = self.arena[:, self.off:self.off + n * esz].bitcast(dtype)
        self.off += nbytes
        if len(shape) == 3:
            ap = ap.rearrange("p (a b) -> p a b", a=shape[1])
        elif len(shape) == 4:
            ap = ap.rearrange("p (a b c) -> p a b c", a=shape[1], b=shape[2])
        self.uid += 1
        return ap, "%s#%d" % (name, self.uid)

    def bank(self, i, dtype=F32):
        ap = self.psum[i][:, :]
        if dtype != F32:
            ap = ap.bitcast(dtype)
        return ap

    def _waits(self, eng, reads, writes):
        need = {}

        def add(ev):
            if ev is None:
                return
            s, v = ev
            if need.get(s, 0) < v:
                need[s] = v
        for k in reads:
            add(self.last_w.get(k))
        for k in writes:
            add(self.last_w.get(k))
            for s, v in self.readers.get(k, {}).items():
                add((s, v))
        out = []
        for s, v in need.items():
            if s == "pe" and eng == "pe":
                continue
            if self.known[eng].get(s, 0) >= v:
                continue
            self.known[eng][s] = v
            out.append((s, v))
        return out

    def _record(self, ev, reads, writes):
        s, v = ev
        for k in reads:
            d = self.readers.setdefault(k, {})
            if d.get(s, 0) < v:
                d[s] = v
        for k in writes:
            self.last_w[k] = ev
            self.readers[k] = {}

    @staticmethod
    def _px(reads, writes):
        pr = tuple(k for k in reads if k.startswith("psb"))
        if pr:
            reads = tuple(k for k in reads if not k.startswith("psb"))
            writes = tuple(writes) + pr
        return reads, writes

    def op(self, eng, fn, reads=(), writes=()):
        reads, writes = self._px(reads, writes)
        waits = self._waits(eng, reads, writes)
        self.cnt[eng] += 1
        ev = (eng, self.cnt[eng])
        self.q[eng].append((waits, fn, eng, 1))
        self._record(ev, reads, writes)

    def dma(self, eng, out, in_, reads=(), writes=()):
        i = self.rr
        self.rr = (self.rr + 1) % self.ndma
        s = ("dma", i)
        waits = self._waits(eng, reads, writes)
        if self.dma_tot[i] > 0 and self.known[eng].get(s, 0) < self.dma_tot[i]:
            self.known[eng][s] = self.dma_tot[i]
            waits.append((s, self.dma_tot[i]))
        self.dma_tot[i] += 16
        ev = (s, self.dma_tot[i])
        self.q[eng].append((waits, lambda e, o=out, a=in_: e.dma_start(out=o, in_=a), s, 16))
        self._record(ev, reads, writes)

    def barrier(self):
        for e in self.ENG:
            waits = []
            for s in ("act", "dve", "pool", "pe"):
                if s != e and self.cnt[s] > self.known[e].get(s, 0):
                    self.known[e][s] = self.cnt[s]
                    waits.append((s, self.cnt[s]))
            for i in range(self.ndma):
                s = ("dma", i)
                if self.dma_tot[i] > self.known[e].get(s, 0):
                    self.known[e][s] = self.dma_tot[i]
                    waits.append((s, self.dma_tot[i]))
            if waits:
                self.q[e].append((waits, None, None, 0))

    def finish(self):
        self.barrier()
        nc = self.nc
        semh = self.semh
        q = self.q
        with nc.Block() as block:
            def mk(name):
                def body(e):
                    for waits, fn, s, amt in q[name]:
                        for ws, wv in waits:
                            e.wait_ge(semh[ws], wv)
                        if fn is not None:
                            fn(e).then_inc(semh[s], amt)
                return body
            block.sync(mk("sp"))
            block.scalar(mk("act"))
            block.vector(mk("dve"))
            block.gpsimd(mk("pool"))
            block.tensor(mk("pe"))

    def mm(self, out, lhsT, rhs, start, stop, reads, writes):
        self.op("pe", lambda e: e.matmul(out, lhsT, rhs, start=start, stop=stop), reads, writes)

    def tr(self, out, in_, ident, reads, writes):
        self.op("pe", lambda e: e.transpose(out, in_, ident), reads, writes)

    def act(self, out, in_, func, reads, writes, bias=None, scale=None, accum=None):
        kw = {}
        if bias is not None:
            kw["bias"] = bias
        if scale is not None:
            kw["scale"] = scale
        if accum is not None:
            kw["accum_out"] = accum
        self.op("act", lambda e: e.activation(out, in_, func, **kw), reads, writes)

    def tt(self, eng, out, a, b, op, reads, writes):
        self.op(eng, lambda e: e.tensor_tensor(out, a, b, op), reads, writes)

    def ts(self, eng, out, a, s1, s2, op0, op1, reads, writes):
        if s2 is None:
            self.op(eng, lambda e: e.tensor_scalar(out, a, s1, None, op0), reads, writes)
        else:
            self.op(eng, lambda e: e.tensor_scalar(out, a, s1, s2, op0, op1), reads, writes)

    def stt(self, eng, out, a, sc, b, op0, op1, reads, writes):
        self.op(eng, lambda e: e.scalar_tensor_tensor(out, a, sc, b, op0, op1), reads, writes)

    def copy(self, eng, out, in_, reads, writes):
        if eng == "act":
            self.op("act", lambda e: e.copy(out, in_), reads, writes)
        else:
            self.op(eng, lambda e: e.tensor_copy(out, in_), reads, writes)

    def reduce(self, eng, out, in_, op, reads, writes):
        self.op(eng, lambda e: e.tensor_reduce(out, in_, AX.X, op), reads, writes)

    def recip(self, eng, out, in_, reads, writes):
        self.op(eng, lambda e: e.reciprocal(out, in_), reads, writes)

    def memset(self, eng, ap, val, writes):
        self.op(eng, lambda e: e.memset(ap, val), (), writes)


class Seq:
    def __init__(self, name, T):
        self.name = name
        self.T = T
        self.L = T + NM
        self.rows = T // 64
        self.ntile = 1 + T // 128

    def tile_range(self, i):
        if i == 0:
            return 0, NM
        return NM + (i - 1) * 128, 128


class Builder:
    def __init__(self, seq_T, debug=False, stages=5, run=(1, 2, 3, 4, 5)):
        self.debug = debug
        self.run = run
        self.stages = stages
        nc = bass.Bass("TRN2", target_bir_lowering=False)
        self.nc = nc
        self.P = Prog(nc)
        self.seqs = [Seq("s%d" % i, T) for i, T in enumerate(seq_T)]
        self.dram = {}
        self._declare_io()

    def din(self, name, shape, dtype=F32):
        ap = self.nc.dram_tensor(name, list(shape), dtype, kind="ExternalInput").ap()
        self.dram[name] = ap
        return ap

    def dscr(self, name, shape, dtype=F32):
        kind = "ExternalOutput" if self.debug else "Internal"
        ap = self.nc.dram_tensor(name, list(shape), dtype, kind=kind).ap()
        self.dram[name] = ap
        return ap

    def _declare_io(self):
        for s in self.seqs:
            self.din("x_" + s.name, [s.T, D])
            self.nc_out = None
        for s in self.seqs:
            ap = self.nc.dram_tensor("y_" + s.name, [s.T, D], F32, kind="ExternalOutput").ap()
            self.dram["y_" + s.name] = ap
        self.din("meta_tokens", [NM, D])
        for nm in ("emb_ln_g", "emb_ln_b", "ln1_g", "ln1_b", "ln2_g", "ln2_b"):
            self.din(nm, [D])
        self.din("w_in", [D, NIN])
        self.din("bias_tab", [HA, 5, 128, NKEY])
        self.din("rwkv_mu", [2, NSH])
        self.din("rwkv_w0", [2, DR])
        self.din("rwkv_w2", [2, 96, DR])
        self.din("rwkv_a0", [2, DR])
        self.din("rwkv_a2", [2, 96, DR])
        self.din("rwkv_g2", [256, DR])
        for nm in ("rwkv_k_k", "rwkv_k_a", "rwkv_r_k", "rwkv_lnx_g", "rwkv_lnx_b"):
            self.din(nm, [DR])
        self.din("w_out", [D, D])
        self.din("ffn_w_in", [D, 2 * DFF])
        self.din("ffn_conv_l", [128, 44, 4])
        self.din("ffn_w_out", [DFF, D])
        self.din("c_ident", [128, 128])
        self.din("c_masks", [6, 128, 128])
        self.din("c_zero", [1, NSH])
        for s in self.seqs:
            n = s.name
            self.dscr("H0_" + n, [s.L, D])
            self.dscr("QT_" + n, [HA, 128, s.L], BF16)
            self.dscr("KT_" + n, [HA, 128, s.L], BF16)
            self.dscr("V_" + n, [s.L, DA], BF16)
            self.dscr("P_" + n, [s.L, NSH])
            self.dscr("G_" + n, [s.L, DR])
            self.dscr("YF_" + n, [s.L, DR])
            self.dscr("BF_" + n, [s.L, DR])
            self.dscr("OT_" + n, [16, 128, s.L], BF16)
            self.dscr("H1_" + n, [s.L, D])
            self.dscr("H1T_" + n, [16, 128, s.L], BF16)

    def setup_consts(self):
        P = self.P
        d = self.dram
        self.ident_f, k1 = P.tile([128, 128], F32, "identf")
        P.dma("sp", self.ident_f, d["c_ident"], (), (k1,))
        self.ident_b, k2 = P.tile([128, 128], BF16, "identb")
        P.copy("dve", self.ident_b, self.ident_f, (k1,), (k2,))
        self.kid = (k1, k2)
        self.const_end = P.off

    def bcast_load(self, name_ap, n, nm):
        P = self.P
        t, k = P.tile([128, n], F32, nm)
        P.dma("sp", t, name_ap.partition_broadcast(128), (), (k,))
        return t, k

    def layer_norm(self, x, kx, n, gb, kgb, out, kout, eps, tmp):
        P = self.P
        (st, kst), (mv, kmv), (rs, krs) = tmp
        g, b = gb
        for c in range(4):
            P.op("dve", lambda e, c=c: e.bn_stats(st[:n, c, :], x[:n, c * 512:(c + 1) * 512]), (kx,), (kst,))
        P.op("dve", lambda e: e.bn_aggr(mv[:n, :], st[:n].rearrange("p a b -> p (a b)")), (kst,), (kmv,))
        P.act(rs[:n, :], mv[:n, 1:2], AF.Sqrt, (kmv,), (krs,), bias=eps)
        P.op("dve", lambda e: e.reciprocal(rs[:n, :], rs[:n, :]), (krs,), (krs,))
        P.ts("dve", x[:n, :], x[:n, :], mv[:n, 0:1], rs[:n, 0:1], ALU.subtract, ALU.mult, (kx, kmv, krs), (kx,))
        P.tt("pool", x[:n, :], x[:n, :], g[:n, :], ALU.mult, (kx,) + kgb, (kx,))
        P.tt("dve", out[:n, :], x[:n, :], b[:n, :], ALU.add, (kx,) + kgb, (kout,))

    def ln_tmp(self):
        P = self.P
        return (P.tile([128, 4, 6], F32, "bnst"), P.tile([128, 2], F32, "bnmv"), P.tile([128, 1], F32, "rstd"))

    def transpose_to_fm(self, src, ksrc, n, dst, kdst, col0, banks, ctr):
        P = self.P
        for g4 in range(4):
            bi = banks[(ctr[0]) % len(banks)]
            ctr[0] += 1
            pb = P.bank(bi)
            kb = "psb%d" % bi
            for j in range(4):
                kc = g4 * 4 + j
                P.tr(pb[:, j * 128:j * 128 + n], src[:n, kc * 128:(kc + 1) * 128], self.ident_f[:n, :n],
                     (ksrc, self.kid[0]), (kb,))
            eng = "act" if g4 % 2 == 0 else "dve"
            P.copy(eng, dst[:, g4 * 4:g4 * 4 + 4, col0:col0 + n],
                   pb.rearrange("p (a b) -> p a b", a=4)[:, :, :n], (kb,), (kdst,))

    def load_w(self, wt, kw, src, rows, col0, ncols):
        P = self.P
        nk = rows // 128
        for k0 in range(0, nk, 4):
            k1 = min(nk, k0 + 4)
            P.dma("pool", wt[:, k0:k1, :ncols],
                  src[k0 * 128:k1 * 128, col0:col0 + ncols].rearrange("(kc p) n -> p kc n", p=128), (), (kw,))

    def stage1(self, s):
        P = self.P
        d = self.dram
        n_ = s.name
        P.barrier()
        P.reset_arena(self.const_end)
        g0, kg = self.bcast_load(d["emb_ln_g"], D, "g0")
        b0, kb = self.bcast_load(d["emb_ln_b"], D, "b0")
        tmp = self.ln_tmp()
        g2, kg2 = P.tile([128, 2, DR], BF16, "g2")
        self.load_w(g2, kg2, d["rwkv_g2"], 256, 0, DR)
        xt = [P.tile([128, D], F32, "xt%d" % i) for i in range(2)]
        ht = [P.tile([128, D], F32, "ht%d" % i) for i in range(2)]
        wbuf = [P.tile([128, 16, 512], BF16, "w%d" % i) for i in range(2)]
        stg = [P.tile([128, 512], F32, "stg%d" % i) for i in range(3)]
        stgb = [P.tile([128, 512], BF16, "stgb%d" % i) for i in range(2)]
        qkT = [P.tile([128, 4, 128], BF16, "qkT%d" % i) for i in range(2)]
        sgT = [P.tile([128, 2, 128], BF16, "sgT%d" % i) for i in range(2)]
        gst = [P.tile([128, DR], F32, "gst%d" % i) for i in range(2)]
        TB = 16
        h0T, kh0T = P.tile([128, 16, TB * 128 + NM], BF16, "h0T")
        groups = []
        for c0 in range(0, NIN, 512):
            groups.append((c0, min(512, NIN - c0)))
        tiles = list(range(s.ntile))
        blocks = [tiles[0:TB + 1]] + [tiles[i:i + TB] for i in range(TB + 1, s.ntile, TB)]
        ctr = [0]
        wi = 0
        ci = 0
        for blk in blocks:
            base = s.tile_range(blk[0])[0]
            for ti in blk:
                t0, n = s.tile_range(ti)
                x, kx = xt[ci % 2]
                h, kh = ht[ci % 2]
                ci += 1
                if ti == 0:
                    P.dma("sp", x[:n, :], d["meta_tokens"], (), (kx,))
                else:
                    P.dma("sp", x[:n, :], d["x_" + n_][t0 - NM:t0 - NM + n, :], (), (kx,))
                self.layer_norm(x, kx, n, (g0, b0), (kg, kb), h, kh, LN_EPS, tmp)
                P.dma("sp", d["H0_" + n_][t0:t0 + n, :], h[:n, :], (kh,), ())
                self.transpose_to_fm(h, kh, n, h0T, kh0T, t0 - base, (0, 1), ctr)
            if self.stages == 0:
                continue
            for gi, (c0, nc_) in enumerate(groups):
                w, kw = wbuf[wi % 2]
                wi += 1
                self.load_w(w, kw, d["w_in"], D, c0, nc_)
                for ti in blk:
                    t0, n = s.tile_range(ti)
                    bi = 2 + (ctr[0] % 2)
                    ctr[0] += 1
                    pb = P.bank(bi)
                    kpb = "psb%d" % bi
                    for kc in range(16):
                        P.mm(pb[:n, :nc_], h0T[:, kc, t0 - base:t0 - base + n], w[:, kc, :nc_],
                             kc == 0, kc == 15, (kh0T, kw), (kpb,))
                    ev = "act" if ctr[0] % 2 == 0 else "dve"
                    if self.stages == -1 or (self.stages == -2 and gi < 4) or (self.stages == -3 and gi >= 4) or (self.stages == -4 and gi < 12):
                        sf, ksf = stg[ctr[0] % 3]
                        P.copy(ev, sf[:n, :], pb[:n, :], (kpb,), (ksf,))
                        continue
                    if gi < 4:
                        sb, ksb = stgb[ctr[0] % 2]
                        if gi < 2:
                            P.act(sb[:n, :], pb[:n, :], AF.Copy, (kpb,), (ksb,), scale=128.0 ** -0.5)
                        else:
                            P.copy(ev, sb[:n, :], pb[:n, :], (kpb,), (ksb,))
                        tb = 4
                        ptb = P.bank(tb, BF16)
                        for j in range(4):
                            P.tr(ptb[:, j * 128:j * 128 + n], sb[:n, j * 128:(j + 1) * 128], self.ident_b[:n, :n],
                                 (ksb, self.kid[1]), ("psb4",))
                        qt, kqt = qkT[ctr[0] % 2]
                        P.copy("dve" if ev == "act" else "act", qt[:, :, :n],
                               ptb[:, 0:512].rearrange("p (a b) -> p a b", a=4)[:, :, :n], ("psb4",), (kqt,))
                        dst = d[("QT_" if gi < 2 else "KT_") + n_]
                        h0 = (gi % 2) * 4
                        P.dma("sp", dst[h0:h0 + 4, :, t0:t0 + n].rearrange("h p t -> p h t"), qt[:, :, :n], (kqt,), ())
                    elif gi < 6:
                        sb, ksb = stgb[ctr[0] % 2]
                        P.copy(ev, sb[:n, :], pb[:n, :], (kpb,), (ksb,))
                        P.dma("sp", d["V_" + n_][t0:t0 + n, (gi - 4) * 512:(gi - 3) * 512], sb[:n, :], (ksb,), ())
                    elif gi < 12:
                        sf, ksf = stg[ctr[0] % 3]
                        P.copy(ev, sf[:n, :], pb[:n, :], (kpb,), (ksf,))
                        P.dma("sp", d["P_" + n_][t0:t0 + n, (gi - 6) * 512:(gi - 5) * 512], sf[:n, :], (ksf,), ())
                    else:
                        sf, ksf = stg[ctr[0] % 3]
                        P.copy("dve", sf[:n, :192], pb[:n, :192], (kpb,), (ksf,))
                        P.dma("sp", d["P_" + n_][t0:t0 + n, 3072:3264], sf[:n, :192], (ksf,), ())
                        sb, ksb = stgb[ctr[0] % 2]
                        P.act(sb[:n, :256], pb[:n, 192:448], AF.Sigmoid, (kpb,), (ksb,))
                        ptb = P.bank(4, BF16)
                        for j in range(2):
                            P.tr(ptb[:, j * 128:j * 128 + n], sb[:n, j * 128:(j + 1) * 128], self.ident_b[:n, :n],
                                 (ksb, self.kid[1]), ("psb4",))
                        sg, ksg = sgT[ctr[0] % 2]
                        P.copy("dve", sg[:, :, :n], ptb[:, 0:256].rearrange("p (a b) -> p a b", a=2)[:, :, :n],
                               ("psb4",), (ksg,))
                        go, kgo = gst[ctr[0] % 2]
                        for hf in range(2):
                            pg = P.bank(5 + hf)
                            kpg = "psb%d" % (5 + hf)
                            for j in range(2):
                                P.mm(pg[:n, :], sg[:, j, :n], g2[:, j, hf * 512:(hf + 1) * 512], j == 0, j == 1,
                                     (ksg, kg2), (kpg,))
                            P.copy("act" if hf == 0 else "dve", go[:n, hf * 512:(hf + 1) * 512], pg[:n, :], (kpg,), (kgo,))
                        P.dma("sp", d["G_" + n_][t0:t0 + n, :], go[:n, :], (kgo,), ())

    def stage2(self, s):
        P = self.P
        d = self.dram
        n_ = s.name
        for dr in (0, 1):
            P.barrier()
            P.reset_arena(self.const_end)
            self._rwkv_dir(s, dr)

    def _rwkv_dir(self, s, dr):
        P = self.P
        d = self.dram
        n_ = s.name
        Pd = d["P_" + n_]
        L = s.L
        mu, kmu = self.bcast_load(d["rwkv_mu"][dr], NSH, "mu")
        w0, kw0 = self.bcast_load(d["rwkv_w0"][dr], DR, "w0")
        a0, ka0 = self.bcast_load(d["rwkv_a0"][dr], DR, "a0")
        kkp, kkkp = self.bcast_load(d["rwkv_k_k"], DR, "k_k")
        kap, kkap = self.bcast_load(d["rwkv_k_a"], DR, "k_a")
        rkp, krkp = self.bcast_load(d["rwkv_r_k"], DR, "r_k")
        w2, kw2 = P.tile([128, DR], F32, "w2")
        a2, ka2 = P.tile([128, DR], F32, "a2")
        P.dma("sp", w2[:96, :], d["rwkv_w2"][dr], (), (kw2,))
        P.dma("sp", a2[:96, :], d["rwkv_a2"][dr], (), (ka2,))
        tri, ktri = P.tile([128, 128], F32, "tri")
        P.dma("sp", tri, d["c_masks"][dr], (), (ktri,))
        ones, kones = P.tile([128, 128], F32, "ones")
        P.memset("pool", ones, 1.0, (kones,))
        m4, km4 = P.tile([128, 4, 128], F32, "mask4")
        for q in range(4):
            P.dma("sp", m4[:, q, :], d["c_masks"][2 + 2 * dr + (q % 2)], (), (km4,))
        suT, ksuT = P.tile([128, 128], F32, "suT")
        P.dma("sp", suT, d["c_masks"][2 + 2 * (1 - dr)], (), (ksuT,))
        if dr == 1:
            lg, klg = self.bcast_load(d["rwkv_lnx_g"], DR, "lnxg")
            lb, klb = self.bcast_load(d["rwkv_lnx_b"], DR, "lnxb")
        Sf, kSf = P.tile([128, 8, 64], F32, "Sf")
        Sb, kSb = P.tile([128, 8, 64], BF16, "Sb")
        P.memset("pool", Sf, 0.0, (kSf,))
        P.memset("pool", Sb, 0.0, (kSb,))
        kSfh = [[kSf + "/%d/%d" % (j, h) for h in range(2)] for j in range(8)]
        kSbh = [[kSb + "/%d/%d" % (j, h) for h in range(2)] for j in range(8)]
        for j in range(8):
            for h in range(2):
                P.last_w[kSfh[j][h]] = P.last_w[kSf]
                P.last_w[kSbh[j][h]] = P.last_w[kSb]
        pt = [P.tile([128, NSH], F32, "p%d" % i) for i in range(1)]
        pn = [P.tile([128, NSH], F32, "pn%d" % i) for i in range(1)]
        T = lambda nm, dt=F32: P.tile([128, DR], dt, nm)
        sg, ksg = T("sg"); av, kav = T("a"); cum, kcum = T("cum"); e1, ke1 = T("e1"); e2, ke2 = T("e2")
        kk, kkk = T("kk"); kq, kkq = T("kq"); ka_, kka_ = T("ka"); tmp, ktmp = T("tmp"); bon, kbon = T("bon")
        Y, kY = T("Y")
        At, kAt = T("At", BF16); Bt, kBt = T("Bt", BF16); Kt, kKt = T("Kt", BF16); Rt, kRt = T("Rt", BF16)
        BW, kBW = T("BW", BF16); KW, kKW = T("KW", BF16); Vb, kVb = T("Vb", BF16)
        ART, kART = P.tile([128, 8, 256], BF16, "ART")
        BTt, kBTt = P.tile([128, 8, 128], BF16, "BT")
        KTt, kKTt = P.tile([128, 8, 128], BF16, "KT")
        thT, kthT = P.tile([128, 2, 128], F32, "thT")
        th, kth = P.tile([128, 192], F32, "th")
        sm, ksm = P.tile([128, 4, 16], F32, "small")
        wc, kwc = P.tile([128, 8], F32, "wc")
        if dr == 1:
            yf, kyf = e1, ke1
            gt, kgt = e2, ke2
            ob, kob = T("ob", BF16)
            oT, koT = P.tile([128, 8, 128], BF16, "oT")
        M4a, kM4a0 = P.tile([128, 16, 4, 128], BF16, "M4a")
        kM4 = [kM4a0 + "/%d" % i for i in range(16)]
        NTa, kNTa0 = P.tile([128, 16, 128], BF16, "NTa")
        kNT = [kNTa0 + "/%d" % g for g in range(4)]
        Ta = []
        PPa = []
        for i in range(2):
            t_, k_ = P.tile([128, 16, 128], BF16, "Ta%d" % i)
            Ta.append((t_, [k_ + "/%d" % g for g in range(4)]))
            t_, k_ = P.tile([128, 2, 16, 128], BF16, "PPa%d" % i)
            PPa.append((t_, [[k_ + "/%d/%d" % (w, g) for g in range(4)] for w in range(2)]))
        Xa, kXa0 = P.tile([128, 16, 64], BF16, "Xa")
        kXa = [kXa0 + "/%d" % b for b in range(2)]
        Ua, kUa0 = P.tile([128, 16, 64], BF16, "Ua")
        kUa = [kUa0 + "/%d" % b for b in range(2)]
        id4, kid4 = P.tile([128, 4, 128], BF16, "id4")
        for q in range(4):
            P.copy("pool", id4[:, q, :], self.ident_b, (self.kid[1],), (kid4,))
        order = list(range(s.ntile)) if dr == 0 else list(range(s.ntile - 1, -1, -1))
        tctr = [0]
        v3 = lambda ap: ap.rearrange("p (h c) -> p h c", h=16)
        p, kp = pt[0]
        q_, kq_ = pn[0]
        f, kf = q_, kq_
        r_ = f[:, 0:DR]; k_ = f[:, DR:2 * DR]; v_ = f[:, 2 * DR:3 * DR]

        def prep1(ti):
            t0, n = s.tile_range(ti)
            P.dma("sp", p[:n, :], Pd[t0:t0 + n, :], (), (kp,))
            if dr == 0:
                if t0 == 0:
                    P.dma("sp", q_[0:1, :], d["c_zero"], (), (kq_,))
                    P.dma("sp", q_[1:n, :], Pd[0:n - 1, :], (), (kq_,))
                else:
                    P.dma("sp", q_[:n, :], Pd[t0 - 1:t0 - 1 + n, :], (), (kq_,))
            else:
                if t0 + n == L:
                    P.dma("sp", q_[n - 1:n, :], d["c_zero"], (), (kq_,))
                    P.dma("sp", q_[0:n - 1, :], Pd[t0 + 1:t0 + n, :], (), (kq_,))
                else:
                    P.dma("sp", q_[:n, :], Pd[t0 + 1:t0 + 1 + n, :], (), (kq_,))
            yield
            P.tt("dve", q_[:n, :], q_[:n, :], p[:n, :], ALU.subtract, (kq_, kp), (kq_,))
            yield
            P.tt("pool", q_[:n, :], q_[:n, :], mu[:n, :], ALU.mult, (kq_, kmu), (kq_,))
            yield
            P.tt("dve", q_[:n, :], q_[:n, :], p[:n, :], ALU.add, (kq_, kp), (kq_,))
            yield
            P.act(th[:n, 0:96], f[:n, 3 * DR:3 * DR + 96], AF.Tanh, (kf,), (kth,))
            P.copy("pool", th[:n, 96:192], f[:n, 3 * DR + 96:3 * DR + 192], (kf,), (kth,))
            pb0 = P.bank(0)
            for q in range(2):
                P.tr(pb0[:96, q * 128:q * 128 + n], th[:n, q * 96:(q + 1) * 96], self.ident_f[:n, :n],
                     (kth, self.kid[0]), ("psb0",))
            P.copy("act", thT[:96, :, :n], pb0[:96, 0:256].rearrange("p (a b) -> p a b", a=2)[:, :, :n], ("psb0",), (kthT,))
            yield
            for which, (wm, kwm, bias, kbias, dst, kdst) in enumerate(((w2, kw2, w0, kw0, sg, ksg), (a2, ka2, a0, ka0, av, kav))):
                for hf in range(2):
                    pb = P.bank(hf); kpb = "psb%d" % hf
                    P.mm(pb[:n, :], thT[:96, which, :n], wm[:96, hf * 512:(hf + 1) * 512], True, True, (kthT, kwm), (kpb,))
                    P.tt("dve", dst[:n, hf * 512:(hf + 1) * 512], pb[:n, :], bias[:n, hf * 512:(hf + 1) * 512], ALU.add,
                         (kpb, kbias), (kdst,))
                    yield
                P.act(dst[:n, :], dst[:n, :], AF.Sigmoid, (kdst,), (kdst,))
                yield
            for hf in range(2):
                pb = P.bank(hf); kpb = "psb%d" % hf
                P.mm(pb[:n, :], tri[:n, :n], sg[:n, hf * 512:(hf + 1) * 512], True, True, (ktri, ksg), (kpb,))
                P.copy("act", cum[:n, hf * 512:(hf + 1) * 512], pb[:n, :], (kpb,), (kcum,))
                yield
            P.tt("pool", kk[:n, :], k_[:n, :], kkp[:n, :], ALU.mult, (kf, kkkp), (kkk,))
            yield
            P.tt("pool", tmp[:n, :], kk[:n, :], kk[:n, :], ALU.mult, (kkk,), (ktmp,))
            yield
            P.reduce("dve", sm[:n, 0, :], v3(tmp[:n, :]), ALU.add, (ktmp,), (ksm,))
            P.act(sm[:n, 0, :], sm[:n, 0, :], AF.Sqrt, (ksm,), (ksm,))
            P.ts("dve", sm[:n, 0, :], sm[:n, 0, :], 1e-12, None, ALU.max, None, (ksm,), (ksm,))
            P.recip("dve", sm[:n, 0, :], sm[:n, 0, :], (ksm,), (ksm,))
            yield
            P.tt("dve", v3(kk[:n, :]), v3(kk[:n, :]), sm[:n, 0, :].unsqueeze(2).to_broadcast([n, 16, 64]), ALU.mult,
                 (kkk, ksm), (kkk,))
            yield
            P.stt("dve", kq[:n, :], av[:n, :], -1.0, kap[:n, :], ALU.add, ALU.mult, (kav, kkap), (kkq,))
            yield
            P.stt("dve", kq[:n, :], kq[:n, :], 1.0, k_[:n, :], ALU.add, ALU.mult, (kkq, kf), (kkq,))
            yield
            P.tt("pool", ka_[:n, :], kk[:n, :], av[:n, :], ALU.mult, (kkk, kav), (kka_,))
            yield

        def prep2(ti):
            t0, n = s.tile_range(ti)
            for hf in range(2):
                pb = P.bank(2 + hf); kpb = "psb%d" % (2 + hf)
                P.mm(pb[:n, :], ones[:n, :n], sg[:n, hf * 512:(hf + 1) * 512], True, True, (kones, ksg), (kpb,))
                P.tt("dve", e2[:n, hf * 512:(hf + 1) * 512], pb[:n, :], cum[:n, hf * 512:(hf + 1) * 512], ALU.subtract,
                     (kpb, kcum), (ke2,))
            pb0 = P.bank(0)
            for j in range(8):
                P.mm(pb0[:, j:j + 1], sg[:n, j * 128:(j + 1) * 128], ones[:n, 0:1], True, True, (ksg, kones), ("psb0",))
            P.act(wc[:, :], pb0[:, 0:8], AF.Exp, ("psb0",), (kwc,), scale=CDEC)
            P.tt("pool", tmp[:n, :], r_[:n, :], kq[:n, :], ALU.mult, (kf, kkq), (ktmp,))
            P.tt("pool", tmp[:n, :], tmp[:n, :], rkp[:n, :], ALU.mult, (ktmp, krkp), (ktmp,))
            P.reduce("dve", sm[:n, 1, :], v3(tmp[:n, :]), ALU.add, (ktmp,), (ksm,))
            P.tt("dve", v3(bon[:n, :]), v3(v_[:n, :]), sm[:n, 1, :].unsqueeze(2).to_broadcast([n, 16, 64]), ALU.mult,
                 (kf, ksm), (kbon,))
            if dr == 0:
                P.dma("sp", d["BF_" + n_][t0:t0 + n, :], bon[:n, :], (kbon,), ())
            P.act(e1[:n, :], cum[:n, :], AF.Exp, (kcum,), (ke1,), scale=CDEC)
            P.tt("dve", Rt[:n, :], r_[:n, :], e1[:n, :], ALU.mult, (kf, ke1), (kRt,))
            P.act(e1[:n, :], cum[:n, :], AF.Exp, (kcum,), (ke1,), scale=-CDEC)
            P.tt("pool", Bt[:n, :], ka_[:n, :], e1[:n, :], ALU.mult, (kka_, ke1), (kBt,))
            P.tt("dve", Kt[:n, :], kq[:n, :], e1[:n, :], ALU.mult, (kkq, ke1), (kKt,))
            P.act(e2[:n, :], e2[:n, :], AF.Exp, (ke2,), (ke2,), scale=CDEC)
            P.tt("pool", BW[:n, :], ka_[:n, :], e2[:n, :], ALU.mult, (kka_, ke2), (kBW,))
            P.tt("dve", KW[:n, :], kq[:n, :], e2[:n, :], ALU.mult, (kkq, ke2), (kKW,))
            P.tt("dve", cum[:n, :], cum[:n, :], sg[:n, :], ALU.subtract, (kcum, ksg), (kcum,))
            P.act(e1[:n, :], cum[:n, :], AF.Exp, (kcum,), (ke1,), scale=CDEC)
            P.stt("dve", At[:n, :], kk[:n, :], -1.0, e1[:n, :], ALU.mult, ALU.mult, (kkk, ke1), (kAt,))
            P.copy("pool", Vb[:n, :], v_[:n, :], (kf,), (kVb,))
            for xi, (src, ksrc) in enumerate(((At, kAt), (Rt, kRt), (Bt, kBt), (Kt, kKt))):
                bi = tctr[0] % 4
                tctr[0] += 1
                ptb = P.bank(bi, BF16)
                kptb = "psb%d" % bi
                for j in range(8):
                    P.tr(ptb[:, j * 128:j * 128 + n], src[:n, j * 128:(j + 1) * 128], self.ident_b[:n, :n],
                         (ksrc, self.kid[1]), (kptb,))
                srcv = ptb.rearrange("p (a b) -> p a b", a=8)[:, :, :n]
                if xi == 0:
                    P.copy("act", ART[:, :, 0:n], srcv, (kptb,), (kART,))
                elif xi == 1:
                    P.copy("dve", ART[:, :, 128:128 + n], srcv, (kptb,), (kART,))
                elif xi == 2:
                    P.copy("act", BTt[:, :, :n], srcv, (kptb,), (kBTt,))
                else:
                    P.copy("dve", KTt[:, :, :n], srcv, (kptb,), (kKTt,))

        def units(ti, gen):
            t0, n = s.tile_range(ti)

            def pump():
                next(gen, None)
            nsq = int(np.log2(n)) - 1
            HD = [dict(hd=hd, j=hd // 2, hb=64 * (hd % 2), col=hd * 64, g=hd // 4, sl=hd % 4) for hd in range(16)]
            bk = lambda g: P.bank(4 + g)
            kbk = lambda g: "psb%d" % (4 + g)
            for H in HD:
                if H["hd"] % 4 == 0:
                    pump()
                hd, j, hb = H["hd"], H["j"], H["hb"]
                b_ = bk(hd % 4); kb_ = kbk(hd % 4)
                P.mm(b_[:n, 0:256], BTt[hb:hb + 64, j, :n], ART[hb:hb + 64, j, :], True, True, (kBTt, kART), (kb_,))
                P.mm(b_[:n, 256:512], KTt[hb:hb + 64, j, :n], ART[hb:hb + 64, j, :], True, True, (kKTt, kART), (kb_,))
                P.tt("dve", M4a[:n, hd, :, :n], b_[:n, :].rearrange("p (a b) -> p a b", a=4)[:, :, :n], m4[:n, :, :n], ALU.mult,
                     (kb_, km4), (kM4[hd],))
            for H in HD:
                if H["hd"] % 4 == 0:
                    pump()
                hd, j, hb = H["hd"], H["j"], H["hb"]
                nb = (hd % 2) * 2 + hd // 8
                sl = (hd % 8) // 2
                P.mm(bk(nb)[:n, sl * 128:sl * 128 + n], ART[hb:hb + 64, j, 0:n], BTt[hb:hb + 64, j, :n], True, True,
                     (kART, kBTt), (kbk(nb),))
                if sl == 3:
                    base = (hd // 8) * 8 + hd % 2
                    P.tt("dve", NTa[:n, base:base + 7:2, :n], bk(nb)[:n, :].rearrange("p (a b) -> p a b", a=4)[:, :, :n],
                         suT[:n, :n].unsqueeze(1).to_broadcast([n, 4, n]), ALU.mult, (kbk(nb), ksuT), (kNT[nb],))
            T0, kT0 = Ta[0]
            for g in range(4):
                P.tt("pool", T0[:n, g * 4:g * 4 + 4, :n], M4a[:n, g * 4:g * 4 + 4, 0, :n], id4[:n, :, :n], ALU.add,
                     tuple(kM4[g * 4:g * 4 + 4]) + (kid4,), (kT0[g],))
            Pget = lambda hd: (M4a[:, hd, 0, :], kM4[hd])
            PTget = lambda hd: (NTa[:, hd, :], kNT[(hd % 2) * 2 + hd // 8])
            tcur = 0
            for it in range(1, nsq + 1):
                last = it == nsq
                PPn, kPPn = PPa[it % 2]
                for H in HD:
                    if H["hd"] % 4 == 0:
                        pump()
                    hd, g, sl = H["hd"], H["g"], H["sl"]
                    Pc, kPc = Pget(hd); PTc, kPTc = PTget(hd)
                    P.mm(bk(g)[:n, sl * 128:sl * 128 + n], Pc[:n, :n], PTc[:n, :n], True, True, (kPc, kPTc), (kbk(g),))
                    if sl == 3:
                        P.copy("act", PPn[:n, 1, g * 4:g * 4 + 4, :n], bk(g)[:n, :].rearrange("p (a b) -> p a b", a=4)[:, :, :n],
                               (kbk(g),), (kPPn[1][g],))
                if not last:
                    for H in HD:
                        if H["hd"] % 4 == 0:
                            pump()
                        hd, g, sl = H["hd"], H["g"], H["sl"]
                        Pc, kPc = Pget(hd); PTc, kPTc = PTget(hd)
                        P.mm(bk(g)[:n, sl * 128:sl * 128 + n], PTc[:n, :n], Pc[:n, :n], True, True, (kPc, kPTc), (kbk(g),))
                        if sl == 3:
                            P.copy("act" if g % 2 else "dve", PPn[:n, 0, g * 4:g * 4 + 4, :n],
                                   bk(g)[:n, :].rearrange("p (a b) -> p a b", a=4)[:, :, :n], (kbk(g),), (kPPn[0][g],))
                Pget = lambda hd, PPn=PPn, kPPn=kPPn: (PPn[:, 0, hd, :], kPPn[0][hd // 4])
                PTget = lambda hd, PPn=PPn, kPPn=kPPn: (PPn[:, 1, hd, :], kPPn[1][hd // 4])
                Tc, kTc = Ta[tcur]
                Tn, kTn = Ta[1 - tcur]
                for H in HD:
                    if H["hd"] % 4 == 0:
                        pump()
                    hd, g, sl = H["hd"], H["g"], H["sl"]
                    PTc, kPTc = PTget(hd)
                    P.mm(bk(g)[:n, sl * 128:sl * 128 + n], PTc[:n, :n], Tc[:n, hd, :n], True, True, (kPTc, kTc[g]), (kbk(g),))
                    if sl == 3:
                        P.tt("dve", Tn[:n, g * 4:g * 4 + 4, :n], bk(g)[:n, :].rearrange("p (a b) -> p a b", a=4)[:, :, :n],
                             Tc[:n, g * 4:g * 4 + 4, :n], ALU.add, (kbk(g), kTc[g]), (kTn[g],))
                tcur = 1 - tcur
            Tc, kTc = Ta[tcur]
            for H in HD:
                if H["hd"] % 4 == 0:
                    pump()
                hd, j, hb, col = H["hd"], H["j"], H["hb"], H["col"]
                b = hd % 2
                o_ = bk(b)[:n, (hd // 2) * 64:(hd // 2) * 64 + 64]
                P.mm(o_, ART[hb:hb + 64, j, 0:n], Sb[hb:hb + 64, j, :], True, False, (kART, kSb), (kbk(b),))
                P.mm(o_, M4a[:n, hd, 2, :n], Vb[:n, col:col + 64], False, True, (kM4[hd], kVb), (kbk(b),))
                if hd >= 14:
                    P.copy("act", Xa[:n, b:16:2, :], bk(b)[:n, :].rearrange("p (a b) -> p a b", a=8), (kbk(b),), (kXa[b],))
            for H in HD:
                if H["hd"] % 4 == 0:
                    pump()
                hd = H["hd"]
                b = hd % 2
                P.mm(bk(2 + b)[:n, (hd // 2) * 64:(hd // 2) * 64 + 64], Tc[:n, hd, :n], Xa[:n, hd, :], True, True,
                     (kTc[hd // 4], kXa[b]), (kbk(2 + b),))
                if hd >= 14:
                    P.copy("act", Ua[:n, b:16:2, :], bk(2 + b)[:n, :].rearrange("p (a b) -> p a b", a=8), (kbk(2 + b),), (kUa[b],))
            Yv = Y[:n, :].rearrange("p (h c) -> p h c", h=16)
            for H in HD:
                if H["hd"] % 4 == 0:
                    pump()
                hd, j, hb, col = H["hd"], H["j"], H["hb"], H["col"]
                b = hd % 2
                o_ = bk(b)[:n, (hd // 2) * 64:(hd // 2) * 64 + 64]
                P.mm(o_, ART[hb:hb + 64, j, 128:128 + n], Sb[hb:hb + 64, j, :], True, False, (kART, kSb), (kbk(b),))
                P.mm(o_, M4a[:n, hd, 1, :n], Ua[:n, hd, :], False, False, (kM4[hd], kUa[b]), (kbk(b),))
                P.mm(o_, M4a[:n, hd, 3, :n], Vb[:n, col:col + 64], False, True, (kM4[hd], kVb), (kbk(b),))
                if hd >= 14:
                    P.copy("act", Yv[:, b:16:2, :], bk(b)[:n, :].rearrange("p (a b) -> p a b", a=8), (kbk(b),), (kY,))
            for H in HD:
                if H["hd"] % 4 == 0:
                    pump()
                hd, j, hb, col = H["hd"], H["j"], H["hb"], H["col"]
                o_ = bk(2)[hb:hb + 64, j * 64:(j + 1) * 64]
                P.mm(o_, BW[:n, col:col + 64], Ua[:n, hd, :], True, False, (kBW, kUa[hd % 2]), (kbk(2),))
                P.mm(o_, KW[:n, col:col + 64], Vb[:n, col:col + 64], False, True, (kKW, kVb), (kbk(2),))
            P.tt("dve", Sf[:, :, :], Sf[:, :, :], wc[:, :].unsqueeze(2).to_broadcast([128, 8, 64]), ALU.mult, (kSf, kwc), (kSf,))
            P.tt("dve", Sf[:, :, :], Sf[:, :, :], bk(2)[:, :].rearrange("p (a b) -> p a b", a=8), ALU.add, (kSf, kbk(2)), (kSf,))
            P.copy("pool", Sb[:, :, :], Sf[:, :, :], (kSf,), (kSb,))

        def epilogue(ti):
            t0, n = s.tile_range(ti)
            if dr == 0:
                P.dma("sp", d["YF_" + n_][t0:t0 + n, :], Y[:n, :], (kY,), ())
            else:
                P.dma("sp", yf[:n, :], d["YF_" + n_][t0:t0 + n, :], (), (kyf,))
                P.dma("sp", gt[:n, :], d["G_" + n_][t0:t0 + n, :], (), (kgt,))
                P.tt("dve", Y[:n, :], Y[:n, :], yf[:n, :], ALU.add, (kY, kyf), (kY,))
                P.dma("sp", yf[:n, :], d["BF_" + n_][t0:t0 + n, :], (kY,), (kyf,))
                P.reduce("dve", sm[:n, 2, :], v3(Y[:n, :]), ALU.add, (kY,), (ksm,))
                P.tt("pool", tmp[:n, :], Y[:n, :], Y[:n, :], ALU.mult, (kY,), (ktmp,))
                P.reduce("dve", sm[:n, 3, :], v3(tmp[:n, :]), ALU.add, (ktmp,), (ksm,))
                P.ts("dve", sm[:n, 2, :], sm[:n, 2, :], 1.0 / 64, None, ALU.mult, None, (ksm,), (ksm,))
                P.ts("dve", sm[:n, 3, :], sm[:n, 3, :], 1.0 / 64, None, ALU.mult, None, (ksm,), (ksm,))
                P.tt("dve", sm[:n, 0, :], sm[:n, 2, :], sm[:n, 2, :], ALU.mult, (ksm,), (ksm,))
                P.tt("dve", sm[:n, 3, :], sm[:n, 3, :], sm[:n, 0, :], ALU.subtract, (ksm,), (ksm,))
                P.act(sm[:n, 3, :], sm[:n, 3, :], AF.Sqrt, (ksm,), (ksm,), bias=GN_EPS)
                P.recip("dve", sm[:n, 3, :], sm[:n, 3, :], (ksm,), (ksm,))
                bc = lambda q: sm[:n, q, :].unsqueeze(2).to_broadcast([n, 16, 64])
                P.tt("dve", v3(Y[:n, :]), v3(Y[:n, :]), bc(2), ALU.subtract, (kY, ksm), (kY,))
                P.tt("dve", v3(Y[:n, :]), v3(Y[:n, :]), bc(3), ALU.mult, (kY, ksm), (kY,))
                P.tt("pool", Y[:n, :], Y[:n, :], lg[:n, :], ALU.mult, (kY, klg), (kY,))
                P.tt("pool", Y[:n, :], Y[:n, :], lb[:n, :], ALU.add, (kY, klb), (kY,))
                P.tt("dve", Y[:n, :], Y[:n, :], bon[:n, :], ALU.add, (kY, kbon), (kY,))
                P.tt("dve", Y[:n, :], Y[:n, :], yf[:n, :], ALU.add, (kY, kyf), (kY,))
                P.tt("dve", ob[:n, :], Y[:n, :], gt[:n, :], ALU.mult, (kY, kgt), (kob,))
                bi = tctr[0] % 4
                tctr[0] += 1
                ptb = P.bank(bi, BF16)
                kptb = "psb%d" % bi
                for j in range(8):
                    P.tr(ptb[:, j * 128:j * 128 + n], ob[:n, j * 128:(j + 1) * 128], self.ident_b[:n, :n],
                         (kob, self.kid[1]), (kptb,))
                P.copy("act", oT[:, :, :n], ptb.rearrange("p (a b) -> p a b", a=8)[:, :, :n], (kptb,), (koT,))
                P.dma("sp", d["OT_" + n_][8:16, :, t0:t0 + n].rearrange("c p t -> p c t"), oT[:, :, :n], (koT,), ())

        for _ in prep1(order[0]):
            pass
        for idx, ti in enumerate(order):
            prep2(ti)
            gen = prep1(order[idx + 1]) if idx + 1 < len(order) else iter(())
            units(ti, gen)
            for _ in gen:
                pass
            epilogue(ti)


    def stage3(self, s):
        P = self.P
        d = self.dram
        n_ = s.name
        L = s.L
        P.barrier()
        P.reset_arena(self.const_end)
        kth = [P.tile([128, L], BF16, "kth%d" % i) for i in range(2)]
        qth = [P.tile([128, L], BF16, "qth%d" % i) for i in range(2)]
        vh = [P.tile([128, s.ntile, 128], BF16, "vh%d" % i) for i in range(2)]
        bt = [P.tile([128, 5, NKEY], F32, "bias%d" % i) for i in range(2)]
        oTh = [P.tile([128, L], BF16, "oTh%d" % i) for i in range(2)]
        sc = [P.tile([128, NKEY], F32, "sc%d" % i) for i in range(2)]
        pr = [P.tile([128, NKEY], BF16, "pr%d" % i) for i in range(2)]
        pT = [P.tile([128, 6, 128], BF16, "pT%d" % i) for i in range(2)]
        ob = [P.tile([128, 128], BF16, "ob%d" % i) for i in range(2)]
        st = [P.tile([128, 4], F32, "st%d" % i) for i in range(4)]
        uc = 0
        for h in range(HA):
            kt, kkt = kth[h % 2]; qt, kqt = qth[h % 2]; v, kv = vh[h % 2]; b, kb = bt[h % 2]; oh, koh = oTh[h % 2]
            P.dma("sp", kt, d["KT_" + n_][h], (), (kkt,))
            P.dma("sp", qt, d["QT_" + n_][h], (), (kqt,))
            P.dma("sp", v[:NM, 0, :], d["V_" + n_][0:NM, h * 128:(h + 1) * 128], (), (kv,))
            for i0 in range(1, s.ntile, 16):
                i1 = min(s.ntile, i0 + 16)
                P.dma("sp", v[:, i0:i1, :],
                      d["V_" + n_][NM + (i0 - 1) * 128:NM + (i1 - 1) * 128, h * 128:(h + 1) * 128].rearrange("(i p) c -> p i c", p=128),
                      (), (kv,))
            P.dma("sp", b, d["bias_tab"][h].rearrange("c q k -> q c k"), (), (kb,))
            units = [("meta", 0)] + [("grid", rp) for rp in range(s.rows // 2)]
            for kind, rp in units:
                u = uc
                uc += 1
                s_, ks_ = sc[u % 2]; p_, kp_ = pr[u % 2]; pt_, kpt_ = pT[u % 2]; o_, ko_ = ob[u % 2]; st_, kst_ = st[u % 4]
                ba = (u % 2) * 2
                pa = P.bank(ba); pbk = P.bank(ba + 1)
                kpa = "psb%d" % ba; kpb = "psb%d" % (ba + 1)
                if kind == "meta":
                    nq = NM
                    q0 = 0
                    P.mm(pbk[:nq, 128:144], qt[:, 0:NM], kt[:, 0:NM], True, True, (kqt, kkt), (kpb,))
                    P.copy("dve", s_[:nq, 640:656], pbk[:nq, 128:144], (kpb,), (ks_,))
                    lo = 640
                    blocks = [(5, NM, 0)]
                else:
                    nq = 128
                    r = 2 * rp
                    ws = min(max(r - 4, 0), s.rows - WINR)
                    assert ws % 2 == 0
                    cls = (r - ws) // 2
                    q0 = NM + rp * 128
                    k0 = NM + ws * 64
                    P.mm(pa[:, :], qt[:, q0:q0 + 128], kt[:, k0:k0 + 512], True, True, (kqt, kkt), (kpa,))
                    P.mm(pbk[:, 0:128], qt[:, q0:q0 + 128], kt[:, k0 + 512:k0 + 640], True, True, (kqt, kkt), (kpb,))
                    P.mm(pbk[:, 128:144], qt[:, q0:q0 + 128], kt[:, 0:NM], True, True, (kqt, kkt), (kpb,))
                    P.tt("dve", s_[:, 0:512], pa[:, :], b[:, cls, 0:512], ALU.add, (kpa, kb), (ks_,))
                    P.tt("dve", s_[:, 512:656], pbk[:, 0:144], b[:, cls, 512:656], ALU.add, (kpb, kb), (ks_,))
                    lo = 0
                    blocks = [(j, 128, ws // 2 + 1 + j) for j in range(5)] + [(5, NM, 0)]
                P.op("dve", lambda e, o=st_[:nq, 0:1], i=s_[:nq, lo:656]: e.reduce_max(o, i, AX.X), (ks_,), (kst_,))
                P.ts("dve", st_[:nq, 1:2], st_[:nq, 0:1], -1.0, None, ALU.mult, None, (kst_,), (kst_,))
                P.act(p_[:nq, lo:656], s_[:nq, lo:656], AF.Exp, (ks_, kst_), (kp_, kst_), bias=st_[:nq, 1:2], accum=st_[:nq, 2:3])
                P.recip("dve", st_[:nq, 3:4], st_[:nq, 2:3], (kst_,), (kst_,))
                tb = 4 + (u % 2)
                ptb = P.bank(tb, BF16)
                kptb = "psb%d" % tb
                for (j, nk, vt) in blocks:
                    P.tr(ptb[:nk, j * 128:j * 128 + nq], p_[:nq, j * 128:j * 128 + nk], self.ident_b[:nq, :nq],
                         (kp_, self.kid[1]), (kptb,))
                j0 = blocks[0][0]
                if kind == "meta":
                    P.copy("act", pt_[:NM, 5, :nq], ptb[:NM, 640:640 + nq], (kptb,), (kpt_,))
                else:
                    P.copy("act", pt_[:, :, :], ptb[:, 0:768].rearrange("p (a b) -> p a b", a=6), (kptb,), (kpt_,))
                po = P.bank(6)
                for bi, (j, nk, vt) in enumerate(blocks):
                    P.mm(po[:nq, 0:128], pt_[:nk, j, :nq], v[:nk, vt, :], bi == 0, bi == len(blocks) - 1, (kpt_, kv), ("psb6",))
                P.act(o_[:nq, :], po[:nq, 0:128], AF.Copy, ("psb6", kst_), (ko_,), scale=st_[:nq, 3:4])
                pot = P.bank(7, BF16)
                P.tr(pot[:, 0:nq], o_[:nq, :], self.ident_b[:nq, :nq], (ko_, self.kid[1]), ("psb7",))
                P.copy("dve", oh[:, q0:q0 + nq], pot[:, 0:nq], ("psb7",), (koh,))
            P.dma("sp", d["OT_" + n_][h], oh, (koh,), ())

    def stage4(self, s):
        P = self.P
        d = self.dram
        n_ = s.name
        P.barrier()
        P.reset_arena(self.const_end)
        g1, kg = self.bcast_load(d["ln1_g"], D, "g1")
        b1, kb = self.bcast_load(d["ln1_b"], D, "b1")
        tmp = self.ln_tmp()
        wo, kwo = P.tile([128, 16, D], BF16, "wo")
        for c0 in range(0, D, 512):
            self.load_w(wo[:, :, c0:c0 + 512], kwo, d["w_out"], D, c0, 512)
        oT = [P.tile([128, 16, 128], BF16, "oT%d" % i) for i in range(2)]
        h0 = [P.tile([128, D], F32, "h0%d" % i) for i in range(2)]
        h1 = [P.tile([128, D], F32, "h1%d" % i) for i in range(2)]
        h1T = [P.tile([128, 16, 128], BF16, "h1T%d" % i) for i in range(2)]
        ctr = [0]
        for ti in range(s.ntile):
            t0, n = s.tile_range(ti)
            o_, ko_ = oT[ti % 2]; x, kx = h0[ti % 2]; y, ky = h1[ti % 2]; yt, kyt = h1T[ti % 2]
            P.dma("sp", o_[:, :, :n], d["OT_" + n_][:, :, t0:t0 + n].rearrange("c p t -> p c t"), (), (ko_,))
            P.dma("sp", x[:n, :], d["H0_" + n_][t0:t0 + n, :], (), (kx,))
            for hf in range(4):
                pb = P.bank(hf); kpb = "psb%d" % hf
                for kc in range(16):
                    P.mm(pb[:n, :], o_[:, kc, :n], wo[:, kc, hf * 512:(hf + 1) * 512], kc == 0, kc == 15, (ko_, kwo), (kpb,))
                P.stt("dve", x[:n, hf * 512:(hf + 1) * 512], x[:n, hf * 512:(hf + 1) * 512], ALPHA, pb[:n, :], ALU.mult, ALU.add,
                      (kx, kpb), (kx,))
            self.layer_norm(x, kx, n, (g1, b1), (kg, kb), y, ky, LN_EPS, tmp)
            P.dma("sp", d["H1_" + n_][t0:t0 + n, :], y[:n, :], (ky,), ())
            self.transpose_to_fm(y, ky, n, yt, kyt, 0, (4, 5, 6, 7), ctr)
            P.dma("sp", d["H1T_" + n_][:, :, t0:t0 + n].rearrange("c p t -> p c t"), yt[:, :, :n], (kyt,), ())

    def stage5(self, s):
        P = self.P
        d = self.dram
        n_ = s.name
        P.barrier()
        P.reset_arena(self.const_end)
        g2_, kg = self.bcast_load(d["ln2_g"], D, "g2")
        b2_, kb = self.bcast_load(d["ln2_b"], D, "b2")
        tmp = self.ln_tmp()
        cw, kcw = P.tile([128, 44, 4], F32, "convw")
        P.dma("sp", cw, d["ffn_conv_l"], (), (kcw,))
        NB = 1024
        hT, khT = P.tile([128, 16, NB + 2], BF16, "hT")
        acc, kacc0 = P.tile([128, 8, D], F32, "acc")
        kacc = [kacc0 + "/%d" % i for i in range(8)]
        GC = 2
        w1 = [P.tile([128, 16, 2 * GC * 128], BF16, "w1_%d" % i) for i in range(2)]
        w2 = [P.tile([128, GC, D], BF16, "w2_%d" % i) for i in range(2)]
        gT = [P.tile([128, 514], F32, "gT%d" % i) for i in range(2)]
        t1 = [P.tile([128, 512], F32, "t1%d" % i) for i in range(2)]
        aT = [P.tile([128, GC, NB], BF16, "aT%d" % i) for i in range(2)]
        xr = [P.tile([128, D], F32, "xr%d" % i) for i in range(1)]
        yo = [P.tile([128, D], F32, "yo%d" % i) for i in range(1)]
        gi = 0
        hc = 0
        for b0 in range(0, s.T, NB):
            ts0 = NM + b0
            ntl = NB // 128
            lo = ts0 - 1
            hi = min(s.L, ts0 + NB + 1)
            P.dma("sp", hT[:, :, 0:hi - lo], d["H1T_" + n_][:, :, lo:hi].rearrange("c p t -> p c t"), (), (khT,))
            if hi - lo < NB + 2:
                P.memset("pool", hT[:, :, NB + 1:NB + 2], 0.0, (khT,))
            for g in range(44 // GC):
                w1_, kw1 = w1[gi % 2]; w2_, kw2 = w2[gi % 2]; a_, ka_ = aT[gi % 2]
                gi += 1
                c0 = g * GC * 128
                self.load_w(w1_[:, :, 0:GC * 128], kw1, d["ffn_w_in"], D, c0, GC * 128)
                self.load_w(w1_[:, :, GC * 128:2 * GC * 128], kw1, d["ffn_w_in"], D, DFF + c0, GC * 128)
                P.dma("pool", w2_, d["ffn_w_out"][c0:c0 + GC * 128, :].rearrange("(c p) n -> p c n", p=128), (), (kw2,))
                for cl in range(GC):
                    fc = g * GC + cl
                    for hh in range(2):
                        g_, kg_ = gT[hc % 2]; t_, kt_ = t1[hc % 2]
                        hc += 1
                        bg = 0 if hh == 0 else 3
                        pg = P.bank(bg); kpg = "psb%d" % bg
                        ph = P.bank(1); pu = P.bank(2)
                        cb = hh * 512
                        for kc in range(16):
                            P.mm(pg[:, :], w1_[:, kc, cl * 128:(cl + 1) * 128], hT[:, kc, cb:cb + 512], kc == 0, kc == 15,
                                 (kw1, khT), (kpg,))
                        for kc in range(16):
                            P.mm(ph[:, 0:2], w1_[:, kc, cl * 128:(cl + 1) * 128], hT[:, kc, cb + 512:cb + 514], kc == 0, kc == 15,
                                 (kw1, khT), ("psb1",))
                        for kc in range(16):
                            P.mm(pu[:, :], w1_[:, kc, (GC + cl) * 128:(GC + cl + 1) * 128], hT[:, kc, cb + 1:cb + 513],
                                 kc == 0, kc == 15, (kw1, khT), ("psb2",))
                        P.copy("act", g_[:, 0:512], pg[:, :], (kpg,), (kg_,))
                        P.copy("act", g_[:, 512:514], ph[:, 0:2], ("psb1",), (kg_,))
                        P.act(t_[:, :], g_[:, 1:513], AF.Identity, (kg_, kcw), (kt_,), bias=cw[:, fc, 3:4], scale=cw[:, fc, 1:2])
                        P.stt("dve", t_[:, :], g_[:, 0:512], cw[:, fc, 0:1], t_[:, :], ALU.mult, ALU.add, (kg_, kcw, kt_), (kt_,))
                        P.stt("dve", t_[:, :], g_[:, 2:514], cw[:, fc, 2:3], t_[:, :], ALU.mult, ALU.add, (kg_, kcw, kt_), (kt_,))
                        P.act(t_[:, :], t_[:, :], AF.Gelu, (kt_,), (kt_,))
                        P.tt("dve", a_[:, cl, cb:cb + 512], t_[:, :], pu[:, :], ALU.mult, (kt_, "psb2"), (ka_,))
                for tl in range(ntl):
                    for hf in range(4):
                        pb = P.bank(4 + hf); kpb = "psb%d" % (4 + hf)
                        for cl in range(GC):
                            P.mm(pb[:, :], a_[:, cl, tl * 128:(tl + 1) * 128], w2_[:, cl, hf * 512:(hf + 1) * 512],
                                 cl == 0, cl == GC - 1, (ka_, kw2), (kpb,))
                        dst = acc[:, tl, hf * 512:(hf + 1) * 512]
                        if g == 0:
                            P.copy("dve" if hf % 2 else "act", dst, pb[:, :], (kpb,), (kacc[tl],))
                        else:
                            P.tt("dve", dst, dst, pb[:, :], ALU.add, (kpb, kacc[tl]), (kacc[tl],))
            for tl in range(ntl):
                x, kx = xr[0]; y, ky = yo[0]
                tq = ts0 + tl * 128
                P.dma("sp", x, d["H1_" + n_][tq:tq + 128, :], (), (kx,))
                P.stt("dve", x, x, ALPHA, acc[:, tl, :], ALU.mult, ALU.add, (kx, kacc[tl]), (kx,))
                self.layer_norm(x, kx, 128, (g2_, b2_), (kg, kb), y, ky, LN_EPS, tmp)
                P.dma("sp", d["y_" + n_][tq - NM:tq - NM + 128, :], y, (ky,), ())

    def build(self):
        self.setup_consts()
        for s in self.seqs:
            if 1 in self.run:
                self.stage1(s)
            if 2 in self.run:
                self.stage2(s)
            if 3 in self.run:
                self.stage3(s)
            if 4 in self.run:
                self.stage4(s)
            if 5 in self.run:
                self.stage5(s)
        self.P.finish()
        return self.nc


def _bias_table(rpb):
    rpb = np.asarray(rpb, np.float32).reshape(HA, 15, 31)
    tab = np.full((HA, 5, 128, NKEY), NEG, np.float32)
    qc = np.arange(64)
    c0 = np.clip(qc - 8, 0, 48)
    kc = np.arange(64)
    colmask = (kc[None, :] >= c0[:, None]) & (kc[None, :] < c0[:, None] + 16)
    dc = np.clip(kc[None, :] - qc[:, None], -15, 15) + 15
    rel = {0: (0, 0), 1: (0, 0), 2: (0, 1), 3: (2, 2), 4: (2, 2)}
    for c in range(5):
        for qr in range(2):
            r_rel = 2 * c + qr
            r0 = rel[c][qr]
            for j in range(8):
                krow = r0 + j
                dr = krow - r_rel + 7
                blk = rpb[:, dr][:, dc]
                blk = np.where(colmask[None], blk, NEG)
                tab[:, c, qr * 64:(qr + 1) * 64, krow * 64:(krow + 1) * 64] = blk
        tab[:, c, :, WINR * 64:] = 0.0
    return tab


def _consts():
    ident = np.eye(128, dtype=np.float32)
    s = np.arange(128)[:, None]
    t = np.arange(128)[None, :]
    masks = np.stack([(s <= t), (s >= t), (s < t), (s <= t), (s > t), (s >= t)]).astype(np.float32)
    return ident, masks


def _common_inputs(inp):
    ident, masks = _consts()
    f = lambda a: np.ascontiguousarray(np.asarray(a, np.float32))
    m = {
        "meta_tokens": f(inp["meta_tokens"]),
        "emb_ln_g": f(inp["emb_ln_g"]), "emb_ln_b": f(inp["emb_ln_b"]),
        "ln1_g": f(inp["ln1_g"][0]), "ln1_b": f(inp["ln1_b"][0]),
        "ln2_g": f(inp["ln2_g"][0]), "ln2_b": f(inp["ln2_b"][0]),
        "w_in": f(inp["w_in"][0]),
        "bias_tab": _bias_table(inp["attn_rpb"][0]),
        "rwkv_mu": f(inp["rwkv_mu"][0]), "rwkv_w0": f(inp["rwkv_w0"][0]), "rwkv_w2": f(inp["rwkv_w2"][0]),
        "rwkv_a0": f(inp["rwkv_a0"][0]), "rwkv_a2": f(inp["rwkv_a2"][0]), "rwkv_g2": f(inp["rwkv_g2"][0]),
        "rwkv_k_k": f(inp["rwkv_k_k"][0]), "rwkv_k_a": f(inp["rwkv_k_a"][0]),
        "rwkv_r_k": f(inp["rwkv_r_k"][0]).reshape(DR),
        "rwkv_lnx_g": f(inp["rwkv_lnx_g"][0]), "rwkv_lnx_b": f(inp["rwkv_lnx_b"][0]),
        "w_out": f(inp["w_out"][0]), "ffn_w_in": f(inp["ffn_w_in"][0]),
        "ffn_conv_l": np.ascontiguousarray(np.concatenate([f(inp["ffn_conv_w"][0]), f(inp["ffn_conv_b"][0])[None]], 0)
                                           .reshape(4, 44, 128).transpose(2, 1, 0)),
        "ffn_w_out": f(inp["ffn_w_out"][0]),
        "c_ident": ident, "c_masks": masks, "c_zero": np.zeros((1, NSH), np.float32),
    }
    return m


def kernel(**inputs):
    xp = np.asarray(inputs["x_prompt"], np.float32)
    xs = np.asarray(inputs["x_sample"], np.float32)
    b = Builder([xs.shape[1], xp.shape[1]])
    nc = b.build()
    common = _common_inputs(inputs)
    in_maps = []
    for c in range(8):
        m = dict(common)
        m["x_s0"] = np.ascontiguousarray(xs[c])
        m["x_s1"] = np.ascontiguousarray(xp[0])
        in_maps.append(m)
    res = run_bass_kernel_spmd(nc, in_maps, core_ids=list(range(8)))
    y_s = np.stack([res.results[c]["y_s0"] for c in range(8)], axis=0)
    y_p = res.results[0]["y_s1"][None]
    return (y_p.astype(np.float32), y_s.astype(np.float32))
```

```python
import numpy as np
import ml_dtypes
import concourse.bass as bass
import concourse.mybir as mybir
from concourse.bass_utils import run_bass_kernel_spmd

F32 = mybir.dt.float32
BF16 = mybir.dt.bfloat16
U8 = mybir.dt.uint8
ALU = mybir.AluOpType
AF = mybir.ActivationFunctionType
AX = mybir.AxisListType

D = 2048
NM = 16
DA = 1024
HA = 8
DR = 1024
HR = 16
NSH = 3 * DR + 192
NIN = 3 * DA + NSH + 256
DFF = 5632
ALPHA = 2.0 ** 0.25
LN_EPS = 1e-5
GN_EPS = 64e-5
CDEC = -float(np.exp(-0.5))
NEG = -30000.0
WINR = 10
NKEY = WINR * 64 + NM


class Prog:
    ENG = ("sp", "act", "dve", "pool", "pe")

    def __init__(self, nc, n_dma_sems=40):
        self.nc = nc
        self.q = {e: [] for e in self.ENG}
        self.cnt = {e: 0 for e in self.ENG}
        self.known = {e: {} for e in self.ENG}
        self.last_w = {}
        self.readers = {}
        self.semh = {}
        for e in ("act", "dve", "pool", "pe"):
            self.semh[e] = nc.alloc_semaphore("sem_" + e)
        self.ndma = n_dma_sems
        for i in range(n_dma_sems):
            self.semh[("dma", i)] = nc.alloc_semaphore("sem_dma%d" % i)
        self.dma_tot = [0] * n_dma_sems
        self.rr = 0
        self.arena = nc.alloc_sbuf_tensor("arena", [128, 204 * 1024], U8)
        self.off = 0
        self.uid = 0
        self.psum = [nc.alloc_psum_tensor("psb%d" % i, [128, 512], F32) for i in range(8)]

    def reset_arena(self, off=0):
        self.off = off

    def tile(self, shape, dtype, name="t"):
        esz = 4 if dtype == F32 else 2
        n = int(np.prod(shape[1:]))
        nbytes = (n * esz + 31) // 32 * 32
        assert self.off + nbytes <= 204 * 1024, ("sbuf overflow", name, self.off, nbytes)
        ap = self.arena[:, self.off:self.off + n * esz].bitcast(dtype)
        self.off += nbytes
        if len(shape) == 3:
            ap = ap.rearrange("p (a b) -> p a b", a=shape[1])
        elif len(shape) == 4:
            ap = ap.rearrange("p (a b c) -> p a b c", a=shape[1], b=shape[2])
        self.uid += 1
        return ap, "%s#%d" % (name, self.uid)

    def bank(self, i, dtype=F32):
        ap = self.psum[i][:, :]
        if dtype != F32:
            ap = ap.bitcast(dtype)
        return ap

    def _waits(self, eng, reads, writes):
        need = {}

        def add(ev):
            if ev is None:
                return
            s, v = ev
            if need.get(s, 0) < v:
                need[s] = v
        for k in reads:
            add(self.last_w.get(k))
        for k in writes:
            add(self.last_w.get(k))
            for s, v in self.readers.get(k, {}).items():
                add((s, v))
        out = []
        for s, v in need.items():
            if s == "pe" and eng == "pe":
                continue
            if self.known[eng].get(s, 0) >= v:
                continue
            self.known[eng][s] = v
            out.append((s, v))
        return out

    def _record(self, ev, reads, writes):
        s, v = ev
        for k in reads:
            d = self.readers.setdefault(k, {})
            if d.get(s, 0) < v:
                d[s] = v
        for k in writes:
            self.last_w[k] = ev
            self.readers[k] = {}

    @staticmethod
    def _px(reads, writes):
        pr = tuple(k for k in reads if k.startswith("psb"))
        if pr:
            reads = tuple(k for k in reads if not k.startswith("psb"))
            writes = tuple(writes) + pr
        return reads, writes

    def op(self, eng, fn, reads=(), writes=()):
        reads, writes = self._px(reads, writes)
        waits = self._waits(eng, reads, writes)
        self.cnt[eng] += 1
        ev = (eng, self.cnt[eng])
        self.q[eng].append((waits, fn, eng, 1))
        self._record(ev, reads, writes)

    def dma(self, eng, out, in_, reads=(), writes=()):
        i = self.rr
        self.rr = (self.rr + 1) % self.ndma
        s = ("dma", i)
        waits = self._waits(eng, reads, writes)
        if self.dma_tot[i] > 0 and self.known[eng].get(s, 0) < self.dma_tot[i]:
            self.known[eng][s] = self.dma_tot[i]
            waits.append((s, self.dma_tot[i]))
        self.dma_tot[i] += 16
        ev = (s, self.dma_tot[i])
        self.q[eng].append((waits, lambda e, o=out, a=in_: e.dma_start(out=o, in_=a), s, 16))
        self._record(ev, reads, writes)

    def raw(self, eng, fn, reads=(), writes=()):
        waits = self._waits(eng, reads, writes)
        self.q[eng].append((waits, fn, None, -1))

    def dma_dyn(self, eng, mk, reads=(), writes=()):
        i = self.rr
        self.rr = (self.rr + 1) % self.ndma
        s = ("dma", i)
        waits = self._waits(eng, reads, writes)
        if self.dma_tot[i] > 0 and self.known[eng].get(s, 0) < self.dma_tot[i]:
            self.known[eng][s] = self.dma_tot[i]
            waits.append((s, self.dma_tot[i]))
        self.dma_tot[i] += 16
        ev = (s, self.dma_tot[i])

        def fn(e, mk=mk):
            o, a = mk()
            return e.dma_start(out=o, in_=a)
        self.q[eng].append((waits, fn, s, 16))
        self._record(ev, reads, writes)

    def barrier(self):
        for e in self.ENG:
            waits = []
            for s in ("act", "dve", "pool", "pe"):
                if s != e and self.cnt[s] > self.known[e].get(s, 0):
                    self.known[e][s] = self.cnt[s]
                    waits.append((s, self.cnt[s]))
            for i in range(self.ndma):
                s = ("dma", i)
                if self.dma_tot[i] > self.known[e].get(s, 0):
                    self.known[e][s] = self.dma_tot[i]
                    waits.append((s, self.dma_tot[i]))
            if waits:
                self.q[e].append((waits, None, None, 0))

    def finish(self):
        self.barrier()
        nc = self.nc
        semh = self.semh
        q = self.q
        with nc.Block() as block:
            def mk(name):
                def body(e):
                    for waits, fn, s, amt in q[name]:
                        for ws, wv in waits:
                            e.wait_ge(semh[ws], wv)
                        if fn is not None:
                            if amt < 0:
                                fn(e)
                            else:
                                fn(e).then_inc(semh[s], amt)
                return body
            block.sync(mk("sp"))
            block.scalar(mk("act"))
            block.vector(mk("dve"))
            block.gpsimd(mk("pool"))
            block.tensor(mk("pe"))

    def mm(self, out, lhsT, rhs, start, stop, reads, writes):
        self.op("pe", lambda e: e.matmul(out, lhsT, rhs, start=start, stop=stop), reads, writes)

    def tr(self, out, in_, ident, reads, writes):
        self.op("pe", lambda e: e.transpose(out, in_, ident), reads, writes)

    def act(self, out, in_, func, reads, writes, bias=None, scale=None, accum=None):
        kw = {}
        if bias is not None:
            kw["bias"] = bias
        if scale is not None:
            kw["scale"] = scale
        if accum is not None:
            kw["accum_out"] = accum
        self.op("act", lambda e: e.activation(out, in_, func, **kw), reads, writes)

    def tt(self, eng, out, a, b, op, reads, writes):
        self.op(eng, lambda e: e.tensor_tensor(out, a, b, op), reads, writes)

    def ts(self, eng, out, a, s1, s2, op0, op1, reads, writes):
        if s2 is None:
            self.op(eng, lambda e: e.tensor_scalar(out, a, s1, None, op0), reads, writes)
        else:
            self.op(eng, lambda e: e.tensor_scalar(out, a, s1, s2, op0, op1), reads, writes)

    def stt(self, eng, out, a, sc, b, op0, op1, reads, writes):
        self.op(eng, lambda e: e.scalar_tensor_tensor(out, a, sc, b, op0, op1), reads, writes)

    def copy(self, eng, out, in_, reads, writes):
        if eng == "act":
            self.op("act", lambda e: e.copy(out, in_), reads, writes)
        else:
            self.op(eng, lambda e: e.tensor_copy(out, in_), reads, writes)

    def reduce(self, eng, out, in_, op, reads, writes):
        self.op(eng, lambda e: e.tensor_reduce(out, in_, AX.X, op), reads, writes)

    def recip(self, eng, out, in_, reads, writes):
        self.op(eng, lambda e: e.reciprocal(out, in_), reads, writes)

    def memset(self, eng, ap, val, writes):
        self.op(eng, lambda e: e.memset(ap, val), (), writes)


class Seq:
    def __init__(self, name, T, sliced=False):
        self.name = name
        self.T = T
        self.sliced = sliced
        self.ng = T // 1024
        self.L = T + NM
        self.rows = T // 64
        self.ntile = 1 + T // 128

    def tile_range(self, i):
        if i == 0:
            return 0, NM
        return NM + (i - 1) * 128, 128


class Builder:
    def __init__(self, seq_T, debug=False, stages=5, run=(1, 2, 3, 4, 5), sliced=None):
        self.debug = debug
        self.run = run
        self.stages = stages
        nc = bass.Bass("TRN2", target_bir_lowering=False)
        self.nc = nc
        self.P = Prog(nc)
        sliced = sliced or [False] * len(seq_T)
        self.seqs = [Seq("s%d" % i, T, sl) for i, (T, sl) in enumerate(zip(seq_T, sliced))]
        self.dram = {}
        self._declare_io()

    def din(self, name, shape, dtype=F32):
        ap = self.nc.dram_tensor(name, list(shape), dtype, kind="ExternalInput").ap()
        self.dram[name] = ap
        return ap

    def dscr(self, name, shape, dtype=F32):
        kind = "ExternalOutput" if self.debug else "Internal"
        ap = self.nc.dram_tensor(name, list(shape), dtype, kind=kind).ap()
        self.dram[name] = ap
        return ap

    def _declare_io(self):
        for s in self.seqs:
            self.din("x_" + s.name, [s.T, D])
            self.nc_out = None
        for s in self.seqs:
            ap = self.nc.dram_tensor("y_" + s.name, [1024 if s.sliced else s.T, D], F32, kind="ExternalOutput").ap()
            self.dram["y_" + s.name] = ap
            if s.sliced:
                self.din("xext_" + s.name, [1280, D])
                self.din("msel_" + s.name, [128, s.ng + 1])
        self.din("c_shift", [NM, 128])
        self.din("meta_tokens", [NM, D])
        for nm in ("emb_ln_g", "emb_ln_b", "ln1_g", "ln1_b", "ln2_g", "ln2_b"):
            self.din(nm, [D])
        self.din("w_in", [D, NIN])
        self.din("bias_tab", [HA, 5, 128, NKEY])
        self.din("rwkv_mu", [2, NSH])
        self.din("rwkv_w0", [2, DR])
        self.din("rwkv_w2", [2, 96, DR])
        self.din("rwkv_a0", [2, DR])
        self.din("rwkv_a2", [2, 96, DR])
        self.din("rwkv_g2", [256, DR])
        for nm in ("rwkv_k_k", "rwkv_k_a", "rwkv_r_k", "rwkv_lnx_g", "rwkv_lnx_b"):
            self.din(nm, [DR])
        self.din("w_out", [D, D])
        self.din("ffn_w_in", [D, 2 * DFF])
        self.din("ffn_conv_l", [128, 44, 4])
        self.din("ffn_w_out", [DFF, D])
        self.din("c_ident", [128, 128])
        self.din("c_masks", [6, 128, 128])
        self.din("c_zero", [1, NSH])
        for s in self.seqs:
            n = s.name
            self.dscr("H0_" + n, [s.L, D])
            self.dscr("QT_" + n, [HA, 128, s.L], BF16)
            self.dscr("KT_" + n, [HA, 128, s.L], BF16)
            self.dscr("V_" + n, [s.L, DA], BF16)
            self.dscr("P_" + n, [s.L, NSH])
            self.dscr("G_" + n, [s.L, DR])
            self.dscr("YF_" + n, [s.L, DR])
            self.dscr("BF_" + n, [s.L, DR])
            self.dscr("OT_" + n, [16, 128, s.L], BF16)
            self.dscr("H1_" + n, [s.L, D])
            self.dscr("H1T_" + n, [16, 128, s.L], BF16)
            if s.sliced:
                self.dscr("OTOK_" + n, [s.L, D], BF16)
                self.dscr("H1s_" + n, [1280, D])
                self.dscr("H1Ts_" + n, [16, 128, 1280], BF16)

    def setup_consts(self):
        P = self.P
        d = self.dram
        self.ident_f, k1 = P.tile([128, 128], F32, "identf")
        P.dma("sp", self.ident_f, d["c_ident"], (), (k1,))
        self.ident_b, k2 = P.tile([128, 128], BF16, "identb")
        P.copy("dve", self.ident_b, self.ident_f, (k1,), (k2,))
        self.kid = (k1, k2)
        self.const_end = P.off

    def bcast_load(self, name_ap, n, nm):
        P = self.P
        t, k = P.tile([128, n], F32, nm)
        P.dma("sp", t, name_ap.partition_broadcast(128), (), (k,))
        return t, k

    def layer_norm(self, x, kx, n, gb, kgb, out, kout, eps, tmp):
        P = self.P
        (st, kst), (mv, kmv), (rs, krs) = tmp
        g, b = gb
        for c in range(4):
            P.op("dve", lambda e, c=c: e.bn_stats(st[:n, c, :], x[:n, c * 512:(c + 1) * 512]), (kx,), (kst,))
        P.op("dve", lambda e: e.bn_aggr(mv[:n, :], st[:n].rearrange("p a b -> p (a b)")), (kst,), (kmv,))
        P.act(rs[:n, :], mv[:n, 1:2], AF.Sqrt, (kmv,), (krs,), bias=eps)
        P.op("dve", lambda e: e.reciprocal(rs[:n, :], rs[:n, :]), (krs,), (krs,))
        P.ts("dve", x[:n, :], x[:n, :], mv[:n, 0:1], rs[:n, 0:1], ALU.subtract, ALU.mult, (kx, kmv, krs), (kx,))
        P.tt("pool", x[:n, :], x[:n, :], g[:n, :], ALU.mult, (kx,) + kgb, (kx,))
        P.tt("dve", out[:n, :], x[:n, :], b[:n, :], ALU.add, (kx,) + kgb, (kout,))

    def ln_tmp(self):
        P = self.P
        return (P.tile([128, 4, 6], F32, "bnst"), P.tile([128, 2], F32, "bnmv"), P.tile([128, 1], F32, "rstd"))

    def transpose_to_fm(self, src, ksrc, n, dst, kdst, col0, banks, ctr):
        P = self.P
        for g4 in range(4):
            bi = banks[(ctr[0]) % len(banks)]
            ctr[0] += 1
            pb = P.bank(bi)
            kb = "psb%d" % bi
            for j in range(4):
                kc = g4 * 4 + j
                P.tr(pb[:, j * 128:j * 128 + n], src[:n, kc * 128:(kc + 1) * 128], self.ident_f[:n, :n],
                     (ksrc, self.kid[0]), (kb,))
            eng = "act" if g4 % 2 == 0 else "dve"
            P.copy(eng, dst[:, g4 * 4:g4 * 4 + 4, col0:col0 + n],
                   pb.rearrange("p (a b) -> p a b", a=4)[:, :, :n], (kb,), (kdst,))

    def load_w(self, wt, kw, src, rows, col0, ncols):
        P = self.P
        nk = rows // 128
        for k0 in range(0, nk, 4):
            k1 = min(nk, k0 + 4)
            P.dma("pool", wt[:, k0:k1, :ncols],
                  src[k0 * 128:k1 * 128, col0:col0 + ncols].rearrange("(kc p) n -> p kc n", p=128), (), (kw,))

    def stage1(self, s):
        P = self.P
        d = self.dram
        n_ = s.name
        P.barrier()
        P.reset_arena(self.const_end)
        g0, kg = self.bcast_load(d["emb_ln_g"], D, "g0")
        b0, kb = self.bcast_load(d["emb_ln_b"], D, "b0")
        tmp = self.ln_tmp()
        g2, kg2 = P.tile([128, 2, DR], BF16, "g2")
        self.load_w(g2, kg2, d["rwkv_g2"], 256, 0, DR)
        xt = [P.tile([128, D], F32, "xt%d" % i) for i in range(2)]
        ht = [P.tile([128, D], F32, "ht%d" % i) for i in range(2)]
        wbuf = [P.tile([128, 16, 512], BF16, "w%d" % i) for i in range(2)]
        stg = [P.tile([128, 512], F32, "stg%d" % i) for i in range(3)]
        stgb = [P.tile([128, 512], BF16, "stgb%d" % i) for i in range(2)]
        qkT = [P.tile([128, 4, 128], BF16, "qkT%d" % i) for i in range(2)]
        sgT = [P.tile([128, 2, 128], BF16, "sgT%d" % i) for i in range(2)]
        gst = [P.tile([128, DR], F32, "gst%d" % i) for i in range(2)]
        TB = 16
        h0T, kh0T = P.tile([128, 16, TB * 128 + NM], BF16, "h0T")
        groups = []
        for c0 in range(0, NIN, 512):
            groups.append((c0, min(512, NIN - c0)))
        tiles = list(range(s.ntile))
        blocks = [tiles[0:TB + 1]] + [tiles[i:i + TB] for i in range(TB + 1, s.ntile, TB)]
        ctr = [0]
        wi = 0
        ci = 0
        for blk in blocks:
            base = s.tile_range(blk[0])[0]
            for ti in blk:
                t0, n = s.tile_range(ti)
                x, kx = xt[ci % 2]
                h, kh = ht[ci % 2]
                ci += 1
                if ti == 0:
                    P.dma("sp", x[:n, :], d["meta_tokens"], (), (kx,))
                else:
                    P.dma("sp", x[:n, :], d["x_" + n_][t0 - NM:t0 - NM + n, :], (), (kx,))
                self.layer_norm(x, kx, n, (g0, b0), (kg, kb), h, kh, LN_EPS, tmp)
                P.dma("sp", d["H0_" + n_][t0:t0 + n, :], h[:n, :], (kh,), ())
                self.transpose_to_fm(h, kh, n, h0T, kh0T, t0 - base, (0, 1), ctr)
            if self.stages == 0:
                continue
            for gi, (c0, nc_) in enumerate(groups):
                w, kw = wbuf[wi % 2]
                wi += 1
                self.load_w(w, kw, d["w_in"], D, c0, nc_)
                for ti in blk:
                    t0, n = s.tile_range(ti)
                    bi = 2 + (ctr[0] % 2)
                    ctr[0] += 1
                    pb = P.bank(bi)
                    kpb = "psb%d" % bi
                    for kc in range(16):
                        P.mm(pb[:n, :nc_], h0T[:, kc, t0 - base:t0 - base + n], w[:, kc, :nc_],
                             kc == 0, kc == 15, (kh0T, kw), (kpb,))
                    ev = "act" if ctr[0] % 2 == 0 else "dve"
                    if self.stages == -1 or (self.stages == -2 and gi < 4) or (self.stages == -3 and gi >= 4) or (self.stages == -4 and gi < 12):
                        sf, ksf = stg[ctr[0] % 3]
                        P.copy(ev, sf[:n, :], pb[:n, :], (kpb,), (ksf,))
                        continue
                    if gi < 4:
                        sb, ksb = stgb[ctr[0] % 2]
                        if gi < 2:
                            P.act(sb[:n, :], pb[:n, :], AF.Copy, (kpb,), (ksb,), scale=128.0 ** -0.5)
                        else:
                            P.copy(ev, sb[:n, :], pb[:n, :], (kpb,), (ksb,))
                        tb = 4
                        ptb = P.bank(tb, BF16)
                        for j in range(4):
                            P.tr(ptb[:, j * 128:j * 128 + n], sb[:n, j * 128:(j + 1) * 128], self.ident_b[:n, :n],
                                 (ksb, self.kid[1]), ("psb4",))
                        qt, kqt = qkT[ctr[0] % 2]
                        P.copy("dve" if ev == "act" else "act", qt[:, :, :n],
                               ptb[:, 0:512].rearrange("p (a b) -> p a b", a=4)[:, :, :n], ("psb4",), (kqt,))
                        dst = d[("QT_" if gi < 2 else "KT_") + n_]
                        h0 = (gi % 2) * 4
                        P.dma("sp", dst[h0:h0 + 4, :, t0:t0 + n].rearrange("h p t -> p h t"), qt[:, :, :n], (kqt,), ())
                    elif gi < 6:
                        sb, ksb = stgb[ctr[0] % 2]
                        P.copy(ev, sb[:n, :], pb[:n, :], (kpb,), (ksb,))
                        P.dma("sp", d["V_" + n_][t0:t0 + n, (gi - 4) * 512:(gi - 3) * 512], sb[:n, :], (ksb,), ())
                    elif gi < 12:
                        sf, ksf = stg[ctr[0] % 3]
                        P.copy(ev, sf[:n, :], pb[:n, :], (kpb,), (ksf,))
                        P.dma("sp", d["P_" + n_][t0:t0 + n, (gi - 6) * 512:(gi - 5) * 512], sf[:n, :], (ksf,), ())
                    else:
                        sf, ksf = stg[ctr[0] % 3]
                        P.copy("dve", sf[:n, :192], pb[:n, :192], (kpb,), (ksf,))
                        P.dma("sp", d["P_" + n_][t0:t0 + n, 3072:3264], sf[:n, :192], (ksf,), ())
                        sb, ksb = stgb[ctr[0] % 2]
                        P.act(sb[:n, :256], pb[:n, 192:448], AF.Sigmoid, (kpb,), (ksb,))
                        ptb = P.bank(4, BF16)
                        for j in range(2):
                            P.tr(ptb[:, j * 128:j * 128 + n], sb[:n, j * 128:(j + 1) * 128], self.ident_b[:n, :n],
                                 (ksb, self.kid[1]), ("psb4",))
                        sg, ksg = sgT[ctr[0] % 2]
                        P.copy("dve", sg[:, :, :n], ptb[:, 0:256].rearrange("p (a b) -> p a b", a=2)[:, :, :n],
                               ("psb4",), (ksg,))
                        go, kgo = gst[ctr[0] % 2]
                        for hf in range(2):
                            pg = P.bank(5 + hf)
                            kpg = "psb%d" % (5 + hf)
                            for j in range(2):
                                P.mm(pg[:n, :], sg[:, j, :n], g2[:, j, hf * 512:(hf + 1) * 512], j == 0, j == 1,
                                     (ksg, kg2), (kpg,))
                            P.copy("act" if hf == 0 else "dve", go[:n, hf * 512:(hf + 1) * 512], pg[:n, :], (kpg,), (kgo,))
                        P.dma("sp", d["G_" + n_][t0:t0 + n, :], go[:n, :], (kgo,), ())

    def stage2(self, s):
        P = self.P
        d = self.dram
        n_ = s.name
        for dr in (0, 1):
            P.barrier()
            P.reset_arena(self.const_end)
            self._rwkv_dir(s, dr)

    def _rwkv_dir(self, s, dr):
        P = self.P
        d = self.dram
        n_ = s.name
        Pd = d["P_" + n_]
        L = s.L
        mu, kmu = self.bcast_load(d["rwkv_mu"][dr], NSH, "mu")
        w0, kw0 = self.bcast_load(d["rwkv_w0"][dr], DR, "w0")
        a0, ka0 = self.bcast_load(d["rwkv_a0"][dr], DR, "a0")
        kkp, kkkp = self.bcast_load(d["rwkv_k_k"], DR, "k_k")
        kap, kkap = self.bcast_load(d["rwkv_k_a"], DR, "k_a")
        rkp, krkp = self.bcast_load(d["rwkv_r_k"], DR, "r_k")
        w2, kw2 = P.tile([128, DR], F32, "w2")
        a2, ka2 = P.tile([128, DR], F32, "a2")
        P.dma("sp", w2[:96, :], d["rwkv_w2"][dr], (), (kw2,))
        P.dma("sp", a2[:96, :], d["rwkv_a2"][dr], (), (ka2,))
        tri, ktri = P.tile([128, 128], F32, "tri")
        P.dma("sp", tri, d["c_masks"][dr], (), (ktri,))
        ones, kones = P.tile([128, 128], F32, "ones")
        P.memset("pool", ones, 1.0, (kones,))
        m4, km4 = P.tile([128, 4, 128], F32, "mask4")
        for q in range(4):
            P.dma("sp", m4[:, q, :], d["c_masks"][2 + 2 * dr + (q % 2)], (), (km4,))
        suT, ksuT = P.tile([128, 128], F32, "suT")
        P.dma("sp", suT, d["c_masks"][2 + 2 * (1 - dr)], (), (ksuT,))
        if dr == 1:
            lg, klg = self.bcast_load(d["rwkv_lnx_g"], DR, "lnxg")
            lb, klb = self.bcast_load(d["rwkv_lnx_b"], DR, "lnxb")
        Sf, kSf = P.tile([128, 8, 64], F32, "Sf")
        Sb, kSb = P.tile([128, 8, 64], BF16, "Sb")
        P.memset("pool", Sf, 0.0, (kSf,))
        P.memset("pool", Sb, 0.0, (kSb,))
        kSfh = [[kSf + "/%d/%d" % (j, h) for h in range(2)] for j in range(8)]
        kSbh = [[kSb + "/%d/%d" % (j, h) for h in range(2)] for j in range(8)]
        for j in range(8):
            for h in range(2):
                P.last_w[kSfh[j][h]] = P.last_w[kSf]
                P.last_w[kSbh[j][h]] = P.last_w[kSb]
        pt = [P.tile([128, NSH], F32, "p%d" % i) for i in range(1)]
        pn = [P.tile([128, NSH], F32, "pn%d" % i) for i in range(1)]
        T = lambda nm, dt=F32: P.tile([128, DR], dt, nm)
        sg, ksg = T("sg"); av, kav = T("a"); cum, kcum = T("cum"); e1, ke1 = T("e1"); e2, ke2 = T("e2")
        kk, kkk = T("kk"); kq, kkq = T("kq"); ka_, kka_ = T("ka"); tmp, ktmp = T("tmp"); bon, kbon = T("bon")
        Y, kY = T("Y")
        At, kAt = T("At", BF16); Bt, kBt = T("Bt", BF16); Kt, kKt = T("Kt", BF16); Rt, kRt = T("Rt", BF16)
        BW, kBW = T("BW", BF16); KW, kKW = T("KW", BF16); Vb, kVb = T("Vb", BF16)
        ART, kART = P.tile([128, 8, 256], BF16, "ART")
        BTt, kBTt = P.tile([128, 8, 128], BF16, "BT")
        KTt, kKTt = P.tile([128, 8, 128], BF16, "KT")
        thT, kthT = P.tile([128, 2, 128], F32, "thT")
        th, kth = P.tile([128, 192], F32, "th")
        sm, ksm = P.tile([128, 4, 16], F32, "small")
        wc, kwc = P.tile([128, 8], F32, "wc")
        if dr == 1:
            yf, kyf = e1, ke1
            gt, kgt = e2, ke2
            ob, kob = T("ob", BF16)
            oT, koT = P.tile([128, 8, 128], BF16, "oT")
        M4a, kM4a0 = P.tile([128, 16, 4, 128], BF16, "M4a")
        kM4 = [kM4a0 + "/%d" % i for i in range(16)]
        NTa, kNTa0 = P.tile([128, 16, 128], BF16, "NTa")
        kNT = [kNTa0 + "/%d" % g for g in range(4)]
        Ta = []
        PPa = []
        for i in range(2):
            t_, k_ = P.tile([128, 16, 128], BF16, "Ta%d" % i)
            Ta.append((t_, [k_ + "/%d" % g for g in range(4)]))
            t_, k_ = P.tile([128, 2, 16, 128], BF16, "PPa%d" % i)
            PPa.append((t_, [[k_ + "/%d/%d" % (w, g) for g in range(4)] for w in range(2)]))
        Xa, kXa0 = P.tile([128, 16, 64], BF16, "Xa")
        kXa = [kXa0 + "/%d" % b for b in range(2)]
        Ua, kUa0 = P.tile([128, 16, 64], BF16, "Ua")
        kUa = [kUa0 + "/%d" % b for b in range(2)]
        id4, kid4 = P.tile([128, 4, 128], BF16, "id4")
        for q in range(4):
            P.copy("pool", id4[:, q, :], self.ident_b, (self.kid[1],), (kid4,))
        order = list(range(s.ntile)) if dr == 0 else list(range(s.ntile - 1, -1, -1))
        tctr = [0]
        v3 = lambda ap: ap.rearrange("p (h c) -> p h c", h=16)
        p, kp = pt[0]
        q_, kq_ = pn[0]
        f, kf = q_, kq_
        r_ = f[:, 0:DR]; k_ = f[:, DR:2 * DR]; v_ = f[:, 2 * DR:3 * DR]

        def prep1(ti):
            t0, n = s.tile_range(ti)
            P.dma("sp", p[:n, :], Pd[t0:t0 + n, :], (), (kp,))
            if dr == 0:
                if t0 == 0:
                    P.dma("sp", q_[0:1, :], d["c_zero"], (), (kq_,))
                    P.dma("sp", q_[1:n, :], Pd[0:n - 1, :], (), (kq_,))
                else:
                    P.dma("sp", q_[:n, :], Pd[t0 - 1:t0 - 1 + n, :], (), (kq_,))
            else:
                if t0 + n == L:
                    P.dma("sp", q_[n - 1:n, :], d["c_zero"], (), (kq_,))
                    P.dma("sp", q_[0:n - 1, :], Pd[t0 + 1:t0 + n, :], (), (kq_,))
                else:
                    P.dma("sp", q_[:n, :], Pd[t0 + 1:t0 + 1 + n, :], (), (kq_,))
            yield
            P.tt("dve", q_[:n, :], q_[:n, :], p[:n, :], ALU.subtract, (kq_, kp), (kq_,))
            yield
            P.tt("pool", q_[:n, :], q_[:n, :], mu[:n, :], ALU.mult, (kq_, kmu), (kq_,))
            yield
            P.tt("dve", q_[:n, :], q_[:n, :], p[:n, :], ALU.add, (kq_, kp), (kq_,))
            yield
            P.act(th[:n, 0:96], f[:n, 3 * DR:3 * DR + 96], AF.Tanh, (kf,), (kth,))
            P.copy("pool", th[:n, 96:192], f[:n, 3 * DR + 96:3 * DR + 192], (kf,), (kth,))
            pb0 = P.bank(0)
            for q in range(2):
                P.tr(pb0[:96, q * 128:q * 128 + n], th[:n, q * 96:(q + 1) * 96], self.ident_f[:n, :n],
                     (kth, self.kid[0]), ("psb0",))
            P.copy("act", thT[:96, :, :n], pb0[:96, 0:256].rearrange("p (a b) -> p a b", a=2)[:, :, :n], ("psb0",), (kthT,))
            yield
            for which, (wm, kwm, bias, kbias, dst, kdst) in enumerate(((w2, kw2, w0, kw0, sg, ksg), (a2, ka2, a0, ka0, av, kav))):
                for hf in range(2):
                    pb = P.bank(hf); kpb = "psb%d" % hf
                    P.mm(pb[:n, :], thT[:96, which, :n], wm[:96, hf * 512:(hf + 1) * 512], True, True, (kthT, kwm), (kpb,))
                    P.tt("dve", dst[:n, hf * 512:(hf + 1) * 512], pb[:n, :], bias[:n, hf * 512:(hf + 1) * 512], ALU.add,
                         (kpb, kbias), (kdst,))
                    yield
                P.act(dst[:n, :], dst[:n, :], AF.Sigmoid, (kdst,), (kdst,))
                yield
            for hf in range(2):
                pb = P.bank(hf); kpb = "psb%d" % hf
                P.mm(pb[:n, :], tri[:n, :n], sg[:n, hf * 512:(hf + 1) * 512], True, True, (ktri, ksg), (kpb,))
                P.copy("act", cum[:n, hf * 512:(hf + 1) * 512], pb[:n, :], (kpb,), (kcum,))
                yield
            P.tt("pool", kk[:n, :], k_[:n, :], kkp[:n, :], ALU.mult, (kf, kkkp), (kkk,))
            yield
            P.tt("pool", tmp[:n, :], kk[:n, :], kk[:n, :], ALU.mult, (kkk,), (ktmp,))
            yield
            P.reduce("dve", sm[:n, 0, :], v3(tmp[:n, :]), ALU.add, (ktmp,), (ksm,))
            P.act(sm[:n, 0, :], sm[:n, 0, :], AF.Sqrt, (ksm,), (ksm,))
            P.ts("dve", sm[:n, 0, :], sm[:n, 0, :], 1e-12, None, ALU.max, None, (ksm,), (ksm,))
            P.recip("dve", sm[:n, 0, :], sm[:n, 0, :], (ksm,), (ksm,))
            yield
            P.tt("dve", v3(kk[:n, :]), v3(kk[:n, :]), sm[:n, 0, :].unsqueeze(2).to_broadcast([n, 16, 64]), ALU.mult,
                 (kkk, ksm), (kkk,))
            yield
            P.stt("dve", kq[:n, :], av[:n, :], -1.0, kap[:n, :], ALU.add, ALU.mult, (kav, kkap), (kkq,))
            yield
            P.stt("dve", kq[:n, :], kq[:n, :], 1.0, k_[:n, :], ALU.add, ALU.mult, (kkq, kf), (kkq,))
            yield
            P.tt("pool", ka_[:n, :], kk[:n, :], av[:n, :], ALU.mult, (kkk, kav), (kka_,))
            yield

        def prep2(ti):
            t0, n = s.tile_range(ti)
            for hf in range(2):
                pb = P.bank(2 + hf); kpb = "psb%d" % (2 + hf)
                P.mm(pb[:n, :], ones[:n, :n], sg[:n, hf * 512:(hf + 1) * 512], True, True, (kones, ksg), (kpb,))
                P.tt("dve", e2[:n, hf * 512:(hf + 1) * 512], pb[:n, :], cum[:n, hf * 512:(hf + 1) * 512], ALU.subtract,
                     (kpb, kcum), (ke2,))
            pb0 = P.bank(0)
            for j in range(8):
                P.mm(pb0[:, j:j + 1], sg[:n, j * 128:(j + 1) * 128], ones[:n, 0:1], True, True, (ksg, kones), ("psb0",))
            P.act(wc[:, :], pb0[:, 0:8], AF.Exp, ("psb0",), (kwc,), scale=CDEC)
            P.tt("pool", tmp[:n, :], r_[:n, :], kq[:n, :], ALU.mult, (kf, kkq), (ktmp,))
            P.tt("pool", tmp[:n, :], tmp[:n, :], rkp[:n, :], ALU.mult, (ktmp, krkp), (ktmp,))
            P.reduce("dve", sm[:n, 1, :], v3(tmp[:n, :]), ALU.add, (ktmp,), (ksm,))
            P.tt("dve", v3(bon[:n, :]), v3(v_[:n, :]), sm[:n, 1, :].unsqueeze(2).to_broadcast([n, 16, 64]), ALU.mult,
                 (kf, ksm), (kbon,))
            if dr == 0:
                P.dma("sp", d["BF_" + n_][t0:t0 + n, :], bon[:n, :], (kbon,), ())
            P.act(e1[:n, :], cum[:n, :], AF.Exp, (kcum,), (ke1,), scale=CDEC)
            P.tt("dve", Rt[:n, :], r_[:n, :], e1[:n, :], ALU.mult, (kf, ke1), (kRt,))
            P.act(e1[:n, :], cum[:n, :], AF.Exp, (kcum,), (ke1,), scale=-CDEC)
            P.tt("pool", Bt[:n, :], ka_[:n, :], e1[:n, :], ALU.mult, (kka_, ke1), (kBt,))
            P.tt("dve", Kt[:n, :], kq[:n, :], e1[:n, :], ALU.mult, (kkq, ke1), (kKt,))
            P.act(e2[:n, :], e2[:n, :], AF.Exp, (ke2,), (ke2,), scale=CDEC)
            P.tt("pool", BW[:n, :], ka_[:n, :], e2[:n, :], ALU.mult, (kka_, ke2), (kBW,))
            P.tt("dve", KW[:n, :], kq[:n, :], e2[:n, :], ALU.mult, (kkq, ke2), (kKW,))
            P.tt("dve", cum[:n, :], cum[:n, :], sg[:n, :], ALU.subtract, (kcum, ksg), (kcum,))
            P.act(e1[:n, :], cum[:n, :], AF.Exp, (kcum,), (ke1,), scale=CDEC)
            P.stt("dve", At[:n, :], kk[:n, :], -1.0, e1[:n, :], ALU.mult, ALU.mult, (kkk, ke1), (kAt,))
            P.copy("pool", Vb[:n, :], v_[:n, :], (kf,), (kVb,))
            for xi, (src, ksrc) in enumerate(((At, kAt), (Rt, kRt), (Bt, kBt), (Kt, kKt))):
                bi = tctr[0] % 4
                tctr[0] += 1
                ptb = P.bank(bi, BF16)
                kptb = "psb%d" % bi
                for j in range(8):
                    P.tr(ptb[:, j * 128:j * 128 + n], src[:n, j * 128:(j + 1) * 128], self.ident_b[:n, :n],
                         (ksrc, self.kid[1]), (kptb,))
                srcv = ptb.rearrange("p (a b) -> p a b", a=8)[:, :, :n]
                if xi == 0:
                    P.copy("act", ART[:, :, 0:n], srcv, (kptb,), (kART,))
                elif xi == 1:
                    P.copy("dve", ART[:, :, 128:128 + n], srcv, (kptb,), (kART,))
                elif xi == 2:
                    P.copy("act", BTt[:, :, :n], srcv, (kptb,), (kBTt,))
                else:
                    P.copy("dve", KTt[:, :, :n], srcv, (kptb,), (kKTt,))

        def units(ti, gen):
            t0, n = s.tile_range(ti)

            def pump():
                next(gen, None)
            nsq = int(np.log2(n)) - 1
            HD = [dict(hd=hd, j=hd // 2, hb=64 * (hd % 2), col=hd * 64, g=hd // 4, sl=hd % 4) for hd in range(16)]
            bk = lambda g: P.bank(4 + g)
            kbk = lambda g: "psb%d" % (4 + g)
            for H in HD:
                if H["hd"] % 4 == 0:
                    pump()
                hd, j, hb = H["hd"], H["j"], H["hb"]
                b_ = bk(hd % 4); kb_ = kbk(hd % 4)
                P.mm(b_[:n, 0:256], BTt[hb:hb + 64, j, :n], ART[hb:hb + 64, j, :], True, True, (kBTt, kART), (kb_,))
                P.mm(b_[:n, 256:512], KTt[hb:hb + 64, j, :n], ART[hb:hb + 64, j, :], True, True, (kKTt, kART), (kb_,))
                P.tt("dve", M4a[:n, hd, :, :n], b_[:n, :].rearrange("p (a b) -> p a b", a=4)[:, :, :n], m4[:n, :, :n], ALU.mult,
                     (kb_, km4), (kM4[hd],))
            for H in HD:
                if H["hd"] % 4 == 0:
                    pump()
                hd, j, hb = H["hd"], H["j"], H["hb"]
                nb = (hd % 2) * 2 + hd // 8
                sl = (hd % 8) // 2
                P.mm(bk(nb)[:n, sl * 128:sl * 128 + n], ART[hb:hb + 64, j, 0:n], BTt[hb:hb + 64, j, :n], True, True,
                     (kART, kBTt), (kbk(nb),))
                if sl == 3:
                    base = (hd // 8) * 8 + hd % 2
                    P.tt("dve", NTa[:n, base:base + 7:2, :n], bk(nb)[:n, :].rearrange("p (a b) -> p a b", a=4)[:, :, :n],
                         suT[:n, :n].unsqueeze(1).to_broadcast([n, 4, n]), ALU.mult, (kbk(nb), ksuT), (kNT[nb],))
            T0, kT0 = Ta[0]
            for g in range(4):
                P.tt("pool", T0[:n, g * 4:g * 4 + 4, :n], M4a[:n, g * 4:g * 4 + 4, 0, :n], id4[:n, :, :n], ALU.add,
                     tuple(kM4[g * 4:g * 4 + 4]) + (kid4,), (kT0[g],))
            Pget = lambda hd: (M4a[:, hd, 0, :], kM4[hd])
            PTget = lambda hd: (NTa[:, hd, :], kNT[(hd % 2) * 2 + hd // 8])
            tcur = 0
            for it in range(1, nsq + 1):
                last = it == nsq
                PPn, kPPn = PPa[it % 2]
                for H in HD:
                    if H["hd"] % 4 == 0:
                        pump()
                    hd, g, sl = H["hd"], H["g"], H["sl"]
                    Pc, kPc = Pget(hd); PTc, kPTc = PTget(hd)
                    P.mm(bk(g)[:n, sl * 128:sl * 128 + n], Pc[:n, :n], PTc[:n, :n], True, True, (kPc, kPTc), (kbk(g),))
                    if sl == 3:
                        P.copy("act", PPn[:n, 1, g * 4:g * 4 + 4, :n], bk(g)[:n, :].rearrange("p (a b) -> p a b", a=4)[:, :, :n],
                               (kbk(g),), (kPPn[1][g],))
                if not last:
                    for H in HD:
                        if H["hd"] % 4 == 0:
                            pump()
                        hd, g, sl = H["hd"], H["g"], H["sl"]
                        Pc, kPc = Pget(hd); PTc, kPTc = PTget(hd)
                        P.mm(bk(g)[:n, sl * 128:sl * 128 + n], PTc[:n, :n], Pc[:n, :n], True, True, (kPc, kPTc), (kbk(g),))
                        if sl == 3:
                            P.copy("act" if g % 2 else "dve", PPn[:n, 0, g * 4:g * 4 + 4, :n],
                                   bk(g)[:n, :].rearrange("p (a b) -> p a b", a=4)[:, :, :n], (kbk(g),), (kPPn[0][g],))
                Pget = lambda hd, PPn=PPn, kPPn=kPPn: (PPn[:, 0, hd, :], kPPn[0][hd // 4])
                PTget = lambda hd, PPn=PPn, kPPn=kPPn: (PPn[:, 1, hd, :], kPPn[1][hd // 4])
                Tc, kTc = Ta[tcur]
                Tn, kTn = Ta[1 - tcur]
                for H in HD:
                    if H["hd"] % 4 == 0:
                        pump()
                    hd, g, sl = H["hd"], H["g"], H["sl"]
                    PTc, kPTc = PTget(hd)
                    P.mm(bk(g)[:n, sl * 128:sl * 128 + n], PTc[:n, :n], Tc[:n, hd, :n], True, True, (kPTc, kTc[g]), (kbk(g),))
                    if sl == 3:
                        P.tt("dve", Tn[:n, g * 4:g * 4 + 4, :n], bk(g)[:n, :].rearrange("p (a b) -> p a b", a=4)[:, :, :n],
                             Tc[:n, g * 4:g * 4 + 4, :n], ALU.add, (kbk(g), kTc[g]), (kTn[g],))
                tcur = 1 - tcur
            Tc, kTc = Ta[tcur]
            for H in HD:
                if H["hd"] % 4 == 0:
                    pump()
                hd, j, hb, col = H["hd"], H["j"], H["hb"], H["col"]
                b = hd % 2
                o_ = bk(b)[:n, (hd // 2) * 64:(hd // 2) * 64 + 64]
                P.mm(o_, ART[hb:hb + 64, j, 0:n], Sb[hb:hb + 64, j, :], True, False, (kART, kSb), (kbk(b),))
                P.mm(o_, M4a[:n, hd, 2, :n], Vb[:n, col:col + 64], False, True, (kM4[hd], kVb), (kbk(b),))
                if hd >= 14:
                    P.copy("act", Xa[:n, b:16:2, :], bk(b)[:n, :].rearrange("p (a b) -> p a b", a=8), (kbk(b),), (kXa[b],))
            for H in HD:
                if H["hd"] % 4 == 0:
                    pump()
                hd = H["hd"]
                b = hd % 2
                P.mm(bk(2 + b)[:n, (hd // 2) * 64:(hd // 2) * 64 + 64], Tc[:n, hd, :n], Xa[:n, hd, :], True, True,
                     (kTc[hd // 4], kXa[b]), (kbk(2 + b),))
                if hd >= 14:
                    P.copy("act", Ua[:n, b:16:2, :], bk(2 + b)[:n, :].rearrange("p (a b) -> p a b", a=8), (kbk(2 + b),), (kUa[b],))
            Yv = Y[:n, :].rearrange("p (h c) -> p h c", h=16)
            for H in HD:
                if H["hd"] % 4 == 0:
                    pump()
                hd, j, hb, col = H["hd"], H["j"], H["hb"], H["col"]
                b = hd % 2
                o_ = bk(b)[:n, (hd // 2) * 64:(hd // 2) * 64 + 64]
                P.mm(o_, ART[hb:hb + 64, j, 128:128 + n], Sb[hb:hb + 64, j, :], True, False, (kART, kSb), (kbk(b),))
                P.mm(o_, M4a[:n, hd, 1, :n], Ua[:n, hd, :], False, False, (kM4[hd], kUa[b]), (kbk(b),))
                P.mm(o_, M4a[:n, hd, 3, :n], Vb[:n, col:col + 64], False, True, (kM4[hd], kVb), (kbk(b),))
                if hd >= 14:
                    P.copy("act", Yv[:, b:16:2, :], bk(b)[:n, :].rearrange("p (a b) -> p a b", a=8), (kbk(b),), (kY,))
            for H in HD:
                if H["hd"] % 4 == 0:
                    pump()
                hd, j, hb, col = H["hd"], H["j"], H["hb"], H["col"]
                o_ = bk(2)[hb:hb + 64, j * 64:(j + 1) * 64]
                P.mm(o_, BW[:n, col:col + 64], Ua[:n, hd, :], True, False, (kBW, kUa[hd % 2]), (kbk(2),))
                P.mm(o_, KW[:n, col:col + 64], Vb[:n, col:col + 64], False, True, (kKW, kVb), (kbk(2),))
            P.tt("dve", Sf[:, :, :], Sf[:, :, :], wc[:, :].unsqueeze(2).to_broadcast([128, 8, 64]), ALU.mult, (kSf, kwc), (kSf,))
            P.tt("dve", Sf[:, :, :], Sf[:, :, :], bk(2)[:, :].rearrange("p (a b) -> p a b", a=8), ALU.add, (kSf, kbk(2)), (kSf,))
            P.copy("pool", Sb[:, :, :], Sf[:, :, :], (kSf,), (kSb,))

        def epilogue(ti):
            t0, n = s.tile_range(ti)
            if dr == 0:
                P.dma("sp", d["YF_" + n_][t0:t0 + n, :], Y[:n, :], (kY,), ())
            else:
                P.dma("sp", yf[:n, :], d["YF_" + n_][t0:t0 + n, :], (), (kyf,))
                P.dma("sp", gt[:n, :], d["G_" + n_][t0:t0 + n, :], (), (kgt,))
                P.tt("dve", Y[:n, :], Y[:n, :], yf[:n, :], ALU.add, (kY, kyf), (kY,))
                P.dma("sp", yf[:n, :], d["BF_" + n_][t0:t0 + n, :], (kY,), (kyf,))
                P.reduce("dve", sm[:n, 2, :], v3(Y[:n, :]), ALU.add, (kY,), (ksm,))
                P.tt("pool", tmp[:n, :], Y[:n, :], Y[:n, :], ALU.mult, (kY,), (ktmp,))
                P.reduce("dve", sm[:n, 3, :], v3(tmp[:n, :]), ALU.add, (ktmp,), (ksm,))
                P.ts("dve", sm[:n, 2, :], sm[:n, 2, :], 1.0 / 64, None, ALU.mult, None, (ksm,), (ksm,))
                P.ts("dve", sm[:n, 3, :], sm[:n, 3, :], 1.0 / 64, None, ALU.mult, None, (ksm,), (ksm,))
                P.tt("dve", sm[:n, 0, :], sm[:n, 2, :], sm[:n, 2, :], ALU.mult, (ksm,), (ksm,))
                P.tt("dve", sm[:n, 3, :], sm[:n, 3, :], sm[:n, 0, :], ALU.subtract, (ksm,), (ksm,))
                P.act(sm[:n, 3, :], sm[:n, 3, :], AF.Sqrt, (ksm,), (ksm,), bias=GN_EPS)
                P.recip("dve", sm[:n, 3, :], sm[:n, 3, :], (ksm,), (ksm,))
                bc = lambda q: sm[:n, q, :].unsqueeze(2).to_broadcast([n, 16, 64])
                P.tt("dve", v3(Y[:n, :]), v3(Y[:n, :]), bc(2), ALU.subtract, (kY, ksm), (kY,))
                P.tt("dve", v3(Y[:n, :]), v3(Y[:n, :]), bc(3), ALU.mult, (kY, ksm), (kY,))
                P.tt("pool", Y[:n, :], Y[:n, :], lg[:n, :], ALU.mult, (kY, klg), (kY,))
                P.tt("pool", Y[:n, :], Y[:n, :], lb[:n, :], ALU.add, (kY, klb), (kY,))
                P.tt("dve", Y[:n, :], Y[:n, :], bon[:n, :], ALU.add, (kY, kbon), (kY,))
                P.tt("dve", Y[:n, :], Y[:n, :], yf[:n, :], ALU.add, (kY, kyf), (kY,))
                P.tt("dve", ob[:n, :], Y[:n, :], gt[:n, :], ALU.mult, (kY, kgt), (kob,))
                if s.sliced:
                    P.dma("sp", d["OTOK_" + n_][t0:t0 + n, DA:D], ob[:n, :], (kob,), ())
                    return
                bi = tctr[0] % 4
                tctr[0] += 1
                ptb = P.bank(bi, BF16)
                kptb = "psb%d" % bi
                for j in range(8):
                    P.tr(ptb[:, j * 128:j * 128 + n], ob[:n, j * 128:(j + 1) * 128], self.ident_b[:n, :n],
                         (kob, self.kid[1]), (kptb,))
                P.copy("act", oT[:, :, :n], ptb.rearrange("p (a b) -> p a b", a=8)[:, :, :n], (kptb,), (koT,))
                P.dma("sp", d["OT_" + n_][8:16, :, t0:t0 + n].rearrange("c p t -> p c t"), oT[:, :, :n], (koT,), ())

        for _ in prep1(order[0]):
            pass
        for idx, ti in enumerate(order):
            prep2(ti)
            gen = prep1(order[idx + 1]) if idx + 1 < len(order) else iter(())
            units(ti, gen)
            for _ in gen:
                pass
            epilogue(ti)


    def stage3(self, s):
        P = self.P
        d = self.dram
        n_ = s.name
        L = s.L
        P.barrier()
        P.reset_arena(self.const_end)
        kth = [P.tile([128, L], BF16, "kth%d" % i) for i in range(2)]
        qth = [P.tile([128, L], BF16, "qth%d" % i) for i in range(2)]
        vh = [P.tile([128, s.ntile, 128], BF16, "vh%d" % i) for i in range(2)]
        bt = [P.tile([128, 5, NKEY], F32, "bias%d" % i) for i in range(2)]
        oTh = [P.tile([128, L], BF16, "oTh%d" % i) for i in range(2)]
        sc = [P.tile([128, NKEY], F32, "sc%d" % i) for i in range(2)]
        pr = [P.tile([128, NKEY], BF16, "pr%d" % i) for i in range(2)]
        pT = [P.tile([128, 6, 128], BF16, "pT%d" % i) for i in range(2)]
        ob = [P.tile([128, 128], BF16, "ob%d" % i) for i in range(2)]
        st = [P.tile([128, 4], F32, "st%d" % i) for i in range(4)]
        uc = 0
        for h in range(HA):
            kt, kkt = kth[h % 2]; qt, kqt = qth[h % 2]; v, kv = vh[h % 2]; b, kb = bt[h % 2]; oh, koh = oTh[h % 2]
            P.dma("sp", kt, d["KT_" + n_][h], (), (kkt,))
            P.dma("sp", qt, d["QT_" + n_][h], (), (kqt,))
            P.dma("sp", v[:NM, 0, :], d["V_" + n_][0:NM, h * 128:(h + 1) * 128], (), (kv,))
            for i0 in range(1, s.ntile, 16):
                i1 = min(s.ntile, i0 + 16)
                P.dma("sp", v[:, i0:i1, :],
                      d["V_" + n_][NM + (i0 - 1) * 128:NM + (i1 - 1) * 128, h * 128:(h + 1) * 128].rearrange("(i p) c -> p i c", p=128),
                      (), (kv,))
            P.dma("sp", b, d["bias_tab"][h].rearrange("c q k -> q c k"), (), (kb,))
            units = [("meta", 0)] + [("grid", rp) for rp in range(s.rows // 2)]
            for kind, rp in units:
                u = uc
                uc += 1
                s_, ks_ = sc[u % 2]; p_, kp_ = pr[u % 2]; pt_, kpt_ = pT[u % 2]; o_, ko_ = ob[u % 2]; st_, kst_ = st[u % 4]
                ba = (u % 2) * 2
                pa = P.bank(ba); pbk = P.bank(ba + 1)
                kpa = "psb%d" % ba; kpb = "psb%d" % (ba + 1)
                if kind == "meta":
                    nq = NM
                    q0 = 0
                    P.mm(pbk[:nq, 128:144], qt[:, 0:NM], kt[:, 0:NM], True, True, (kqt, kkt), (kpb,))
                    P.copy("dve", s_[:nq, 640:656], pbk[:nq, 128:144], (kpb,), (ks_,))
                    lo = 640
                    blocks = [(5, NM, 0)]
                else:
                    nq = 128
                    r = 2 * rp
                    ws = min(max(r - 4, 0), s.rows - WINR)
                    assert ws % 2 == 0
                    cls = (r - ws) // 2
                    q0 = NM + rp * 128
                    k0 = NM + ws * 64
                    P.mm(pa[:, :], qt[:, q0:q0 + 128], kt[:, k0:k0 + 512], True, True, (kqt, kkt), (kpa,))
                    P.mm(pbk[:, 0:128], qt[:, q0:q0 + 128], kt[:, k0 + 512:k0 + 640], True, True, (kqt, kkt), (kpb,))
                    P.mm(pbk[:, 128:144], qt[:, q0:q0 + 128], kt[:, 0:NM], True, True, (kqt, kkt), (kpb,))
                    P.tt("dve", s_[:, 0:512], pa[:, :], b[:, cls, 0:512], ALU.add, (kpa, kb), (ks_,))
                    P.tt("dve", s_[:, 512:656], pbk[:, 0:144], b[:, cls, 512:656], ALU.add, (kpb, kb), (ks_,))
                    lo = 0
                    blocks = [(j, 128, ws // 2 + 1 + j) for j in range(5)] + [(5, NM, 0)]
                P.op("dve", lambda e, o=st_[:nq, 0:1], i=s_[:nq, lo:656]: e.reduce_max(o, i, AX.X), (ks_,), (kst_,))
                P.ts("dve", st_[:nq, 1:2], st_[:nq, 0:1], -1.0, None, ALU.mult, None, (kst_,), (kst_,))
                P.act(p_[:nq, lo:656], s_[:nq, lo:656], AF.Exp, (ks_, kst_), (kp_, kst_), bias=st_[:nq, 1:2], accum=st_[:nq, 2:3])
                P.recip("dve", st_[:nq, 3:4], st_[:nq, 2:3], (kst_,), (kst_,))
                tb = 4 + (u % 2)
                ptb = P.bank(tb, BF16)
                kptb = "psb%d" % tb
                for (j, nk, vt) in blocks:
                    P.tr(ptb[:nk, j * 128:j * 128 + nq], p_[:nq, j * 128:j * 128 + nk], self.ident_b[:nq, :nq],
                         (kp_, self.kid[1]), (kptb,))
                j0 = blocks[0][0]
                if kind == "meta":
                    P.copy("act", pt_[:NM, 5, :nq], ptb[:NM, 640:640 + nq], (kptb,), (kpt_,))
                else:
                    P.copy("act", pt_[:, :, :], ptb[:, 0:768].rearrange("p (a b) -> p a b", a=6), (kptb,), (kpt_,))
                po = P.bank(6)
                for bi, (j, nk, vt) in enumerate(blocks):
                    P.mm(po[:nq, 0:128], pt_[:nk, j, :nq], v[:nk, vt, :], bi == 0, bi == len(blocks) - 1, (kpt_, kv), ("psb6",))
                P.act(o_[:nq, :], po[:nq, 0:128], AF.Copy, ("psb6", kst_), (ko_,), scale=st_[:nq, 3:4])
                if s.sliced:
                    P.dma("sp", d["OTOK_" + n_][q0:q0 + nq, h * 128:(h + 1) * 128], o_[:nq, :], (ko_,), ())
                    continue
                pot = P.bank(7, BF16)
                P.tr(pot[:, 0:nq], o_[:nq, :], self.ident_b[:nq, :nq], (ko_, self.kid[1]), ("psb7",))
                P.copy("dve", oh[:, q0:q0 + nq], pot[:, 0:nq], ("psb7",), (koh,))
            if not s.sliced:
                P.dma("sp", d["OT_" + n_][h], oh, (koh,), ())

    def stage4(self, s):
        P = self.P
        d = self.dram
        n_ = s.name
        P.barrier()
        P.reset_arena(self.const_end)
        g1, kg = self.bcast_load(d["ln1_g"], D, "g1")
        b1, kb = self.bcast_load(d["ln1_b"], D, "b1")
        tmp = self.ln_tmp()
        wo, kwo = P.tile([128, 16, D], BF16, "wo")
        for c0 in range(0, D, 512):
            self.load_w(wo[:, :, c0:c0 + 512], kwo, d["w_out"], D, c0, 512)
        oT = [P.tile([128, 16, 128], BF16, "oT%d" % i) for i in range(2)]
        h0 = [P.tile([128, D], F32, "h0%d" % i) for i in range(2)]
        h1 = [P.tile([128, D], F32, "h1%d" % i) for i in range(2)]
        h1T = [P.tile([128, 16, 128], BF16, "h1T%d" % i) for i in range(2)]
        ctr = [0]
        for ti in range(s.ntile):
            t0, n = s.tile_range(ti)
            o_, ko_ = oT[ti % 2]; x, kx = h0[ti % 2]; y, ky = h1[ti % 2]; yt, kyt = h1T[ti % 2]
            P.dma("sp", o_[:, :, :n], d["OT_" + n_][:, :, t0:t0 + n].rearrange("c p t -> p c t"), (), (ko_,))
            P.dma("sp", x[:n, :], d["H0_" + n_][t0:t0 + n, :], (), (kx,))
            for hf in range(4):
                pb = P.bank(hf); kpb = "psb%d" % hf
                for kc in range(16):
                    P.mm(pb[:n, :], o_[:, kc, :n], wo[:, kc, hf * 512:(hf + 1) * 512], kc == 0, kc == 15, (ko_, kwo), (kpb,))
                P.stt("dve", x[:n, hf * 512:(hf + 1) * 512], x[:n, hf * 512:(hf + 1) * 512], ALPHA, pb[:n, :], ALU.mult, ALU.add,
                      (kx, kpb), (kx,))
            self.layer_norm(x, kx, n, (g1, b1), (kg, kb), y, ky, LN_EPS, tmp)
            P.dma("sp", d["H1_" + n_][t0:t0 + n, :], y[:n, :], (ky,), ())
            self.transpose_to_fm(y, ky, n, yt, kyt, 0, (4, 5, 6, 7), ctr)
            P.dma("sp", d["H1T_" + n_][:, :, t0:t0 + n].rearrange("c p t -> p c t"), yt[:, :, :n], (kyt,), ())

    def stage4_sliced(self, s):
        P = self.P
        d = self.dram
        n_ = s.name
        NG = s.ng
        P.barrier()
        P.reset_arena(self.const_end)
        g0, kg0 = self.bcast_load(d["emb_ln_g"], D, "g0")
        b0, kb0 = self.bcast_load(d["emb_ln_b"], D, "b0")
        g1, kg1 = self.bcast_load(d["ln1_g"], D, "g1")
        b1, kb1 = self.bcast_load(d["ln1_b"], D, "b1")
        tmp = self.ln_tmp()
        wo, kwo = P.tile([128, 16, D], BF16, "wo")
        for c0 in range(0, D, 512):
            self.load_w(wo[:, :, c0:c0 + 512], kwo, d["w_out"], D, c0, 512)
        ms, kms = P.tile([128, NG + 1], F32, "msel")
        P.dma("sp", ms, d["msel_" + n_], (), (kms,))
        mI, kmI = P.tile([128, NG, 128], BF16, "mI")
        for c in range(NG):
            P.ts("dve", mI[:, c, :], self.ident_f, ms[:, c:c + 1], None, ALU.mult, None, (self.kid[0], kms), (kmI,))
        shf, kshf = P.tile([128, 128], F32, "shf")
        P.dma("sp", shf[:NM, :], d["c_shift"], (), (kshf,))
        shI, kshI = P.tile([128, 128], BF16, "shI")
        P.ts("dve", shI[:NM, :], shf[:NM, :], ms[:NM, 0:1], None, ALU.mult, None, (kshf, kms), (kshI,))
        ot = [P.tile([128, D], BF16, "ot%d" % i) for i in range(NG)]
        oTs, koTs = P.tile([128, 16, 128], BF16, "oTs")
        x, kx = P.tile([128, D], F32, "x")
        h0, kh0 = P.tile([128, D], F32, "h0")
        y, ky = P.tile([128, D], F32, "h1")
        yt, kyt = P.tile([128, 16, 128], BF16, "h1T")
        ctr = [0]
        for sl in range(10):
            cands = []
            for c in range(NG):
                ti = 8 * c + sl
                if 0 <= ti < s.ntile:
                    cands.append((c, ti))
            for c, ti in cands:
                t0, n = s.tile_range(ti)
                P.dma("sp", ot[c][0][:n, :], d["OTOK_" + n_][t0:t0 + n, :], (), (ot[c][1],))
            for fc in range(16):
                bi = fc // 4
                pb = P.bank(bi); kpb = "psb%d" % bi
                for ci, (c, ti) in enumerate(cands):
                    t0, n = s.tile_range(ti)
                    rhs = shI[:NM, :] if ti == 0 else mI[:, c, :]
                    krhs = kshI if ti == 0 else kmI
                    P.mm(pb[:, (fc % 4) * 128:(fc % 4 + 1) * 128], ot[c][0][:n, fc * 128:(fc + 1) * 128], rhs,
                         ci == 0, ci == len(cands) - 1, (ot[c][1], krhs), (kpb,))
                if fc % 4 == 3:
                    P.copy("act" if bi % 2 == 0 else "dve", oTs[:, fc - 3:fc + 1, :], pb.rearrange("p (a b) -> p a b", a=4),
                           (kpb,), (koTs,))
            P.dma("sp", x, d["xext_" + n_][sl * 128:(sl + 1) * 128, :], (), (kx,))
            self.layer_norm(x, kx, 128, (g0, b0), (kg0, kb0), h0, kh0, LN_EPS, tmp)
            for hf in range(4):
                pb = P.bank(4 + hf); kpb = "psb%d" % (4 + hf)
                for kc in range(16):
                    P.mm(pb[:, :], oTs[:, kc, :], wo[:, kc, hf * 512:(hf + 1) * 512], kc == 0, kc == 15, (koTs, kwo), (kpb,))
                P.stt("dve", h0[:, hf * 512:(hf + 1) * 512], h0[:, hf * 512:(hf + 1) * 512], ALPHA, pb[:, :], ALU.mult, ALU.add,
                      (kh0, kpb), (kh0,))
            self.layer_norm(h0, kh0, 128, (g1, b1), (kg1, kb1), y, ky, LN_EPS, tmp)
            P.dma("sp", d["H1s_" + n_][sl * 128:(sl + 1) * 128, :], y, (ky,), ())
            self.transpose_to_fm(y, ky, 128, yt, kyt, 0, (0, 1, 2, 3), ctr)
            P.dma("sp", d["H1Ts_" + n_][:, :, sl * 128:(sl + 1) * 128].rearrange("c p t -> p c t"), yt, (kyt,), ())

    def stage5(self, s):
        P = self.P
        d = self.dram
        n_ = s.name
        P.barrier()
        P.reset_arena(self.const_end)
        g2_, kg = self.bcast_load(d["ln2_g"], D, "g2")
        b2_, kb = self.bcast_load(d["ln2_b"], D, "b2")
        tmp = self.ln_tmp()
        cw, kcw = P.tile([128, 44, 4], F32, "convw")
        P.dma("sp", cw, d["ffn_conv_l"], (), (kcw,))
        NB = 1024
        hT, khT = P.tile([128, 16, NB + 2], BF16, "hT")
        acc, kacc0 = P.tile([128, 8, D], F32, "acc")
        kacc = [kacc0 + "/%d" % i for i in range(8)]
        GC = 2
        w1 = [P.tile([128, 16, 2 * GC * 128], BF16, "w1_%d" % i) for i in range(2)]
        w2 = [P.tile([128, GC, D], BF16, "w2_%d" % i) for i in range(2)]
        gT = [P.tile([128, 514], F32, "gT%d" % i) for i in range(2)]
        t1 = [P.tile([128, 512], F32, "t1%d" % i) for i in range(2)]
        aT = [P.tile([128, GC, NB], BF16, "aT%d" % i) for i in range(2)]
        xr = [P.tile([128, D], F32, "xr%d" % i) for i in range(1)]
        yo = [P.tile([128, D], F32, "yo%d" % i) for i in range(1)]
        gi = 0
        hc = 0
        if s.sliced:
            h1t_d = d["H1Ts_" + n_]; h1_d = d["H1s_" + n_]; Ltot = 1280
            blocks = [(128, 0)]
            ms, kms = P.tile([128, s.ng + 1], F32, "msel")
            P.dma("sp", ms, d["msel_" + n_], (), (kms,))
        else:
            h1t_d = d["H1T_" + n_]; h1_d = d["H1_" + n_]; Ltot = s.L
            blocks = [(NM + b0, b0) for b0 in range(0, s.T, NB)]
        for ts0, orow in blocks:
            ntl = NB // 128
            lo = ts0 - 1
            hi = min(Ltot, ts0 + NB + 1)
            P.dma("sp", hT[:, :, 0:hi - lo], h1t_d[:, :, lo:hi].rearrange("c p t -> p c t"), (), (khT,))
            if hi - lo < NB + 2:
                P.memset("pool", hT[:, :, NB + 1:NB + 2], 0.0, (khT,))
            if s.sliced:
                P.ts("pool", hT[:, :, NB + 1:NB + 2], hT[:, :, NB + 1:NB + 2], ms[:, s.ng:s.ng + 1], None, ALU.mult, None,
                     (khT, kms), (khT,))
            for g in range(44 // GC):
                w1_, kw1 = w1[gi % 2]; w2_, kw2 = w2[gi % 2]; a_, ka_ = aT[gi % 2]
                gi += 1
                c0 = g * GC * 128
                self.load_w(w1_[:, :, 0:GC * 128], kw1, d["ffn_w_in"], D, c0, GC * 128)
                self.load_w(w1_[:, :, GC * 128:2 * GC * 128], kw1, d["ffn_w_in"], D, DFF + c0, GC * 128)
                P.dma("pool", w2_, d["ffn_w_out"][c0:c0 + GC * 128, :].rearrange("(c p) n -> p c n", p=128), (), (kw2,))
                for cl in range(GC):
                    fc = g * GC + cl
                    for hh in range(2):
                        g_, kg_ = gT[hc % 2]; t_, kt_ = t1[hc % 2]
                        hc += 1
                        bg = 0 if hh == 0 else 3
                        pg = P.bank(bg); kpg = "psb%d" % bg
                        ph = P.bank(1); pu = P.bank(2)
                        cb = hh * 512
                        for kc in range(16):
                            P.mm(pg[:, :], w1_[:, kc, cl * 128:(cl + 1) * 128], hT[:, kc, cb:cb + 512], kc == 0, kc == 15,
                                 (kw1, khT), (kpg,))
                        for kc in range(16):
                            P.mm(ph[:, 0:2], w1_[:, kc, cl * 128:(cl + 1) * 128], hT[:, kc, cb + 512:cb + 514], kc == 0, kc == 15,
                                 (kw1, khT), ("psb1",))
                        for kc in range(16):
                            P.mm(pu[:, :], w1_[:, kc, (GC + cl) * 128:(GC + cl + 1) * 128], hT[:, kc, cb + 1:cb + 513],
                                 kc == 0, kc == 15, (kw1, khT), ("psb2",))
                        P.copy("act", g_[:, 0:512], pg[:, :], (kpg,), (kg_,))
                        P.copy("act", g_[:, 512:514], ph[:, 0:2], ("psb1",), (kg_,))
                        P.act(t_[:, :], g_[:, 1:513], AF.Identity, (kg_, kcw), (kt_,), bias=cw[:, fc, 3:4], scale=cw[:, fc, 1:2])
                        P.stt("dve", t_[:, :], g_[:, 0:512], cw[:, fc, 0:1], t_[:, :], ALU.mult, ALU.add, (kg_, kcw, kt_), (kt_,))
                        P.stt("dve", t_[:, :], g_[:, 2:514], cw[:, fc, 2:3], t_[:, :], ALU.mult, ALU.add, (kg_, kcw, kt_), (kt_,))
                        P.act(t_[:, :], t_[:, :], AF.Gelu, (kt_,), (kt_,))
                        P.tt("dve", a_[:, cl, cb:cb + 512], t_[:, :], pu[:, :], ALU.mult, (kt_, "psb2"), (ka_,))
                for tl in range(ntl):
                    for hf in range(4):
                        pb = P.bank(4 + hf); kpb = "psb%d" % (4 + hf)
                        for cl in range(GC):
                            P.mm(pb[:, :], a_[:, cl, tl * 128:(tl + 1) * 128], w2_[:, cl, hf * 512:(hf + 1) * 512],
                                 cl == 0, cl == GC - 1, (ka_, kw2), (kpb,))
                        dst = acc[:, tl, hf * 512:(hf + 1) * 512]
                        if g == 0:
                            P.copy("dve" if hf % 2 else "act", dst, pb[:, :], (kpb,), (kacc[tl],))
                        else:
                            P.tt("dve", dst, dst, pb[:, :], ALU.add, (kpb, kacc[tl]), (kacc[tl],))
            for tl in range(ntl):
                x, kx = xr[0]; y, ky = yo[0]
                tq = ts0 + tl * 128
                P.dma("sp", x, h1_d[tq:tq + 128, :], (), (kx,))
                P.stt("dve", x, x, ALPHA, acc[:, tl, :], ALU.mult, ALU.add, (kx, kacc[tl]), (kx,))
                self.layer_norm(x, kx, 128, (g2_, b2_), (kg, kb), y, ky, LN_EPS, tmp)
                P.dma("sp", d["y_" + n_][orow + tl * 128:orow + tl * 128 + 128, :], y, (ky,), ())

    def build(self):
        self.setup_consts()
        for s in self.seqs:
            if 1 in self.run:
                self.stage1(s)
            if 2 in self.run:
                self.stage2(s)
            if 3 in self.run:
                self.stage3(s)
            if 4 in self.run:
                if s.sliced:
                    self.stage4_sliced(s)
                else:
                    self.stage4(s)
            if 5 in self.run:
                self.stage5(s)
        self.P.finish()
        return self.nc


def _bias_table(rpb):
    rpb = np.asarray(rpb, np.float32).reshape(HA, 15, 31)
    tab = np.full((HA, 5, 128, NKEY), NEG, np.float32)
    qc = np.arange(64)
    c0 = np.clip(qc - 8, 0, 48)
    kc = np.arange(64)
    colmask = (kc[None, :] >= c0[:, None]) & (kc[None, :] < c0[:, None] + 16)
    dc = np.clip(kc[None, :] - qc[:, None], -15, 15) + 15
    rel = {0: (0, 0), 1: (0, 0), 2: (0, 1), 3: (2, 2), 4: (2, 2)}
    for c in range(5):
        for qr in range(2):
            r_rel = 2 * c + qr
            r0 = rel[c][qr]
            for j in range(8):
                krow = r0 + j
                dr = krow - r_rel + 7
                blk = rpb[:, dr][:, dc]
                blk = np.where(colmask[None], blk, NEG)
                tab[:, c, qr * 64:(qr + 1) * 64, krow * 64:(krow + 1) * 64] = blk
        tab[:, c, :, WINR * 64:] = 0.0
    return tab


def _consts():
    ident = np.eye(128, dtype=np.float32)
    s = np.arange(128)[:, None]
    t = np.arange(128)[None, :]
    masks = np.stack([(s <= t), (s >= t), (s < t), (s <= t), (s > t), (s >= t)]).astype(np.float32)
    return ident, masks


def _common_inputs(inp):
    ident, masks = _consts()
    f = lambda a: np.ascontiguousarray(np.asarray(a, np.float32))
    m = {
        "meta_tokens": f(inp["meta_tokens"]),
        "emb_ln_g": f(inp["emb_ln_g"]), "emb_ln_b": f(inp["emb_ln_b"]),
        "ln1_g": f(inp["ln1_g"][0]), "ln1_b": f(inp["ln1_b"][0]),
        "ln2_g": f(inp["ln2_g"][0]), "ln2_b": f(inp["ln2_b"][0]),
        "w_in": f(inp["w_in"][0]),
        "bias_tab": _bias_table(inp["attn_rpb"][0]),
        "rwkv_mu": f(inp["rwkv_mu"][0]), "rwkv_w0": f(inp["rwkv_w0"][0]), "rwkv_w2": f(inp["rwkv_w2"][0]),
        "rwkv_a0": f(inp["rwkv_a0"][0]), "rwkv_a2": f(inp["rwkv_a2"][0]), "rwkv_g2": f(inp["rwkv_g2"][0]),
        "rwkv_k_k": f(inp["rwkv_k_k"][0]), "rwkv_k_a": f(inp["rwkv_k_a"][0]),
        "rwkv_r_k": f(inp["rwkv_r_k"][0]).reshape(DR),
        "rwkv_lnx_g": f(inp["rwkv_lnx_g"][0]), "rwkv_lnx_b": f(inp["rwkv_lnx_b"][0]),
        "w_out": f(inp["w_out"][0]), "ffn_w_in": f(inp["ffn_w_in"][0]),
        "ffn_conv_l": np.ascontiguousarray(np.concatenate([f(inp["ffn_conv_w"][0]), f(inp["ffn_conv_b"][0])[None]], 0)
                                           .reshape(4, 44, 128).transpose(2, 1, 0)),
        "ffn_w_out": f(inp["ffn_w_out"][0]),
        "c_ident": ident, "c_masks": masks, "c_zero": np.zeros((1, NSH), np.float32),
        "c_shift": np.eye(NM, 128, 128 - NM, dtype=np.float32),
    }
    return m


def slice_inputs(x, meta, c, ng):
    x = np.asarray(x, np.float32)
    prev = x[1024 * c - 128:1024 * c] if c > 0 else np.concatenate([np.zeros((128 - NM, D), np.float32), np.asarray(meta, np.float32)], 0)
    nxt = x[1024 * c + 1024:1024 * c + 1152] if c < ng - 1 else np.zeros((128, D), np.float32)
    xext = np.ascontiguousarray(np.concatenate([prev, x[1024 * c:1024 * c + 1024], nxt], 0))
    msel = np.zeros((128, ng + 1), np.float32)
    msel[:, c] = 1.0
    msel[:, ng] = 1.0 if c < ng - 1 else 0.0
    return xext, msel


def kernel(**inputs):
    xp = np.asarray(inputs["x_prompt"], np.float32)
    xs = np.asarray(inputs["x_sample"], np.float32)
    b = Builder([xs.shape[1], xp.shape[1]], sliced=[False, True])
    nc = b.build()
    common = _common_inputs(inputs)
    in_maps = []
    for c in range(8):
        m = dict(common)
        m["x_s0"] = np.ascontiguousarray(xs[c])
        m["x_s1"] = np.ascontiguousarray(xp[0])
        m["xext_s1"], m["msel_s1"] = slice_inputs(xp[0], inputs["meta_tokens"], c, 8)
        in_maps.append(m)
    res = run_bass_kernel_spmd(nc, in_maps, core_ids=list(range(8)))
    y_s = np.stack([res.results[c]["y_s0"] for c in range(8)], axis=0)
    y_p = np.concatenate([res.results[c]["y_s1"] for c in range(8)], axis=0)[None]
    return (y_p.astype(np.float32), y_s.astype(np.float32))
```

```python
import numpy as np
import ml_dtypes
import concourse.bass as bass
import concourse.mybir as mybir
from concourse.bass_utils import run_bass_kernel_spmd

F32 = mybir.dt.float32
BF16 = mybir.dt.bfloat16
U8 = mybir.dt.uint8
ALU = mybir.AluOpType
AF = mybir.ActivationFunctionType
AX = mybir.AxisListType

D = 2048
NM = 16
DA = 1024
HA = 8
DR = 1024
HR = 16
NSH = 3 * DR + 192
NIN = 3 * DA + NSH + 256
DFF = 5632
ALPHA = 2.0 ** 0.25
LN_EPS = 1e-5
GN_EPS = 64e-5
CDEC = -float(np.exp(-0.5))
NEG = -30000.0
WINR = 10
NKEY = WINR * 64 + NM


class Prog:
    ENG = ("sp", "act", "dve", "pool", "pe")

    def __init__(self, nc, n_dma_sems=40):
        self.nc = nc
        self.q = {e: [] for e in self.ENG}
        self.cnt = {e: 0 for e in self.ENG}
        self.known = {e: {} for e in self.ENG}
        self.last_w = {}
        self.readers = {}
        self.semh = {}
        for e in ("act", "dve", "pool", "pe"):
            self.semh[e] = nc.alloc_semaphore("sem_" + e)
        self.ndma = n_dma_sems
        for i in range(n_dma_sems):
            self.semh[("dma", i)] = nc.alloc_semaphore("sem_dma%d" % i)
        self.dma_tot = [0] * n_dma_sems
        self.rr = 0
        self.arena = nc.alloc_sbuf_tensor("arena", [128, 204 * 1024], U8)
        self.off = 0
        self.uid = 0
        self.psum = [nc.alloc_psum_tensor("psb%d" % i, [128, 512], F32) for i in range(8)]

    def reset_arena(self, off=0):
        self.off = off

    def tile(self, shape, dtype, name="t"):
        esz = 4 if dtype == F32 else 2
        n = int(np.prod(shape[1:]))
        nbytes = (n * esz + 31) // 32 * 32
        assert self.off + nbytes <= 204 * 1024, ("sbuf overflow", name, self.off, nbytes)
        ap = self.arena[:, self.off:self.off + n * esz].bitcast(dtype)
        self.off += nbytes
        if len(shape) == 3:
            ap = ap.rearrange("p (a b) -> p a b", a=shape[1])
        elif len(shape) == 4:
            ap = ap.rearrange("p (a b c) -> p a b c", a=shape[1], b=shape[2])
        self.uid += 1
        return ap, "%s#%d" % (name, self.uid)

    def bank(self, i, dtype=F32):
        ap = self.psum[i][:, :]
        if dtype != F32:
            ap = ap.bitcast(dtype)
        return ap

    def _waits(self, eng, reads, writes):
        need = {}

        def add(ev):
            if ev is None:
                return
            s, v = ev
            if need.get(s, 0) < v:
                need[s] = v
        for k in reads:
            add(self.last_w.get(k))
        for k in writes:
            add(self.last_w.get(k))
            for s, v in self.readers.get(k, {}).items():
                add((s, v))
        out = []
        for s, v in need.items():
            if s == "pe" and eng == "pe":
                continue
            if self.known[eng].get(s, 0) >= v:
                continue
            self.known[eng][s] = v
            out.append((s, v))
        return out

    def _record(self, ev, reads, writes):
        s, v = ev
        for k in reads:
            d = self.readers.setdefault(k, {})
            if d.get(s, 0) < v:
                d[s] = v
        for k in writes:
            self.last_w[k] = ev
            self.readers[k] = {}

    @staticmethod
    def _px(reads, writes):
        pr = tuple(k for k in reads if k.startswith("psb"))
        if pr:
            reads = tuple(k for k in reads if not k.startswith("psb"))
            writes = tuple(writes) + pr
        return reads, writes

    def op(self, eng, fn, reads=(), writes=()):
        reads, writes = self._px(reads, writes)
        waits = self._waits(eng, reads, writes)
        self.cnt[eng] += 1
        ev = (eng, self.cnt[eng])
        self.q[eng].append((waits, fn, eng, 1))
        self._record(ev, reads, writes)

    def dma(self, eng, out, in_, reads=(), writes=()):
        i = self.rr
        self.rr = (self.rr + 1) % self.ndma
        s = ("dma", i)
        waits = self._waits(eng, reads, writes)
        if self.dma_tot[i] > 0 and self.known[eng].get(s, 0) < self.dma_tot[i]:
            self.known[eng][s] = self.dma_tot[i]
            waits.append((s, self.dma_tot[i]))
        self.dma_tot[i] += 16
        ev = (s, self.dma_tot[i])
        self.q[eng].append((waits, lambda e, o=out, a=in_: e.dma_start(out=o, in_=a), s, 16))
        self._record(ev, reads, writes)

    def raw(self, eng, fn, reads=(), writes=()):
        waits = self._waits(eng, reads, writes)
        self.q[eng].append((waits, fn, None, -1))

    def dma_dyn(self, eng, mk, reads=(), writes=()):
        i = self.rr
        self.rr = (self.rr + 1) % self.ndma
        s = ("dma", i)
        waits = self._waits(eng, reads, writes)
        if self.dma_tot[i] > 0 and self.known[eng].get(s, 0) < self.dma_tot[i]:
            self.known[eng][s] = self.dma_tot[i]
            waits.append((s, self.dma_tot[i]))
        self.dma_tot[i] += 16
        ev = (s, self.dma_tot[i])

        def fn(e, mk=mk):
            o, a = mk()
            return e.dma_start(out=o, in_=a)
        self.q[eng].append((waits, fn, s, 16))
        self._record(ev, reads, writes)

    def barrier(self):
        for e in self.ENG:
            waits = []
            for s in ("act", "dve", "pool", "pe"):
                if s != e and self.cnt[s] > self.known[e].get(s, 0):
                    self.known[e][s] = self.cnt[s]
                    waits.append((s, self.cnt[s]))
            for i in range(self.ndma):
                s = ("dma", i)
                if self.dma_tot[i] > self.known[e].get(s, 0):
                    self.known[e][s] = self.dma_tot[i]
                    waits.append((s, self.dma_tot[i]))
            if waits:
                self.q[e].append((waits, None, None, 0))

    def finish(self):
        self.barrier()
        nc = self.nc
        semh = self.semh
        q = self.q
        with nc.Block() as block:
            def mk(name):
                def body(e):
                    for waits, fn, s, amt in q[name]:
                        for ws, wv in waits:
                            e.wait_ge(semh[ws], wv)
                        if fn is not None:
                            if amt < 0:
                                fn(e)
                            else:
                                fn(e).then_inc(semh[s], amt)
                return body
            block.sync(mk("sp"))
            block.scalar(mk("act"))
            block.vector(mk("dve"))
            block.gpsimd(mk("pool"))
            block.tensor(mk("pe"))

    def mm(self, out, lhsT, rhs, start, stop, reads, writes):
        self.op("pe", lambda e: e.matmul(out, lhsT, rhs, start=start, stop=stop), reads, writes)

    def tr(self, out, in_, ident, reads, writes):
        self.op("pe", lambda e: e.transpose(out, in_, ident), reads, writes)

    def act(self, out, in_, func, reads, writes, bias=None, scale=None, accum=None):
        kw = {}
        if bias is not None:
            kw["bias"] = bias
        if scale is not None:
            kw["scale"] = scale
        if accum is not None:
            kw["accum_out"] = accum
        self.op("act", lambda e: e.activation(out, in_, func, **kw), reads, writes)

    def tt(self, eng, out, a, b, op, reads, writes):
        self.op(eng, lambda e: e.tensor_tensor(out, a, b, op), reads, writes)

    def ts(self, eng, out, a, s1, s2, op0, op1, reads, writes):
        if s2 is None:
            self.op(eng, lambda e: e.tensor_scalar(out, a, s1, None, op0), reads, writes)
        else:
            self.op(eng, lambda e: e.tensor_scalar(out, a, s1, s2, op0, op1), reads, writes)

    def stt(self, eng, out, a, sc, b, op0, op1, reads, writes):
        self.op(eng, lambda e: e.scalar_tensor_tensor(out, a, sc, b, op0, op1), reads, writes)

    def copy(self, eng, out, in_, reads, writes):
        if eng == "act":
            self.op("act", lambda e: e.copy(out, in_), reads, writes)
        else:
            self.op(eng, lambda e: e.tensor_copy(out, in_), reads, writes)

    def reduce(self, eng, out, in_, op, reads, writes):
        self.op(eng, lambda e: e.tensor_reduce(out, in_, AX.X, op), reads, writes)

    def recip(self, eng, out, in_, reads, writes):
        self.op(eng, lambda e: e.reciprocal(out, in_), reads, writes)

    def memset(self, eng, ap, val, writes):
        self.op(eng, lambda e: e.memset(ap, val), (), writes)


class Seq:
    def __init__(self, name, T, sliced=False):
        self.name = name
        self.T = T
        self.sliced = sliced
        self.ng = T // 1024
        self.L = T + NM
        self.rows = T // 64
        self.ntile = 1 + T // 128

    def tile_range(self, i):
        if i == 0:
            return 0, NM
        return NM + (i - 1) * 128, 128


class Builder:
    def __init__(self, seq_T, debug=False, stages=5, run=(1, 2, 3, 4, 5), sliced=None):
        self.debug = debug
        self.run = run
        self.stages = stages
        nc = bass.Bass("TRN2", target_bir_lowering=False)
        self.nc = nc
        self.P = Prog(nc)
        sliced = sliced or [False] * len(seq_T)
        self.seqs = [Seq("s%d" % i, T, sl) for i, (T, sl) in enumerate(zip(seq_T, sliced))]
        self.dram = {}
        self._declare_io()

    def din(self, name, shape, dtype=F32):
        ap = self.nc.dram_tensor(name, list(shape), dtype, kind="ExternalInput").ap()
        self.dram[name] = ap
        return ap

    def dscr(self, name, shape, dtype=F32):
        kind = "ExternalOutput" if self.debug else "Internal"
        ap = self.nc.dram_tensor(name, list(shape), dtype, kind=kind).ap()
        self.dram[name] = ap
        return ap

    def _declare_io(self):
        for s in self.seqs:
            self.din("x_" + s.name, [s.T, D])
            self.nc_out = None
        for s in self.seqs:
            ap = self.nc.dram_tensor("y_" + s.name, [1024 if s.sliced else s.T, D], F32, kind="ExternalOutput").ap()
            self.dram["y_" + s.name] = ap
            if s.sliced:
                self.din("xext_" + s.name, [1280, D])
                self.din("msel_" + s.name, [128, s.ng + 1])
        self.din("c_shift", [NM, 128])
        self.din("meta_tokens", [NM, D])
        for nm in ("emb_ln_g", "emb_ln_b", "ln1_g", "ln1_b", "ln2_g", "ln2_b"):
            self.din(nm, [D])
        self.din("w_in", [D, NIN])
        self.din("bias_tab", [HA, 5, 128, NKEY])
        self.din("rwkv_mu", [2, NSH])
        self.din("rwkv_w0", [2, DR])
        self.din("rwkv_w2", [2, 96, DR])
        self.din("rwkv_a0", [2, DR])
        self.din("rwkv_a2", [2, 96, DR])
        self.din("rwkv_g2", [256, DR])
        for nm in ("rwkv_k_k", "rwkv_k_a", "rwkv_r_k", "rwkv_lnx_g", "rwkv_lnx_b"):
            self.din(nm, [DR])
        self.din("w_out", [D, D])
        self.din("ffn_w_in", [D, 2 * DFF])
        self.din("ffn_conv_l", [128, 44, 4])
        self.din("ffn_w_out", [DFF, D])
        self.din("c_ident", [128, 128])
        self.din("c_masks", [6, 128, 128])
        self.din("c_zero", [1, NSH])
        for s in self.seqs:
            n = s.name
            self.dscr("H0_" + n, [s.L, D])
            self.dscr("QT_" + n, [HA, 128, s.L], BF16)
            self.dscr("KT_" + n, [HA, 128, s.L], BF16)
            self.dscr("V_" + n, [s.L, DA], BF16)
            self.dscr("P_" + n, [s.L, NSH])
            self.dscr("G_" + n, [s.L, DR])
            self.dscr("YF_" + n, [s.L, DR])
            self.dscr("BF_" + n, [s.L, DR])
            self.dscr("OT_" + n, [16, 128, s.L], BF16)
            self.dscr("H1_" + n, [s.L, D])
            self.dscr("H1T_" + n, [16, 128, s.L], BF16)
            if s.sliced:
                self.dscr("OTOK_" + n, [s.L, D], BF16)
                self.dscr("H1s_" + n, [1280, D])
                self.dscr("H1Ts_" + n, [16, 128, 1280], BF16)

    def setup_consts(self):
        P = self.P
        d = self.dram
        self.ident_f, k1 = P.tile([128, 128], F32, "identf")
        P.dma("sp", self.ident_f, d["c_ident"], (), (k1,))
        self.ident_b, k2 = P.tile([128, 128], BF16, "identb")
        P.copy("dve", self.ident_b, self.ident_f, (k1,), (k2,))
        self.kid = (k1, k2)
        self.const_end = P.off

    def bcast_load(self, name_ap, n, nm):
        P = self.P
        t, k = P.tile([128, n], F32, nm)
        P.dma("sp", t, name_ap.partition_broadcast(128), (), (k,))
        return t, k

    def layer_norm(self, x, kx, n, gb, kgb, out, kout, eps, tmp):
        P = self.P
        (st, kst), (mv, kmv), (rs, krs) = tmp
        g, b = gb
        for c in range(4):
            P.op("dve", lambda e, c=c: e.bn_stats(st[:n, c, :], x[:n, c * 512:(c + 1) * 512]), (kx,), (kst,))
        P.op("dve", lambda e: e.bn_aggr(mv[:n, :], st[:n].rearrange("p a b -> p (a b)")), (kst,), (kmv,))
        P.act(rs[:n, :], mv[:n, 1:2], AF.Sqrt, (kmv,), (krs,), bias=eps)
        P.op("dve", lambda e: e.reciprocal(rs[:n, :], rs[:n, :]), (krs,), (krs,))
        P.ts("dve", x[:n, :], x[:n, :], mv[:n, 0:1], rs[:n, 0:1], ALU.subtract, ALU.mult, (kx, kmv, krs), (kx,))
        P.tt("pool", x[:n, :], x[:n, :], g[:n, :], ALU.mult, (kx,) + kgb, (kx,))
        P.tt("dve", out[:n, :], x[:n, :], b[:n, :], ALU.add, (kx,) + kgb, (kout,))

    def ln_tmp(self):
        P = self.P
        return (P.tile([128, 4, 6], F32, "bnst"), P.tile([128, 2], F32, "bnmv"), P.tile([128, 1], F32, "rstd"))

    def transpose_to_fm(self, src, ksrc, n, dst, kdst, col0, banks, ctr):
        P = self.P
        for g4 in range(4):
            bi = banks[(ctr[0]) % len(banks)]
            ctr[0] += 1
            pb = P.bank(bi)
            kb = "psb%d" % bi
            for j in range(4):
                kc = g4 * 4 + j
                P.tr(pb[:, j * 128:j * 128 + n], src[:n, kc * 128:(kc + 1) * 128], self.ident_f[:n, :n],
                     (ksrc, self.kid[0]), (kb,))
            eng = "act" if g4 % 2 == 0 else "dve"
            P.copy(eng, dst[:, g4 * 4:g4 * 4 + 4, col0:col0 + n],
                   pb.rearrange("p (a b) -> p a b", a=4)[:, :, :n], (kb,), (kdst,))

    def load_w(self, wt, kw, src, rows, col0, ncols):
        P = self.P
        nk = rows // 128
        for k0 in range(0, nk, 4):
            k1 = min(nk, k0 + 4)
            P.dma("pool", wt[:, k0:k1, :ncols],
                  src[k0 * 128:k1 * 128, col0:col0 + ncols].rearrange("(kc p) n -> p kc n", p=128), (), (kw,))

    def stage1(self, s):
        P = self.P
        d = self.dram
        n_ = s.name
        P.barrier()
        P.reset_arena(self.const_end)
        g0, kg = self.bcast_load(d["emb_ln_g"], D, "g0")
        b0, kb = self.bcast_load(d["emb_ln_b"], D, "b0")
        tmp = self.ln_tmp()
        g2, kg2 = P.tile([128, 2, DR], BF16, "g2")
        self.load_w(g2, kg2, d["rwkv_g2"], 256, 0, DR)
        xt = [P.tile([128, D], F32, "xt%d" % i) for i in range(2)]
        ht = [P.tile([128, D], F32, "ht%d" % i) for i in range(2)]
        wbuf = [P.tile([128, 16, 512], BF16, "w%d" % i) for i in range(2)]
        stg = [P.tile([128, 512], F32, "stg%d" % i) for i in range(3)]
        stgb = [P.tile([128, 512], BF16, "stgb%d" % i) for i in range(2)]
        qkT = [P.tile([128, 4, 128], BF16, "qkT%d" % i) for i in range(2)]
        sgT = [P.tile([128, 2, 128], BF16, "sgT%d" % i) for i in range(2)]
        gst = [P.tile([128, DR], F32, "gst%d" % i) for i in range(2)]
        TB = 16
        h0T, kh0T = P.tile([128, 16, TB * 128 + NM], BF16, "h0T")
        groups = []
        for c0 in range(0, NIN, 512):
            groups.append((c0, min(512, NIN - c0)))
        tiles = list(range(s.ntile))
        blocks = [tiles[0:TB + 1]] + [tiles[i:i + TB] for i in range(TB + 1, s.ntile, TB)]
        ctr = [0]
        wi = 0
        ci = 0
        for blk in blocks:
            base = s.tile_range(blk[0])[0]
            for ti in blk:
                t0, n = s.tile_range(ti)
                x, kx = xt[ci % 2]
                h, kh = ht[ci % 2]
                ci += 1
                if ti == 0:
                    P.dma("sp", x[:n, :], d["meta_tokens"], (), (kx,))
                else:
                    P.dma("sp", x[:n, :], d["x_" + n_][t0 - NM:t0 - NM + n, :], (), (kx,))
                self.layer_norm(x, kx, n, (g0, b0), (kg, kb), h, kh, LN_EPS, tmp)
                if not s.sliced:
                    P.dma("sp", d["H0_" + n_][t0:t0 + n, :], h[:n, :], (kh,), ())
                self.transpose_to_fm(h, kh, n, h0T, kh0T, t0 - base, (0, 1), ctr)
            if self.stages == 0:
                continue
            for gi, (c0, nc_) in enumerate(groups):
                w, kw = wbuf[wi % 2]
                wi += 1
                self.load_w(w, kw, d["w_in"], D, c0, nc_)
                for ti in blk:
                    t0, n = s.tile_range(ti)
                    bi = 2 + (ctr[0] % 2)
                    ctr[0] += 1
                    pb = P.bank(bi)
                    kpb = "psb%d" % bi
                    for kc in range(16):
                        P.mm(pb[:n, :nc_], h0T[:, kc, t0 - base:t0 - base + n], w[:, kc, :nc_],
                             kc == 0, kc == 15, (kh0T, kw), (kpb,))
                    ev = "act" if ctr[0] % 2 == 0 else "dve"
                    if self.stages == -1 or (self.stages == -2 and gi < 4) or (self.stages == -3 and gi >= 4) or (self.stages == -4 and gi < 12):
                        sf, ksf = stg[ctr[0] % 3]
                        P.copy(ev, sf[:n, :], pb[:n, :], (kpb,), (ksf,))
                        continue
                    if gi < 4:
                        sb, ksb = stgb[ctr[0] % 2]
                        if gi < 2:
                            P.act(sb[:n, :], pb[:n, :], AF.Copy, (kpb,), (ksb,), scale=128.0 ** -0.5)
                        else:
                            P.copy(ev, sb[:n, :], pb[:n, :], (kpb,), (ksb,))
                        tb = 4
                        ptb = P.bank(tb, BF16)
                        for j in range(4):
                            P.tr(ptb[:, j * 128:j * 128 + n], sb[:n, j * 128:(j + 1) * 128], self.ident_b[:n, :n],
                                 (ksb, self.kid[1]), ("psb4",))
                        qt, kqt = qkT[ctr[0] % 2]
                        P.copy("dve" if ev == "act" else "act", qt[:, :, :n],
                               ptb[:, 0:512].rearrange("p (a b) -> p a b", a=4)[:, :, :n], ("psb4",), (kqt,))
                        dst = d[("QT_" if gi < 2 else "KT_") + n_]
                        h0 = (gi % 2) * 4
                        P.dma("sp", dst[h0:h0 + 4, :, t0:t0 + n].rearrange("h p t -> p h t"), qt[:, :, :n], (kqt,), ())
                    elif gi < 6:
                        sb, ksb = stgb[ctr[0] % 2]
                        P.copy(ev, sb[:n, :], pb[:n, :], (kpb,), (ksb,))
                        P.dma("sp", d["V_" + n_][t0:t0 + n, (gi - 4) * 512:(gi - 3) * 512], sb[:n, :], (ksb,), ())
                    elif gi < 12:
                        sf, ksf = stg[ctr[0] % 3]
                        P.copy(ev, sf[:n, :], pb[:n, :], (kpb,), (ksf,))
                        P.dma("sp", d["P_" + n_][t0:t0 + n, (gi - 6) * 512:(gi - 5) * 512], sf[:n, :], (ksf,), ())
                    else:
                        sf, ksf = stg[ctr[0] % 3]
                        P.copy("dve", sf[:n, :192], pb[:n, :192], (kpb,), (ksf,))
                        P.dma("sp", d["P_" + n_][t0:t0 + n, 3072:3264], sf[:n, :192], (ksf,), ())
                        sb, ksb = stgb[ctr[0] % 2]
                        P.act(sb[:n, :256], pb[:n, 192:448], AF.Sigmoid, (kpb,), (ksb,))
                        ptb = P.bank(4, BF16)
                        for j in range(2):
                            P.tr(ptb[:, j * 128:j * 128 + n], sb[:n, j * 128:(j + 1) * 128], self.ident_b[:n, :n],
                                 (ksb, self.kid[1]), ("psb4",))
                        sg, ksg = sgT[ctr[0] % 2]
                        P.copy("dve", sg[:, :, :n], ptb[:, 0:256].rearrange("p (a b) -> p a b", a=2)[:, :, :n],
                               ("psb4",), (ksg,))
                        go, kgo = gst[ctr[0] % 2]
                        for hf in range(2):
                            pg = P.bank(5 + hf)
                            kpg = "psb%d" % (5 + hf)
                            for j in range(2):
                                P.mm(pg[:n, :], sg[:, j, :n], g2[:, j, hf * 512:(hf + 1) * 512], j == 0, j == 1,
                                     (ksg, kg2), (kpg,))
                            P.copy("act" if hf == 0 else "dve", go[:n, hf * 512:(hf + 1) * 512], pg[:n, :], (kpg,), (kgo,))
                        P.dma("sp", d["G_" + n_][t0:t0 + n, :], go[:n, :], (kgo,), ())

    def stage2(self, s):
        P = self.P
        d = self.dram
        n_ = s.name
        for dr in (0, 1):
            P.barrier()
            P.reset_arena(self.const_end)
            self._rwkv_dir(s, dr)

    def _rwkv_dir(self, s, dr):
        P = self.P
        d = self.dram
        n_ = s.name
        Pd = d["P_" + n_]
        L = s.L
        mu, kmu = self.bcast_load(d["rwkv_mu"][dr], NSH, "mu")
        w0, kw0 = self.bcast_load(d["rwkv_w0"][dr], DR, "w0")
        a0, ka0 = self.bcast_load(d["rwkv_a0"][dr], DR, "a0")
        kkp, kkkp = self.bcast_load(d["rwkv_k_k"], DR, "k_k")
        kap, kkap = self.bcast_load(d["rwkv_k_a"], DR, "k_a")
        rkp, krkp = self.bcast_load(d["rwkv_r_k"], DR, "r_k")
        w2, kw2 = P.tile([128, DR], F32, "w2")
        a2, ka2 = P.tile([128, DR], F32, "a2")
        P.dma("sp", w2[:96, :], d["rwkv_w2"][dr], (), (kw2,))
        P.dma("sp", a2[:96, :], d["rwkv_a2"][dr], (), (ka2,))
        tri, ktri = P.tile([128, 128], F32, "tri")
        P.dma("sp", tri, d["c_masks"][dr], (), (ktri,))
        ones, kones = P.tile([128, 128], F32, "ones")
        P.memset("pool", ones, 1.0, (kones,))
        m4, km4 = P.tile([128, 4, 128], F32, "mask4")
        for q in range(4):
            P.dma("sp", m4[:, q, :], d["c_masks"][2 + 2 * dr + (q % 2)], (), (km4,))
        suT, ksuT = P.tile([128, 128], F32, "suT")
        P.dma("sp", suT, d["c_masks"][2 + 2 * (1 - dr)], (), (ksuT,))
        if dr == 1:
            lg, klg = self.bcast_load(d["rwkv_lnx_g"], DR, "lnxg")
            lb, klb = self.bcast_load(d["rwkv_lnx_b"], DR, "lnxb")
        Sf, kSf = P.tile([128, 8, 64], F32, "Sf")
        Sb, kSb = P.tile([128, 8, 64], BF16, "Sb")
        P.memset("pool", Sf, 0.0, (kSf,))
        P.memset("pool", Sb, 0.0, (kSb,))
        kSfh = [[kSf + "/%d/%d" % (j, h) for h in range(2)] for j in range(8)]
        kSbh = [[kSb + "/%d/%d" % (j, h) for h in range(2)] for j in range(8)]
        for j in range(8):
            for h in range(2):
                P.last_w[kSfh[j][h]] = P.last_w[kSf]
                P.last_w[kSbh[j][h]] = P.last_w[kSb]
        pt = [P.tile([128, NSH], F32, "p%d" % i) for i in range(1)]
        pn = [P.tile([128, NSH], F32, "pn%d" % i) for i in range(1)]
        T = lambda nm, dt=F32: P.tile([128, DR], dt, nm)
        sg, ksg = T("sg"); av, kav = T("a"); cum, kcum = T("cum"); e1, ke1 = T("e1"); e2, ke2 = T("e2")
        kk, kkk = T("kk"); kq, kkq = T("kq"); ka_, kka_ = T("ka"); tmp, ktmp = T("tmp"); bon, kbon = T("bon")
        Y, kY = T("Y")
        At, kAt = T("At", BF16); Bt, kBt = T("Bt", BF16); Kt, kKt = T("Kt", BF16); Rt, kRt = T("Rt", BF16)
        BW, kBW = T("BW", BF16); KW, kKW = T("KW", BF16); Vb, kVb = T("Vb", BF16)
        ART, kART = P.tile([128, 8, 256], BF16, "ART")
        BTt, kBTt = P.tile([128, 8, 128], BF16, "BT")
        KTt, kKTt = P.tile([128, 8, 128], BF16, "KT")
        thT, kthT = P.tile([128, 2, 128], F32, "thT")
        th, kth = P.tile([128, 192], F32, "th")
        sm, ksm = P.tile([128, 4, 16], F32, "small")
        wc, kwc = P.tile([128, 8], F32, "wc")
        if dr == 1:
            yf, kyf = e1, ke1
            gt, kgt = e2, ke2
            ob, kob = T("ob", BF16)
            oT, koT = P.tile([128, 8, 128], BF16, "oT")
        M4a, kM4a0 = P.tile([128, 16, 4, 128], BF16, "M4a")
        kM4 = [kM4a0 + "/%d" % i for i in range(16)]
        NTa, kNTa0 = P.tile([128, 16, 128], BF16, "NTa")
        kNT = [kNTa0 + "/%d" % g for g in range(4)]
        Ta = []
        PPa = []
        for i in range(2):
            t_, k_ = P.tile([128, 16, 128], BF16, "Ta%d" % i)
            Ta.append((t_, [k_ + "/%d" % g for g in range(4)]))
            t_, k_ = P.tile([128, 2, 16, 128], BF16, "PPa%d" % i)
            PPa.append((t_, [[k_ + "/%d/%d" % (w, g) for g in range(4)] for w in range(2)]))
        Xa, kXa0 = P.tile([128, 16, 64], BF16, "Xa")
        kXa = [kXa0 + "/%d" % b for b in range(2)]
        Ua, kUa0 = P.tile([128, 16, 64], BF16, "Ua")
        kUa = [kUa0 + "/%d" % b for b in range(2)]
        id4, kid4 = P.tile([128, 4, 128], BF16, "id4")
        for q in range(4):
            P.copy("pool", id4[:, q, :], self.ident_b, (self.kid[1],), (kid4,))
        order = list(range(s.ntile)) if dr == 0 else list(range(s.ntile - 1, -1, -1))
        tctr = [0]
        v3 = lambda ap: ap.rearrange("p (h c) -> p h c", h=16)
        p, kp = pt[0]
        q_, kq_ = pn[0]
        f, kf = q_, kq_
        r_ = f[:, 0:DR]; k_ = f[:, DR:2 * DR]; v_ = f[:, 2 * DR:3 * DR]

        def prep1(ti):
            t0, n = s.tile_range(ti)
            P.dma("sp", p[:n, :], Pd[t0:t0 + n, :], (), (kp,))
            if dr == 0:
                if t0 == 0:
                    P.dma("sp", q_[0:1, :], d["c_zero"], (), (kq_,))
                    P.dma("sp", q_[1:n, :], Pd[0:n - 1, :], (), (kq_,))
                else:
                    P.dma("sp", q_[:n, :], Pd[t0 - 1:t0 - 1 + n, :], (), (kq_,))
            else:
                if t0 + n == L:
                    P.dma("sp", q_[n - 1:n, :], d["c_zero"], (), (kq_,))
                    P.dma("sp", q_[0:n - 1, :], Pd[t0 + 1:t0 + n, :], (), (kq_,))
                else:
                    P.dma("sp", q_[:n, :], Pd[t0 + 1:t0 + 1 + n, :], (), (kq_,))
            yield
            P.tt("dve", q_[:n, :], q_[:n, :], p[:n, :], ALU.subtract, (kq_, kp), (kq_,))
            yield
            P.tt("pool", q_[:n, :], q_[:n, :], mu[:n, :], ALU.mult, (kq_, kmu), (kq_,))
            yield
            P.tt("dve", q_[:n, :], q_[:n, :], p[:n, :], ALU.add, (kq_, kp), (kq_,))
            yield
            P.act(th[:n, 0:96], f[:n, 3 * DR:3 * DR + 96], AF.Tanh, (kf,), (kth,))
            P.copy("pool", th[:n, 96:192], f[:n, 3 * DR + 96:3 * DR + 192], (kf,), (kth,))
            pb0 = P.bank(0)
            for q in range(2):
                P.tr(pb0[:96, q * 128:q * 128 + n], th[:n, q * 96:(q + 1) * 96], self.ident_f[:n, :n],
                     (kth, self.kid[0]), ("psb0",))
            P.copy("act", thT[:96, :, :n], pb0[:96, 0:256].rearrange("p (a b) -> p a b", a=2)[:, :, :n], ("psb0",), (kthT,))
            yield
            for which, (wm, kwm, bias, kbias, dst, kdst) in enumerate(((w2, kw2, w0, kw0, sg, ksg), (a2, ka2, a0, ka0, av, kav))):
                for hf in range(2):
                    pb = P.bank(hf); kpb = "psb%d" % hf
                    P.mm(pb[:n, :], thT[:96, which, :n], wm[:96, hf * 512:(hf + 1) * 512], True, True, (kthT, kwm), (kpb,))
                    P.tt("dve", dst[:n, hf * 512:(hf + 1) * 512], pb[:n, :], bias[:n, hf * 512:(hf + 1) * 512], ALU.add,
                         (kpb, kbias), (kdst,))
                    yield
                P.act(dst[:n, :], dst[:n, :], AF.Sigmoid, (kdst,), (kdst,))
                yield
            for hf in range(2):
                pb = P.bank(hf); kpb = "psb%d" % hf
                P.mm(pb[:n, :], tri[:n, :n], sg[:n, hf * 512:(hf + 1) * 512], True, True, (ktri, ksg), (kpb,))
                P.copy("act", cum[:n, hf * 512:(hf + 1) * 512], pb[:n, :], (kpb,), (kcum,))
                yield
            P.tt("pool", kk[:n, :], k_[:n, :], kkp[:n, :], ALU.mult, (kf, kkkp), (kkk,))
            yield
            P.tt("pool", tmp[:n, :], kk[:n, :], kk[:n, :], ALU.mult, (kkk,), (ktmp,))
            yield
            P.reduce("dve", sm[:n, 0, :], v3(tmp[:n, :]), ALU.add, (ktmp,), (ksm,))
            P.act(sm[:n, 0, :], sm[:n, 0, :], AF.Sqrt, (ksm,), (ksm,))
            P.ts("dve", sm[:n, 0, :], sm[:n, 0, :], 1e-12, None, ALU.max, None, (ksm,), (ksm,))
            P.recip("dve", sm[:n, 0, :], sm[:n, 0, :], (ksm,), (ksm,))
            yield
            P.tt("dve", v3(kk[:n, :]), v3(kk[:n, :]), sm[:n, 0, :].unsqueeze(2).to_broadcast([n, 16, 64]), ALU.mult,
                 (kkk, ksm), (kkk,))
            yield
            P.stt("dve", kq[:n, :], av[:n, :], -1.0, kap[:n, :], ALU.add, ALU.mult, (kav, kkap), (kkq,))
            yield
            P.stt("dve", kq[:n, :], kq[:n, :], 1.0, k_[:n, :], ALU.add, ALU.mult, (kkq, kf), (kkq,))
            yield
            P.tt("pool", ka_[:n, :], kk[:n, :], av[:n, :], ALU.mult, (kkk, kav), (kka_,))
            yield

        def prep2(ti):
            t0, n = s.tile_range(ti)
            for hf in range(2):
                pb = P.bank(2 + hf); kpb = "psb%d" % (2 + hf)
                P.mm(pb[:n, :], ones[:n, :n], sg[:n, hf * 512:(hf + 1) * 512], True, True, (kones, ksg), (kpb,))
                P.tt("dve", e2[:n, hf * 512:(hf + 1) * 512], pb[:n, :], cum[:n, hf * 512:(hf + 1) * 512], ALU.subtract,
                     (kpb, kcum), (ke2,))
            pb0 = P.bank(0)
            for j in range(8):
                P.mm(pb0[:, j:j + 1], sg[:n, j * 128:(j + 1) * 128], ones[:n, 0:1], True, True, (ksg, kones), ("psb0",))
            P.act(wc[:, :], pb0[:, 0:8], AF.Exp, ("psb0",), (kwc,), scale=CDEC)
            P.tt("pool", tmp[:n, :], r_[:n, :], kq[:n, :], ALU.mult, (kf, kkq), (ktmp,))
            P.tt("pool", tmp[:n, :], tmp[:n, :], rkp[:n, :], ALU.mult, (ktmp, krkp), (ktmp,))
            P.reduce("dve", sm[:n, 1, :], v3(tmp[:n, :]), ALU.add, (ktmp,), (ksm,))
            P.tt("dve", v3(bon[:n, :]), v3(v_[:n, :]), sm[:n, 1, :].unsqueeze(2).to_broadcast([n, 16, 64]), ALU.mult,
                 (kf, ksm), (kbon,))
            if dr == 0:
                P.dma("sp", d["BF_" + n_][t0:t0 + n, :], bon[:n, :], (kbon,), ())
            P.act(e1[:n, :], cum[:n, :], AF.Exp, (kcum,), (ke1,), scale=CDEC)
            P.tt("dve", Rt[:n, :], r_[:n, :], e1[:n, :], ALU.mult, (kf, ke1), (kRt,))
            P.act(e1[:n, :], cum[:n, :], AF.Exp, (kcum,), (ke1,), scale=-CDEC)
            P.tt("pool", Bt[:n, :], ka_[:n, :], e1[:n, :], ALU.mult, (kka_, ke1), (kBt,))
            P.tt("dve", Kt[:n, :], kq[:n, :], e1[:n, :], ALU.mult, (kkq, ke1), (kKt,))
            P.act(e2[:n, :], e2[:n, :], AF.Exp, (ke2,), (ke2,), scale=CDEC)
            P.tt("pool", BW[:n, :], ka_[:n, :], e2[:n, :], ALU.mult, (kka_, ke2), (kBW,))
            P.tt("dve", KW[:n, :], kq[:n, :], e2[:n, :], ALU.mult, (kkq, ke2), (kKW,))
            P.tt("dve", cum[:n, :], cum[:n, :], sg[:n, :], ALU.subtract, (kcum, ksg), (kcum,))
            P.act(e1[:n, :], cum[:n, :], AF.Exp, (kcum,), (ke1,), scale=CDEC)
            P.stt("dve", At[:n, :], kk[:n, :], -1.0, e1[:n, :], ALU.mult, ALU.mult, (kkk, ke1), (kAt,))
            P.copy("pool", Vb[:n, :], v_[:n, :], (kf,), (kVb,))
            for xi, (src, ksrc) in enumerate(((At, kAt), (Rt, kRt), (Bt, kBt), (Kt, kKt))):
                bi = tctr[0] % 4
                tctr[0] += 1
                ptb = P.bank(bi, BF16)
                kptb = "psb%d" % bi
                for j in range(8):
                    P.tr(ptb[:, j * 128:j * 128 + n], src[:n, j * 128:(j + 1) * 128], self.ident_b[:n, :n],
                         (ksrc, self.kid[1]), (kptb,))
                srcv = ptb.rearrange("p (a b) -> p a b", a=8)[:, :, :n]
                if xi == 0:
                    P.copy("act", ART[:, :, 0:n], srcv, (kptb,), (kART,))
                elif xi == 1:
                    P.copy("dve", ART[:, :, 128:128 + n], srcv, (kptb,), (kART,))
                elif xi == 2:
                    P.copy("act", BTt[:, :, :n], srcv, (kptb,), (kBTt,))
                else:
                    P.copy("dve", KTt[:, :, :n], srcv, (kptb,), (kKTt,))

        def units(ti, gen):
            t0, n = s.tile_range(ti)

            def pump():
                next(gen, None)
            nsq = int(np.log2(n)) - 1
            HD = [dict(hd=hd, j=hd // 2, hb=64 * (hd % 2), col=hd * 64, g=hd // 4, sl=hd % 4) for hd in range(16)]
            bk = lambda g: P.bank(4 + g)
            kbk = lambda g: "psb%d" % (4 + g)
            for H in HD:
                if H["hd"] % 4 == 0:
                    pump()
                hd, j, hb = H["hd"], H["j"], H["hb"]
                b_ = bk(hd % 4); kb_ = kbk(hd % 4)
                P.mm(b_[:n, 0:256], BTt[hb:hb + 64, j, :n], ART[hb:hb + 64, j, :], True, True, (kBTt, kART), (kb_,))
                P.mm(b_[:n, 256:512], KTt[hb:hb + 64, j, :n], ART[hb:hb + 64, j, :], True, True, (kKTt, kART), (kb_,))
                P.tt("dve", M4a[:n, hd, :, :n], b_[:n, :].rearrange("p (a b) -> p a b", a=4)[:, :, :n], m4[:n, :, :n], ALU.mult,
                     (kb_, km4), (kM4[hd],))
            for H in HD:
                if H["hd"] % 4 == 0:
                    pump()
                hd, j, hb = H["hd"], H["j"], H["hb"]
                nb = (hd % 2) * 2 + hd // 8
                sl = (hd % 8) // 2
                P.mm(bk(nb)[:n, sl * 128:sl * 128 + n], ART[hb:hb + 64, j, 0:n], BTt[hb:hb + 64, j, :n], True, True,
                     (kART, kBTt), (kbk(nb),))
                if sl == 3:
                    base = (hd // 8) * 8 + hd % 2
                    P.tt("dve", NTa[:n, base:base + 7:2, :n], bk(nb)[:n, :].rearrange("p (a b) -> p a b", a=4)[:, :, :n],
                         suT[:n, :n].unsqueeze(1).to_broadcast([n, 4, n]), ALU.mult, (kbk(nb), ksuT), (kNT[nb],))
            T0, kT0 = Ta[0]
            for g in range(4):
                P.tt("pool", T0[:n, g * 4:g * 4 + 4, :n], M4a[:n, g * 4:g * 4 + 4, 0, :n], id4[:n, :, :n], ALU.add,
                     tuple(kM4[g * 4:g * 4 + 4]) + (kid4,), (kT0[g],))
            Pget = lambda hd: (M4a[:, hd, 0, :], kM4[hd])
            PTget = lambda hd: (NTa[:, hd, :], kNT[(hd % 2) * 2 + hd // 8])
            tcur = 0
            for it in range(1, nsq + 1):
                last = it == nsq
                PPn, kPPn = PPa[it % 2]
                for H in HD:
                    if H["hd"] % 4 == 0:
                        pump()
                    hd, g, sl = H["hd"], H["g"], H["sl"]
                    Pc, kPc = Pget(hd); PTc, kPTc = PTget(hd)
                    P.mm(bk(g)[:n, sl * 128:sl * 128 + n], Pc[:n, :n], PTc[:n, :n], True, True, (kPc, kPTc), (kbk(g),))
                    if sl == 3:
                        P.copy("act", PPn[:n, 1, g * 4:g * 4 + 4, :n], bk(g)[:n, :].rearrange("p (a b) -> p a b", a=4)[:, :, :n],
                               (kbk(g),), (kPPn[1][g],))
                if not last:
                    for H in HD:
                        if H["hd"] % 4 == 0:
                            pump()
                        hd, g, sl = H["hd"], H["g"], H["sl"]
                        Pc, kPc = Pget(hd); PTc, kPTc = PTget(hd)
                        P.mm(bk(g)[:n, sl * 128:sl * 128 + n], PTc[:n, :n], Pc[:n, :n], True, True, (kPc, kPTc), (kbk(g),))
                        if sl == 3:
                            P.copy("act" if g % 2 else "dve", PPn[:n, 0, g * 4:g * 4 + 4, :n],
                                   bk(g)[:n, :].rearrange("p (a b) -> p a b", a=4)[:, :, :n], (kbk(g),), (kPPn[0][g],))
                Pget = lambda hd, PPn=PPn, kPPn=kPPn: (PPn[:, 0, hd, :], kPPn[0][hd // 4])
                PTget = lambda hd, PPn=PPn, kPPn=kPPn: (PPn[:, 1, hd, :], kPPn[1][hd // 4])
                Tc, kTc = Ta[tcur]
                Tn, kTn = Ta[1 - tcur]
                for H in HD:
                    if H["hd"] % 4 == 0:
                        pump()
                    hd, g, sl = H["hd"], H["g"], H["sl"]
                    PTc, kPTc = PTget(hd)
                    P.mm(bk(g)[:n, sl * 128:sl * 128 + n], PTc[:n, :n], Tc[:n, hd, :n], True, True, (kPTc, kTc[g]), (kbk(g),))
                    if sl == 3:
                        P.tt("dve", Tn[:n, g * 4:g * 4 + 4, :n], bk(g)[:n, :].rearrange("p (a b) -> p a b", a=4)[:, :, :n],
                             Tc[:n, g * 4:g * 4 + 4, :n], ALU.add, (kbk(g), kTc[g]), (kTn[g],))
                tcur = 1 - tcur
            Tc, kTc = Ta[tcur]
            for H in HD:
                if H["hd"] % 4 == 0:
                    pump()
                hd, j, hb, col = H["hd"], H["j"], H["hb"], H["col"]
                b = hd % 2
                o_ = bk(b)[:n, (hd // 2) * 64:(hd // 2) * 64 + 64]
                P.mm(o_, ART[hb:hb + 64, j, 0:n], Sb[hb:hb + 64, j, :], True, False, (kART, kSb), (kbk(b),))
                P.mm(o_, M4a[:n, hd, 2, :n], Vb[:n, col:col + 64], False, True, (kM4[hd], kVb), (kbk(b),))
                if hd >= 14:
                    P.copy("act", Xa[:n, b:16:2, :], bk(b)[:n, :].rearrange("p (a b) -> p a b", a=8), (kbk(b),), (kXa[b],))
            for H in HD:
                if H["hd"] % 4 == 0:
                    pump()
                hd = H["hd"]
                b = hd % 2
                P.mm(bk(2 + b)[:n, (hd // 2) * 64:(hd // 2) * 64 + 64], Tc[:n, hd, :n], Xa[:n, hd, :], True, True,
                     (kTc[hd // 4], kXa[b]), (kbk(2 + b),))
                if hd >= 14:
                    P.copy("act", Ua[:n, b:16:2, :], bk(2 + b)[:n, :].rearrange("p (a b) -> p a b", a=8), (kbk(2 + b),), (kUa[b],))
            Yv = Y[:n, :].rearrange("p (h c) -> p h c", h=16)
            for H in HD:
                if H["hd"] % 4 == 0:
                    pump()
                hd, j, hb, col = H["hd"], H["j"], H["hb"], H["col"]
                b = hd % 2
                o_ = bk(b)[:n, (hd // 2) * 64:(hd // 2) * 64 + 64]
                P.mm(o_, ART[hb:hb + 64, j, 128:128 + n], Sb[hb:hb + 64, j, :], True, False, (kART, kSb), (kbk(b),))
                P.mm(o_, M4a[:n, hd, 1, :n], Ua[:n, hd, :], False, False, (kM4[hd], kUa[b]), (kbk(b),))
                P.mm(o_, M4a[:n, hd, 3, :n], Vb[:n, col:col + 64], False, True, (kM4[hd], kVb), (kbk(b),))
                if hd >= 14:
                    P.copy("act", Yv[:, b:16:2, :], bk(b)[:n, :].rearrange("p (a b) -> p a b", a=8), (kbk(b),), (kY,))
            for H in HD:
                if H["hd"] % 4 == 0:
                    pump()
                hd, j, hb, col = H["hd"], H["j"], H["hb"], H["col"]
                o_ = bk(2)[hb:hb + 64, j * 64:(j + 1) * 64]
                P.mm(o_, BW[:n, col:col + 64], Ua[:n, hd, :], True, False, (kBW, kUa[hd % 2]), (kbk(2),))
                P.mm(o_, KW[:n, col:col + 64], Vb[:n, col:col + 64], False, True, (kKW, kVb), (kbk(2),))
            P.tt("dve", Sf[:, :, :], Sf[:, :, :], wc[:, :].unsqueeze(2).to_broadcast([128, 8, 64]), ALU.mult, (kSf, kwc), (kSf,))
            P.tt("dve", Sf[:, :, :], Sf[:, :, :], bk(2)[:, :].rearrange("p (a b) -> p a b", a=8), ALU.add, (kSf, kbk(2)), (kSf,))
            P.copy("pool", Sb[:, :, :], Sf[:, :, :], (kSf,), (kSb,))

        def epilogue(ti):
            t0, n = s.tile_range(ti)
            if dr == 0:
                P.dma("sp", d["YF_" + n_][t0:t0 + n, :], Y[:n, :], (kY,), ())
            else:
                P.dma("sp", yf[:n, :], d["YF_" + n_][t0:t0 + n, :], (), (kyf,))
                P.dma("sp", gt[:n, :], d["G_" + n_][t0:t0 + n, :], (), (kgt,))
                P.tt("dve", Y[:n, :], Y[:n, :], yf[:n, :], ALU.add, (kY, kyf), (kY,))
                P.dma("sp", yf[:n, :], d["BF_" + n_][t0:t0 + n, :], (kY,), (kyf,))
                P.reduce("dve", sm[:n, 2, :], v3(Y[:n, :]), ALU.add, (kY,), (ksm,))
                P.tt("pool", tmp[:n, :], Y[:n, :], Y[:n, :], ALU.mult, (kY,), (ktmp,))
                P.reduce("dve", sm[:n, 3, :], v3(tmp[:n, :]), ALU.add, (ktmp,), (ksm,))
                P.ts("dve", sm[:n, 2, :], sm[:n, 2, :], 1.0 / 64, None, ALU.mult, None, (ksm,), (ksm,))
                P.ts("dve", sm[:n, 3, :], sm[:n, 3, :], 1.0 / 64, None, ALU.mult, None, (ksm,), (ksm,))
                P.tt("dve", sm[:n, 0, :], sm[:n, 2, :], sm[:n, 2, :], ALU.mult, (ksm,), (ksm,))
                P.tt("dve", sm[:n, 3, :], sm[:n, 3, :], sm[:n, 0, :], ALU.subtract, (ksm,), (ksm,))
                P.act(sm[:n, 3, :], sm[:n, 3, :], AF.Sqrt, (ksm,), (ksm,), bias=GN_EPS)
                P.recip("dve", sm[:n, 3, :], sm[:n, 3, :], (ksm,), (ksm,))
                bc = lambda q: sm[:n, q, :].unsqueeze(2).to_broadcast([n, 16, 64])
                P.tt("dve", v3(Y[:n, :]), v3(Y[:n, :]), bc(2), ALU.subtract, (kY, ksm), (kY,))
                P.tt("dve", v3(Y[:n, :]), v3(Y[:n, :]), bc(3), ALU.mult, (kY, ksm), (kY,))
                P.tt("pool", Y[:n, :], Y[:n, :], lg[:n, :], ALU.mult, (kY, klg), (kY,))
                P.tt("pool", Y[:n, :], Y[:n, :], lb[:n, :], ALU.add, (kY, klb), (kY,))
                P.tt("dve", Y[:n, :], Y[:n, :], bon[:n, :], ALU.add, (kY, kbon), (kY,))
                P.tt("dve", Y[:n, :], Y[:n, :], yf[:n, :], ALU.add, (kY, kyf), (kY,))
                P.tt("dve", ob[:n, :], Y[:n, :], gt[:n, :], ALU.mult, (kY, kgt), (kob,))
                if s.sliced:
                    P.dma("sp", d["OTOK_" + n_][t0:t0 + n, DA:D], ob[:n, :], (kob,), ())
                    return
                bi = tctr[0] % 4
                tctr[0] += 1
                ptb = P.bank(bi, BF16)
                kptb = "psb%d" % bi
                for j in range(8):
                    P.tr(ptb[:, j * 128:j * 128 + n], ob[:n, j * 128:(j + 1) * 128], self.ident_b[:n, :n],
                         (kob, self.kid[1]), (kptb,))
                P.copy("act", oT[:, :, :n], ptb.rearrange("p (a b) -> p a b", a=8)[:, :, :n], (kptb,), (koT,))
                P.dma("sp", d["OT_" + n_][8:16, :, t0:t0 + n].rearrange("c p t -> p c t"), oT[:, :, :n], (koT,), ())

        for _ in prep1(order[0]):
            pass
        for idx, ti in enumerate(order):
            prep2(ti)
            gen = prep1(order[idx + 1]) if idx + 1 < len(order) else iter(())
            units(ti, gen)
            for _ in gen:
                pass
            epilogue(ti)


    def stage3(self, s):
        P = self.P
        d = self.dram
        n_ = s.name
        L = s.L
        P.barrier()
        P.reset_arena(self.const_end)
        kth = [P.tile([128, L], BF16, "kth%d" % i) for i in range(2)]
        qth = [P.tile([128, L], BF16, "qth%d" % i) for i in range(2)]
        vh = [P.tile([128, s.ntile, 128], BF16, "vh%d" % i) for i in range(2)]
        bt = [P.tile([128, 5, NKEY], F32, "bias%d" % i) for i in range(2)]
        oTh = [P.tile([128, L], BF16, "oTh%d" % i) for i in range(2)]
        sc = [P.tile([128, NKEY], F32, "sc%d" % i) for i in range(2)]
        pr = [P.tile([128, NKEY], BF16, "pr%d" % i) for i in range(2)]
        pT = [P.tile([128, 6, 128], BF16, "pT%d" % i) for i in range(2)]
        ob = [P.tile([128, 128], BF16, "ob%d" % i) for i in range(2)]
        st = [P.tile([128, 4], F32, "st%d" % i) for i in range(4)]
        uc = 0
        for h in range(HA):
            kt, kkt = kth[h % 2]; qt, kqt = qth[h % 2]; v, kv = vh[h % 2]; b, kb = bt[h % 2]; oh, koh = oTh[h % 2]
            P.dma("sp", kt, d["KT_" + n_][h], (), (kkt,))
            P.dma("sp", qt, d["QT_" + n_][h], (), (kqt,))
            P.dma("sp", v[:NM, 0, :], d["V_" + n_][0:NM, h * 128:(h + 1) * 128], (), (kv,))
            for i0 in range(1, s.ntile, 16):
                i1 = min(s.ntile, i0 + 16)
                P.dma("sp", v[:, i0:i1, :],
                      d["V_" + n_][NM + (i0 - 1) * 128:NM + (i1 - 1) * 128, h * 128:(h + 1) * 128].rearrange("(i p) c -> p i c", p=128),
                      (), (kv,))
            P.dma("sp", b, d["bias_tab"][h].rearrange("c q k -> q c k"), (), (kb,))
            units = [("meta", 0)] + [("grid", rp) for rp in range(s.rows // 2)]
            for kind, rp in units:
                u = uc
                uc += 1
                s_, ks_ = sc[u % 2]; p_, kp_ = pr[u % 2]; pt_, kpt_ = pT[u % 2]; o_, ko_ = ob[u % 2]; st_, kst_ = st[u % 4]
                ba = (u % 2) * 2
                pa = P.bank(ba); pbk = P.bank(ba + 1)
                kpa = "psb%d" % ba; kpb = "psb%d" % (ba + 1)
                if kind == "meta":
                    nq = NM
                    q0 = 0
                    P.mm(pbk[:nq, 128:144], qt[:, 0:NM], kt[:, 0:NM], True, True, (kqt, kkt), (kpb,))
                    P.copy("dve", s_[:nq, 640:656], pbk[:nq, 128:144], (kpb,), (ks_,))
                    lo = 640
                    blocks = [(5, NM, 0)]
                else:
                    nq = 128
                    r = 2 * rp
                    ws = min(max(r - 4, 0), s.rows - WINR)
                    assert ws % 2 == 0
                    cls = (r - ws) // 2
                    q0 = NM + rp * 128
                    k0 = NM + ws * 64
                    P.mm(pa[:, :], qt[:, q0:q0 + 128], kt[:, k0:k0 + 512], True, True, (kqt, kkt), (kpa,))
                    P.mm(pbk[:, 0:128], qt[:, q0:q0 + 128], kt[:, k0 + 512:k0 + 640], True, True, (kqt, kkt), (kpb,))
                    P.mm(pbk[:, 128:144], qt[:, q0:q0 + 128], kt[:, 0:NM], True, True, (kqt, kkt), (kpb,))
                    P.tt("dve", s_[:, 0:512], pa[:, :], b[:, cls, 0:512], ALU.add, (kpa, kb), (ks_,))
                    P.tt("dve", s_[:, 512:656], pbk[:, 0:144], b[:, cls, 512:656], ALU.add, (kpb, kb), (ks_,))
                    lo = 0
                    blocks = [(j, 128, ws // 2 + 1 + j) for j in range(5)] + [(5, NM, 0)]
                P.op("dve", lambda e, o=st_[:nq, 0:1], i=s_[:nq, lo:656]: e.reduce_max(o, i, AX.X), (ks_,), (kst_,))
                P.ts("dve", st_[:nq, 1:2], st_[:nq, 0:1], -1.0, None, ALU.mult, None, (kst_,), (kst_,))
                P.act(p_[:nq, lo:656], s_[:nq, lo:656], AF.Exp, (ks_, kst_), (kp_, kst_), bias=st_[:nq, 1:2], accum=st_[:nq, 2:3])
                P.recip("dve", st_[:nq, 3:4], st_[:nq, 2:3], (kst_,), (kst_,))
                tb = 4 + (u % 2)
                ptb = P.bank(tb, BF16)
                kptb = "psb%d" % tb
                for (j, nk, vt) in blocks:
                    P.tr(ptb[:nk, j * 128:j * 128 + nq], p_[:nq, j * 128:j * 128 + nk], self.ident_b[:nq, :nq],
                         (kp_, self.kid[1]), (kptb,))
                j0 = blocks[0][0]
                if kind == "meta":
                    P.copy("act", pt_[:NM, 5, :nq], ptb[:NM, 640:640 + nq], (kptb,), (kpt_,))
                else:
                    P.copy("act", pt_[:, :, :], ptb[:, 0:768].rearrange("p (a b) -> p a b", a=6), (kptb,), (kpt_,))
                po = P.bank(6)
                for bi, (j, nk, vt) in enumerate(blocks):
                    P.mm(po[:nq, 0:128], pt_[:nk, j, :nq], v[:nk, vt, :], bi == 0, bi == len(blocks) - 1, (kpt_, kv), ("psb6",))
                P.act(o_[:nq, :], po[:nq, 0:128], AF.Copy, ("psb6", kst_), (ko_,), scale=st_[:nq, 3:4])
                if s.sliced:
                    P.dma("sp", d["OTOK_" + n_][q0:q0 + nq, h * 128:(h + 1) * 128], o_[:nq, :], (ko_,), ())
                    continue
                pot = P.bank(7, BF16)
                P.tr(pot[:, 0:nq], o_[:nq, :], self.ident_b[:nq, :nq], (ko_, self.kid[1]), ("psb7",))
                P.copy("dve", oh[:, q0:q0 + nq], pot[:, 0:nq], ("psb7",), (koh,))
            if not s.sliced:
                P.dma("sp", d["OT_" + n_][h], oh, (koh,), ())

    def stage4(self, s):
        P = self.P
        d = self.dram
        n_ = s.name
        P.barrier()
        P.reset_arena(self.const_end)
        g1, kg = self.bcast_load(d["ln1_g"], D, "g1")
        b1, kb = self.bcast_load(d["ln1_b"], D, "b1")
        tmp = self.ln_tmp()
        wo, kwo = P.tile([128, 16, D], BF16, "wo")
        for c0 in range(0, D, 512):
            self.load_w(wo[:, :, c0:c0 + 512], kwo, d["w_out"], D, c0, 512)
        oT = [P.tile([128, 16, 128], BF16, "oT%d" % i) for i in range(2)]
        h0 = [P.tile([128, D], F32, "h0%d" % i) for i in range(2)]
        h1 = [P.tile([128, D], F32, "h1%d" % i) for i in range(2)]
        h1T = [P.tile([128, 16, 128], BF16, "h1T%d" % i) for i in range(2)]
        ctr = [0]
        for ti in range(s.ntile):
            t0, n = s.tile_range(ti)
            o_, ko_ = oT[ti % 2]; x, kx = h0[ti % 2]; y, ky = h1[ti % 2]; yt, kyt = h1T[ti % 2]
            P.dma("sp", o_[:, :, :n], d["OT_" + n_][:, :, t0:t0 + n].rearrange("c p t -> p c t"), (), (ko_,))
            P.dma("sp", x[:n, :], d["H0_" + n_][t0:t0 + n, :], (), (kx,))
            for hf in range(4):
                pb = P.bank(hf); kpb = "psb%d" % hf
                for kc in range(16):
                    P.mm(pb[:n, :], o_[:, kc, :n], wo[:, kc, hf * 512:(hf + 1) * 512], kc == 0, kc == 15, (ko_, kwo), (kpb,))
                P.stt("dve", x[:n, hf * 512:(hf + 1) * 512], x[:n, hf * 512:(hf + 1) * 512], ALPHA, pb[:n, :], ALU.mult, ALU.add,
                      (kx, kpb), (kx,))
            self.layer_norm(x, kx, n, (g1, b1), (kg, kb), y, ky, LN_EPS, tmp)
            P.dma("sp", d["H1_" + n_][t0:t0 + n, :], y[:n, :], (ky,), ())
            self.transpose_to_fm(y, ky, n, yt, kyt, 0, (4, 5, 6, 7), ctr)
            P.dma("sp", d["H1T_" + n_][:, :, t0:t0 + n].rearrange("c p t -> p c t"), yt[:, :, :n], (kyt,), ())

    def stage4_sliced(self, s):
        P = self.P
        d = self.dram
        n_ = s.name
        NG = s.ng
        P.barrier()
        P.reset_arena(self.const_end)
        g0, kg0 = self.bcast_load(d["emb_ln_g"], D, "g0")
        b0, kb0 = self.bcast_load(d["emb_ln_b"], D, "b0")
        g1, kg1 = self.bcast_load(d["ln1_g"], D, "g1")
        b1, kb1 = self.bcast_load(d["ln1_b"], D, "b1")
        tmp = self.ln_tmp()
        wo, kwo = P.tile([128, 16, D], BF16, "wo")
        for c0 in range(0, D, 512):
            self.load_w(wo[:, :, c0:c0 + 512], kwo, d["w_out"], D, c0, 512)
        ms, kms = P.tile([128, NG + 1], F32, "msel")
        P.dma("sp", ms, d["msel_" + n_], (), (kms,))
        mI, kmI = P.tile([128, NG, 128], BF16, "mI")
        for c in range(NG):
            P.ts("dve", mI[:, c, :], self.ident_f, ms[:, c:c + 1], None, ALU.mult, None, (self.kid[0], kms), (kmI,))
        shf, kshf = P.tile([128, 128], F32, "shf")
        P.dma("sp", shf[:NM, :], d["c_shift"], (), (kshf,))
        shI, kshI = P.tile([128, 128], BF16, "shI")
        P.ts("dve", shI[:NM, :], shf[:NM, :], ms[:NM, 0:1], None, ALU.mult, None, (kshf, kms), (kshI,))
        ot = [P.tile([128, D], BF16, "ot%d" % i) for i in range(NG)]
        oTs, koTs = P.tile([128, 16, 128], BF16, "oTs")
        x, kx = P.tile([128, D], F32, "x")
        h0, kh0 = P.tile([128, D], F32, "h0")
        y, ky = P.tile([128, D], F32, "h1")
        yt, kyt = P.tile([128, 16, 128], BF16, "h1T")
        ctr = [0]
        for sl in range(10):
            cands = []
            for c in range(NG):
                ti = 8 * c + sl
                if 0 <= ti < s.ntile:
                    cands.append((c, ti))
            for c, ti in cands:
                t0, n = s.tile_range(ti)
                P.dma("sp", ot[c][0][:n, :], d["OTOK_" + n_][t0:t0 + n, :], (), (ot[c][1],))
            for fc in range(16):
                bi = fc // 4
                pb = P.bank(bi); kpb = "psb%d" % bi
                for ci, (c, ti) in enumerate(cands):
                    t0, n = s.tile_range(ti)
                    rhs = shI[:NM, :] if ti == 0 else mI[:, c, :]
                    krhs = kshI if ti == 0 else kmI
                    P.mm(pb[:, (fc % 4) * 128:(fc % 4 + 1) * 128], ot[c][0][:n, fc * 128:(fc + 1) * 128], rhs,
                         ci == 0, ci == len(cands) - 1, (ot[c][1], krhs), (kpb,))
                if fc % 4 == 3:
                    P.copy("act" if bi % 2 == 0 else "dve", oTs[:, fc - 3:fc + 1, :], pb.rearrange("p (a b) -> p a b", a=4),
                           (kpb,), (koTs,))
            P.dma("sp", x, d["xext_" + n_][sl * 128:(sl + 1) * 128, :], (), (kx,))
            self.layer_norm(x, kx, 128, (g0, b0), (kg0, kb0), h0, kh0, LN_EPS, tmp)
            for hf in range(4):
                pb = P.bank(4 + hf); kpb = "psb%d" % (4 + hf)
                for kc in range(16):
                    P.mm(pb[:, :], oTs[:, kc, :], wo[:, kc, hf * 512:(hf + 1) * 512], kc == 0, kc == 15, (koTs, kwo), (kpb,))
                P.stt("dve", h0[:, hf * 512:(hf + 1) * 512], h0[:, hf * 512:(hf + 1) * 512], ALPHA, pb[:, :], ALU.mult, ALU.add,
                      (kh0, kpb), (kh0,))
            self.layer_norm(h0, kh0, 128, (g1, b1), (kg1, kb1), y, ky, LN_EPS, tmp)
            P.dma("sp", d["H1s_" + n_][sl * 128:(sl + 1) * 128, :], y, (ky,), ())
            self.transpose_to_fm(y, ky, 128, yt, kyt, 0, (0, 1, 2, 3), ctr)
            P.dma("sp", d["H1Ts_" + n_][:, :, sl * 128:(sl + 1) * 128].rearrange("c p t -> p c t"), yt, (kyt,), ())

    def stage5(self, s):
        P = self.P
        d = self.dram
        n_ = s.name
        P.barrier()
        P.reset_arena(self.const_end)
        g2_, kg = self.bcast_load(d["ln2_g"], D, "g2")
        b2_, kb = self.bcast_load(d["ln2_b"], D, "b2")
        tmp = self.ln_tmp()
        cw, kcw = P.tile([128, 44, 4], F32, "convw")
        P.dma("sp", cw, d["ffn_conv_l"], (), (kcw,))
        NB = 1024
        hT, khT = P.tile([128, 16, NB + 2], BF16, "hT")
        acc, kacc0 = P.tile([128, 8, D], F32, "acc")
        kacc = [kacc0 + "/%d" % i for i in range(8)]
        GC = 2
        w1 = [P.tile([128, 16, 2 * GC * 128], BF16, "w1_%d" % i) for i in range(2)]
        w2 = [P.tile([128, GC, D], BF16, "w2_%d" % i) for i in range(2)]
        gT = [P.tile([128, 514], F32, "gT%d" % i) for i in range(2)]
        t1 = [P.tile([128, 512], F32, "t1%d" % i) for i in range(2)]
        aT = [P.tile([128, GC, NB], BF16, "aT%d" % i) for i in range(2)]
        xr = [P.tile([128, D], F32, "xr%d" % i) for i in range(1)]
        yo = [P.tile([128, D], F32, "yo%d" % i) for i in range(1)]
        gi = 0
        hc = 0
        if s.sliced:
            h1t_d = d["H1Ts_" + n_]; h1_d = d["H1s_" + n_]; Ltot = 1280
            blocks = [(128, 0)]
            ms, kms = P.tile([128, s.ng + 1], F32, "msel")
            P.dma("sp", ms, d["msel_" + n_], (), (kms,))
        else:
            h1t_d = d["H1T_" + n_]; h1_d = d["H1_" + n_]; Ltot = s.L
            blocks = [(NM + b0, b0) for b0 in range(0, s.T, NB)]
        for ts0, orow in blocks:
            ntl = NB // 128
            lo = ts0 - 1
            hi = min(Ltot, ts0 + NB + 1)
            P.dma("sp", hT[:, :, 0:hi - lo], h1t_d[:, :, lo:hi].rearrange("c p t -> p c t"), (), (khT,))
            if hi - lo < NB + 2:
                P.memset("pool", hT[:, :, NB + 1:NB + 2], 0.0, (khT,))
            if s.sliced:
                P.ts("pool", hT[:, :, NB + 1:NB + 2], hT[:, :, NB + 1:NB + 2], ms[:, s.ng:s.ng + 1], None, ALU.mult, None,
                     (khT, kms), (khT,))
            for g in range(44 // GC):
                w1_, kw1 = w1[gi % 2]; w2_, kw2 = w2[gi % 2]; a_, ka_ = aT[gi % 2]
                gi += 1
                c0 = g * GC * 128
                self.load_w(w1_[:, :, 0:GC * 128], kw1, d["ffn_w_in"], D, c0, GC * 128)
                self.load_w(w1_[:, :, GC * 128:2 * GC * 128], kw1, d["ffn_w_in"], D, DFF + c0, GC * 128)
                P.dma("pool", w2_, d["ffn_w_out"][c0:c0 + GC * 128, :].rearrange("(c p) n -> p c n", p=128), (), (kw2,))
                for cl in range(GC):
                    fc = g * GC + cl
                    for hh in range(2):
                        g_, kg_ = gT[hc % 2]; t_, kt_ = t1[hc % 2]
                        hc += 1
                        bg = 0 if hh == 0 else 3
                        pg = P.bank(bg); kpg = "psb%d" % bg
                        ph = P.bank(1); pu = P.bank(2)
                        cb = hh * 512
                        for kc in range(16):
                            P.mm(pg[:, :], w1_[:, kc, cl * 128:(cl + 1) * 128], hT[:, kc, cb:cb + 512], kc == 0, kc == 15,
                                 (kw1, khT), (kpg,))
                        for kc in range(16):
                            P.mm(ph[:, 0:2], w1_[:, kc, cl * 128:(cl + 1) * 128], hT[:, kc, cb + 512:cb + 514], kc == 0, kc == 15,
                                 (kw1, khT), ("psb1",))
                        for kc in range(16):
                            P.mm(pu[:, :], w1_[:, kc, (GC + cl) * 128:(GC + cl + 1) * 128], hT[:, kc, cb + 1:cb + 513],
                                 kc == 0, kc == 15, (kw1, khT), ("psb2",))
                        P.copy("act", g_[:, 0:512], pg[:, :], (kpg,), (kg_,))
                        P.copy("act", g_[:, 512:514], ph[:, 0:2], ("psb1",), (kg_,))
                        P.act(t_[:, :], g_[:, 1:513], AF.Identity, (kg_, kcw), (kt_,), bias=cw[:, fc, 3:4], scale=cw[:, fc, 1:2])
                        P.stt("dve", t_[:, :], g_[:, 0:512], cw[:, fc, 0:1], t_[:, :], ALU.mult, ALU.add, (kg_, kcw, kt_), (kt_,))
                        P.stt("dve", t_[:, :], g_[:, 2:514], cw[:, fc, 2:3], t_[:, :], ALU.mult, ALU.add, (kg_, kcw, kt_), (kt_,))
                        P.act(t_[:, :], t_[:, :], AF.Gelu, (kt_,), (kt_,))
                        P.tt("dve", a_[:, cl, cb:cb + 512], t_[:, :], pu[:, :], ALU.mult, (kt_, "psb2"), (ka_,))
                for tl in range(ntl):
                    for hf in range(4):
                        pb = P.bank(4 + hf); kpb = "psb%d" % (4 + hf)
                        for cl in range(GC):
                            P.mm(pb[:, :], a_[:, cl, tl * 128:(tl + 1) * 128], w2_[:, cl, hf * 512:(hf + 1) * 512],
                                 cl == 0, cl == GC - 1, (ka_, kw2), (kpb,))
                        dst = acc[:, tl, hf * 512:(hf + 1) * 512]
                        if g == 0:
                            P.copy("dve" if hf % 2 else "act", dst, pb[:, :], (kpb,), (kacc[tl],))
                        else:
                            P.tt("dve", dst, dst, pb[:, :], ALU.add, (kpb, kacc[tl]), (kacc[tl],))
            for tl in range(ntl):
                x, kx = xr[0]; y, ky = yo[0]
                tq = ts0 + tl * 128
                P.dma("sp", x, h1_d[tq:tq + 128, :], (), (kx,))
                P.stt("dve", x, x, ALPHA, acc[:, tl, :], ALU.mult, ALU.add, (kx, kacc[tl]), (kx,))
                self.layer_norm(x, kx, 128, (g2_, b2_), (kg, kb), y, ky, LN_EPS, tmp)
                P.dma("sp", d["y_" + n_][orow + tl * 128:orow + tl * 128 + 128, :], y, (ky,), ())

    def build(self):
        self.setup_consts()
        for s in self.seqs:
            if 1 in self.run:
                self.stage1(s)
            if 2 in self.run:
                self.stage2(s)
            if 3 in self.run:
                self.stage3(s)
            if 4 in self.run:
                if s.sliced:
                    self.stage4_sliced(s)
                else:
                    self.stage4(s)
            if 5 in self.run:
                self.stage5(s)
        self.P.finish()
        return self.nc


def _bias_table(rpb):
    rpb = np.asarray(rpb, np.float32).reshape(HA, 15, 31)
    tab = np.full((HA, 5, 128, NKEY), NEG, np.float32)
    qc = np.arange(64)
    c0 = np.clip(qc - 8, 0, 48)
    kc = np.arange(64)
    colmask = (kc[None, :] >= c0[:, None]) & (kc[None, :] < c0[:, None] + 16)
    dc = np.clip(kc[None, :] - qc[:, None], -15, 15) + 15
    rel = {0: (0, 0), 1: (0, 0), 2: (0, 1), 3: (2, 2), 4: (2, 2)}
    for c in range(5):
        for qr in range(2):
            r_rel = 2 * c + qr
            r0 = rel[c][qr]
            for j in range(8):
                krow = r0 + j
                dr = krow - r_rel + 7
                blk = rpb[:, dr][:, dc]
                blk = np.where(colmask[None], blk, NEG)
                tab[:, c, qr * 64:(qr + 1) * 64, krow * 64:(krow + 1) * 64] = blk
        tab[:, c, :, WINR * 64:] = 0.0
    return tab


def _consts():
    ident = np.eye(128, dtype=np.float32)
    s = np.arange(128)[:, None]
    t = np.arange(128)[None, :]
    masks = np.stack([(s <= t), (s >= t), (s < t), (s <= t), (s > t), (s >= t)]).astype(np.float32)
    return ident, masks


def _common_inputs(inp):
    ident, masks = _consts()
    f = lambda a: np.ascontiguousarray(np.asarray(a, np.float32))
    m = {
        "meta_tokens": f(inp["meta_tokens"]),
        "emb_ln_g": f(inp["emb_ln_g"]), "emb_ln_b": f(inp["emb_ln_b"]),
        "ln1_g": f(inp["ln1_g"][0]), "ln1_b": f(inp["ln1_b"][0]),
        "ln2_g": f(inp["ln2_g"][0]), "ln2_b": f(inp["ln2_b"][0]),
        "w_in": f(inp["w_in"][0]),
        "bias_tab": _bias_table(inp["attn_rpb"][0]),
        "rwkv_mu": f(inp["rwkv_mu"][0]), "rwkv_w0": f(inp["rwkv_w0"][0]), "rwkv_w2": f(inp["rwkv_w2"][0]),
        "rwkv_a0": f(inp["rwkv_a0"][0]), "rwkv_a2": f(inp["rwkv_a2"][0]), "rwkv_g2": f(inp["rwkv_g2"][0]),
        "rwkv_k_k": f(inp["rwkv_k_k"][0]), "rwkv_k_a": f(inp["rwkv_k_a"][0]),
        "rwkv_r_k": f(inp["rwkv_r_k"][0]).reshape(DR),
        "rwkv_lnx_g": f(inp["rwkv_lnx_g"][0]), "rwkv_lnx_b": f(inp["rwkv_lnx_b"][0]),
        "w_out": f(inp["w_out"][0]), "ffn_w_in": f(inp["ffn_w_in"][0]),
        "ffn_conv_l": np.ascontiguousarray(np.concatenate([f(inp["ffn_conv_w"][0]), f(inp["ffn_conv_b"][0])[None]], 0)
                                           .reshape(4, 44, 128).transpose(2, 1, 0)),
        "ffn_w_out": f(inp["ffn_w_out"][0]),
        "c_ident": ident, "c_masks": masks, "c_zero": np.zeros((1, NSH), np.float32),
        "c_shift": np.eye(NM, 128, 128 - NM, dtype=np.float32),
    }
    return m


def slice_inputs(x, meta, c, ng):
    x = np.asarray(x, np.float32)
    prev = x[1024 * c - 128:1024 * c] if c > 0 else np.concatenate([np.zeros((128 - NM, D), np.float32), np.asarray(meta, np.float32)], 0)
    nxt = x[1024 * c + 1024:1024 * c + 1152] if c < ng - 1 else np.zeros((128, D), np.float32)
    xext = np.ascontiguousarray(np.concatenate([prev, x[1024 * c:1024 * c + 1024], nxt], 0))
    msel = np.zeros((128, ng + 1), np.float32)
    msel[:, c] = 1.0
    msel[:, ng] = 1.0 if c < ng - 1 else 0.0
    return xext, msel


def kernel(**inputs):
    xp = np.asarray(inputs["x_prompt"], np.float32)
    xs = np.asarray(inputs["x_sample"], np.float32)
    b = Builder([xs.shape[1], xp.shape[1]], sliced=[False, True])
    nc = b.build()
    common = _common_inputs(inputs)
    in_maps = []
    for c in range(8):
        m = dict(common)
        m["x_s0"] = np.ascontiguousarray(xs[c])
        m["x_s1"] = np.ascontiguousarray(xp[0])
        m["xext_s1"], m["msel_s1"] = slice_inputs(xp[0], inputs["meta_tokens"], c, 8)
        in_maps.append(m)
    res = run_bass_kernel_spmd(nc, in_maps, core_ids=list(range(8)))
    y_s = np.stack([res.results[c]["y_s0"] for c in range(8)], axis=0)
    y_p = np.concatenate([res.results[c]["y_s1"] for c in range(8)], axis=0)[None]
    return (y_p.astype(np.float32), y_s.astype(np.float32))
```

```python
import numpy as np
import ml_dtypes
import concourse.bass as bass
import concourse.mybir as mybir
from concourse.bass_utils import run_bass_kernel_spmd

F32 = mybir.dt.float32
BF16 = mybir.dt.bfloat16
U8 = mybir.dt.uint8
ALU = mybir.AluOpType
AF = mybir.ActivationFunctionType
AX = mybir.AxisListType

D = 2048
NM = 16
DA = 1024
HA = 8
DR = 1024
HR = 16
NSH = 3 * DR + 192
NIN = 3 * DA + NSH + 256
DFF = 5632
ALPHA = 2.0 ** 0.25
LN_EPS = 1e-5
GN_EPS = 64e-5
CDEC = -float(np.exp(-0.5))
NEG = -30000.0
WINR = 10
NKEY = WINR * 64 + NM


class Prog:
    ENG = ("sp", "act", "dve", "pool", "pe")

    def __init__(self, nc, n_dma_sems=40):
        self.nc = nc
        self.q = {e: [] for e in self.ENG}
        self.cnt = {e: 0 for e in self.ENG}
        self.known = {e: {} for e in self.ENG}
        self.last_w = {}
        self.readers = {}
        self.semh = {}
        for e in ("act", "dve", "pool", "pe"):
            self.semh[e] = nc.alloc_semaphore("sem_" + e)
        self.ndma = n_dma_sems
        for i in range(n_dma_sems):
            self.semh[("dma", i)] = nc.alloc_semaphore("sem_dma%d" % i)
        self.dma_tot = [0] * n_dma_sems
        self.rr = 0
        self.arena = nc.alloc_sbuf_tensor("arena", [128, 204 * 1024], U8)
        self.off = 0
        self.uid = 0
        self.psum = [nc.alloc_psum_tensor("psb%d" % i, [128, 512], F32) for i in range(8)]

    def reset_arena(self, off=0):
        self.off = off

    def tile(self, shape, dtype, name="t"):
        esz = 4 if dtype == F32 else 2
        n = int(np.prod(shape[1:]))
        nbytes = (n * esz + 31) // 32 * 32
        assert self.off + nbytes <= 204 * 1024, ("sbuf overflow", name, self.off, nbytes)
        ap = self.arena[:, self.off:self.off + n * esz].bitcast(dtype)
        self.off += nbytes
        if len(shape) == 3:
            ap = ap.rearrange("p (a b) -> p a b", a=shape[1])
        elif len(shape) == 4:
            ap = ap.rearrange("p (a b c) -> p a b c", a=shape[1], b=shape[2])
        self.uid += 1
        return ap, "%s#%d" % (name, self.uid)

    def bank(self, i, dtype=F32):
        ap = self.psum[i][:, :]
        if dtype != F32:
            ap = ap.bitcast(dtype)
        return ap

    def _waits(self, eng, reads, writes):
        need = {}

        def add(ev):
            if ev is None:
                return
            s, v = ev
            if need.get(s, 0) < v:
                need[s] = v
        for k in reads:
            add(self.last_w.get(k))
        for k in writes:
            add(self.last_w.get(k))
            for s, v in self.readers.get(k, {}).items():
                add((s, v))
        out = []
        for s, v in need.items():
            if s == "pe" and eng == "pe":
                continue
            if self.known[eng].get(s, 0) >= v:
                continue
            self.known[eng][s] = v
            out.append((s, v))
        return out

    def _record(self, ev, reads, writes):
        s, v = ev
        for k in reads:
            d = self.readers.setdefault(k, {})
            if d.get(s, 0) < v:
                d[s] = v
        for k in writes:
            self.last_w[k] = ev
            self.readers[k] = {}

    @staticmethod
    def _px(reads, writes):
        pr = tuple(k for k in reads if k.startswith("psb"))
        if pr:
            reads = tuple(k for k in reads if not k.startswith("psb"))
            writes = tuple(writes) + pr
        return reads, writes

    def op(self, eng, fn, reads=(), writes=()):
        reads, writes = self._px(reads, writes)
        waits = self._waits(eng, reads, writes)
        self.cnt[eng] += 1
        ev = (eng, self.cnt[eng])
        self.q[eng].append((waits, fn, eng, 1))
        self._record(ev, reads, writes)

    def dma(self, eng, out, in_, reads=(), writes=()):
        i = self.rr
        self.rr = (self.rr + 1) % self.ndma
        s = ("dma", i)
        waits = self._waits(eng, reads, writes)
        if self.dma_tot[i] > 0 and self.known[eng].get(s, 0) < self.dma_tot[i]:
            self.known[eng][s] = self.dma_tot[i]
            waits.append((s, self.dma_tot[i]))
        self.dma_tot[i] += 16
        ev = (s, self.dma_tot[i])
        self.q[eng].append((waits, lambda e, o=out, a=in_: e.dma_start(out=o, in_=a), s, 16))
        self._record(ev, reads, writes)

    def raw(self, eng, fn, reads=(), writes=()):
        waits = self._waits(eng, reads, writes)
        self.q[eng].append((waits, fn, None, -1))

    def dma_dyn(self, eng, mk, reads=(), writes=()):
        i = self.rr
        self.rr = (self.rr + 1) % self.ndma
        s = ("dma", i)
        waits = self._waits(eng, reads, writes)
        if self.dma_tot[i] > 0 and self.known[eng].get(s, 0) < self.dma_tot[i]:
            self.known[eng][s] = self.dma_tot[i]
            waits.append((s, self.dma_tot[i]))
        self.dma_tot[i] += 16
        ev = (s, self.dma_tot[i])

        def fn(e, mk=mk):
            o, a = mk()
            return e.dma_start(out=o, in_=a)
        self.q[eng].append((waits, fn, s, 16))
        self._record(ev, reads, writes)

    def barrier(self):
        for e in self.ENG:
            waits = []
            for s in ("act", "dve", "pool", "pe"):
                if s != e and self.cnt[s] > self.known[e].get(s, 0):
                    self.known[e][s] = self.cnt[s]
                    waits.append((s, self.cnt[s]))
            for i in range(self.ndma):
                s = ("dma", i)
                if self.dma_tot[i] > self.known[e].get(s, 0):
                    self.known[e][s] = self.dma_tot[i]
                    waits.append((s, self.dma_tot[i]))
            if waits:
                self.q[e].append((waits, None, None, 0))

    def finish(self):
        self.barrier()
        nc = self.nc
        semh = self.semh
        q = self.q
        with nc.Block() as block:
            def mk(name):
                def body(e):
                    for waits, fn, s, amt in q[name]:
                        for ws, wv in waits:
                            e.wait_ge(semh[ws], wv)
                        if fn is not None:
                            if amt < 0:
                                fn(e)
                            else:
                                fn(e).then_inc(semh[s], amt)
                return body
            block.sync(mk("sp"))
            block.scalar(mk("act"))
            block.vector(mk("dve"))
            block.gpsimd(mk("pool"))
            block.tensor(mk("pe"))

    def mm(self, out, lhsT, rhs, start, stop, reads, writes):
        self.op("pe", lambda e: e.matmul(out, lhsT, rhs, start=start, stop=stop), reads, writes)

    def tr(self, out, in_, ident, reads, writes):
        self.op("pe", lambda e: e.transpose(out, in_, ident), reads, writes)

    def act(self, out, in_, func, reads, writes, bias=None, scale=None, accum=None):
        kw = {}
        if bias is not None:
            kw["bias"] = bias
        if scale is not None:
            kw["scale"] = scale
        if accum is not None:
            kw["accum_out"] = accum
        self.op("act", lambda e: e.activation(out, in_, func, **kw), reads, writes)

    def tt(self, eng, out, a, b, op, reads, writes):
        self.op(eng, lambda e: e.tensor_tensor(out, a, b, op), reads, writes)

    def ts(self, eng, out, a, s1, s2, op0, op1, reads, writes):
        if s2 is None:
            self.op(eng, lambda e: e.tensor_scalar(out, a, s1, None, op0), reads, writes)
        else:
            self.op(eng, lambda e: e.tensor_scalar(out, a, s1, s2, op0, op1), reads, writes)

    def stt(self, eng, out, a, sc, b, op0, op1, reads, writes):
        self.op(eng, lambda e: e.scalar_tensor_tensor(out, a, sc, b, op0, op1), reads, writes)

    def copy(self, eng, out, in_, reads, writes):
        if eng == "act":
            self.op("act", lambda e: e.copy(out, in_), reads, writes)
        else:
            self.op(eng, lambda e: e.tensor_copy(out, in_), reads, writes)

    def reduce(self, eng, out, in_, op, reads, writes):
        self.op(eng, lambda e: e.tensor_reduce(out, in_, AX.X, op), reads, writes)

    def recip(self, eng, out, in_, reads, writes):
        self.op(eng, lambda e: e.reciprocal(out, in_), reads, writes)

    def memset(self, eng, ap, val, writes):
        self.op(eng, lambda e: e.memset(ap, val), (), writes)


class Seq:
    def __init__(self, name, T, sliced=False):
        self.name = name
        self.T = T
        self.sliced = sliced
        self.ng = T // 1024
        self.L = T + NM
        self.rows = T // 64
        self.ntile = 1 + T // 128

    def tile_range(self, i):
        if i == 0:
            return 0, NM
        return NM + (i - 1) * 128, 128


class Builder:
    def __init__(self, seq_T, debug=False, stages=5, run=(1, 2, 3, 4, 5), sliced=None):
        self.debug = debug
        self.run = run
        self.stages = stages
        nc = bass.Bass("TRN2", target_bir_lowering=False)
        self.nc = nc
        self.P = Prog(nc)
        sliced = sliced or [False] * len(seq_T)
        self.seqs = [Seq("s%d" % i, T, sl) for i, (T, sl) in enumerate(zip(seq_T, sliced))]
        self.dram = {}
        self._declare_io()

    def din(self, name, shape, dtype=F32):
        ap = self.nc.dram_tensor(name, list(shape), dtype, kind="ExternalInput").ap()
        self.dram[name] = ap
        return ap

    def dscr(self, name, shape, dtype=F32):
        kind = "ExternalOutput" if self.debug else "Internal"
        ap = self.nc.dram_tensor(name, list(shape), dtype, kind=kind).ap()
        self.dram[name] = ap
        return ap

    def _declare_io(self):
        for s in self.seqs:
            self.din("x_" + s.name, [s.T, D])
            self.nc_out = None
        for s in self.seqs:
            ap = self.nc.dram_tensor("y_" + s.name, [1024 if s.sliced else s.T, D], F32, kind="ExternalOutput").ap()
            self.dram["y_" + s.name] = ap
            if s.sliced:
                self.din("xext_" + s.name, [1280, D])
                self.din("msel_" + s.name, [128, s.ng + 1])
        self.din("c_shift", [NM, 128])
        self.din("meta_tokens", [NM, D])
        for nm in ("emb_ln_g", "emb_ln_b", "ln1_g", "ln1_b", "ln2_g", "ln2_b"):
            self.din(nm, [D])
        self.din("w_in", [D, NIN])
        self.din("bias_tab", [HA, 5, 128, NKEY])
        self.din("rwkv_mu", [2, NSH])
        self.din("rwkv_w0", [2, DR])
        self.din("rwkv_w2", [2, 96, DR])
        self.din("rwkv_a0", [2, DR])
        self.din("rwkv_a2", [2, 96, DR])
        self.din("rwkv_g2", [256, DR])
        for nm in ("rwkv_k_k", "rwkv_k_a", "rwkv_r_k", "rwkv_lnx_g", "rwkv_lnx_b"):
            self.din(nm, [DR])
        self.din("w_out", [D, D])
        self.din("ffn_w_in", [D, 2 * DFF])
        self.din("ffn_conv_l", [128, 44, 4])
        self.din("ffn_w_out", [DFF, D])
        self.din("c_ident", [128, 128])
        self.din("c_masks", [6, 128, 128])
        self.din("c_zero", [1, NSH])
        for s in self.seqs:
            n = s.name
            self.dscr("H0_" + n, [s.L, D])
            self.dscr("QT_" + n, [HA, 128, s.L], BF16)
            self.dscr("KT_" + n, [HA, 128, s.L], BF16)
            self.dscr("V_" + n, [s.L, DA], BF16)
            self.dscr("P_" + n, [s.L, NSH])
            self.dscr("G_" + n, [s.L, DR])
            self.dscr("YF_" + n, [s.L, DR])
            self.dscr("BF_" + n, [s.L, DR])
            self.dscr("OT_" + n, [16, 128, s.L], BF16)
            self.dscr("H1_" + n, [s.L, D])
            self.dscr("H1T_" + n, [16, 128, s.L], BF16)
            if s.sliced:
                self.dscr("OTOK_" + n, [s.L, D], BF16)
                self.dscr("H1s_" + n, [1280, D])
                self.dscr("H1Ts_" + n, [16, 128, 1280], BF16)

    def setup_consts(self):
        P = self.P
        d = self.dram
        self.ident_f, k1 = P.tile([128, 128], F32, "identf")
        P.dma("sp", self.ident_f, d["c_ident"], (), (k1,))
        self.ident_b, k2 = P.tile([128, 128], BF16, "identb")
        P.copy("dve", self.ident_b, self.ident_f, (k1,), (k2,))
        self.kid = (k1, k2)
        self.const_end = P.off

    def bcast_load(self, name_ap, n, nm):
        P = self.P
        t, k = P.tile([128, n], F32, nm)
        P.dma("sp", t, name_ap.partition_broadcast(128), (), (k,))
        return t, k

    def layer_norm(self, x, kx, n, gb, kgb, out, kout, eps, tmp):
        P = self.P
        (st, kst), (mv, kmv), (rs, krs) = tmp
        g, b = gb
        for c in range(4):
            P.op("dve", lambda e, c=c: e.bn_stats(st[:n, c, :], x[:n, c * 512:(c + 1) * 512]), (kx,), (kst,))
        P.op("dve", lambda e: e.bn_aggr(mv[:n, :], st[:n].rearrange("p a b -> p (a b)")), (kst,), (kmv,))
        P.act(rs[:n, :], mv[:n, 1:2], AF.Sqrt, (kmv,), (krs,), bias=eps)
        P.op("dve", lambda e: e.reciprocal(rs[:n, :], rs[:n, :]), (krs,), (krs,))
        P.ts("dve", x[:n, :], x[:n, :], mv[:n, 0:1], rs[:n, 0:1], ALU.subtract, ALU.mult, (kx, kmv, krs), (kx,))
        P.tt("pool", x[:n, :], x[:n, :], g[:n, :], ALU.mult, (kx,) + kgb, (kx,))
        P.tt("dve", out[:n, :], x[:n, :], b[:n, :], ALU.add, (kx,) + kgb, (kout,))

    def ln_tmp(self):
        P = self.P
        return (P.tile([128, 4, 6], F32, "bnst"), P.tile([128, 2], F32, "bnmv"), P.tile([128, 1], F32, "rstd"))

    def transpose_to_fm(self, src, ksrc, n, dst, kdst, col0, banks, ctr):
        P = self.P
        for g4 in range(4):
            bi = banks[(ctr[0]) % len(banks)]
            ctr[0] += 1
            pb = P.bank(bi)
            kb = "psb%d" % bi
            for j in range(4):
                kc = g4 * 4 + j
                P.tr(pb[:, j * 128:j * 128 + n], src[:n, kc * 128:(kc + 1) * 128], self.ident_f[:n, :n],
                     (ksrc, self.kid[0]), (kb,))
            eng = "act" if g4 % 2 == 0 else "dve"
            P.copy(eng, dst[:, g4 * 4:g4 * 4 + 4, col0:col0 + n],
                   pb.rearrange("p (a b) -> p a b", a=4)[:, :, :n], (kb,), (kdst,))

    def load_w(self, wt, kw, src, rows, col0, ncols):
        P = self.P
        nk = rows // 128
        for k0 in range(0, nk, 4):
            k1 = min(nk, k0 + 4)
            P.dma("pool", wt[:, k0:k1, :ncols],
                  src[k0 * 128:k1 * 128, col0:col0 + ncols].rearrange("(kc p) n -> p kc n", p=128), (), (kw,))

    def stage1(self, s):
        P = self.P
        d = self.dram
        n_ = s.name
        P.barrier()
        P.reset_arena(self.const_end)
        g0, kg = self.bcast_load(d["emb_ln_g"], D, "g0")
        b0, kb = self.bcast_load(d["emb_ln_b"], D, "b0")
        tmp = self.ln_tmp()
        g2, kg2 = P.tile([128, 2, DR], BF16, "g2")
        self.load_w(g2, kg2, d["rwkv_g2"], 256, 0, DR)
        xt = [P.tile([128, D], F32, "xt%d" % i) for i in range(2)]
        ht = [P.tile([128, D], F32, "ht%d" % i) for i in range(2)]
        wbuf = [P.tile([128, 16, 512], BF16, "w%d" % i) for i in range(2)]
        stg = [P.tile([128, 512], F32, "stg%d" % i) for i in range(3)]
        stgb = [P.tile([128, 512], BF16, "stgb%d" % i) for i in range(2)]
        qkT = [P.tile([128, 4, 128], BF16, "qkT%d" % i) for i in range(2)]
        sgT = [P.tile([128, 2, 128], BF16, "sgT%d" % i) for i in range(2)]
        gst = [P.tile([128, DR], F32, "gst%d" % i) for i in range(2)]
        TB = 16
        h0T, kh0T = P.tile([128, 16, TB * 128 + NM], BF16, "h0T")
        groups = []
        for c0 in range(0, NIN, 512):
            groups.append((c0, min(512, NIN - c0)))
        tiles = list(range(s.ntile))
        blocks = [tiles[0:TB + 1]] + [tiles[i:i + TB] for i in range(TB + 1, s.ntile, TB)]
        ctr = [0]
        wi = 0
        ci = 0
        for blk in blocks:
            base = s.tile_range(blk[0])[0]
            for ti in blk:
                t0, n = s.tile_range(ti)
                x, kx = xt[ci % 2]
                h, kh = ht[ci % 2]
                ci += 1
                if ti == 0:
                    P.dma("sp", x[:n, :], d["meta_tokens"], (), (kx,))
                else:
                    P.dma("sp", x[:n, :], d["x_" + n_][t0 - NM:t0 - NM + n, :], (), (kx,))
                self.layer_norm(x, kx, n, (g0, b0), (kg, kb), h, kh, LN_EPS, tmp)
                if not s.sliced:
                    P.dma("sp", d["H0_" + n_][t0:t0 + n, :], h[:n, :], (kh,), ())
                self.transpose_to_fm(h, kh, n, h0T, kh0T, t0 - base, (0, 1), ctr)
            if self.stages == 0:
                continue
            for gi, (c0, nc_) in enumerate(groups):
                w, kw = wbuf[wi % 2]
                wi += 1
                self.load_w(w, kw, d["w_in"], D, c0, nc_)
                for ti in blk:
                    t0, n = s.tile_range(ti)
                    bi = (2, 3, 6, 7)[ctr[0] % 4]
                    ctr[0] += 1
                    pb = P.bank(bi)
                    kpb = "psb%d" % bi
                    for kc in range(16):
                        P.mm(pb[:n, :nc_], h0T[:, kc, t0 - base:t0 - base + n], w[:, kc, :nc_],
                             kc == 0, kc == 15, (kh0T, kw), (kpb,))
                    ev = "act" if ctr[0] % 2 == 0 else "dve"
                    if self.stages == -1 or (self.stages == -2 and gi < 4) or (self.stages == -3 and gi >= 4) or (self.stages == -4 and gi < 12):
                        sf, ksf = stg[ctr[0] % 3]
                        P.copy(ev, sf[:n, :], pb[:n, :], (kpb,), (ksf,))
                        continue
                    if gi < 4:
                        sb, ksb = stgb[ctr[0] % 2]
                        if gi < 2:
                            P.act(sb[:n, :], pb[:n, :], AF.Copy, (kpb,), (ksb,), scale=128.0 ** -0.5)
                        else:
                            P.copy(ev, sb[:n, :], pb[:n, :], (kpb,), (ksb,))
                        tb = 4
                        ptb = P.bank(tb, BF16)
                        for j in range(4):
                            P.tr(ptb[:, j * 128:j * 128 + n], sb[:n, j * 128:(j + 1) * 128], self.ident_b[:n, :n],
                                 (ksb, self.kid[1]), ("psb4",))
                        qt, kqt = qkT[ctr[0] % 2]
                        P.copy("dve" if ev == "act" else "act", qt[:, :, :n],
                               ptb[:, 0:512].rearrange("p (a b) -> p a b", a=4)[:, :, :n], ("psb4",), (kqt,))
                        dst = d[("QT_" if gi < 2 else "KT_") + n_]
                        h0 = (gi % 2) * 4
                        P.dma("sp", dst[h0:h0 + 4, :, t0:t0 + n].rearrange("h p t -> p h t"), qt[:, :, :n], (kqt,), ())
                    elif gi < 6:
                        sb, ksb = stgb[ctr[0] % 2]
                        P.copy(ev, sb[:n, :], pb[:n, :], (kpb,), (ksb,))
                        P.dma("sp", d["V_" + n_][t0:t0 + n, (gi - 4) * 512:(gi - 3) * 512], sb[:n, :], (ksb,), ())
                    elif gi < 12:
                        sf, ksf = stg[ctr[0] % 3]
                        P.copy(ev, sf[:n, :], pb[:n, :], (kpb,), (ksf,))
                        P.dma("sp", d["P_" + n_][t0:t0 + n, (gi - 6) * 512:(gi - 5) * 512], sf[:n, :], (ksf,), ())
                    else:
                        sf, ksf = stg[ctr[0] % 3]
                        P.copy("dve", sf[:n, :192], pb[:n, :192], (kpb,), (ksf,))
                        P.dma("sp", d["P_" + n_][t0:t0 + n, 3072:3264], sf[:n, :192], (ksf,), ())
                        sb, ksb = stgb[ctr[0] % 2]
                        P.act(sb[:n, :256], pb[:n, 192:448], AF.Sigmoid, (kpb,), (ksb,))
                        ptb = P.bank(4, BF16)
                        for j in range(2):
                            P.tr(ptb[:, j * 128:j * 128 + n], sb[:n, j * 128:(j + 1) * 128], self.ident_b[:n, :n],
                                 (ksb, self.kid[1]), ("psb4",))
                        sg, ksg = sgT[ctr[0] % 2]
                        P.copy("dve", sg[:, :, :n], ptb[:, 0:256].rearrange("p (a b) -> p a b", a=2)[:, :, :n],
                               ("psb4",), (ksg,))
                        go, kgo = gst[ctr[0] % 2]
                        for hf in range(2):
                            pg = P.bank(5)
                            kpg = "psb5"
                            for j in range(2):
                                P.mm(pg[:n, :], sg[:, j, :n], g2[:, j, hf * 512:(hf + 1) * 512], j == 0, j == 1,
                                     (ksg, kg2), (kpg,))
                            P.copy("act" if hf == 0 else "dve", go[:n, hf * 512:(hf + 1) * 512], pg[:n, :], (kpg,), (kgo,))
                        P.dma("sp", d["G_" + n_][t0:t0 + n, :], go[:n, :], (kgo,), ())

    def stage2(self, s):
        P = self.P
        d = self.dram
        n_ = s.name
        for dr in (0, 1):
            P.barrier()
            P.reset_arena(self.const_end)
            self._rwkv_dir(s, dr)

    def _rwkv_dir(self, s, dr):
        P = self.P
        d = self.dram
        n_ = s.name
        Pd = d["P_" + n_]
        L = s.L
        mu, kmu = self.bcast_load(d["rwkv_mu"][dr], NSH, "mu")
        w0, kw0 = self.bcast_load(d["rwkv_w0"][dr], DR, "w0")
        a0, ka0 = self.bcast_load(d["rwkv_a0"][dr], DR, "a0")
        kkp, kkkp = self.bcast_load(d["rwkv_k_k"], DR, "k_k")
        kap, kkap = self.bcast_load(d["rwkv_k_a"], DR, "k_a")
        rkp, krkp = self.bcast_load(d["rwkv_r_k"], DR, "r_k")
        w2, kw2 = P.tile([128, DR], F32, "w2")
        a2, ka2 = P.tile([128, DR], F32, "a2")
        P.dma("sp", w2[:96, :], d["rwkv_w2"][dr], (), (kw2,))
        P.dma("sp", a2[:96, :], d["rwkv_a2"][dr], (), (ka2,))
        tri, ktri = P.tile([128, 128], F32, "tri")
        P.dma("sp", tri, d["c_masks"][dr], (), (ktri,))
        ones, kones = P.tile([128, 128], F32, "ones")
        P.memset("pool", ones, 1.0, (kones,))
        m4, km4 = P.tile([128, 4, 128], F32, "mask4")
        for q in range(4):
            P.dma("sp", m4[:, q, :], d["c_masks"][2 + 2 * dr + (q % 2)], (), (km4,))
        suT, ksuT = P.tile([128, 128], F32, "suT")
        P.dma("sp", suT, d["c_masks"][2 + 2 * (1 - dr)], (), (ksuT,))
        if dr == 1:
            lg, klg = self.bcast_load(d["rwkv_lnx_g"], DR, "lnxg")
            lb, klb = self.bcast_load(d["rwkv_lnx_b"], DR, "lnxb")
        Sf, kSf = P.tile([128, 8, 64], F32, "Sf")
        Sb, kSb = P.tile([128, 8, 64], BF16, "Sb")
        P.memset("pool", Sf, 0.0, (kSf,))
        P.memset("pool", Sb, 0.0, (kSb,))
        kSfh = [[kSf + "/%d/%d" % (j, h) for h in range(2)] for j in range(8)]
        kSbh = [[kSb + "/%d/%d" % (j, h) for h in range(2)] for j in range(8)]
        for j in range(8):
            for h in range(2):
                P.last_w[kSfh[j][h]] = P.last_w[kSf]
                P.last_w[kSbh[j][h]] = P.last_w[kSb]
        pt = [P.tile([128, NSH], F32, "p%d" % i) for i in range(1)]
        pn = [P.tile([128, NSH], F32, "pn%d" % i) for i in range(1)]
        T = lambda nm, dt=F32: P.tile([128, DR], dt, nm)
        sg, ksg = T("sg"); av, kav = T("a"); cum, kcum = T("cum"); e1, ke1 = T("e1"); e2, ke2 = T("e2")
        kk, kkk = T("kk"); kq, kkq = T("kq"); ka_, kka_ = T("ka"); tmp, ktmp = T("tmp"); bon, kbon = T("bon")
        Y, kY = T("Y")
        At, kAt = T("At", BF16); Bt, kBt = T("Bt", BF16); Kt, kKt = T("Kt", BF16); Rt, kRt = T("Rt", BF16)
        BW, kBW = T("BW", BF16); KW, kKW = T("KW", BF16); Vb, kVb = T("Vb", BF16)
        ART, kART = P.tile([128, 8, 256], BF16, "ART")
        BTt, kBTt = P.tile([128, 8, 128], BF16, "BT")
        KTt, kKTt = P.tile([128, 8, 128], BF16, "KT")
        thT, kthT = P.tile([128, 2, 128], F32, "thT")
        th, kth = P.tile([128, 192], F32, "th")
        sm, ksm = P.tile([128, 4, 16], F32, "small")
        wc, kwc = P.tile([128, 8], F32, "wc")
        if dr == 1:
            yf, kyf = e1, ke1
            gt, kgt = e2, ke2
            ob, kob = T("ob", BF16)
            oT, koT = P.tile([128, 8, 128], BF16, "oT")
        M4a, kM4a0 = P.tile([128, 16, 4, 128], BF16, "M4a")
        kM4 = [kM4a0 + "/%d" % i for i in range(16)]
        NTa, kNTa0 = P.tile([128, 16, 128], BF16, "NTa")
        kNT = [kNTa0 + "/%d" % g for g in range(4)]
        Ta = []
        PPa = []
        for i in range(2):
            t_, k_ = P.tile([128, 16, 128], BF16, "Ta%d" % i)
            Ta.append((t_, [k_ + "/%d" % g for g in range(4)]))
            t_, k_ = P.tile([128, 2, 16, 128], BF16, "PPa%d" % i)
            PPa.append((t_, [[k_ + "/%d/%d" % (w, g) for g in range(4)] for w in range(2)]))
        Xa, kXa0 = P.tile([128, 16, 64], BF16, "Xa")
        kXa = [kXa0 + "/%d" % b for b in range(2)]
        Ua, kUa0 = P.tile([128, 16, 64], BF16, "Ua")
        kUa = [kUa0 + "/%d" % b for b in range(2)]
        id4, kid4 = P.tile([128, 4, 128], BF16, "id4")
        for q in range(4):
            P.copy("pool", id4[:, q, :], self.ident_b, (self.kid[1],), (kid4,))
        order = list(range(s.ntile)) if dr == 0 else list(range(s.ntile - 1, -1, -1))
        tctr = [0]
        v3 = lambda ap: ap.rearrange("p (h c) -> p h c", h=16)
        p, kp = pt[0]
        q_, kq_ = pn[0]
        f, kf = q_, kq_
        r_ = f[:, 0:DR]; k_ = f[:, DR:2 * DR]; v_ = f[:, 2 * DR:3 * DR]

        def prep1(ti):
            t0, n = s.tile_range(ti)
            P.dma("sp", p[:n, :], Pd[t0:t0 + n, :], (), (kp,))
            if dr == 0:
                if t0 == 0:
                    P.dma("sp", q_[0:1, :], d["c_zero"], (), (kq_,))
                    P.dma("sp", q_[1:n, :], Pd[0:n - 1, :], (), (kq_,))
                else:
                    P.dma("sp", q_[:n, :], Pd[t0 - 1:t0 - 1 + n, :], (), (kq_,))
            else:
                if t0 + n == L:
                    P.dma("sp", q_[n - 1:n, :], d["c_zero"], (), (kq_,))
                    P.dma("sp", q_[0:n - 1, :], Pd[t0 + 1:t0 + n, :], (), (kq_,))
                else:
                    P.dma("sp", q_[:n, :], Pd[t0 + 1:t0 + 1 + n, :], (), (kq_,))
            yield
            P.tt("dve", q_[:n, :], q_[:n, :], p[:n, :], ALU.subtract, (kq_, kp), (kq_,))
            yield
            P.tt("pool", q_[:n, :], q_[:n, :], mu[:n, :], ALU.mult, (kq_, kmu), (kq_,))
            yield
            P.tt("dve", q_[:n, :], q_[:n, :], p[:n, :], ALU.add, (kq_, kp), (kq_,))
            yield
            P.act(th[:n, 0:96], f[:n, 3 * DR:3 * DR + 96], AF.Tanh, (kf,), (kth,))
            P.copy("pool", th[:n, 96:192], f[:n, 3 * DR + 96:3 * DR + 192], (kf,), (kth,))
            pb0 = P.bank(0)
            for q in range(2):
                P.tr(pb0[:96, q * 128:q * 128 + n], th[:n, q * 96:(q + 1) * 96], self.ident_f[:n, :n],
                     (kth, self.kid[0]), ("psb0",))
            P.copy("act", thT[:96, :, :n], pb0[:96, 0:256].rearrange("p (a b) -> p a b", a=2)[:, :, :n], ("psb0",), (kthT,))
            yield
            for which, (wm, kwm, bias, kbias, dst, kdst) in enumerate(((w2, kw2, w0, kw0, sg, ksg), (a2, ka2, a0, ka0, av, kav))):
                for hf in range(2):
                    pb = P.bank(hf); kpb = "psb%d" % hf
                    P.mm(pb[:n, :], thT[:96, which, :n], wm[:96, hf * 512:(hf + 1) * 512], True, True, (kthT, kwm), (kpb,))
                    P.tt("dve", dst[:n, hf * 512:(hf + 1) * 512], pb[:n, :], bias[:n, hf * 512:(hf + 1) * 512], ALU.add,
                         (kpb, kbias), (kdst,))
                    yield
                P.act(dst[:n, :], dst[:n, :], AF.Sigmoid, (kdst,), (kdst,))
                yield
            for hf in range(2):
                pb = P.bank(hf); kpb = "psb%d" % hf
                P.mm(pb[:n, :], tri[:n, :n], sg[:n, hf * 512:(hf + 1) * 512], True, True, (ktri, ksg), (kpb,))
                P.copy("act", cum[:n, hf * 512:(hf + 1) * 512], pb[:n, :], (kpb,), (kcum,))
                yield
            P.tt("pool", kk[:n, :], k_[:n, :], kkp[:n, :], ALU.mult, (kf, kkkp), (kkk,))
            yield
            P.tt("pool", tmp[:n, :], kk[:n, :], kk[:n, :], ALU.mult, (kkk,), (ktmp,))
            yield
            P.reduce("dve", sm[:n, 0, :], v3(tmp[:n, :]), ALU.add, (ktmp,), (ksm,))
            P.act(sm[:n, 0, :], sm[:n, 0, :], AF.Sqrt, (ksm,), (ksm,))
            P.ts("dve", sm[:n, 0, :], sm[:n, 0, :], 1e-12, None, ALU.max, None, (ksm,), (ksm,))
            P.recip("dve", sm[:n, 0, :], sm[:n, 0, :], (ksm,), (ksm,))
            yield
            P.tt("dve", v3(kk[:n, :]), v3(kk[:n, :]), sm[:n, 0, :].unsqueeze(2).to_broadcast([n, 16, 64]), ALU.mult,
                 (kkk, ksm), (kkk,))
            yield
            P.stt("dve", kq[:n, :], av[:n, :], -1.0, kap[:n, :], ALU.add, ALU.mult, (kav, kkap), (kkq,))
            yield
            P.stt("dve", kq[:n, :], kq[:n, :], 1.0, k_[:n, :], ALU.add, ALU.mult, (kkq, kf), (kkq,))
            yield
            P.tt("pool", ka_[:n, :], kk[:n, :], av[:n, :], ALU.mult, (kkk, kav), (kka_,))
            yield

        def prep2(ti):
            t0, n = s.tile_range(ti)
            for hf in range(2):
                pb = P.bank(2 + hf); kpb = "psb%d" % (2 + hf)
                P.mm(pb[:n, :], ones[:n, :n], sg[:n, hf * 512:(hf + 1) * 512], True, True, (kones, ksg), (kpb,))
                P.tt("dve", e2[:n, hf * 512:(hf + 1) * 512], pb[:n, :], cum[:n, hf * 512:(hf + 1) * 512], ALU.subtract,
                     (kpb, kcum), (ke2,))
            pb0 = P.bank(0)
            for j in range(8):
                P.mm(pb0[:, j:j + 1], sg[:n, j * 128:(j + 1) * 128], ones[:n, 0:1], True, True, (ksg, kones), ("psb0",))
            P.act(wc[:, :], pb0[:, 0:8], AF.Exp, ("psb0",), (kwc,), scale=CDEC)
            P.tt("pool", tmp[:n, :], r_[:n, :], kq[:n, :], ALU.mult, (kf, kkq), (ktmp,))
            P.tt("pool", tmp[:n, :], tmp[:n, :], rkp[:n, :], ALU.mult, (ktmp, krkp), (ktmp,))
            P.reduce("dve", sm[:n, 1, :], v3(tmp[:n, :]), ALU.add, (ktmp,), (ksm,))
            P.tt("dve", v3(bon[:n, :]), v3(v_[:n, :]), sm[:n, 1, :].unsqueeze(2).to_broadcast([n, 16, 64]), ALU.mult,
                 (kf, ksm), (kbon,))
            if dr == 0:
                P.dma("sp", d["BF_" + n_][t0:t0 + n, :], bon[:n, :], (kbon,), ())
            P.act(e1[:n, :], cum[:n, :], AF.Exp, (kcum,), (ke1,), scale=CDEC)
            P.tt("dve", Rt[:n, :], r_[:n, :], e1[:n, :], ALU.mult, (kf, ke1), (kRt,))
            P.act(e1[:n, :], cum[:n, :], AF.Exp, (kcum,), (ke1,), scale=-CDEC)
            P.tt("pool", Bt[:n, :], ka_[:n, :], e1[:n, :], ALU.mult, (kka_, ke1), (kBt,))
            P.tt("dve", Kt[:n, :], kq[:n, :], e1[:n, :], ALU.mult, (kkq, ke1), (kKt,))
            P.act(e2[:n, :], e2[:n, :], AF.Exp, (ke2,), (ke2,), scale=CDEC)
            P.tt("pool", BW[:n, :], ka_[:n, :], e2[:n, :], ALU.mult, (kka_, ke2), (kBW,))
            P.tt("dve", KW[:n, :], kq[:n, :], e2[:n, :], ALU.mult, (kkq, ke2), (kKW,))
            P.tt("dve", cum[:n, :], cum[:n, :], sg[:n, :], ALU.subtract, (kcum, ksg), (kcum,))
            P.act(e1[:n, :], cum[:n, :], AF.Exp, (kcum,), (ke1,), scale=CDEC)
            P.stt("dve", At[:n, :], kk[:n, :], -1.0, e1[:n, :], ALU.mult, ALU.mult, (kkk, ke1), (kAt,))
            P.copy("pool", Vb[:n, :], v_[:n, :], (kf,), (kVb,))
            for xi, (src, ksrc) in enumerate(((At, kAt), (Rt, kRt), (Bt, kBt), (Kt, kKt))):
                bi = tctr[0] % 4
                tctr[0] += 1
                ptb = P.bank(bi, BF16)
                kptb = "psb%d" % bi
                for j in range(8):
                    P.tr(ptb[:, j * 128:j * 128 + n], src[:n, j * 128:(j + 1) * 128], self.ident_b[:n, :n],
                         (ksrc, self.kid[1]), (kptb,))
                srcv = ptb.rearrange("p (a b) -> p a b", a=8)[:, :, :n]
                if xi == 0:
                    P.copy("act", ART[:, :, 0:n], srcv, (kptb,), (kART,))
                elif xi == 1:
                    P.copy("dve", ART[:, :, 128:128 + n], srcv, (kptb,), (kART,))
                elif xi == 2:
                    P.copy("act", BTt[:, :, :n], srcv, (kptb,), (kBTt,))
                else:
                    P.copy("dve", KTt[:, :, :n], srcv, (kptb,), (kKTt,))

        def units(ti, gen):
            t0, n = s.tile_range(ti)

            def pump():
                next(gen, None)
            nsq = int(np.log2(n)) - 1
            HD = [dict(hd=hd, j=hd // 2, hb=64 * (hd % 2), col=hd * 64, g=hd // 4, sl=hd % 4) for hd in range(16)]
            bk = lambda g: P.bank(4 + g)
            kbk = lambda g: "psb%d" % (4 + g)
            for H in HD:
                if H["hd"] % 4 == 0:
                    pump()
                hd, j, hb = H["hd"], H["j"], H["hb"]
                b_ = bk(hd % 4); kb_ = kbk(hd % 4)
                P.mm(b_[:n, 0:256], BTt[hb:hb + 64, j, :n], ART[hb:hb + 64, j, :], True, True, (kBTt, kART), (kb_,))
                P.mm(b_[:n, 256:512], KTt[hb:hb + 64, j, :n], ART[hb:hb + 64, j, :], True, True, (kKTt, kART), (kb_,))
                P.tt("dve", M4a[:n, hd, :, :n], b_[:n, :].rearrange("p (a b) -> p a b", a=4)[:, :, :n], m4[:n, :, :n], ALU.mult,
                     (kb_, km4), (kM4[hd],))
            for H in HD:
                if H["hd"] % 4 == 0:
                    pump()
                hd, j, hb = H["hd"], H["j"], H["hb"]
                nb = (hd % 2) * 2 + hd // 8
                sl = (hd % 8) // 2
                P.mm(bk(nb)[:n, sl * 128:sl * 128 + n], ART[hb:hb + 64, j, 0:n], BTt[hb:hb + 64, j, :n], True, True,
                     (kART, kBTt), (kbk(nb),))
                if sl == 3:
                    base = (hd // 8) * 8 + hd % 2
                    P.tt("dve", NTa[:n, base:base + 7:2, :n], bk(nb)[:n, :].rearrange("p (a b) -> p a b", a=4)[:, :, :n],
                         suT[:n, :n].unsqueeze(1).to_broadcast([n, 4, n]), ALU.mult, (kbk(nb), ksuT), (kNT[nb],))
            T0, kT0 = Ta[0]
            for g in range(4):
                P.tt("pool", T0[:n, g * 4:g * 4 + 4, :n], M4a[:n, g * 4:g * 4 + 4, 0, :n], id4[:n, :, :n], ALU.add,
                     tuple(kM4[g * 4:g * 4 + 4]) + (kid4,), (kT0[g],))
            Pget = lambda hd: (M4a[:, hd, 0, :], kM4[hd])
            PTget = lambda hd: (NTa[:, hd, :], kNT[(hd % 2) * 2 + hd // 8])
            tcur = 0
            for it in range(1, nsq + 1):
                last = it == nsq
                PPn, kPPn = PPa[it % 2]
                for H in HD:
                    if H["hd"] % 4 == 0:
                        pump()
                    hd, g, sl = H["hd"], H["g"], H["sl"]
                    Pc, kPc = Pget(hd); PTc, kPTc = PTget(hd)
                    P.mm(bk(g)[:n, sl * 128:sl * 128 + n], Pc[:n, :n], PTc[:n, :n], True, True, (kPc, kPTc), (kbk(g),))
                    if sl == 3:
                        P.copy("act", PPn[:n, 1, g * 4:g * 4 + 4, :n], bk(g)[:n, :].rearrange("p (a b) -> p a b", a=4)[:, :, :n],
                               (kbk(g),), (kPPn[1][g],))
                if not last:
                    for H in HD:
                        if H["hd"] % 4 == 0:
                            pump()
                        hd, g, sl = H["hd"], H["g"], H["sl"]
                        Pc, kPc = Pget(hd); PTc, kPTc = PTget(hd)
                        P.mm(bk(g)[:n, sl * 128:sl * 128 + n], PTc[:n, :n], Pc[:n, :n], True, True, (kPc, kPTc), (kbk(g),))
                        if sl == 3:
                            P.copy("act" if g % 2 else "dve", PPn[:n, 0, g * 4:g * 4 + 4, :n],
                                   bk(g)[:n, :].rearrange("p (a b) -> p a b", a=4)[:, :, :n], (kbk(g),), (kPPn[0][g],))
                Pget = lambda hd, PPn=PPn, kPPn=kPPn: (PPn[:, 0, hd, :], kPPn[0][hd // 4])
                PTget = lambda hd, PPn=PPn, kPPn=kPPn: (PPn[:, 1, hd, :], kPPn[1][hd // 4])
                Tc, kTc = Ta[tcur]
                Tn, kTn = Ta[1 - tcur]
                for H in HD:
                    if H["hd"] % 4 == 0:
                        pump()
                    hd, g, sl = H["hd"], H["g"], H["sl"]
                    PTc, kPTc = PTget(hd)
                    P.mm(bk(g)[:n, sl * 128:sl * 128 + n], PTc[:n, :n], Tc[:n, hd, :n], True, True, (kPTc, kTc[g]), (kbk(g),))
                    if sl == 3:
                        P.tt("dve", Tn[:n, g * 4:g * 4 + 4, :n], bk(g)[:n, :].rearrange("p (a b) -> p a b", a=4)[:, :, :n],
                             Tc[:n, g * 4:g * 4 + 4, :n], ALU.add, (kbk(g), kTc[g]), (kTn[g],))
                tcur = 1 - tcur
            Tc, kTc = Ta[tcur]
            for H in HD:
                if H["hd"] % 4 == 0:
                    pump()
                hd, j, hb, col = H["hd"], H["j"], H["hb"], H["col"]
                b = hd % 2
                o_ = bk(b)[:n, (hd // 2) * 64:(hd // 2) * 64 + 64]
                P.mm(o_, ART[hb:hb + 64, j, 0:n], Sb[hb:hb + 64, j, :], True, False, (kART, kSb), (kbk(b),))
                P.mm(o_, M4a[:n, hd, 2, :n], Vb[:n, col:col + 64], False, True, (kM4[hd], kVb), (kbk(b),))
                if hd >= 14:
                    P.copy("act", Xa[:n, b:16:2, :], bk(b)[:n, :].rearrange("p (a b) -> p a b", a=8), (kbk(b),), (kXa[b],))
            for H in HD:
                if H["hd"] % 4 == 0:
                    pump()
                hd = H["hd"]
                b = hd % 2
                P.mm(bk(2 + b)[:n, (hd // 2) * 64:(hd // 2) * 64 + 64], Tc[:n, hd, :n], Xa[:n, hd, :], True, True,
                     (kTc[hd // 4], kXa[b]), (kbk(2 + b),))
                if hd >= 14:
                    P.copy("act", Ua[:n, b:16:2, :], bk(2 + b)[:n, :].rearrange("p (a b) -> p a b", a=8), (kbk(2 + b),), (kUa[b],))
            Yv = Y[:n, :].rearrange("p (h c) -> p h c", h=16)
            for H in HD:
                if H["hd"] % 4 == 0:
                    pump()
                hd, j, hb, col = H["hd"], H["j"], H["hb"], H["col"]
                b = hd % 2
                o_ = bk(b)[:n, (hd // 2) * 64:(hd // 2) * 64 + 64]
                P.mm(o_, ART[hb:hb + 64, j, 128:128 + n], Sb[hb:hb + 64, j, :], True, False, (kART, kSb), (kbk(b),))
                P.mm(o_, M4a[:n, hd, 1, :n], Ua[:n, hd, :], False, False, (kM4[hd], kUa[b]), (kbk(b),))
                P.mm(o_, M4a[:n, hd, 3, :n], Vb[:n, col:col + 64], False, True, (kM4[hd], kVb), (kbk(b),))
                if hd >= 14:
                    P.copy("act", Yv[:, b:16:2, :], bk(b)[:n, :].rearrange("p (a b) -> p a b", a=8), (kbk(b),), (kY,))
            for H in HD:
                if H["hd"] % 4 == 0:
                    pump()
                hd, j, hb, col = H["hd"], H["j"], H["hb"], H["col"]
                o_ = bk(2)[hb:hb + 64, j * 64:(j + 1) * 64]
                P.mm(o_, BW[:n, col:col + 64], Ua[:n, hd, :], True, False, (kBW, kUa[hd % 2]), (kbk(2),))
                P.mm(o_, KW[:n, col:col + 64], Vb[:n, col:col + 64], False, True, (kKW, kVb), (kbk(2),))
            P.tt("dve", Sf[:, :, :], Sf[:, :, :], wc[:, :].unsqueeze(2).to_broadcast([128, 8, 64]), ALU.mult, (kSf, kwc), (kSf,))
            P.tt("dve", Sf[:, :, :], Sf[:, :, :], bk(2)[:, :].rearrange("p (a b) -> p a b", a=8), ALU.add, (kSf, kbk(2)), (kSf,))
            P.copy("pool", Sb[:, :, :], Sf[:, :, :], (kSf,), (kSb,))

        def epilogue(ti):
            t0, n = s.tile_range(ti)
            if dr == 0:
                P.dma("sp", d["YF_" + n_][t0:t0 + n, :], Y[:n, :], (kY,), ())
            else:
                P.dma("sp", yf[:n, :], d["YF_" + n_][t0:t0 + n, :], (), (kyf,))
                P.dma("sp", gt[:n, :], d["G_" + n_][t0:t0 + n, :], (), (kgt,))
                P.tt("dve", Y[:n, :], Y[:n, :], yf[:n, :], ALU.add, (kY, kyf), (kY,))
                P.dma("sp", yf[:n, :], d["BF_" + n_][t0:t0 + n, :], (kY,), (kyf,))
                P.reduce("dve", sm[:n, 2, :], v3(Y[:n, :]), ALU.add, (kY,), (ksm,))
                P.tt("pool", tmp[:n, :], Y[:n, :], Y[:n, :], ALU.mult, (kY,), (ktmp,))
                P.reduce("dve", sm[:n, 3, :], v3(tmp[:n, :]), ALU.add, (ktmp,), (ksm,))
                P.ts("dve", sm[:n, 2, :], sm[:n, 2, :], 1.0 / 64, None, ALU.mult, None, (ksm,), (ksm,))
                P.ts("dve", sm[:n, 3, :], sm[:n, 3, :], 1.0 / 64, None, ALU.mult, None, (ksm,), (ksm,))
                P.tt("dve", sm[:n, 0, :], sm[:n, 2, :], sm[:n, 2, :], ALU.mult, (ksm,), (ksm,))
                P.tt("dve", sm[:n, 3, :], sm[:n, 3, :], sm[:n, 0, :], ALU.subtract, (ksm,), (ksm,))
                P.act(sm[:n, 3, :], sm[:n, 3, :], AF.Sqrt, (ksm,), (ksm,), bias=GN_EPS)
                P.recip("dve", sm[:n, 3, :], sm[:n, 3, :], (ksm,), (ksm,))
                bc = lambda q: sm[:n, q, :].unsqueeze(2).to_broadcast([n, 16, 64])
                P.tt("dve", v3(Y[:n, :]), v3(Y[:n, :]), bc(2), ALU.subtract, (kY, ksm), (kY,))
                P.tt("dve", v3(Y[:n, :]), v3(Y[:n, :]), bc(3), ALU.mult, (kY, ksm), (kY,))
                P.tt("pool", Y[:n, :], Y[:n, :], lg[:n, :], ALU.mult, (kY, klg), (kY,))
                P.tt("pool", Y[:n, :], Y[:n, :], lb[:n, :], ALU.add, (kY, klb), (kY,))
                P.tt("dve", Y[:n, :], Y[:n, :], bon[:n, :], ALU.add, (kY, kbon), (kY,))
                P.tt("dve", Y[:n, :], Y[:n, :], yf[:n, :], ALU.add, (kY, kyf), (kY,))
                P.tt("dve", ob[:n, :], Y[:n, :], gt[:n, :], ALU.mult, (kY, kgt), (kob,))
                if s.sliced:
                    P.dma("sp", d["OTOK_" + n_][t0:t0 + n, DA:D], ob[:n, :], (kob,), ())
                    return
                bi = tctr[0] % 4
                tctr[0] += 1
                ptb = P.bank(bi, BF16)
                kptb = "psb%d" % bi
                for j in range(8):
                    P.tr(ptb[:, j * 128:j * 128 + n], ob[:n, j * 128:(j + 1) * 128], self.ident_b[:n, :n],
                         (kob, self.kid[1]), (kptb,))
                P.copy("act", oT[:, :, :n], ptb.rearrange("p (a b) -> p a b", a=8)[:, :, :n], (kptb,), (koT,))
                P.dma("sp", d["OT_" + n_][8:16, :, t0:t0 + n].rearrange("c p t -> p c t"), oT[:, :, :n], (koT,), ())

        for _ in prep1(order[0]):
            pass
        for idx, ti in enumerate(order):
            prep2(ti)
            gen = prep1(order[idx + 1]) if idx + 1 < len(order) else iter(())
            units(ti, gen)
            for _ in gen:
                pass
            epilogue(ti)


    def stage3(self, s):
        P = self.P
        d = self.dram
        n_ = s.name
        L = s.L
        P.barrier()
        P.reset_arena(self.const_end)
        kth = [P.tile([128, L], BF16, "kth%d" % i) for i in range(2)]
        qth = [P.tile([128, L], BF16, "qth%d" % i) for i in range(2)]
        vh = [P.tile([128, s.ntile, 128], BF16, "vh%d" % i) for i in range(2)]
        bt = [P.tile([128, 5, NKEY], F32, "bias%d" % i) for i in range(2)]
        oTh = [P.tile([128, L], BF16, "oTh%d" % i) for i in range(2)]
        ND = 4
        sc = [P.tile([128, NKEY], F32, "sc%d" % i) for i in range(ND)]
        pr = [P.tile([128, NKEY], BF16, "pr%d" % i) for i in range(ND)]
        pT = [P.tile([128, 6, 128], BF16, "pT%d" % i) for i in range(ND)]
        ob = [P.tile([128, 128], BF16, "ob%d" % i) for i in range(ND)]
        st = [P.tile([128, 4], F32, "st%d" % i) for i in range(2 * ND)]
        uc = [0]

        def mk_unit(kind, rp, h, tiles):
            kt, kkt, qt, kqt, v, kv, b, kb, oh, koh = tiles
            u = uc[0]
            uc[0] += 1
            U = dict(kind=kind, h=h, tiles=tiles)
            U["sc"] = sc[u % ND]; U["pr"] = pr[u % ND]; U["pT"] = pT[u % ND]; U["ob"] = ob[u % ND]; U["st"] = st[u % (2 * ND)]
            ba = (u % ND) * 2
            U["ba"] = ba
            if kind == "meta":
                U.update(nq=NM, q0=0, lo=640, blocks=[(5, NM, 0)])
            else:
                r = 2 * rp
                ws = min(max(r - 4, 0), s.rows - WINR)
                assert ws % 2 == 0
                U.update(nq=128, q0=NM + rp * 128, k0=NM + ws * 64, cls=(r - ws) // 2, lo=0,
                         blocks=[(j, 128, ws // 2 + 1 + j) for j in range(5)] + [(5, NM, 0)])
            return U

        def phA(U):
            kt, kkt, qt, kqt, v, kv, b, kb, oh, koh = U["tiles"]
            s_, ks_ = U["sc"]; st_, kst_ = U["st"]
            ba = U["ba"]; nq = U["nq"]; q0 = U["q0"]; lo = U["lo"]
            pa = P.bank(ba); pbk = P.bank(ba + 1)
            kpa = "psb%d" % ba; kpb = "psb%d" % (ba + 1)
            if U["kind"] == "meta":
                P.mm(pbk[:nq, 128:144], qt[:, 0:NM], kt[:, 0:NM], True, True, (kqt, kkt), (kpb,))
                P.copy("dve", s_[:nq, 640:656], pbk[:nq, 128:144], (kpb,), (ks_,))
            else:
                k0 = U["k0"]; cls = U["cls"]
                P.mm(pa[:, :], qt[:, q0:q0 + 128], kt[:, k0:k0 + 512], True, True, (kqt, kkt), (kpa,))
                P.mm(pbk[:, 0:128], qt[:, q0:q0 + 128], kt[:, k0 + 512:k0 + 640], True, True, (kqt, kkt), (kpb,))
                P.mm(pbk[:, 128:144], qt[:, q0:q0 + 128], kt[:, 0:NM], True, True, (kqt, kkt), (kpb,))
                P.tt("dve", s_[:, 0:512], pa[:, :], b[:, cls, 0:512], ALU.add, (kpa, kb), (ks_,))
                P.tt("dve", s_[:, 512:656], pbk[:, 0:144], b[:, cls, 512:656], ALU.add, (kpb, kb), (ks_,))
            P.op("dve", lambda e, o=st_[:nq, 0:1], i=s_[:nq, lo:656]: e.reduce_max(o, i, AX.X), (ks_,), (kst_,))
            P.ts("dve", st_[:nq, 1:2], st_[:nq, 0:1], -1.0, None, ALU.mult, None, (kst_,), (kst_,))

        def phB(U):
            s_, ks_ = U["sc"]; st_, kst_ = U["st"]; p_, kp_ = U["pr"]
            nq = U["nq"]; lo = U["lo"]
            P.act(p_[:nq, lo:656], s_[:nq, lo:656], AF.Exp, (ks_, kst_), (kp_, kst_), bias=st_[:nq, 1:2], accum=st_[:nq, 2:3])
            P.recip("dve", st_[:nq, 3:4], st_[:nq, 2:3], (kst_,), (kst_,))

        def phC(U):
            p_, kp_ = U["pr"]; pt_, kpt_ = U["pT"]
            ba = U["ba"]; nq = U["nq"]
            ptb = P.bank(ba, BF16)
            kptb = "psb%d" % ba
            for (j, nk, vt) in U["blocks"]:
                P.tr(ptb[:nk, j * 128:j * 128 + nq], p_[:nq, j * 128:j * 128 + nk], self.ident_b[:nq, :nq],
                     (kp_, self.kid[1]), (kptb,))
            if U["kind"] == "meta":
                P.copy("act", pt_[:NM, 5, :nq], ptb[:NM, 640:640 + nq], (kptb,), (kpt_,))
            else:
                P.copy("act", pt_[:, :, :], ptb[:, 0:768].rearrange("p (a b) -> p a b", a=6), (kptb,), (kpt_,))

        def phD(U):
            kt, kkt, qt, kqt, v, kv, b, kb, oh, koh = U["tiles"]
            pt_, kpt_ = U["pT"]; o_, ko_ = U["ob"]; st_, kst_ = U["st"]
            ba = U["ba"]; nq = U["nq"]; q0 = U["q0"]; h = U["h"]
            pbk = P.bank(ba + 1)
            kpb = "psb%d" % (ba + 1)
            blocks = U["blocks"]
            for bi, (j, nk, vt) in enumerate(blocks):
                P.mm(pbk[:nq, 256:384], pt_[:nk, j, :nq], v[:nk, vt, :], bi == 0, bi == len(blocks) - 1, (kpt_, kv), (kpb,))
            P.act(o_[:nq, :], pbk[:nq, 256:384], AF.Copy, (kpb, kst_), (ko_,), scale=st_[:nq, 3:4])
            if s.sliced:
                P.dma("sp", d["OTOK_" + n_][q0:q0 + nq, h * 128:(h + 1) * 128], o_[:nq, :], (ko_,), ())
                return
            pot = pbk.bitcast(BF16)
            P.tr(pot[:, 800:800 + nq], o_[:nq, :], self.ident_b[:nq, :nq], (ko_, self.kid[1]), (kpb,))
            P.copy("dve", oh[:, q0:q0 + nq], pot[:, 800:800 + nq], (kpb,), (koh,))

        for h in range(HA):
            kt, kkt = kth[h % 2]; qt, kqt = qth[h % 2]; v, kv = vh[h % 2]; b, kb = bt[h % 2]; oh, koh = oTh[h % 2]
            P.dma("sp", kt, d["KT_" + n_][h], (), (kkt,))
            P.dma("sp", qt, d["QT_" + n_][h], (), (kqt,))
            P.dma("sp", v[:NM, 0, :], d["V_" + n_][0:NM, h * 128:(h + 1) * 128], (), (kv,))
            for i0 in range(1, s.ntile, 16):
                i1 = min(s.ntile, i0 + 16)
                P.dma("sp", v[:, i0:i1, :],
                      d["V_" + n_][NM + (i0 - 1) * 128:NM + (i1 - 1) * 128, h * 128:(h + 1) * 128].rearrange("(i p) c -> p i c", p=128),
                      (), (kv,))
            P.dma("sp", b, d["bias_tab"][h].rearrange("c q k -> q c k"), (), (kb,))
            tiles = (kt, kkt, qt, kqt, v, kv, b, kb, oh, koh)
            Us = [mk_unit("meta", 0, h, tiles)] + [mk_unit("grid", rp, h, tiles) for rp in range(s.rows // 2)]
            nU = len(Us)
            for step in range(nU + 3):
                if step < nU:
                    phA(Us[step])
                if 0 <= step - 1 < nU:
                    phB(Us[step - 1])
                if 0 <= step - 2 < nU:
                    phC(Us[step - 2])
                if 0 <= step - 3 < nU:
                    phD(Us[step - 3])
            if not s.sliced:
                P.dma("sp", d["OT_" + n_][h], oh, (koh,), ())

    def stage4(self, s):
        P = self.P
        d = self.dram
        n_ = s.name
        P.barrier()
        P.reset_arena(self.const_end)
        g1, kg = self.bcast_load(d["ln1_g"], D, "g1")
        b1, kb = self.bcast_load(d["ln1_b"], D, "b1")
        tmp = self.ln_tmp()
        wo, kwo = P.tile([128, 16, D], BF16, "wo")
        for c0 in range(0, D, 512):
            self.load_w(wo[:, :, c0:c0 + 512], kwo, d["w_out"], D, c0, 512)
        oT = [P.tile([128, 16, 128], BF16, "oT%d" % i) for i in range(2)]
        h0 = [P.tile([128, D], F32, "h0%d" % i) for i in range(2)]
        h1 = [P.tile([128, D], F32, "h1%d" % i) for i in range(2)]
        h1T = [P.tile([128, 16, 128], BF16, "h1T%d" % i) for i in range(2)]
        ctr = [0]
        for ti in range(s.ntile):
            t0, n = s.tile_range(ti)
            o_, ko_ = oT[ti % 2]; x, kx = h0[ti % 2]; y, ky = h1[ti % 2]; yt, kyt = h1T[ti % 2]
            P.dma("sp", o_[:, :, :n], d["OT_" + n_][:, :, t0:t0 + n].rearrange("c p t -> p c t"), (), (ko_,))
            P.dma("sp", x[:n, :], d["H0_" + n_][t0:t0 + n, :], (), (kx,))
            for hf in range(4):
                pb = P.bank(hf); kpb = "psb%d" % hf
                for kc in range(16):
                    P.mm(pb[:n, :], o_[:, kc, :n], wo[:, kc, hf * 512:(hf + 1) * 512], kc == 0, kc == 15, (ko_, kwo), (kpb,))
                P.stt("dve", x[:n, hf * 512:(hf + 1) * 512], x[:n, hf * 512:(hf + 1) * 512], ALPHA, pb[:n, :], ALU.mult, ALU.add,
                      (kx, kpb), (kx,))
            self.layer_norm(x, kx, n, (g1, b1), (kg, kb), y, ky, LN_EPS, tmp)
            P.dma("sp", d["H1_" + n_][t0:t0 + n, :], y[:n, :], (ky,), ())
            self.transpose_to_fm(y, ky, n, yt, kyt, 0, (4, 5, 6, 7), ctr)
            P.dma("sp", d["H1T_" + n_][:, :, t0:t0 + n].rearrange("c p t -> p c t"), yt[:, :, :n], (kyt,), ())

    def stage4_sliced(self, s):
        P = self.P
        d = self.dram
        n_ = s.name
        NG = s.ng
        P.barrier()
        P.reset_arena(self.const_end)
        g0, kg0 = self.bcast_load(d["emb_ln_g"], D, "g0")
        b0, kb0 = self.bcast_load(d["emb_ln_b"], D, "b0")
        g1, kg1 = self.bcast_load(d["ln1_g"], D, "g1")
        b1, kb1 = self.bcast_load(d["ln1_b"], D, "b1")
        tmp = self.ln_tmp()
        wo, kwo = P.tile([128, 16, D], BF16, "wo")
        for c0 in range(0, D, 512):
            self.load_w(wo[:, :, c0:c0 + 512], kwo, d["w_out"], D, c0, 512)
        ms, kms = P.tile([128, NG + 1], F32, "msel")
        P.dma("sp", ms, d["msel_" + n_], (), (kms,))
        mI, kmI = P.tile([128, NG, 128], BF16, "mI")
        for c in range(NG):
            P.ts("dve", mI[:, c, :], self.ident_f, ms[:, c:c + 1], None, ALU.mult, None, (self.kid[0], kms), (kmI,))
        shf, kshf = P.tile([128, 128], F32, "shf")
        P.dma("sp", shf[:NM, :], d["c_shift"], (), (kshf,))
        shI, kshI = P.tile([128, 128], BF16, "shI")
        P.ts("dve", shI[:NM, :], shf[:NM, :], ms[:NM, 0:1], None, ALU.mult, None, (kshf, kms), (kshI,))
        ot = [P.tile([128, D], BF16, "ot%d" % i) for i in range(NG)]
        oTs, koTs = P.tile([128, 16, 128], BF16, "oTs")
        x, kx = P.tile([128, D], F32, "x")
        h0, kh0 = P.tile([128, D], F32, "h0")
        y, ky = P.tile([128, D], F32, "h1")
        yt, kyt = P.tile([128, 16, 128], BF16, "h1T")
        ctr = [0]
        for sl in range(10):
            cands = []
            for c in range(NG):
                ti = 8 * c + sl
                if 0 <= ti < s.ntile:
                    cands.append((c, ti))
            for c, ti in cands:
                t0, n = s.tile_range(ti)
                P.dma("sp", ot[c][0][:n, :], d["OTOK_" + n_][t0:t0 + n, :], (), (ot[c][1],))
            for fc in range(16):
                bi = fc // 4
                pb = P.bank(bi); kpb = "psb%d" % bi
                for ci, (c, ti) in enumerate(cands):
                    t0, n = s.tile_range(ti)
                    rhs = shI[:NM, :] if ti == 0 else mI[:, c, :]
                    krhs = kshI if ti == 0 else kmI
                    P.mm(pb[:, (fc % 4) * 128:(fc % 4 + 1) * 128], ot[c][0][:n, fc * 128:(fc + 1) * 128], rhs,
                         ci == 0, ci == len(cands) - 1, (ot[c][1], krhs), (kpb,))
                if fc % 4 == 3:
                    P.copy("act" if bi % 2 == 0 else "dve", oTs[:, fc - 3:fc + 1, :], pb.rearrange("p (a b) -> p a b", a=4),
                           (kpb,), (koTs,))
            P.dma("sp", x, d["xext_" + n_][sl * 128:(sl + 1) * 128, :], (), (kx,))
            self.layer_norm(x, kx, 128, (g0, b0), (kg0, kb0), h0, kh0, LN_EPS, tmp)
            for hf in range(4):
                pb = P.bank(4 + hf); kpb = "psb%d" % (4 + hf)
                for kc in range(16):
                    P.mm(pb[:, :], oTs[:, kc, :], wo[:, kc, hf * 512:(hf + 1) * 512], kc == 0, kc == 15, (koTs, kwo), (kpb,))
                P.stt("dve", h0[:, hf * 512:(hf + 1) * 512], h0[:, hf * 512:(hf + 1) * 512], ALPHA, pb[:, :], ALU.mult, ALU.add,
                      (kh0, kpb), (kh0,))
            self.layer_norm(h0, kh0, 128, (g1, b1), (kg1, kb1), y, ky, LN_EPS, tmp)
            P.dma("sp", d["H1s_" + n_][sl * 128:(sl + 1) * 128, :], y, (ky,), ())
            self.transpose_to_fm(y, ky, 128, yt, kyt, 0, (0, 1, 2, 3), ctr)
            P.dma("sp", d["H1Ts_" + n_][:, :, sl * 128:(sl + 1) * 128].rearrange("c p t -> p c t"), yt, (kyt,), ())

    def stage5(self, s):
        P = self.P
        d = self.dram
        n_ = s.name
        P.barrier()
        P.reset_arena(self.const_end)
        g2_, kg = self.bcast_load(d["ln2_g"], D, "g2")
        b2_, kb = self.bcast_load(d["ln2_b"], D, "b2")
        tmp = self.ln_tmp()
        cw, kcw = P.tile([128, 44, 4], F32, "convw")
        P.dma("sp", cw, d["ffn_conv_l"], (), (kcw,))
        NB = 1024
        hT, khT = P.tile([128, 16, NB + 2], BF16, "hT")
        acc, kacc0 = P.tile([128, 8, D], F32, "acc")
        kacc = [kacc0 + "/%d" % i for i in range(8)]
        GC = 2
        w1 = [P.tile([128, 16, 2 * GC * 128], BF16, "w1_%d" % i) for i in range(2)]
        w2 = [P.tile([128, GC, D], BF16, "w2_%d" % i) for i in range(2)]
        gT = [P.tile([128, 514], F32, "gT%d" % i) for i in range(2)]
        t1 = [P.tile([128, 512], F32, "t1%d" % i) for i in range(2)]
        aT = [P.tile([128, GC, NB], BF16, "aT%d" % i) for i in range(2)]
        xr = [P.tile([128, D], F32, "xr%d" % i) for i in range(1)]
        yo = [P.tile([128, D], F32, "yo%d" % i) for i in range(1)]
        gi = 0
        hc = 0
        if s.sliced:
            h1t_d = d["H1Ts_" + n_]; h1_d = d["H1s_" + n_]; Ltot = 1280
            blocks = [(128, 0)]
            ms, kms = P.tile([128, s.ng + 1], F32, "msel")
            P.dma("sp", ms, d["msel_" + n_], (), (kms,))
        else:
            h1t_d = d["H1T_" + n_]; h1_d = d["H1_" + n_]; Ltot = s.L
            blocks = [(NM + b0, b0) for b0 in range(0, s.T, NB)]
        for ts0, orow in blocks:
            ntl = NB // 128
            lo = ts0 - 1
            hi = min(Ltot, ts0 + NB + 1)
            P.dma("sp", hT[:, :, 0:hi - lo], h1t_d[:, :, lo:hi].rearrange("c p t -> p c t"), (), (khT,))
            if hi - lo < NB + 2:
                P.memset("pool", hT[:, :, NB + 1:NB + 2], 0.0, (khT,))
            if s.sliced:
                P.ts("pool", hT[:, :, NB + 1:NB + 2], hT[:, :, NB + 1:NB + 2], ms[:, s.ng:s.ng + 1], None, ALU.mult, None,
                     (khT, kms), (khT,))
            for g in range(44 // GC):
                w1_, kw1 = w1[gi % 2]; w2_, kw2 = w2[gi % 2]; a_, ka_ = aT[gi % 2]
                gi += 1
                c0 = g * GC * 128
                self.load_w(w1_[:, :, 0:GC * 128], kw1, d["ffn_w_in"], D, c0, GC * 128)
                self.load_w(w1_[:, :, GC * 128:2 * GC * 128], kw1, d["ffn_w_in"], D, DFF + c0, GC * 128)
                P.dma("pool", w2_, d["ffn_w_out"][c0:c0 + GC * 128, :].rearrange("(c p) n -> p c n", p=128), (), (kw2,))
                for cl in range(GC):
                    fc = g * GC + cl
                    for hh in range(2):
                        g_, kg_ = gT[hc % 2]; t_, kt_ = t1[hc % 2]
                        hc += 1
                        bg = 0 if hh == 0 else 3
                        pg = P.bank(bg); kpg = "psb%d" % bg
                        ph = P.bank(1); pu = P.bank(2)
                        cb = hh * 512
                        for kc in range(16):
                            P.mm(pg[:, :], w1_[:, kc, cl * 128:(cl + 1) * 128], hT[:, kc, cb:cb + 512], kc == 0, kc == 15,
                                 (kw1, khT), (kpg,))
                        for kc in range(16):
                            P.mm(ph[:, 0:2], w1_[:, kc, cl * 128:(cl + 1) * 128], hT[:, kc, cb + 512:cb + 514], kc == 0, kc == 15,
                                 (kw1, khT), ("psb1",))
                        for kc in range(16):
                            P.mm(pu[:, :], w1_[:, kc, (GC + cl) * 128:(GC + cl + 1) * 128], hT[:, kc, cb + 1:cb + 513],
                                 kc == 0, kc == 15, (kw1, khT), ("psb2",))
                        P.copy("act", g_[:, 0:512], pg[:, :], (kpg,), (kg_,))
                        P.copy("act", g_[:, 512:514], ph[:, 0:2], ("psb1",), (kg_,))
                        P.act(t_[:, :], g_[:, 1:513], AF.Identity, (kg_, kcw), (kt_,), bias=cw[:, fc, 3:4], scale=cw[:, fc, 1:2])
                        P.stt("dve", t_[:, :], g_[:, 0:512], cw[:, fc, 0:1], t_[:, :], ALU.mult, ALU.add, (kg_, kcw, kt_), (kt_,))
                        P.stt("dve", t_[:, :], g_[:, 2:514], cw[:, fc, 2:3], t_[:, :], ALU.mult, ALU.add, (kg_, kcw, kt_), (kt_,))
                        P.act(t_[:, :], t_[:, :], AF.Gelu, (kt_,), (kt_,))
                        P.tt("dve", a_[:, cl, cb:cb + 512], t_[:, :], pu[:, :], ALU.mult, (kt_, "psb2"), (ka_,))
                for tl in range(ntl):
                    for hf in range(4):
                        pb = P.bank(4 + hf); kpb = "psb%d" % (4 + hf)
                        for cl in range(GC):
                            P.mm(pb[:, :], a_[:, cl, tl * 128:(tl + 1) * 128], w2_[:, cl, hf * 512:(hf + 1) * 512],
                                 cl == 0, cl == GC - 1, (ka_, kw2), (kpb,))
                        dst = acc[:, tl, hf * 512:(hf + 1) * 512]
                        if g == 0:
                            P.copy("dve" if hf % 2 else "act", dst, pb[:, :], (kpb,), (kacc[tl],))
                        else:
                            P.tt("dve", dst, dst, pb[:, :], ALU.add, (kpb, kacc[tl]), (kacc[tl],))
            for tl in range(ntl):
                x, kx = xr[0]; y, ky = yo[0]
                tq = ts0 + tl * 128
                P.dma("sp", x, h1_d[tq:tq + 128, :], (), (kx,))
                P.stt("dve", x, x, ALPHA, acc[:, tl, :], ALU.mult, ALU.add, (kx, kacc[tl]), (kx,))
                self.layer_norm(x, kx, 128, (g2_, b2_), (kg, kb), y, ky, LN_EPS, tmp)
                P.dma("sp", d["y_" + n_][orow + tl * 128:orow + tl * 128 + 128, :], y, (ky,), ())

    def build(self):
        self.setup_consts()
        for s in self.seqs:
            if 1 in self.run:
                self.stage1(s)
            if 2 in self.run:
                self.stage2(s)
            if 3 in self.run:
                self.stage3(s)
            if 4 in self.run:
                if s.sliced:
                    self.stage4_sliced(s)
                else:
                    self.stage4(s)
            if 5 in self.run:
                self.stage5(s)
        self.P.finish()
        return self.nc


def _bias_table(rpb):
    rpb = np.asarray(rpb, np.float32).reshape(HA, 15, 31)
    tab = np.full((HA, 5, 128, NKEY), NEG, np.float32)
    qc = np.arange(64)
    c0 = np.clip(qc - 8, 0, 48)
    kc = np.arange(64)
    colmask = (kc[None, :] >= c0[:, None]) & (kc[None, :] < c0[:, None] + 16)
    dc = np.clip(kc[None, :] - qc[:, None], -15, 15) + 15
    rel = {0: (0, 0), 1: (0, 0), 2: (0, 1), 3: (2, 2), 4: (2, 2)}
    for c in range(5):
        for qr in range(2):
            r_rel = 2 * c + qr
            r0 = rel[c][qr]
            for j in range(8):
                krow = r0 + j
                dr = krow - r_rel + 7
                blk = rpb[:, dr][:, dc]
                blk = np.where(colmask[None], blk, NEG)
                tab[:, c, qr * 64:(qr + 1) * 64, krow * 64:(krow + 1) * 64] = blk
        tab[:, c, :, WINR * 64:] = 0.0
    return tab


def _consts():
    ident = np.eye(128, dtype=np.float32)
    s = np.arange(128)[:, None]
    t = np.arange(128)[None, :]
    masks = np.stack([(s <= t), (s >= t), (s < t), (s <= t), (s > t), (s >= t)]).astype(np.float32)
    return ident, masks


def _common_inputs(inp):
    ident, masks = _consts()
    f = lambda a: np.ascontiguousarray(np.asarray(a, np.float32))
    m = {
        "meta_tokens": f(inp["meta_tokens"]),
        "emb_ln_g": f(inp["emb_ln_g"]), "emb_ln_b": f(inp["emb_ln_b"]),
        "ln1_g": f(inp["ln1_g"][0]), "ln1_b": f(inp["ln1_b"][0]),
        "ln2_g": f(inp["ln2_g"][0]), "ln2_b": f(inp["ln2_b"][0]),
        "w_in": f(inp["w_in"][0]),
        "bias_tab": _bias_table(inp["attn_rpb"][0]),
        "rwkv_mu": f(inp["rwkv_mu"][0]), "rwkv_w0": f(inp["rwkv_w0"][0]), "rwkv_w2": f(inp["rwkv_w2"][0]),
        "rwkv_a0": f(inp["rwkv_a0"][0]), "rwkv_a2": f(inp["rwkv_a2"][0]), "rwkv_g2": f(inp["rwkv_g2"][0]),
        "rwkv_k_k": f(inp["rwkv_k_k"][0]), "rwkv_k_a": f(inp["rwkv_k_a"][0]),
        "rwkv_r_k": f(inp["rwkv_r_k"][0]).reshape(DR),
        "rwkv_lnx_g": f(inp["rwkv_lnx_g"][0]), "rwkv_lnx_b": f(inp["rwkv_lnx_b"][0]),
        "w_out": f(inp["w_out"][0]), "ffn_w_in": f(inp["ffn_w_in"][0]),
        "ffn_conv_l": np.ascontiguousarray(np.concatenate([f(inp["ffn_conv_w"][0]), f(inp["ffn_conv_b"][0])[None]], 0)
                                           .reshape(4, 44, 128).transpose(2, 1, 0)),
        "ffn_w_out": f(inp["ffn_w_out"][0]),
        "c_ident": ident, "c_masks": masks, "c_zero": np.zeros((1, NSH), np.float32),
        "c_shift": np.eye(NM, 128, 128 - NM, dtype=np.float32),
    }
    return m


def slice_inputs(x, meta, c, ng):
    x = np.asarray(x, np.float32)
    prev = x[1024 * c - 128:1024 * c] if c > 0 else np.concatenate([np.zeros((128 - NM, D), np.float32), np.asarray(meta, np.float32)], 0)
    nxt = x[1024 * c + 1024:1024 * c + 1152] if c < ng - 1 else np.zeros((128, D), np.float32)
    xext = np.ascontiguousarray(np.concatenate([prev, x[1024 * c:1024 * c + 1024], nxt], 0))
    msel = np.zeros((128, ng + 1), np.float32)
    msel[:, c] = 1.0
    msel[:, ng] = 1.0 if c < ng - 1 else 0.0
    return xext, msel


def kernel(**inputs):
    xp = np.asarray(inputs["x_prompt"], np.float32)
    xs = np.asarray(inputs["x_sample"], np.float32)
    b = Builder([xs.shape[1], xp.shape[1]], sliced=[False, True])
    nc = b.build()
    common = _common_inputs(inputs)
    in_maps = []
    for c in range(8):
        m = dict(common)
        m["x_s0"] = np.ascontiguousarray(xs[c])
        m["x_s1"] = np.ascontiguousarray(xp[0])
        m["xext_s1"], m["msel_s1"] = slice_inputs(xp[0], inputs["meta_tokens"], c, 8)
        in_maps.append(m)
    res = run_bass_kernel_spmd(nc, in_maps, core_ids=list(range(8)))
    y_s = np.stack([res.results[c]["y_s0"] for c in range(8)], axis=0)
    y_p = np.concatenate([res.results[c]["y_s1"] for c in range(8)], axis=0)[None]
    return (y_p.astype(np.float32), y_s.astype(np.float32))
```

```python
import numpy as np
import ml_dtypes
import concourse.bass as bass
import concourse.mybir as mybir
from concourse.bass_utils import run_bass_kernel_spmd

F32 = mybir.dt.float32
BF16 = mybir.dt.bfloat16
U8 = mybir.dt.uint8
ALU = mybir.AluOpType
AF = mybir.ActivationFunctionType
AX = mybir.AxisListType

D = 2048
NM = 16
DA = 1024
HA = 8
DR = 1024
HR = 16
NSH = 3 * DR + 192
NIN = 3 * DA + NSH + 256
DFF = 5632
ALPHA = 2.0 ** 0.25
LN_EPS = 1e-5
GN_EPS = 64e-5
CDEC = -float(np.exp(-0.5))
NEG = -30000.0
WINR = 10
NKEY = WINR * 64 + NM


class Prog:
    ENG = ("sp", "act", "dve", "pool", "pe")

    def __init__(self, nc, n_dma_sems=40):
        self.nc = nc
        self.q = {e: [] for e in self.ENG}
        self.cnt = {e: 0 for e in self.ENG}
        self.known = {e: {} for e in self.ENG}
        self.last_w = {}
        self.readers = {}
        self.semh = {}
        for e in ("act", "dve", "pool", "pe"):
            self.semh[e] = nc.alloc_semaphore("sem_" + e)
        self.ndma = n_dma_sems
        for i in range(n_dma_sems):
            self.semh[("dma", i)] = nc.alloc_semaphore("sem_dma%d" % i)
        self.dma_tot = [0] * n_dma_sems
        self.rr = 0
        self.arena = nc.alloc_sbuf_tensor("arena", [128, 204 * 1024], U8)
        self.off = 0
        self.uid = 0
        self.psum = [nc.alloc_psum_tensor("psb%d" % i, [128, 512], F32) for i in range(8)]

    def reset_arena(self, off=0):
        self.off = off

    def tile(self, shape, dtype, name="t"):
        esz = 4 if dtype == F32 else 2
        n = int(np.prod(shape[1:]))
        nbytes = (n * esz + 31) // 32 * 32
        assert self.off + nbytes <= 204 * 1024, ("sbuf overflow", name, self.off, nbytes)
        ap = self.arena[:, self.off:self.off + n * esz].bitcast(dtype)
        self.off += nbytes
        if len(shape) == 3:
            ap = ap.rearrange("p (a b) -> p a b", a=shape[1])
        elif len(shape) == 4:
            ap = ap.rearrange("p (a b c) -> p a b c", a=shape[1], b=shape[2])
        self.uid += 1
        return ap, "%s#%d" % (name, self.uid)

    def bank(self, i, dtype=F32):
        ap = self.psum[i][:, :]
        if dtype != F32:
            ap = ap.bitcast(dtype)
        return ap

    def _waits(self, eng, reads, writes):
        need = {}

        def add(ev):
            if ev is None:
                return
            s, v = ev
            if need.get(s, 0) < v:
                need[s] = v
        for k in reads:
            add(self.last_w.get(k))
        for k in writes:
            add(self.last_w.get(k))
            for s, v in self.readers.get(k, {}).items():
                add((s, v))
        out = []
        for s, v in need.items():
            if s == "pe" and eng == "pe":
                continue
            if self.known[eng].get(s, 0) >= v:
                continue
            self.known[eng][s] = v
            out.append((s, v))
        return out

    def _record(self, ev, reads, writes):
        s, v = ev
        for k in reads:
            d = self.readers.setdefault(k, {})
            if d.get(s, 0) < v:
                d[s] = v
        for k in writes:
            self.last_w[k] = ev
            self.readers[k] = {}

    @staticmethod
    def _px(reads, writes):
        pr = tuple(k for k in reads if k.startswith("psb"))
        if pr:
            reads = tuple(k for k in reads if not k.startswith("psb"))
            writes = tuple(writes) + pr
        return reads, writes

    def op(self, eng, fn, reads=(), writes=()):
        reads, writes = self._px(reads, writes)
        waits = self._waits(eng, reads, writes)
        self.cnt[eng] += 1
        ev = (eng, self.cnt[eng])
        self.q[eng].append((waits, fn, eng, 1))
        self._record(ev, reads, writes)

    def dma(self, eng, out, in_, reads=(), writes=()):
        i = self.rr
        self.rr = (self.rr + 1) % self.ndma
        s = ("dma", i)
        waits = self._waits(eng, reads, writes)
        if self.dma_tot[i] > 0 and self.known[eng].get(s, 0) < self.dma_tot[i]:
            self.known[eng][s] = self.dma_tot[i]
            waits.append((s, self.dma_tot[i]))
        self.dma_tot[i] += 16
        ev = (s, self.dma_tot[i])
        self.q[eng].append((waits, lambda e, o=out, a=in_: e.dma_start(out=o, in_=a), s, 16))
        self._record(ev, reads, writes)

    def raw(self, eng, fn, reads=(), writes=()):
        waits = self._waits(eng, reads, writes)
        self.q[eng].append((waits, fn, None, -1))

    def dma_dyn(self, eng, mk, reads=(), writes=()):
        i = self.rr
        self.rr = (self.rr + 1) % self.ndma
        s = ("dma", i)
        waits = self._waits(eng, reads, writes)
        if self.dma_tot[i] > 0 and self.known[eng].get(s, 0) < self.dma_tot[i]:
            self.known[eng][s] = self.dma_tot[i]
            waits.append((s, self.dma_tot[i]))
        self.dma_tot[i] += 16
        ev = (s, self.dma_tot[i])

        def fn(e, mk=mk):
            o, a = mk()
            return e.dma_start(out=o, in_=a)
        self.q[eng].append((waits, fn, s, 16))
        self._record(ev, reads, writes)

    def barrier(self):
        for e in self.ENG:
            waits = []
            for s in ("act", "dve", "pool", "pe"):
                if s != e and self.cnt[s] > self.known[e].get(s, 0):
                    self.known[e][s] = self.cnt[s]
                    waits.append((s, self.cnt[s]))
            for i in range(self.ndma):
                s = ("dma", i)
                if self.dma_tot[i] > self.known[e].get(s, 0):
                    self.known[e][s] = self.dma_tot[i]
                    waits.append((s, self.dma_tot[i]))
            if waits:
                self.q[e].append((waits, None, None, 0))

    def finish(self):
        self.barrier()
        nc = self.nc
        semh = self.semh
        q = self.q
        with nc.Block() as block:
            def mk(name):
                def body(e):
                    for waits, fn, s, amt in q[name]:
                        for ws, wv in waits:
                            e.wait_ge(semh[ws], wv)
                        if fn is not None:
                            if amt < 0:
                                fn(e)
                            else:
                                fn(e).then_inc(semh[s], amt)
                return body
            block.sync(mk("sp"))
            block.scalar(mk("act"))
            block.vector(mk("dve"))
            block.gpsimd(mk("pool"))
            block.tensor(mk("pe"))

    def mm(self, out, lhsT, rhs, start, stop, reads, writes):
        self.op("pe", lambda e: e.matmul(out, lhsT, rhs, start=start, stop=stop), reads, writes)

    def tr(self, out, in_, ident, reads, writes):
        self.op("pe", lambda e: e.transpose(out, in_, ident), reads, writes)

    def act(self, out, in_, func, reads, writes, bias=None, scale=None, accum=None):
        kw = {}
        if bias is not None:
            kw["bias"] = bias
        if scale is not None:
            kw["scale"] = scale
        if accum is not None:
            kw["accum_out"] = accum
        self.op("act", lambda e: e.activation(out, in_, func, **kw), reads, writes)

    def tt(self, eng, out, a, b, op, reads, writes):
        self.op(eng, lambda e: e.tensor_tensor(out, a, b, op), reads, writes)

    def ts(self, eng, out, a, s1, s2, op0, op1, reads, writes):
        if s2 is None:
            self.op(eng, lambda e: e.tensor_scalar(out, a, s1, None, op0), reads, writes)
        else:
            self.op(eng, lambda e: e.tensor_scalar(out, a, s1, s2, op0, op1), reads, writes)

    def stt(self, eng, out, a, sc, b, op0, op1, reads, writes):
        self.op(eng, lambda e: e.scalar_tensor_tensor(out, a, sc, b, op0, op1), reads, writes)

    def copy(self, eng, out, in_, reads, writes):
        if eng == "act":
            self.op("act", lambda e: e.copy(out, in_), reads, writes)
        else:
            self.op(eng, lambda e: e.tensor_copy(out, in_), reads, writes)

    def reduce(self, eng, out, in_, op, reads, writes):
        self.op(eng, lambda e: e.tensor_reduce(out, in_, AX.X, op), reads, writes)

    def recip(self, eng, out, in_, reads, writes):
        self.op(eng, lambda e: e.reciprocal(out, in_), reads, writes)

    def memset(self, eng, ap, val, writes):
        self.op(eng, lambda e: e.memset(ap, val), (), writes)


class Seq:
    def __init__(self, name, T, sliced=False):
        self.name = name
        self.T = T
        self.sliced = sliced
        self.ng = T // 1024
        self.L = T + NM
        self.rows = T // 64
        self.ntile = 1 + T // 128

    def tile_range(self, i):
        if i == 0:
            return 0, NM
        return NM + (i - 1) * 128, 128


class Builder:
    def __init__(self, seq_T, debug=False, stages=5, run=(1, 2, 3, 4, 5), sliced=None):
        self.debug = debug
        self.run = run
        self.stages = stages
        nc = bass.Bass("TRN2", target_bir_lowering=False)
        self.nc = nc
        self.P = Prog(nc)
        sliced = sliced or [False] * len(seq_T)
        self.seqs = [Seq("s%d" % i, T, sl) for i, (T, sl) in enumerate(zip(seq_T, sliced))]
        self.dram = {}
        self._declare_io()

    def din(self, name, shape, dtype=F32):
        ap = self.nc.dram_tensor(name, list(shape), dtype, kind="ExternalInput").ap()
        self.dram[name] = ap
        return ap

    def dscr(self, name, shape, dtype=F32):
        kind = "ExternalOutput" if self.debug else "Internal"
        ap = self.nc.dram_tensor(name, list(shape), dtype, kind=kind).ap()
        self.dram[name] = ap
        return ap

    def _declare_io(self):
        for s in self.seqs:
            self.din("x_" + s.name, [s.T, D])
            self.nc_out = None
        for s in self.seqs:
            ap = self.nc.dram_tensor("y_" + s.name, [1024 if s.sliced else s.T, D], F32, kind="ExternalOutput").ap()
            self.dram["y_" + s.name] = ap
            if s.sliced:
                self.din("xext_" + s.name, [1280, D])
                self.din("msel_" + s.name, [128, s.ng + 1])
        self.din("c_shift", [NM, 128])
        self.din("meta_tokens", [NM, D])
        for nm in ("emb_ln_g", "emb_ln_b", "ln1_g", "ln1_b", "ln2_g", "ln2_b"):
            self.din(nm, [D])
        self.din("w_in", [D, NIN])
        self.din("bias_tab", [HA, 5, 128, NKEY])
        self.din("rwkv_mu", [2, NSH])
        self.din("rwkv_w0", [2, DR])
        self.din("rwkv_w2", [2, 96, DR])
        self.din("rwkv_a0", [2, DR])
        self.din("rwkv_a2", [2, 96, DR])
        self.din("rwkv_g2", [256, DR])
        for nm in ("rwkv_k_k", "rwkv_k_a", "rwkv_r_k", "rwkv_lnx_g", "rwkv_lnx_b"):
            self.din(nm, [DR])
        self.din("w_out", [D, D])
        self.din("ffn_w_in", [D, 2 * DFF])
        self.din("ffn_conv_l", [128, 44, 4])
        self.din("ffn_w_out", [DFF, D])
        self.din("c_ident", [128, 128])
        self.din("c_masks", [6, 128, 128])
        self.din("c_zero", [1, NSH])
        for s in self.seqs:
            n = s.name
            self.dscr("H0_" + n, [s.L, D])
            self.dscr("QT_" + n, [HA, 128, s.L], BF16)
            self.dscr("KT_" + n, [HA, 128, s.L], BF16)
            self.dscr("V_" + n, [s.L, DA], BF16)
            self.dscr("P_" + n, [s.L, NSH])
            self.dscr("G_" + n, [s.L, DR])
            self.dscr("YF_" + n, [s.L, DR])
            self.dscr("BF_" + n, [s.L, DR])
            self.dscr("OT_" + n, [16, 128, s.L], BF16)
            self.dscr("H1_" + n, [s.L, D])
            self.dscr("H1T_" + n, [16, 128, s.L], BF16)
            if s.sliced:
                self.dscr("OTOK_" + n, [s.L, D], BF16)
                self.dscr("H1s_" + n, [1280, D])
                self.dscr("H1Ts_" + n, [16, 128, 1280], BF16)

    def setup_consts(self):
        P = self.P
        d = self.dram
        self.ident_f, k1 = P.tile([128, 128], F32, "identf")
        P.dma("sp", self.ident_f, d["c_ident"], (), (k1,))
        self.ident_b, k2 = P.tile([128, 128], BF16, "identb")
        P.copy("dve", self.ident_b, self.ident_f, (k1,), (k2,))
        self.kid = (k1, k2)
        self.const_end = P.off

    def bcast_load(self, name_ap, n, nm):
        P = self.P
        t, k = P.tile([128, n], F32, nm)
        P.dma("sp", t, name_ap.partition_broadcast(128), (), (k,))
        return t, k

    def layer_norm(self, x, kx, n, gb, kgb, out, kout, eps, tmp):
        P = self.P
        (st, kst), (mv, kmv), (rs, krs) = tmp
        g, b = gb
        for c in range(4):
            P.op("dve", lambda e, c=c: e.bn_stats(st[:n, c, :], x[:n, c * 512:(c + 1) * 512]), (kx,), (kst,))
        P.op("dve", lambda e: e.bn_aggr(mv[:n, :], st[:n].rearrange("p a b -> p (a b)")), (kst,), (kmv,))
        P.act(rs[:n, :], mv[:n, 1:2], AF.Sqrt, (kmv,), (krs,), bias=eps)
        P.op("dve", lambda e: e.reciprocal(rs[:n, :], rs[:n, :]), (krs,), (krs,))
        P.ts("dve", x[:n, :], x[:n, :], mv[:n, 0:1], rs[:n, 0:1], ALU.subtract, ALU.mult, (kx, kmv, krs), (kx,))
        P.tt("pool", x[:n, :], x[:n, :], g[:n, :], ALU.mult, (kx,) + kgb, (kx,))
        P.tt("dve", out[:n, :], x[:n, :], b[:n, :], ALU.add, (kx,) + kgb, (kout,))

    def ln_tmp(self):
        P = self.P
        return (P.tile([128, 4, 6], F32, "bnst"), P.tile([128, 2], F32, "bnmv"), P.tile([128, 1], F32, "rstd"))

    def transpose_to_fm(self, src, ksrc, n, dst, kdst, col0, banks, ctr):
        P = self.P
        for g4 in range(4):
            bi = banks[(ctr[0]) % len(banks)]
            ctr[0] += 1
            pb = P.bank(bi)
            kb = "psb%d" % bi
            for j in range(4):
                kc = g4 * 4 + j
                P.tr(pb[:, j * 128:j * 128 + n], src[:n, kc * 128:(kc + 1) * 128], self.ident_f[:n, :n],
                     (ksrc, self.kid[0]), (kb,))
            eng = "act" if g4 % 2 == 0 else "dve"
            P.copy(eng, dst[:, g4 * 4:g4 * 4 + 4, col0:col0 + n],
                   pb.rearrange("p (a b) -> p a b", a=4)[:, :, :n], (kb,), (kdst,))

    def load_w(self, wt, kw, src, rows, col0, ncols):
        P = self.P
        nk = rows // 128
        for k0 in range(0, nk, 4):
            k1 = min(nk, k0 + 4)
            P.dma("pool", wt[:, k0:k1, :ncols],
                  src[k0 * 128:k1 * 128, col0:col0 + ncols].rearrange("(kc p) n -> p kc n", p=128), (), (kw,))

    def stage1(self, s):
        P = self.P
        d = self.dram
        n_ = s.name
        P.barrier()
        P.reset_arena(self.const_end)
        g0, kg = self.bcast_load(d["emb_ln_g"], D, "g0")
        b0, kb = self.bcast_load(d["emb_ln_b"], D, "b0")
        tmp = self.ln_tmp()
        g2, kg2 = P.tile([128, 2, DR], BF16, "g2")
        self.load_w(g2, kg2, d["rwkv_g2"], 256, 0, DR)
        xt = [P.tile([128, D], F32, "xt%d" % i) for i in range(2)]
        ht = [P.tile([128, D], F32, "ht%d" % i) for i in range(2)]
        wbuf = [P.tile([128, 16, 512], BF16, "w%d" % i) for i in range(2)]
        stg = [P.tile([128, 512], F32, "stg%d" % i) for i in range(3)]
        stgb = [P.tile([128, 512], BF16, "stgb%d" % i) for i in range(2)]
        qkT = [P.tile([128, 4, 128], BF16, "qkT%d" % i) for i in range(2)]
        sgT = [P.tile([128, 2, 128], BF16, "sgT%d" % i) for i in range(2)]
        gst = [P.tile([128, DR], F32, "gst%d" % i) for i in range(2)]
        TB = 16
        h0T, kh0T = P.tile([128, 16, TB * 128 + NM], BF16, "h0T")
        groups = []
        for c0 in range(0, NIN, 512):
            groups.append((c0, min(512, NIN - c0)))
        tiles = list(range(s.ntile))
        blocks = [tiles[0:TB + 1]] + [tiles[i:i + TB] for i in range(TB + 1, s.ntile, TB)]
        ctr = [0]
        wi = 0
        ci = 0
        for blk in blocks:
            base = s.tile_range(blk[0])[0]
            def ln_phase(ti, slot):
                t0, n = s.tile_range(ti)
                x, kx = xt[slot]
                h, kh = ht[slot]
                if ti == 0:
                    P.dma("sp", x[:n, :], d["meta_tokens"], (), (kx,))
                else:
                    P.dma("sp", x[:n, :], d["x_" + n_][t0 - NM:t0 - NM + n, :], (), (kx,))
                self.layer_norm(x, kx, n, (g0, b0), (kg, kb), h, kh, LN_EPS, tmp)

            def tr_phase(ti, slot):
                t0, n = s.tile_range(ti)
                h, kh = ht[slot]
                if not s.sliced:
                    P.dma("sp", d["H0_" + n_][t0:t0 + n, :], h[:n, :], (kh,), ())
                self.transpose_to_fm(h, kh, n, h0T, kh0T, t0 - base, (0, 1), ctr)

            ln_phase(blk[0], ci % 2)
            for bi_, ti in enumerate(blk):
                if bi_ + 1 < len(blk):
                    ln_phase(blk[bi_ + 1], (ci + 1) % 2)
                tr_phase(ti, ci % 2)
                ci += 1
            if self.stages == 0:
                continue
            for gi, (c0, nc_) in enumerate(groups):
                w, kw = wbuf[wi % 2]
                wi += 1
                self.load_w(w, kw, d["w_in"], D, c0, nc_)
                for ti in blk:
                    t0, n = s.tile_range(ti)
                    bi = (2, 3, 6, 7)[ctr[0] % 4]
                    ctr[0] += 1
                    pb = P.bank(bi)
                    kpb = "psb%d" % bi
                    for kc in range(16):
                        P.mm(pb[:n, :nc_], h0T[:, kc, t0 - base:t0 - base + n], w[:, kc, :nc_],
                             kc == 0, kc == 15, (kh0T, kw), (kpb,))
                    ev = "act" if ctr[0] % 2 == 0 else "dve"
                    if self.stages == -1 or (self.stages == -2 and gi < 4) or (self.stages == -3 and gi >= 4) or (self.stages == -4 and gi < 12):
                        sf, ksf = stg[ctr[0] % 3]
                        P.copy(ev, sf[:n, :], pb[:n, :], (kpb,), (ksf,))
                        continue
                    if gi < 4:
                        sb, ksb = stgb[ctr[0] % 2]
                        if gi < 2:
                            P.act(sb[:n, :], pb[:n, :], AF.Copy, (kpb,), (ksb,), scale=128.0 ** -0.5)
                        else:
                            P.copy(ev, sb[:n, :], pb[:n, :], (kpb,), (ksb,))
                        tb = 4
                        ptb = P.bank(tb, BF16)
                        for j in range(4):
                            P.tr(ptb[:, j * 128:j * 128 + n], sb[:n, j * 128:(j + 1) * 128], self.ident_b[:n, :n],
                                 (ksb, self.kid[1]), ("psb4",))
                        qt, kqt = qkT[ctr[0] % 2]
                        P.copy("dve" if ev == "act" else "act", qt[:, :, :n],
                               ptb[:, 0:512].rearrange("p (a b) -> p a b", a=4)[:, :, :n], ("psb4",), (kqt,))
                        dst = d[("QT_" if gi < 2 else "KT_") + n_]
                        h0 = (gi % 2) * 4
                        P.dma("sp", dst[h0:h0 + 4, :, t0:t0 + n].rearrange("h p t -> p h t"), qt[:, :, :n], (kqt,), ())
                    elif gi < 6:
                        sb, ksb = stgb[ctr[0] % 2]
                        P.copy(ev, sb[:n, :], pb[:n, :], (kpb,), (ksb,))
                        P.dma("sp", d["V_" + n_][t0:t0 + n, (gi - 4) * 512:(gi - 3) * 512], sb[:n, :], (ksb,), ())
                    elif gi < 12:
                        sf, ksf = stg[ctr[0] % 3]
                        P.copy(ev, sf[:n, :], pb[:n, :], (kpb,), (ksf,))
                        P.dma("sp", d["P_" + n_][t0:t0 + n, (gi - 6) * 512:(gi - 5) * 512], sf[:n, :], (ksf,), ())
                    else:
                        sf, ksf = stg[ctr[0] % 3]
                        P.copy("dve", sf[:n, :192], pb[:n, :192], (kpb,), (ksf,))
                        P.dma("sp", d["P_" + n_][t0:t0 + n, 3072:3264], sf[:n, :192], (ksf,), ())
                        sb, ksb = stgb[ctr[0] % 2]
                        P.act(sb[:n, :256], pb[:n, 192:448], AF.Sigmoid, (kpb,), (ksb,))
                        ptb = P.bank(4, BF16)
                        for j in range(2):
                            P.tr(ptb[:, j * 128:j * 128 + n], sb[:n, j * 128:(j + 1) * 128], self.ident_b[:n, :n],
                                 (ksb, self.kid[1]), ("psb4",))
                        sg, ksg = sgT[ctr[0] % 2]
                        P.copy("dve", sg[:, :, :n], ptb[:, 0:256].rearrange("p (a b) -> p a b", a=2)[:, :, :n],
                               ("psb4",), (ksg,))
                        go, kgo = gst[ctr[0] % 2]
                        for hf in range(2):
                            pg = P.bank(5)
                            kpg = "psb5"
                            for j in range(2):
                                P.mm(pg[:n, :], sg[:, j, :n], g2[:, j, hf * 512:(hf + 1) * 512], j == 0, j == 1,
                                     (ksg, kg2), (kpg,))
                            P.copy("act" if hf == 0 else "dve", go[:n, hf * 512:(hf + 1) * 512], pg[:n, :], (kpg,), (kgo,))
                        P.dma("sp", d["G_" + n_][t0:t0 + n, :], go[:n, :], (kgo,), ())

    def stage2(self, s):
        P = self.P
        d = self.dram
        n_ = s.name
        for dr in (0, 1):
            P.barrier()
            P.reset_arena(self.const_end)
            self._rwkv_dir(s, dr)

    def _rwkv_dir(self, s, dr):
        P = self.P
        d = self.dram
        n_ = s.name
        Pd = d["P_" + n_]
        L = s.L
        mu, kmu = self.bcast_load(d["rwkv_mu"][dr], NSH, "mu")
        w0, kw0 = self.bcast_load(d["rwkv_w0"][dr], DR, "w0")
        a0, ka0 = self.bcast_load(d["rwkv_a0"][dr], DR, "a0")
        kkp, kkkp = self.bcast_load(d["rwkv_k_k"], DR, "k_k")
        kap, kkap = self.bcast_load(d["rwkv_k_a"], DR, "k_a")
        rkp, krkp = self.bcast_load(d["rwkv_r_k"], DR, "r_k")
        w2, kw2 = P.tile([128, DR], F32, "w2")
        a2, ka2 = P.tile([128, DR], F32, "a2")
        P.dma("sp", w2[:96, :], d["rwkv_w2"][dr], (), (kw2,))
        P.dma("sp", a2[:96, :], d["rwkv_a2"][dr], (), (ka2,))
        tri, ktri = P.tile([128, 128], F32, "tri")
        P.dma("sp", tri, d["c_masks"][dr], (), (ktri,))
        ones, kones = P.tile([128, 128], F32, "ones")
        P.memset("pool", ones, 1.0, (kones,))
        m4, km4 = P.tile([128, 4, 128], F32, "mask4")
        for q in range(4):
            P.dma("sp", m4[:, q, :], d["c_masks"][2 + 2 * dr + (q % 2)], (), (km4,))
        suT, ksuT = P.tile([128, 128], F32, "suT")
        P.dma("sp", suT, d["c_masks"][2 + 2 * (1 - dr)], (), (ksuT,))
        if dr == 1:
            lg, klg = self.bcast_load(d["rwkv_lnx_g"], DR, "lnxg")
            lb, klb = self.bcast_load(d["rwkv_lnx_b"], DR, "lnxb")
        Sf, kSf = P.tile([128, 8, 64], F32, "Sf")
        Sb, kSb = P.tile([128, 8, 64], BF16, "Sb")
        P.memset("pool", Sf, 0.0, (kSf,))
        P.memset("pool", Sb, 0.0, (kSb,))
        kSfh = [[kSf + "/%d/%d" % (j, h) for h in range(2)] for j in range(8)]
        kSbh = [[kSb + "/%d/%d" % (j, h) for h in range(2)] for j in range(8)]
        for j in range(8):
            for h in range(2):
                P.last_w[kSfh[j][h]] = P.last_w[kSf]
                P.last_w[kSbh[j][h]] = P.last_w[kSb]
        pt = [P.tile([128, NSH], F32, "p%d" % i) for i in range(1)]
        pn = [P.tile([128, NSH], F32, "pn%d" % i) for i in range(1)]
        T = lambda nm, dt=F32: P.tile([128, DR], dt, nm)
        sg, ksg = T("sg"); av, kav = T("a"); cum, kcum = T("cum"); e1, ke1 = T("e1"); e2, ke2 = T("e2")
        kk, kkk = T("kk"); kq, kkq = T("kq"); ka_, kka_ = T("ka"); tmp, ktmp = T("tmp"); bon, kbon = T("bon")
        Y, kY = T("Y")
        At, kAt = T("At", BF16); Bt, kBt = T("Bt", BF16); Kt, kKt = T("Kt", BF16); Rt, kRt = T("Rt", BF16)
        BW, kBW = T("BW", BF16); KW, kKW = T("KW", BF16); Vb, kVb = T("Vb", BF16)
        ART, kART = P.tile([128, 8, 256], BF16, "ART")
        BTt, kBTt = P.tile([128, 8, 128], BF16, "BT")
        KTt, kKTt = P.tile([128, 8, 128], BF16, "KT")
        thT, kthT = P.tile([128, 2, 128], F32, "thT")
        th, kth = P.tile([128, 192], F32, "th")
        sm, ksm = P.tile([128, 4, 16], F32, "small")
        wc, kwc = P.tile([128, 8], F32, "wc")
        if dr == 1:
            yf, kyf = e1, ke1
            gt, kgt = e2, ke2
            ob, kob = T("ob", BF16)
            oT, koT = P.tile([128, 8, 128], BF16, "oT")
        M4a, kM4a0 = P.tile([128, 16, 4, 128], BF16, "M4a")
        kM4 = [kM4a0 + "/%d" % i for i in range(16)]
        NTa, kNTa0 = P.tile([128, 16, 128], BF16, "NTa")
        kNT = [kNTa0 + "/%d" % g for g in range(4)]
        Ta = []
        PPa = []
        for i in range(2):
            t_, k_ = P.tile([128, 16, 128], BF16, "Ta%d" % i)
            Ta.append((t_, [k_ + "/%d" % g for g in range(4)]))
            t_, k_ = P.tile([128, 2, 16, 128], BF16, "PPa%d" % i)
            PPa.append((t_, [[k_ + "/%d/%d" % (w, g) for g in range(4)] for w in range(2)]))
        Xa, kXa0 = P.tile([128, 16, 64], BF16, "Xa")
        kXa = [kXa0 + "/%d" % b for b in range(2)]
        Ua, kUa0 = P.tile([128, 16, 64], BF16, "Ua")
        kUa = [kUa0 + "/%d" % b for b in range(2)]
        id4, kid4 = P.tile([128, 4, 128], BF16, "id4")
        for q in range(4):
            P.copy("pool", id4[:, q, :], self.ident_b, (self.kid[1],), (kid4,))
        order = list(range(s.ntile)) if dr == 0 else list(range(s.ntile - 1, -1, -1))
        tctr = [0]
        v3 = lambda ap: ap.rearrange("p (h c) -> p h c", h=16)
        p, kp = pt[0]
        q_, kq_ = pn[0]
        f, kf = q_, kq_
        r_ = f[:, 0:DR]; k_ = f[:, DR:2 * DR]; v_ = f[:, 2 * DR:3 * DR]

        def prep1(ti):
            t0, n = s.tile_range(ti)
            P.dma("sp", p[:n, :], Pd[t0:t0 + n, :], (), (kp,))
            if dr == 0:
                if t0 == 0:
                    P.dma("sp", q_[0:1, :], d["c_zero"], (), (kq_,))
                    P.dma("sp", q_[1:n, :], Pd[0:n - 1, :], (), (kq_,))
                else:
                    P.dma("sp", q_[:n, :], Pd[t0 - 1:t0 - 1 + n, :], (), (kq_,))
            else:
                if t0 + n == L:
                    P.dma("sp", q_[n - 1:n, :], d["c_zero"], (), (kq_,))
                    P.dma("sp", q_[0:n - 1, :], Pd[t0 + 1:t0 + n, :], (), (kq_,))
                else:
                    P.dma("sp", q_[:n, :], Pd[t0 + 1:t0 + 1 + n, :], (), (kq_,))
            yield
            P.tt("dve", q_[:n, :], q_[:n, :], p[:n, :], ALU.subtract, (kq_, kp), (kq_,))
            yield
            P.tt("pool", q_[:n, :], q_[:n, :], mu[:n, :], ALU.mult, (kq_, kmu), (kq_,))
            yield
            P.tt("dve", q_[:n, :], q_[:n, :], p[:n, :], ALU.add, (kq_, kp), (kq_,))
            yield
            P.act(th[:n, 0:96], f[:n, 3 * DR:3 * DR + 96], AF.Tanh, (kf,), (kth,))
            P.copy("pool", th[:n, 96:192], f[:n, 3 * DR + 96:3 * DR + 192], (kf,), (kth,))
            pb0 = P.bank(0)
            for q in range(2):
                P.tr(pb0[:96, q * 128:q * 128 + n], th[:n, q * 96:(q + 1) * 96], self.ident_f[:n, :n],
                     (kth, self.kid[0]), ("psb0",))
            P.copy("act", thT[:96, :, :n], pb0[:96, 0:256].rearrange("p (a b) -> p a b", a=2)[:, :, :n], ("psb0",), (kthT,))
            yield
            for which, (wm, kwm, bias, kbias, dst, kdst) in enumerate(((w2, kw2, w0, kw0, sg, ksg), (a2, ka2, a0, ka0, av, kav))):
                for hf in range(2):
                    pb = P.bank(hf); kpb = "psb%d" % hf
                    P.mm(pb[:n, :], thT[:96, which, :n], wm[:96, hf * 512:(hf + 1) * 512], True, True, (kthT, kwm), (kpb,))
                    P.tt("dve", dst[:n, hf * 512:(hf + 1) * 512], pb[:n, :], bias[:n, hf * 512:(hf + 1) * 512], ALU.add,
                         (kpb, kbias), (kdst,))
                    yield
                P.act(dst[:n, :], dst[:n, :], AF.Sigmoid, (kdst,), (kdst,))
                yield
            for hf in range(2):
                pb = P.bank(hf); kpb = "psb%d" % hf
                P.mm(pb[:n, :], tri[:n, :n], sg[:n, hf * 512:(hf + 1) * 512], True, True, (ktri, ksg), (kpb,))
                P.copy("act", cum[:n, hf * 512:(hf + 1) * 512], pb[:n, :], (kpb,), (kcum,))
                yield
            P.tt("pool", kk[:n, :], k_[:n, :], kkp[:n, :], ALU.mult, (kf, kkkp), (kkk,))
            yield
            P.tt("pool", tmp[:n, :], kk[:n, :], kk[:n, :], ALU.mult, (kkk,), (ktmp,))
            yield
            P.reduce("dve", sm[:n, 0, :], v3(tmp[:n, :]), ALU.add, (ktmp,), (ksm,))
            P.act(sm[:n, 0, :], sm[:n, 0, :], AF.Sqrt, (ksm,), (ksm,))
            P.ts("dve", sm[:n, 0, :], sm[:n, 0, :], 1e-12, None, ALU.max, None, (ksm,), (ksm,))
            P.recip("dve", sm[:n, 0, :], sm[:n, 0, :], (ksm,), (ksm,))
            yield
            P.tt("dve", v3(kk[:n, :]), v3(kk[:n, :]), sm[:n, 0, :].unsqueeze(2).to_broadcast([n, 16, 64]), ALU.mult,
                 (kkk, ksm), (kkk,))
            yield
            P.stt("dve", kq[:n, :], av[:n, :], -1.0, kap[:n, :], ALU.add, ALU.mult, (kav, kkap), (kkq,))
            yield
            P.stt("dve", kq[:n, :], kq[:n, :], 1.0, k_[:n, :], ALU.add, ALU.mult, (kkq, kf), (kkq,))
            yield
            P.tt("pool", ka_[:n, :], kk[:n, :], av[:n, :], ALU.mult, (kkk, kav), (kka_,))
            yield

        def prep2(ti):
            t0, n = s.tile_range(ti)
            for hf in range(2):
                pb = P.bank(2 + hf); kpb = "psb%d" % (2 + hf)
                P.mm(pb[:n, :], ones[:n, :n], sg[:n, hf * 512:(hf + 1) * 512], True, True, (kones, ksg), (kpb,))
                P.tt("dve", e2[:n, hf * 512:(hf + 1) * 512], pb[:n, :], cum[:n, hf * 512:(hf + 1) * 512], ALU.subtract,
                     (kpb, kcum), (ke2,))
            pb0 = P.bank(0)
            for j in range(8):
                P.mm(pb0[:, j:j + 1], sg[:n, j * 128:(j + 1) * 128], ones[:n, 0:1], True, True, (ksg, kones), ("psb0",))
            P.act(wc[:, :], pb0[:, 0:8], AF.Exp, ("psb0",), (kwc,), scale=CDEC)
            P.tt("pool", tmp[:n, :], r_[:n, :], kq[:n, :], ALU.mult, (kf, kkq), (ktmp,))
            P.tt("pool", tmp[:n, :], tmp[:n, :], rkp[:n, :], ALU.mult, (ktmp, krkp), (ktmp,))
            P.reduce("dve", sm[:n, 1, :], v3(tmp[:n, :]), ALU.add, (ktmp,), (ksm,))
            P.tt("dve", v3(bon[:n, :]), v3(v_[:n, :]), sm[:n, 1, :].unsqueeze(2).to_broadcast([n, 16, 64]), ALU.mult,
                 (kf, ksm), (kbon,))
            if dr == 0:
                P.dma("sp", d["BF_" + n_][t0:t0 + n, :], bon[:n, :], (kbon,), ())
            P.act(e1[:n, :], cum[:n, :], AF.Exp, (kcum,), (ke1,), scale=CDEC)
            P.tt("dve", Rt[:n, :], r_[:n, :], e1[:n, :], ALU.mult, (kf, ke1), (kRt,))
            P.act(e1[:n, :], cum[:n, :], AF.Exp, (kcum,), (ke1,), scale=-CDEC)
            P.tt("pool", Bt[:n, :], ka_[:n, :], e1[:n, :], ALU.mult, (kka_, ke1), (kBt,))
            P.tt("dve", Kt[:n, :], kq[:n, :], e1[:n, :], ALU.mult, (kkq, ke1), (kKt,))
            P.act(e2[:n, :], e2[:n, :], AF.Exp, (ke2,), (ke2,), scale=CDEC)
            P.tt("pool", BW[:n, :], ka_[:n, :], e2[:n, :], ALU.mult, (kka_, ke2), (kBW,))
            P.tt("dve", KW[:n, :], kq[:n, :], e2[:n, :], ALU.mult, (kkq, ke2), (kKW,))
            P.tt("dve", cum[:n, :], cum[:n, :], sg[:n, :], ALU.subtract, (kcum, ksg), (kcum,))
            P.act(e1[:n, :], cum[:n, :], AF.Exp, (kcum,), (ke1,), scale=CDEC)
            P.stt("dve", At[:n, :], kk[:n, :], -1.0, e1[:n, :], ALU.mult, ALU.mult, (kkk, ke1), (kAt,))
            P.copy("pool", Vb[:n, :], v_[:n, :], (kf,), (kVb,))
            for xi, (src, ksrc) in enumerate(((At, kAt), (Rt, kRt), (Bt, kBt), (Kt, kKt))):
                bi = tctr[0] % 4
                tctr[0] += 1
                ptb = P.bank(bi, BF16)
                kptb = "psb%d" % bi
                for j in range(8):
                    P.tr(ptb[:, j * 128:j * 128 + n], src[:n, j * 128:(j + 1) * 128], self.ident_b[:n, :n],
                         (ksrc, self.kid[1]), (kptb,))
                srcv = ptb.rearrange("p (a b) -> p a b", a=8)[:, :, :n]
                if xi == 0:
                    P.copy("act", ART[:, :, 0:n], srcv, (kptb,), (kART,))
                elif xi == 1:
                    P.copy("dve", ART[:, :, 128:128 + n], srcv, (kptb,), (kART,))
                elif xi == 2:
                    P.copy("act", BTt[:, :, :n], srcv, (kptb,), (kBTt,))
                else:
                    P.copy("dve", KTt[:, :, :n], srcv, (kptb,), (kKTt,))

        def units(ti, gen):
            t0, n = s.tile_range(ti)

            def pump():
                next(gen, None)
            nsq = int(np.log2(n)) - 1
            HD = [dict(hd=hd, j=hd // 2, hb=64 * (hd % 2), col=hd * 64, g=hd // 4, sl=hd % 4) for hd in range(16)]
            bk = lambda g: P.bank(4 + g)
            kbk = lambda g: "psb%d" % (4 + g)
            for H in HD:
                if H["hd"] % 4 == 0:
                    pump()
                hd, j, hb = H["hd"], H["j"], H["hb"]
                b_ = bk(hd % 4); kb_ = kbk(hd % 4)
                P.mm(b_[:n, 0:256], BTt[hb:hb + 64, j, :n], ART[hb:hb + 64, j, :], True, True, (kBTt, kART), (kb_,))
                P.mm(b_[:n, 256:512], KTt[hb:hb + 64, j, :n], ART[hb:hb + 64, j, :], True, True, (kKTt, kART), (kb_,))
                P.tt("dve", M4a[:n, hd, :, :n], b_[:n, :].rearrange("p (a b) -> p a b", a=4)[:, :, :n], m4[:n, :, :n], ALU.mult,
                     (kb_, km4), (kM4[hd],))
            for H in HD:
                if H["hd"] % 4 == 0:
                    pump()
                hd, j, hb = H["hd"], H["j"], H["hb"]
                nb = (hd % 2) * 2 + hd // 8
                sl = (hd % 8) // 2
                P.mm(bk(nb)[:n, sl * 128:sl * 128 + n], ART[hb:hb + 64, j, 0:n], BTt[hb:hb + 64, j, :n], True, True,
                     (kART, kBTt), (kbk(nb),))
                if sl == 3:
                    base = (hd // 8) * 8 + hd % 2
                    P.tt("dve", NTa[:n, base:base + 7:2, :n], bk(nb)[:n, :].rearrange("p (a b) -> p a b", a=4)[:, :, :n],
                         suT[:n, :n].unsqueeze(1).to_broadcast([n, 4, n]), ALU.mult, (kbk(nb), ksuT), (kNT[nb],))
            T0, kT0 = Ta[0]
            for g in range(4):
                P.tt("pool", T0[:n, g * 4:g * 4 + 4, :n], M4a[:n, g * 4:g * 4 + 4, 0, :n], id4[:n, :, :n], ALU.add,
                     tuple(kM4[g * 4:g * 4 + 4]) + (kid4,), (kT0[g],))
            Pget = lambda hd: (M4a[:, hd, 0, :], kM4[hd])
            PTget = lambda hd: (NTa[:, hd, :], kNT[(hd % 2) * 2 + hd // 8])
            tcur = 0
            for it in range(1, nsq + 1):
                last = it == nsq
                PPn, kPPn = PPa[it % 2]
                for H in HD:
                    if H["hd"] % 4 == 0:
                        pump()
                    hd, g, sl = H["hd"], H["g"], H["sl"]
                    Pc, kPc = Pget(hd); PTc, kPTc = PTget(hd)
                    P.mm(bk(g)[:n, sl * 128:sl * 128 + n], Pc[:n, :n], PTc[:n, :n], True, True, (kPc, kPTc), (kbk(g),))
                    if sl == 3:
                        P.copy("act", PPn[:n, 1, g * 4:g * 4 + 4, :n], bk(g)[:n, :].rearrange("p (a b) -> p a b", a=4)[:, :, :n],
                               (kbk(g),), (kPPn[1][g],))
                if not last:
                    for H in HD:
                        if H["hd"] % 4 == 0:
                            pump()
                        hd, g, sl = H["hd"], H["g"], H["sl"]
                        Pc, kPc = Pget(hd); PTc, kPTc = PTget(hd)
                        P.mm(bk(g)[:n, sl * 128:sl * 128 + n], PTc[:n, :n], Pc[:n, :n], True, True, (kPc, kPTc), (kbk(g),))
                        if sl == 3:
                            P.copy("act" if g % 2 else "dve", PPn[:n, 0, g * 4:g * 4 + 4, :n],
                                   bk(g)[:n, :].rearrange("p (a b) -> p a b", a=4)[:, :, :n], (kbk(g),), (kPPn[0][g],))
                Pget = lambda hd, PPn=PPn, kPPn=kPPn: (PPn[:, 0, hd, :], kPPn[0][hd // 4])
                PTget = lambda hd, PPn=PPn, kPPn=kPPn: (PPn[:, 1, hd, :], kPPn[1][hd // 4])
                Tc, kTc = Ta[tcur]
                Tn, kTn = Ta[1 - tcur]
                for H in HD:
                    if H["hd"] % 4 == 0:
                        pump()
                    hd, g, sl = H["hd"], H["g"], H["sl"]
                    PTc, kPTc = PTget(hd)
                    P.mm(bk(g)[:n, sl * 128:sl * 128 + n], PTc[:n, :n], Tc[:n, hd, :n], True, True, (kPTc, kTc[g]), (kbk(g),))
                    if sl == 3:
                        P.tt("dve", Tn[:n, g * 4:g * 4 + 4, :n], bk(g)[:n, :].rearrange("p (a b) -> p a b", a=4)[:, :, :n],
                             Tc[:n, g * 4:g * 4 + 4, :n], ALU.add, (kbk(g), kTc[g]), (kTn[g],))
                tcur = 1 - tcur
            Tc, kTc = Ta[tcur]
            for H in HD:
                if H["hd"] % 4 == 0:
                    pump()
                hd, j, hb, col = H["hd"], H["j"], H["hb"], H["col"]
                b = hd % 2
                o_ = bk(b)[:n, (hd // 2) * 64:(hd // 2) * 64 + 64]
                P.mm(o_, ART[hb:hb + 64, j, 0:n], Sb[hb:hb + 64, j, :], True, False, (kART, kSb), (kbk(b),))
                P.mm(o_, M4a[:n, hd, 2, :n], Vb[:n, col:col + 64], False, True, (kM4[hd], kVb), (kbk(b),))
                if hd >= 14:
                    P.copy("act", Xa[:n, b:16:2, :], bk(b)[:n, :].rearrange("p (a b) -> p a b", a=8), (kbk(b),), (kXa[b],))
            for H in HD:
                if H["hd"] % 4 == 0:
                    pump()
                hd = H["hd"]
                b = hd % 2
                P.mm(bk(2 + b)[:n, (hd // 2) * 64:(hd // 2) * 64 + 64], Tc[:n, hd, :n], Xa[:n, hd, :], True, True,
                     (kTc[hd // 4], kXa[b]), (kbk(2 + b),))
                if hd >= 14:
                    P.copy("act", Ua[:n, b:16:2, :], bk(2 + b)[:n, :].rearrange("p (a b) -> p a b", a=8), (kbk(2 + b),), (kUa[b],))
            Yv = Y[:n, :].rearrange("p (h c) -> p h c", h=16)
            for H in HD:
                if H["hd"] % 4 == 0:
                    pump()
                hd, j, hb, col = H["hd"], H["j"], H["hb"], H["col"]
                b = hd % 2
                o_ = bk(b)[:n, (hd // 2) * 64:(hd // 2) * 64 + 64]
                P.mm(o_, ART[hb:hb + 64, j, 128:128 + n], Sb[hb:hb + 64, j, :], True, False, (kART, kSb), (kbk(b),))
                P.mm(o_, M4a[:n, hd, 1, :n], Ua[:n, hd, :], False, False, (kM4[hd], kUa[b]), (kbk(b),))
                P.mm(o_, M4a[:n, hd, 3, :n], Vb[:n, col:col + 64], False, True, (kM4[hd], kVb), (kbk(b),))
                if hd >= 14:
                    P.copy("act", Yv[:, b:16:2, :], bk(b)[:n, :].rearrange("p (a b) -> p a b", a=8), (kbk(b),), (kY,))
            for H in HD:
                if H["hd"] % 4 == 0:
                    pump()
                hd, j, hb, col = H["hd"], H["j"], H["hb"], H["col"]
                o_ = bk(2)[hb:hb + 64, j * 64:(j + 1) * 64]
                P.mm(o_, BW[:n, col:col + 64], Ua[:n, hd, :], True, False, (kBW, kUa[hd % 2]), (kbk(2),))
                P.mm(o_, KW[:n, col:col + 64], Vb[:n, col:col + 64], False, True, (kKW, kVb), (kbk(2),))
            P.tt("dve", Sf[:, :, :], Sf[:, :, :], wc[:, :].unsqueeze(2).to_broadcast([128, 8, 64]), ALU.mult, (kSf, kwc), (kSf,))
            P.tt("dve", Sf[:, :, :], Sf[:, :, :], bk(2)[:, :].rearrange("p (a b) -> p a b", a=8), ALU.add, (kSf, kbk(2)), (kSf,))
            P.copy("pool", Sb[:, :, :], Sf[:, :, :], (kSf,), (kSb,))

        def epilogue(ti):
            t0, n = s.tile_range(ti)
            if dr == 0:
                P.dma("sp", d["YF_" + n_][t0:t0 + n, :], Y[:n, :], (kY,), ())
            else:
                P.dma("sp", yf[:n, :], d["YF_" + n_][t0:t0 + n, :], (), (kyf,))
                P.dma("sp", gt[:n, :], d["G_" + n_][t0:t0 + n, :], (), (kgt,))
                P.tt("dve", Y[:n, :], Y[:n, :], yf[:n, :], ALU.add, (kY, kyf), (kY,))
                P.dma("sp", yf[:n, :], d["BF_" + n_][t0:t0 + n, :], (kY,), (kyf,))
                P.reduce("dve", sm[:n, 2, :], v3(Y[:n, :]), ALU.add, (kY,), (ksm,))
                P.tt("pool", tmp[:n, :], Y[:n, :], Y[:n, :], ALU.mult, (kY,), (ktmp,))
                P.reduce("dve", sm[:n, 3, :], v3(tmp[:n, :]), ALU.add, (ktmp,), (ksm,))
                P.ts("dve", sm[:n, 2, :], sm[:n, 2, :], 1.0 / 64, None, ALU.mult, None, (ksm,), (ksm,))
                P.ts("dve", sm[:n, 3, :], sm[:n, 3, :], 1.0 / 64, None, ALU.mult, None, (ksm,), (ksm,))
                P.tt("dve", sm[:n, 0, :], sm[:n, 2, :], sm[:n, 2, :], ALU.mult, (ksm,), (ksm,))
                P.tt("dve", sm[:n, 3, :], sm[:n, 3, :], sm[:n, 0, :], ALU.subtract, (ksm,), (ksm,))
                P.act(sm[:n, 3, :], sm[:n, 3, :], AF.Sqrt, (ksm,), (ksm,), bias=GN_EPS)
                P.recip("dve", sm[:n, 3, :], sm[:n, 3, :], (ksm,), (ksm,))
                bc = lambda q: sm[:n, q, :].unsqueeze(2).to_broadcast([n, 16, 64])
                P.tt("dve", v3(Y[:n, :]), v3(Y[:n, :]), bc(2), ALU.subtract, (kY, ksm), (kY,))
                P.tt("dve", v3(Y[:n, :]), v3(Y[:n, :]), bc(3), ALU.mult, (kY, ksm), (kY,))
                P.tt("pool", Y[:n, :], Y[:n, :], lg[:n, :], ALU.mult, (kY, klg), (kY,))
                P.tt("pool", Y[:n, :], Y[:n, :], lb[:n, :], ALU.add, (kY, klb), (kY,))
                P.tt("dve", Y[:n, :], Y[:n, :], bon[:n, :], ALU.add, (kY, kbon), (kY,))
                P.tt("dve", Y[:n, :], Y[:n, :], yf[:n, :], ALU.add, (kY, kyf), (kY,))
                P.tt("dve", ob[:n, :], Y[:n, :], gt[:n, :], ALU.mult, (kY, kgt), (kob,))
                if s.sliced:
                    P.dma("sp", d["OTOK_" + n_][t0:t0 + n, DA:D], ob[:n, :], (kob,), ())
                    return
                bi = tctr[0] % 4
                tctr[0] += 1
                ptb = P.bank(bi, BF16)
                kptb = "psb%d" % bi
                for j in range(8):
                    P.tr(ptb[:, j * 128:j * 128 + n], ob[:n, j * 128:(j + 1) * 128], self.ident_b[:n, :n],
                         (kob, self.kid[1]), (kptb,))
                P.copy("act", oT[:, :, :n], ptb.rearrange("p (a b) -> p a b", a=8)[:, :, :n], (kptb,), (koT,))
                P.dma("sp", d["OT_" + n_][8:16, :, t0:t0 + n].rearrange("c p t -> p c t"), oT[:, :, :n], (koT,), ())

        for _ in prep1(order[0]):
            pass
        for idx, ti in enumerate(order):
            prep2(ti)
            gen = prep1(order[idx + 1]) if idx + 1 < len(order) else iter(())
            units(ti, gen)
            for _ in gen:
                pass
            epilogue(ti)


    def stage3(self, s):
        P = self.P
        d = self.dram
        n_ = s.name
        L = s.L
        P.barrier()
        P.reset_arena(self.const_end)
        kth = [P.tile([128, L], BF16, "kth%d" % i) for i in range(2)]
        qth = [P.tile([128, L], BF16, "qth%d" % i) for i in range(2)]
        vh = [P.tile([128, s.ntile, 128], BF16, "vh%d" % i) for i in range(2)]
        bt = [P.tile([128, 5, NKEY], F32, "bias%d" % i) for i in range(2)]
        oTh = [P.tile([128, L], BF16, "oTh%d" % i) for i in range(2)]
        ND = 4
        sc = [P.tile([128, NKEY], F32, "sc%d" % i) for i in range(ND)]
        pr = [P.tile([128, NKEY], BF16, "pr%d" % i) for i in range(ND)]
        pT = [P.tile([128, 6, 128], BF16, "pT%d" % i) for i in range(ND)]
        ob = [P.tile([128, 128], BF16, "ob%d" % i) for i in range(ND)]
        st = [P.tile([128, 4], F32, "st%d" % i) for i in range(2 * ND)]
        uc = [0]

        def mk_unit(kind, rp, h, tiles):
            kt, kkt, qt, kqt, v, kv, b, kb, oh, koh = tiles
            u = uc[0]
            uc[0] += 1
            U = dict(kind=kind, h=h, tiles=tiles)
            U["sc"] = sc[u % ND]; U["pr"] = pr[u % ND]; U["pT"] = pT[u % ND]; U["ob"] = ob[u % ND]; U["st"] = st[u % (2 * ND)]
            ba = (u % ND) * 2
            U["ba"] = ba
            if kind == "meta":
                U.update(nq=NM, q0=0, lo=640, blocks=[(5, NM, 0)])
            else:
                r = 2 * rp
                ws = min(max(r - 4, 0), s.rows - WINR)
                assert ws % 2 == 0
                U.update(nq=128, q0=NM + rp * 128, k0=NM + ws * 64, cls=(r - ws) // 2, lo=0,
                         blocks=[(j, 128, ws // 2 + 1 + j) for j in range(5)] + [(5, NM, 0)])
            return U

        def phA(U):
            kt, kkt, qt, kqt, v, kv, b, kb, oh, koh = U["tiles"]
            s_, ks_ = U["sc"]; st_, kst_ = U["st"]
            ba = U["ba"]; nq = U["nq"]; q0 = U["q0"]; lo = U["lo"]
            pa = P.bank(ba); pbk = P.bank(ba + 1)
            kpa = "psb%d" % ba; kpb = "psb%d" % (ba + 1)
            if U["kind"] == "meta":
                P.mm(pbk[:nq, 128:144], qt[:, 0:NM], kt[:, 0:NM], True, True, (kqt, kkt), (kpb,))
                P.copy("dve", s_[:nq, 640:656], pbk[:nq, 128:144], (kpb,), (ks_,))
            else:
                k0 = U["k0"]; cls = U["cls"]
                P.mm(pa[:, :], qt[:, q0:q0 + 128], kt[:, k0:k0 + 512], True, True, (kqt, kkt), (kpa,))
                P.mm(pbk[:, 0:128], qt[:, q0:q0 + 128], kt[:, k0 + 512:k0 + 640], True, True, (kqt, kkt), (kpb,))
                P.mm(pbk[:, 128:144], qt[:, q0:q0 + 128], kt[:, 0:NM], True, True, (kqt, kkt), (kpb,))
                P.tt("dve", s_[:, 0:512], pa[:, :], b[:, cls, 0:512], ALU.add, (kpa, kb), (ks_,))
                P.tt("dve", s_[:, 512:656], pbk[:, 0:144], b[:, cls, 512:656], ALU.add, (kpb, kb), (ks_,))
            P.op("dve", lambda e, o=st_[:nq, 0:1], i=s_[:nq, lo:656]: e.reduce_max(o, i, AX.X), (ks_,), (kst_,))
            P.ts("dve", st_[:nq, 1:2], st_[:nq, 0:1], -1.0, None, ALU.mult, None, (kst_,), (kst_,))

        def phB(U):
            s_, ks_ = U["sc"]; st_, kst_ = U["st"]; p_, kp_ = U["pr"]
            nq = U["nq"]; lo = U["lo"]
            P.act(p_[:nq, lo:656], s_[:nq, lo:656], AF.Exp, (ks_, kst_), (kp_, kst_), bias=st_[:nq, 1:2], accum=st_[:nq, 2:3])
            P.recip("dve", st_[:nq, 3:4], st_[:nq, 2:3], (kst_,), (kst_,))

        def phC(U):
            p_, kp_ = U["pr"]; pt_, kpt_ = U["pT"]
            ba = U["ba"]; nq = U["nq"]
            ptb = P.bank(ba, BF16)
            kptb = "psb%d" % ba
            for (j, nk, vt) in U["blocks"]:
                P.tr(ptb[:nk, j * 128:j * 128 + nq], p_[:nq, j * 128:j * 128 + nk], self.ident_b[:nq, :nq],
                     (kp_, self.kid[1]), (kptb,))
            if U["kind"] == "meta":
                P.copy("act", pt_[:NM, 5, :nq], ptb[:NM, 640:640 + nq], (kptb,), (kpt_,))
            else:
                P.copy("act", pt_[:, :, :], ptb[:, 0:768].rearrange("p (a b) -> p a b", a=6), (kptb,), (kpt_,))

        def phD(U):
            kt, kkt, qt, kqt, v, kv, b, kb, oh, koh = U["tiles"]
            pt_, kpt_ = U["pT"]; o_, ko_ = U["ob"]; st_, kst_ = U["st"]
            ba = U["ba"]; nq = U["nq"]; q0 = U["q0"]; h = U["h"]
            pbk = P.bank(ba + 1)
            kpb = "psb%d" % (ba + 1)
            blocks = U["blocks"]
            for bi, (j, nk, vt) in enumerate(blocks):
                P.mm(pbk[:nq, 256:384], pt_[:nk, j, :nq], v[:nk, vt, :], bi == 0, bi == len(blocks) - 1, (kpt_, kv), (kpb,))
            P.act(o_[:nq, :], pbk[:nq, 256:384], AF.Copy, (kpb, kst_), (ko_,), scale=st_[:nq, 3:4])
            if s.sliced:
                P.dma("sp", d["OTOK_" + n_][q0:q0 + nq, h * 128:(h + 1) * 128], o_[:nq, :], (ko_,), ())
                return
            pot = pbk.bitcast(BF16)
            P.tr(pot[:, 800:800 + nq], o_[:nq, :], self.ident_b[:nq, :nq], (ko_, self.kid[1]), (kpb,))
            P.copy("dve", oh[:, q0:q0 + nq], pot[:, 800:800 + nq], (kpb,), (koh,))

        for h in range(HA):
            kt, kkt = kth[h % 2]; qt, kqt = qth[h % 2]; v, kv = vh[h % 2]; b, kb = bt[h % 2]; oh, koh = oTh[h % 2]
            P.dma("sp", kt, d["KT_" + n_][h], (), (kkt,))
            P.dma("sp", qt, d["QT_" + n_][h], (), (kqt,))
            P.dma("sp", v[:NM, 0, :], d["V_" + n_][0:NM, h * 128:(h + 1) * 128], (), (kv,))
            for i0 in range(1, s.ntile, 16):
                i1 = min(s.ntile, i0 + 16)
                P.dma("sp", v[:, i0:i1, :],
                      d["V_" + n_][NM + (i0 - 1) * 128:NM + (i1 - 1) * 128, h * 128:(h + 1) * 128].rearrange("(i p) c -> p i c", p=128),
                      (), (kv,))
            P.dma("sp", b, d["bias_tab"][h].rearrange("c q k -> q c k"), (), (kb,))
            tiles = (kt, kkt, qt, kqt, v, kv, b, kb, oh, koh)
            Us = [mk_unit("meta", 0, h, tiles)] + [mk_unit("grid", rp, h, tiles) for rp in range(s.rows // 2)]
            nU = len(Us)
            for step in range(nU + 3):
                if step < nU:
                    phA(Us[step])
                if 0 <= step - 1 < nU:
                    phB(Us[step - 1])
                if 0 <= step - 2 < nU:
                    phC(Us[step - 2])
                if 0 <= step - 3 < nU:
                    phD(Us[step - 3])
            if not s.sliced:
                P.dma("sp", d["OT_" + n_][h], oh, (koh,), ())

    def stage4(self, s):
        P = self.P
        d = self.dram
        n_ = s.name
        P.barrier()
        P.reset_arena(self.const_end)
        g1, kg = self.bcast_load(d["ln1_g"], D, "g1")
        b1, kb = self.bcast_load(d["ln1_b"], D, "b1")
        tmp = self.ln_tmp()
        wo, kwo = P.tile([128, 16, D], BF16, "wo")
        for c0 in range(0, D, 512):
            self.load_w(wo[:, :, c0:c0 + 512], kwo, d["w_out"], D, c0, 512)
        oT = [P.tile([128, 16, 128], BF16, "oT%d" % i) for i in range(2)]
        h0 = [P.tile([128, D], F32, "h0%d" % i) for i in range(2)]
        h1 = [P.tile([128, D], F32, "h1%d" % i) for i in range(2)]
        h1T = [P.tile([128, 16, 128], BF16, "h1T%d" % i) for i in range(2)]
        ctr = [0]
        def mm_phase(ti):
            t0, n = s.tile_range(ti)
            o_, ko_ = oT[ti % 2]; x, kx = h0[ti % 2]
            P.dma("sp", o_[:, :, :n], d["OT_" + n_][:, :, t0:t0 + n].rearrange("c p t -> p c t"), (), (ko_,))
            P.dma("sp", x[:n, :], d["H0_" + n_][t0:t0 + n, :], (), (kx,))
            for hf in range(4):
                pb = P.bank(hf); kpb = "psb%d" % hf
                for kc in range(16):
                    P.mm(pb[:n, :], o_[:, kc, :n], wo[:, kc, hf * 512:(hf + 1) * 512], kc == 0, kc == 15, (ko_, kwo), (kpb,))
                P.stt("dve", x[:n, hf * 512:(hf + 1) * 512], x[:n, hf * 512:(hf + 1) * 512], ALPHA, pb[:n, :], ALU.mult, ALU.add,
                      (kx, kpb), (kx,))

        def ln_phase(ti):
            t0, n = s.tile_range(ti)
            x, kx = h0[ti % 2]; y, ky = h1[ti % 2]; yt, kyt = h1T[ti % 2]
            self.layer_norm(x, kx, n, (g1, b1), (kg, kb), y, ky, LN_EPS, tmp)
            P.dma("sp", d["H1_" + n_][t0:t0 + n, :], y[:n, :], (ky,), ())
            self.transpose_to_fm(y, ky, n, yt, kyt, 0, (4, 5, 6, 7), ctr)
            P.dma("sp", d["H1T_" + n_][:, :, t0:t0 + n].rearrange("c p t -> p c t"), yt[:, :, :n], (kyt,), ())

        mm_phase(0)
        for ti in range(s.ntile):
            if ti + 1 < s.ntile:
                mm_phase(ti + 1)
            ln_phase(ti)

    def stage4_sliced(self, s):
        P = self.P
        d = self.dram
        n_ = s.name
        NG = s.ng
        P.barrier()
        P.reset_arena(self.const_end)
        g0, kg0 = self.bcast_load(d["emb_ln_g"], D, "g0")
        b0, kb0 = self.bcast_load(d["emb_ln_b"], D, "b0")
        g1, kg1 = self.bcast_load(d["ln1_g"], D, "g1")
        b1, kb1 = self.bcast_load(d["ln1_b"], D, "b1")
        tmp = self.ln_tmp()
        wo, kwo = P.tile([128, 16, D], BF16, "wo")
        for c0 in range(0, D, 512):
            self.load_w(wo[:, :, c0:c0 + 512], kwo, d["w_out"], D, c0, 512)
        ms, kms = P.tile([128, NG + 1], F32, "msel")
        P.dma("sp", ms, d["msel_" + n_], (), (kms,))
        mI, kmI = P.tile([128, NG, 128], BF16, "mI")
        for c in range(NG):
            P.ts("dve", mI[:, c, :], self.ident_f, ms[:, c:c + 1], None, ALU.mult, None, (self.kid[0], kms), (kmI,))
        shf, kshf = P.tile([128, 128], F32, "shf")
        P.dma("sp", shf[:NM, :], d["c_shift"], (), (kshf,))
        shI, kshI = P.tile([128, 128], BF16, "shI")
        P.ts("dve", shI[:NM, :], shf[:NM, :], ms[:NM, 0:1], None, ALU.mult, None, (kshf, kms), (kshI,))
        ot = [P.tile([128, D], BF16, "ot%d" % i) for i in range(NG)]
        oTs, koTs = P.tile([128, 16, 128], BF16, "oTs")
        x, kx = P.tile([128, D], F32, "x")
        h0, kh0 = P.tile([128, D], F32, "h0")
        y, ky = P.tile([128, D], F32, "h1")
        yt, kyt = P.tile([128, 16, 128], BF16, "h1T")
        ctr = [0]
        for sl in range(10):
            cands = []
            for c in range(NG):
                ti = 8 * c + sl
                if 0 <= ti < s.ntile:
                    cands.append((c, ti))
            for c, ti in cands:
                t0, n = s.tile_range(ti)
                P.dma("sp", ot[c][0][:n, :], d["OTOK_" + n_][t0:t0 + n, :], (), (ot[c][1],))
            for fc in range(16):
                bi = fc // 4
                pb = P.bank(bi); kpb = "psb%d" % bi
                for ci, (c, ti) in enumerate(cands):
                    t0, n = s.tile_range(ti)
                    rhs = shI[:NM, :] if ti == 0 else mI[:, c, :]
                    krhs = kshI if ti == 0 else kmI
                    P.mm(pb[:, (fc % 4) * 128:(fc % 4 + 1) * 128], ot[c][0][:n, fc * 128:(fc + 1) * 128], rhs,
                         ci == 0, ci == len(cands) - 1, (ot[c][1], krhs), (kpb,))
                if fc % 4 == 3:
                    P.copy("act" if bi % 2 == 0 else "dve", oTs[:, fc - 3:fc + 1, :], pb.rearrange("p (a b) -> p a b", a=4),
                           (kpb,), (koTs,))
            P.dma("sp", x, d["xext_" + n_][sl * 128:(sl + 1) * 128, :], (), (kx,))
            self.layer_norm(x, kx, 128, (g0, b0), (kg0, kb0), h0, kh0, LN_EPS, tmp)
            for hf in range(4):
                pb = P.bank(4 + hf); kpb = "psb%d" % (4 + hf)
                for kc in range(16):
                    P.mm(pb[:, :], oTs[:, kc, :], wo[:, kc, hf * 512:(hf + 1) * 512], kc == 0, kc == 15, (koTs, kwo), (kpb,))
                P.stt("dve", h0[:, hf * 512:(hf + 1) * 512], h0[:, hf * 512:(hf + 1) * 512], ALPHA, pb[:, :], ALU.mult, ALU.add,
                      (kh0, kpb), (kh0,))
            self.layer_norm(h0, kh0, 128, (g1, b1), (kg1, kb1), y, ky, LN_EPS, tmp)
            P.dma("sp", d["H1s_" + n_][sl * 128:(sl + 1) * 128, :], y, (ky,), ())
            self.transpose_to_fm(y, ky, 128, yt, kyt, 0, (0, 1, 2, 3), ctr)
            P.dma("sp", d["H1Ts_" + n_][:, :, sl * 128:(sl + 1) * 128].rearrange("c p t -> p c t"), yt, (kyt,), ())

    def stage5(self, s):
        P = self.P
        d = self.dram
        n_ = s.name
        P.barrier()
        P.reset_arena(self.const_end)
        g2_, kg = self.bcast_load(d["ln2_g"], D, "g2")
        b2_, kb = self.bcast_load(d["ln2_b"], D, "b2")
        tmp = self.ln_tmp()
        cw, kcw = P.tile([128, 44, 4], F32, "convw")
        P.dma("sp", cw, d["ffn_conv_l"], (), (kcw,))
        NB = 1024
        hT, khT = P.tile([128, 16, NB + 2], BF16, "hT")
        acc, kacc0 = P.tile([128, 8, D], F32, "acc")
        kacc = [kacc0 + "/%d" % i for i in range(8)]
        GC = 2
        w1 = [P.tile([128, 16, 2 * GC * 128], BF16, "w1_%d" % i) for i in range(2)]
        w2 = [P.tile([128, GC, D], BF16, "w2_%d" % i) for i in range(2)]
        gT = [P.tile([128, 514], F32, "gT%d" % i) for i in range(2)]
        t1 = [P.tile([128, 512], F32, "t1%d" % i) for i in range(2)]
        aT = [P.tile([128, GC, NB], BF16, "aT%d" % i) for i in range(2)]
        xr = [P.tile([128, D], F32, "xr%d" % i) for i in range(1)]
        yo = [P.tile([128, D], F32, "yo%d" % i) for i in range(1)]
        gi = 0
        hc = 0
        if s.sliced:
            h1t_d = d["H1Ts_" + n_]; h1_d = d["H1s_" + n_]; Ltot = 1280
            blocks = [(128, 0)]
            ms, kms = P.tile([128, s.ng + 1], F32, "msel")
            P.dma("sp", ms, d["msel_" + n_], (), (kms,))
        else:
            h1t_d = d["H1T_" + n_]; h1_d = d["H1_" + n_]; Ltot = s.L
            blocks = [(NM + b0, b0) for b0 in range(0, s.T, NB)]
        for ts0, orow in blocks:
            ntl = NB // 128
            lo = ts0 - 1
            hi = min(Ltot, ts0 + NB + 1)
            P.dma("sp", hT[:, :, 0:hi - lo], h1t_d[:, :, lo:hi].rearrange("c p t -> p c t"), (), (khT,))
            if hi - lo < NB + 2:
                P.memset("pool", hT[:, :, NB + 1:NB + 2], 0.0, (khT,))
            if s.sliced:
                P.ts("pool", hT[:, :, NB + 1:NB + 2], hT[:, :, NB + 1:NB + 2], ms[:, s.ng:s.ng + 1], None, ALU.mult, None,
                     (khT, kms), (khT,))
            for g in range(44 // GC):
                w1_, kw1 = w1[gi % 2]; w2_, kw2 = w2[gi % 2]; a_, ka_ = aT[gi % 2]
                gi += 1
                c0 = g * GC * 128
                self.load_w(w1_[:, :, 0:GC * 128], kw1, d["ffn_w_in"], D, c0, GC * 128)
                self.load_w(w1_[:, :, GC * 128:2 * GC * 128], kw1, d["ffn_w_in"], D, DFF + c0, GC * 128)
                P.dma("pool", w2_, d["ffn_w_out"][c0:c0 + GC * 128, :].rearrange("(c p) n -> p c n", p=128), (), (kw2,))
                for cl in range(GC):
                    fc = g * GC + cl
                    for hh in range(2):
                        g_, kg_ = gT[hc % 2]; t_, kt_ = t1[hc % 2]
                        hc += 1
                        bg = 0 if hh == 0 else 3
                        pg = P.bank(bg); kpg = "psb%d" % bg
                        ph = P.bank(1); pu = P.bank(2)
                        cb = hh * 512
                        for kc in range(16):
                            P.mm(pg[:, :], w1_[:, kc, cl * 128:(cl + 1) * 128], hT[:, kc, cb:cb + 512], kc == 0, kc == 15,
                                 (kw1, khT), (kpg,))
                        for kc in range(16):
                            P.mm(ph[:, 0:2], w1_[:, kc, cl * 128:(cl + 1) * 128], hT[:, kc, cb + 512:cb + 514], kc == 0, kc == 15,
                                 (kw1, khT), ("psb1",))
                        for kc in range(16):
                            P.mm(pu[:, :], w1_[:, kc, (GC + cl) * 128:(GC + cl + 1) * 128], hT[:, kc, cb + 1:cb + 513],
                                 kc == 0, kc == 15, (kw1, khT), ("psb2",))
                        P.copy("act", g_[:, 0:512], pg[:, :], (kpg,), (kg_,))
                        P.copy("act", g_[:, 512:514], ph[:, 0:2], ("psb1",), (kg_,))
                        P.act(t_[:, :], g_[:, 1:513], AF.Identity, (kg_, kcw), (kt_,), bias=cw[:, fc, 3:4], scale=cw[:, fc, 1:2])
                        P.stt("dve", t_[:, :], g_[:, 0:512], cw[:, fc, 0:1], t_[:, :], ALU.mult, ALU.add, (kg_, kcw, kt_), (kt_,))
                        P.stt("dve", t_[:, :], g_[:, 2:514], cw[:, fc, 2:3], t_[:, :], ALU.mult, ALU.add, (kg_, kcw, kt_), (kt_,))
                        P.act(t_[:, :], t_[:, :], AF.Gelu, (kt_,), (kt_,))
                        P.tt("dve", a_[:, cl, cb:cb + 512], t_[:, :], pu[:, :], ALU.mult, (kt_, "psb2"), (ka_,))
                for tl in range(ntl):
                    for hf in range(4):
                        pb = P.bank(4 + hf); kpb = "psb%d" % (4 + hf)
                        for cl in range(GC):
                            P.mm(pb[:, :], a_[:, cl, tl * 128:(tl + 1) * 128], w2_[:, cl, hf * 512:(hf + 1) * 512],
                                 cl == 0, cl == GC - 1, (ka_, kw2), (kpb,))
                        dst = acc[:, tl, hf * 512:(hf + 1) * 512]
                        if g == 0:
                            P.copy("dve" if hf % 2 else "act", dst, pb[:, :], (kpb,), (kacc[tl],))
                        else:
                            P.tt("dve", dst, dst, pb[:, :], ALU.add, (kpb, kacc[tl]), (kacc[tl],))
            for tl in range(ntl):
                x, kx = xr[0]; y, ky = yo[0]
                tq = ts0 + tl * 128
                P.dma("sp", x, h1_d[tq:tq + 128, :], (), (kx,))
                P.stt("dve", x, x, ALPHA, acc[:, tl, :], ALU.mult, ALU.add, (kx, kacc[tl]), (kx,))
                self.layer_norm(x, kx, 128, (g2_, b2_), (kg, kb), y, ky, LN_EPS, tmp)
                P.dma("sp", d["y_" + n_][orow + tl * 128:orow + tl * 128 + 128, :], y, (ky,), ())

    def build(self):
        self.setup_consts()
        for s in self.seqs:
            if 1 in self.run:
                self.stage1(s)
            if 2 in self.run:
                self.stage2(s)
            if 3 in self.run:
                self.stage3(s)
            if 4 in self.run:
                if s.sliced:
                    self.stage4_sliced(s)
                else:
                    self.stage4(s)
            if 5 in self.run:
                self.stage5(s)
        self.P.finish()
        return self.nc


def _bias_table(rpb):
    rpb = np.asarray(rpb, np.float32).reshape(HA, 15, 31)
    tab = np.full((HA, 5, 128, NKEY), NEG, np.float32)
    qc = np.arange(64)
    c0 = np.clip(qc - 8, 0, 48)
    kc = np.arange(64)
    colmask = (kc[None, :] >= c0[:, None]) & (kc[None, :] < c0[:, None] + 16)
    dc = np.clip(kc[None, :] - qc[:, None], -15, 15) + 15
    rel = {0: (0, 0), 1: (0, 0), 2: (0, 1), 3: (2, 2), 4: (2, 2)}
    for c in range(5):
        for qr in range(2):
            r_rel = 2 * c + qr
            r0 = rel[c][qr]
            for j in range(8):
                krow = r0 + j
                dr = krow - r_rel + 7
                blk = rpb[:, dr][:, dc]
                blk = np.where(colmask[None], blk, NEG)
                tab[:, c, qr * 64:(qr + 1) * 64, krow * 64:(krow + 1) * 64] = blk
        tab[:, c, :, WINR * 64:] = 0.0
    return tab


def _consts():
    ident = np.eye(128, dtype=np.float32)
    s = np.arange(128)[:, None]
    t = np.arange(128)[None, :]
    masks = np.stack([(s <= t), (s >= t), (s < t), (s <= t), (s > t), (s >= t)]).astype(np.float32)
    return ident, masks


def _common_inputs(inp):
    ident, masks = _consts()
    f = lambda a: np.ascontiguousarray(np.asarray(a, np.float32))
    m = {
        "meta_tokens": f(inp["meta_tokens"]),
        "emb_ln_g": f(inp["emb_ln_g"]), "emb_ln_b": f(inp["emb_ln_b"]),
        "ln1_g": f(inp["ln1_g"][0]), "ln1_b": f(inp["ln1_b"][0]),
        "ln2_g": f(inp["ln2_g"][0]), "ln2_b": f(inp["ln2_b"][0]),
        "w_in": f(inp["w_in"][0]),
        "bias_tab": _bias_table(inp["attn_rpb"][0]),
        "rwkv_mu": f(inp["rwkv_mu"][0]), "rwkv_w0": f(inp["rwkv_w0"][0]), "rwkv_w2": f(inp["rwkv_w2"][0]),
        "rwkv_a0": f(inp["rwkv_a0"][0]), "rwkv_a2": f(inp["rwkv_a2"][0]), "rwkv_g2": f(inp["rwkv_g2"][0]),
        "rwkv_k_k": f(inp["rwkv_k_k"][0]), "rwkv_k_a": f(inp["rwkv_k_a"][0]),
        "rwkv_r_k": f(inp["rwkv_r_k"][0]).reshape(DR),
        "rwkv_lnx_g": f(inp["rwkv_lnx_g"][0]), "rwkv_lnx_b": f(inp["rwkv_lnx_b"][0]),
        "w_out": f(inp["w_out"][0]), "ffn_w_in": f(inp["ffn_w_in"][0]),
        "ffn_conv_l": np.ascontiguousarray(np.concatenate([f(inp["ffn_conv_w"][0]), f(inp["ffn_conv_b"][0])[None]], 0)
                                           .reshape(4, 44, 128).transpose(2, 1, 0)),
        "ffn_w_out": f(inp["ffn_w_out"][0]),
        "c_ident": ident, "c_masks": masks, "c_zero": np.zeros((1, NSH), np.float32),
        "c_shift": np.eye(NM, 128, 128 - NM, dtype=np.float32),
    }
    return m


def slice_inputs(x, meta, c, ng):
    x = np.asarray(x, np.float32)
    prev = x[1024 * c - 128:1024 * c] if c > 0 else np.concatenate([np.zeros((128 - NM, D), np.float32), np.asarray(meta, np.float32)], 0)
    nxt = x[1024 * c + 1024:1024 * c + 1152] if c < ng - 1 else np.zeros((128, D), np.float32)
    xext = np.ascontiguousarray(np.concatenate([prev, x[1024 * c:1024 * c + 1024], nxt], 0))
    msel = np.zeros((128, ng + 1), np.float32)
    msel[:, c] = 1.0
    msel[:, ng] = 1.0 if c < ng - 1 else 0.0
    return xext, msel


def kernel(**inputs):
    xp = np.asarray(inputs["x_prompt"], np.float32)
    xs = np.asarray(inputs["x_sample"], np.float32)
    b = Builder([xs.shape[1], xp.shape[1]], sliced=[False, True])
    nc = b.build()
    common = _common_inputs(inputs)
    in_maps = []
    for c in range(8):
        m = dict(common)
        m["x_s0"] = np.ascontiguousarray(xs[c])
        m["x_s1"] = np.ascontiguousarray(xp[0])
        m["xext_s1"], m["msel_s1"] = slice_inputs(xp[0], inputs["meta_tokens"], c, 8)
        in_maps.append(m)
    res = run_bass_kernel_spmd(nc, in_maps, core_ids=list(range(8)))
    y_s = np.stack([res.results[c]["y_s0"] for c in range(8)], axis=0)
    y_p = np.concatenate([res.results[c]["y_s1"] for c in range(8)], axis=0)[None]
    return (y_p.astype(np.float32), y_s.astype(np.float32))
```

```python
import numpy as np
import ml_dtypes
import concourse.bass as bass
import concourse.mybir as mybir
from concourse.bass_utils import run_bass_kernel_spmd

F32 = mybir.dt.float32
BF16 = mybir.dt.bfloat16
U8 = mybir.dt.uint8
ALU = mybir.AluOpType
AF = mybir.ActivationFunctionType
AX = mybir.AxisListType

D = 2048
NM = 16
DA = 1024
HA = 8
DR = 1024
HR = 16
NSH = 3 * DR + 192
NIN = 3 * DA + NSH + 256
DFF = 5632
ALPHA = 2.0 ** 0.25
LN_EPS = 1e-5
GN_EPS = 64e-5
CDEC = -float(np.exp(-0.5))
NEG = -30000.0
WINR = 10
NKEY = WINR * 64 + NM


class Prog:
    ENG = ("sp", "act", "dve", "pool", "pe")

    def __init__(self, nc, n_dma_sems=40):
        self.nc = nc
        self.q = {e: [] for e in self.ENG}
        self.cnt = {e: 0 for e in self.ENG}
        self.known = {e: {} for e in self.ENG}
        self.last_w = {}
        self.readers = {}
        self.semh = {}
        for e in ("act", "dve", "pool", "pe"):
            self.semh[e] = nc.alloc_semaphore("sem_" + e)
        self.ndma = n_dma_sems
        for i in range(n_dma_sems):
            self.semh[("dma", i)] = nc.alloc_semaphore("sem_dma%d" % i)
        self.dma_tot = [0] * n_dma_sems
        self.rr = 0
        self.arena = nc.alloc_sbuf_tensor("arena", [128, 204 * 1024], U8)
        self.off = 0
        self.uid = 0
        self.psum = [nc.alloc_psum_tensor("psb%d" % i, [128, 512], F32) for i in range(8)]

    def reset_arena(self, off=0):
        self.off = off

    def tile(self, shape, dtype, name="t"):
        esz = 4 if dtype == F32 else 2
        n = int(np.prod(shape[1:]))
        nbytes = (n * esz + 31) // 32 * 32
        assert self.off + nbytes <= 204 * 1024, ("sbuf overflow", name, self.off, nbytes)
        ap = self.arena[:, self.off:self.off + n * esz].bitcast(dtype)
        self.off += nbytes
        if len(shape) == 3:
            ap = ap.rearrange("p (a b) -> p a b", a=shape[1])
        elif len(shape) == 4:
            ap = ap.rearrange("p (a b c) -> p a b c", a=shape[1], b=shape[2])
        self.uid += 1
        return ap, "%s#%d" % (name, self.uid)

    def bank(self, i, dtype=F32):
        ap = self.psum[i][:, :]
        if dtype != F32:
            ap = ap.bitcast(dtype)
        return ap

    def _waits(self, eng, reads, writes):
        need = {}

        def add(ev):
            if ev is None:
                return
            s, v = ev
            if need.get(s, 0) < v:
                need[s] = v
        for k in reads:
            add(self.last_w.get(k))
        for k in writes:
            add(self.last_w.get(k))
            for s, v in self.readers.get(k, {}).items():
                add((s, v))
        out = []
        for s, v in need.items():
            if s == "pe" and eng == "pe":
                continue
            if self.known[eng].get(s, 0) >= v:
                continue
            self.known[eng][s] = v
            out.append((s, v))
        return out

    def _record(self, ev, reads, writes):
        s, v = ev
        for k in reads:
            d = self.readers.setdefault(k, {})
            if d.get(s, 0) < v:
                d[s] = v
        for k in writes:
            self.last_w[k] = ev
            self.readers[k] = {}

    @staticmethod
    def _px(reads, writes):
        pr = tuple(k for k in reads if k.startswith("psb"))
        if pr:
            reads = tuple(k for k in reads if not k.startswith("psb"))
            writes = tuple(writes) + pr
        return reads, writes

    def op(self, eng, fn, reads=(), writes=()):
        reads, writes = self._px(reads, writes)
        waits = self._waits(eng, reads, writes)
        self.cnt[eng] += 1
        ev = (eng, self.cnt[eng])
        self.q[eng].append((waits, fn, eng, 1))
        self._record(ev, reads, writes)

    def dma(self, eng, out, in_, reads=(), writes=()):
        i = self.rr
        self.rr = (self.rr + 1) % self.ndma
        s = ("dma", i)
        waits = self._waits(eng, reads, writes)
        if self.dma_tot[i] > 0 and self.known[eng].get(s, 0) < self.dma_tot[i]:
            self.known[eng][s] = self.dma_tot[i]
            waits.append((s, self.dma_tot[i]))
        self.dma_tot[i] += 16
        ev = (s, self.dma_tot[i])
        self.q[eng].append((waits, lambda e, o=out, a=in_: e.dma_start(out=o, in_=a), s, 16))
        self._record(ev, reads, writes)

    def raw(self, eng, fn, reads=(), writes=()):
        waits = self._waits(eng, reads, writes)
        self.q[eng].append((waits, fn, None, -1))

    def dma_dyn(self, eng, mk, reads=(), writes=()):
        i = self.rr
        self.rr = (self.rr + 1) % self.ndma
        s = ("dma", i)
        waits = self._waits(eng, reads, writes)
        if self.dma_tot[i] > 0 and self.known[eng].get(s, 0) < self.dma_tot[i]:
            self.known[eng][s] = self.dma_tot[i]
            waits.append((s, self.dma_tot[i]))
        self.dma_tot[i] += 16
        ev = (s, self.dma_tot[i])

        def fn(e, mk=mk):
            o, a = mk()
            return e.dma_start(out=o, in_=a)
        self.q[eng].append((waits, fn, s, 16))
        self._record(ev, reads, writes)

    def barrier(self):
        for e in self.ENG:
            waits = []
            for s in ("act", "dve", "pool", "pe"):
                if s != e and self.cnt[s] > self.known[e].get(s, 0):
                    self.known[e][s] = self.cnt[s]
                    waits.append((s, self.cnt[s]))
            for i in range(self.ndma):
                s = ("dma", i)
                if self.dma_tot[i] > self.known[e].get(s, 0):
                    self.known[e][s] = self.dma_tot[i]
                    waits.append((s, self.dma_tot[i]))
            if waits:
                self.q[e].append((waits, None, None, 0))

    def finish(self):
        self.barrier()
        nc = self.nc
        semh = self.semh
        q = self.q
        with nc.Block() as block:
            def mk(name):
                def body(e):
                    for waits, fn, s, amt in q[name]:
                        for ws, wv in waits:
                            e.wait_ge(semh[ws], wv)
                        if fn is not None:
                            if amt < 0:
                                fn(e)
                            else:
                                fn(e).then_inc(semh[s], amt)
                return body
            block.sync(mk("sp"))
            block.scalar(mk("act"))
            block.vector(mk("dve"))
            block.gpsimd(mk("pool"))
            block.tensor(mk("pe"))

    def mm(self, out, lhsT, rhs, start, stop, reads, writes):
        self.op("pe", lambda e: e.matmul(out, lhsT, rhs, start=start, stop=stop), reads, writes)

    def tr(self, out, in_, ident, reads, writes):
        self.op("pe", lambda e: e.transpose(out, in_, ident), reads, writes)

    def act(self, out, in_, func, reads, writes, bias=None, scale=None, accum=None):
        kw = {}
        if bias is not None:
            kw["bias"] = bias
        if scale is not None:
            kw["scale"] = scale
        if accum is not None:
            kw["accum_out"] = accum
        self.op("act", lambda e: e.activation(out, in_, func, **kw), reads, writes)

    def tt(self, eng, out, a, b, op, reads, writes):
        self.op(eng, lambda e: e.tensor_tensor(out, a, b, op), reads, writes)

    def ts(self, eng, out, a, s1, s2, op0, op1, reads, writes):
        if s2 is None:
            self.op(eng, lambda e: e.tensor_scalar(out, a, s1, None, op0), reads, writes)
        else:
            self.op(eng, lambda e: e.tensor_scalar(out, a, s1, s2, op0, op1), reads, writes)

    def stt(self, eng, out, a, sc, b, op0, op1, reads, writes):
        self.op(eng, lambda e: e.scalar_tensor_tensor(out, a, sc, b, op0, op1), reads, writes)

    def copy(self, eng, out, in_, reads, writes):
        if eng == "act":
            self.op("act", lambda e: e.copy(out, in_), reads, writes)
        else:
            self.op(eng, lambda e: e.tensor_copy(out, in_), reads, writes)

    def reduce(self, eng, out, in_, op, reads, writes):
        self.op(eng, lambda e: e.tensor_reduce(out, in_, AX.X, op), reads, writes)

    def recip(self, eng, out, in_, reads, writes):
        self.op(eng, lambda e: e.reciprocal(out, in_), reads, writes)

    def memset(self, eng, ap, val, writes):
        self.op(eng, lambda e: e.memset(ap, val), (), writes)


class Seq:
    def __init__(self, name, T, sliced=False):
        self.name = name
        self.T = T
        self.sliced = sliced
        self.ng = T // 1024
        self.L = T + NM
        self.rows = T // 64
        self.ntile = 1 + T // 128

    def tile_range(self, i):
        if i == 0:
            return 0, NM
        return NM + (i - 1) * 128, 128


class Builder:
    def __init__(self, seq_T, debug=False, stages=5, run=(1, 2, 3, 4, 5), sliced=None):
        self.debug = debug
        self.run = run
        self.stages = stages
        nc = bass.Bass("TRN2", target_bir_lowering=False)
        self.nc = nc
        self.P = Prog(nc)
        sliced = sliced or [False] * len(seq_T)
        self.seqs = [Seq("s%d" % i, T, sl) for i, (T, sl) in enumerate(zip(seq_T, sliced))]
        self.dram = {}
        self._declare_io()

    def din(self, name, shape, dtype=F32):
        ap = self.nc.dram_tensor(name, list(shape), dtype, kind="ExternalInput").ap()
        self.dram[name] = ap
        return ap

    def dscr(self, name, shape, dtype=F32):
        kind = "ExternalOutput" if self.debug else "Internal"
        ap = self.nc.dram_tensor(name, list(shape), dtype, kind=kind).ap()
        self.dram[name] = ap
        return ap

    def _declare_io(self):
        for s in self.seqs:
            self.din("x_" + s.name, [s.T, D])
            self.nc_out = None
        for s in self.seqs:
            ap = self.nc.dram_tensor("y_" + s.name, [1024 if s.sliced else s.T, D], F32, kind="ExternalOutput").ap()
            self.dram["y_" + s.name] = ap
            if s.sliced:
                self.din("xext_" + s.name, [1280, D])
                self.din("msel_" + s.name, [128, s.ng + 1])
        self.din("c_shift", [NM, 128])
        self.din("meta_tokens", [NM, D])
        for nm in ("emb_ln_g", "emb_ln_b", "ln1_g", "ln1_b", "ln2_g", "ln2_b"):
            self.din(nm, [D])
        self.din("w_in", [D, NIN])
        self.din("bias_tab", [HA, 5, 128, NKEY])
        self.din("rwkv_mu", [2, NSH])
        self.din("rwkv_w0", [2, DR])
        self.din("rwkv_w2", [2, 96, DR])
        self.din("rwkv_a0", [2, DR])
        self.din("rwkv_a2", [2, 96, DR])
        self.din("rwkv_g2", [256, DR])
        for nm in ("rwkv_k_k", "rwkv_k_a", "rwkv_r_k", "rwkv_lnx_g", "rwkv_lnx_b"):
            self.din(nm, [DR])
        self.din("w_out", [D, D])
        self.din("ffn_w_in", [D, 2 * DFF])
        self.din("ffn_conv_l", [128, 44, 4])
        self.din("ffn_w_out", [DFF, D])
        self.din("c_ident", [128, 128])
        self.din("c_masks", [6, 128, 128])
        self.din("c_zero", [1, NSH])
        for s in self.seqs:
            n = s.name
            self.dscr("H0_" + n, [s.L, D])
            self.dscr("QT_" + n, [HA, 128, s.L], BF16)
            self.dscr("KT_" + n, [HA, 128, s.L], BF16)
            self.dscr("V_" + n, [s.L, DA], BF16)
            self.dscr("P_" + n, [s.L, NSH])
            self.dscr("G_" + n, [s.L, DR])
            self.dscr("YF_" + n, [s.L, DR])
            self.dscr("BF_" + n, [s.L, DR])
            self.dscr("OT_" + n, [16, 128, s.L], BF16)
            self.dscr("H1_" + n, [s.L, D])
            self.dscr("H1T_" + n, [16, 128, s.L], BF16)
            if s.sliced:
                self.dscr("OTOK_" + n, [s.L, D], BF16)
                self.dscr("H1s_" + n, [1280, D])
                self.dscr("H1Ts_" + n, [16, 128, 1280], BF16)

    def setup_consts(self):
        P = self.P
        d = self.dram
        self.ident_f, k1 = P.tile([128, 128], F32, "identf")
        P.dma("sp", self.ident_f, d["c_ident"], (), (k1,))
        self.ident_b, k2 = P.tile([128, 128], BF16, "identb")
        P.copy("dve", self.ident_b, self.ident_f, (k1,), (k2,))
        self.kid = (k1, k2)
        self.const_end = P.off

    def bcast_load(self, name_ap, n, nm):
        P = self.P
        t, k = P.tile([128, n], F32, nm)
        P.dma("sp", t, name_ap.partition_broadcast(128), (), (k,))
        return t, k

    def layer_norm(self, x, kx, n, gb, kgb, out, kout, eps, tmp):
        P = self.P
        (st, kst), (mv, kmv), (rs, krs) = tmp
        g, b = gb
        for c in range(4):
            P.op("dve", lambda e, c=c: e.bn_stats(st[:n, c, :], x[:n, c * 512:(c + 1) * 512]), (kx,), (kst,))
        P.op("dve", lambda e: e.bn_aggr(mv[:n, :], st[:n].rearrange("p a b -> p (a b)")), (kst,), (kmv,))
        P.act(rs[:n, :], mv[:n, 1:2], AF.Sqrt, (kmv,), (krs,), bias=eps)
        P.op("dve", lambda e: e.reciprocal(rs[:n, :], rs[:n, :]), (krs,), (krs,))
        P.ts("dve", x[:n, :], x[:n, :], mv[:n, 0:1], rs[:n, 0:1], ALU.subtract, ALU.mult, (kx, kmv, krs), (kx,))
        P.tt("pool", x[:n, :], x[:n, :], g[:n, :], ALU.mult, (kx,) + kgb, (kx,))
        P.tt("dve", out[:n, :], x[:n, :], b[:n, :], ALU.add, (kx,) + kgb, (kout,))

    def ln_tmp(self):
        P = self.P
        return (P.tile([128, 4, 6], F32, "bnst"), P.tile([128, 2], F32, "bnmv"), P.tile([128, 1], F32, "rstd"))

    def transpose_to_fm(self, src, ksrc, n, dst, kdst, col0, banks, ctr):
        P = self.P
        for g4 in range(4):
            bi = banks[(ctr[0]) % len(banks)]
            ctr[0] += 1
            pb = P.bank(bi)
            kb = "psb%d" % bi
            for j in range(4):
                kc = g4 * 4 + j
                P.tr(pb[:, j * 128:j * 128 + n], src[:n, kc * 128:(kc + 1) * 128], self.ident_f[:n, :n],
                     (ksrc, self.kid[0]), (kb,))
            eng = "act" if g4 % 2 == 0 else "dve"
            P.copy(eng, dst[:, g4 * 4:g4 * 4 + 4, col0:col0 + n],
                   pb.rearrange("p (a b) -> p a b", a=4)[:, :, :n], (kb,), (kdst,))

    def load_w(self, wt, kw, src, rows, col0, ncols):
        P = self.P
        nk = rows // 128
        for k0 in range(0, nk, 4):
            k1 = min(nk, k0 + 4)
            P.dma("pool", wt[:, k0:k1, :ncols],
                  src[k0 * 128:k1 * 128, col0:col0 + ncols].rearrange("(kc p) n -> p kc n", p=128), (), (kw,))

    def stage1(self, s):
        P = self.P
        d = self.dram
        n_ = s.name
        P.barrier()
        P.reset_arena(self.const_end)
        g0, kg = self.bcast_load(d["emb_ln_g"], D, "g0")
        b0, kb = self.bcast_load(d["emb_ln_b"], D, "b0")
        tmp = self.ln_tmp()
        g2, kg2 = P.tile([128, 2, DR], BF16, "g2")
        self.load_w(g2, kg2, d["rwkv_g2"], 256, 0, DR)
        xt = [P.tile([128, D], F32, "xt%d" % i) for i in range(2)]
        ht = [P.tile([128, D], F32, "ht%d" % i) for i in range(2)]
        wbuf = [P.tile([128, 16, 512], BF16, "w%d" % i) for i in range(2)]
        stg = [P.tile([128, 512], F32, "stg%d" % i) for i in range(3)]
        stgb = [P.tile([128, 512], BF16, "stgb%d" % i) for i in range(2)]
        qkT = [P.tile([128, 4, 128], BF16, "qkT%d" % i) for i in range(2)]
        sgT = [P.tile([128, 2, 128], BF16, "sgT%d" % i) for i in range(2)]
        gst = [P.tile([128, DR], F32, "gst%d" % i) for i in range(2)]
        TB = 16
        h0T, kh0T = P.tile([128, 16, TB * 128 + NM], BF16, "h0T")
        groups = []
        for c0 in range(0, NIN, 512):
            groups.append((c0, min(512, NIN - c0)))
        tiles = list(range(s.ntile))
        blocks = [tiles[0:TB + 1]] + [tiles[i:i + TB] for i in range(TB + 1, s.ntile, TB)]
        ctr = [0]
        wi = 0
        ci = 0
        for blk in blocks:
            base = s.tile_range(blk[0])[0]
            def ln_phase(ti, slot):
                t0, n = s.tile_range(ti)
                x, kx = xt[slot]
                h, kh = ht[slot]
                if ti == 0:
                    P.dma("sp", x[:n, :], d["meta_tokens"], (), (kx,))
                else:
                    P.dma("sp", x[:n, :], d["x_" + n_][t0 - NM:t0 - NM + n, :], (), (kx,))
                self.layer_norm(x, kx, n, (g0, b0), (kg, kb), h, kh, LN_EPS, tmp)

            def tr_phase(ti, slot):
                t0, n = s.tile_range(ti)
                h, kh = ht[slot]
                if not s.sliced:
                    P.dma("sp", d["H0_" + n_][t0:t0 + n, :], h[:n, :], (kh,), ())
                self.transpose_to_fm(h, kh, n, h0T, kh0T, t0 - base, (0, 1), ctr)

            ln_phase(blk[0], ci % 2)
            for bi_, ti in enumerate(blk):
                if bi_ + 1 < len(blk):
                    ln_phase(blk[bi_ + 1], (ci + 1) % 2)
                tr_phase(ti, ci % 2)
                ci += 1
            if self.stages == 0:
                continue
            for gi, (c0, nc_) in enumerate(groups):
                w, kw = wbuf[wi % 2]
                wi += 1
                self.load_w(w, kw, d["w_in"], D, c0, nc_)
                for ti in blk:
                    t0, n = s.tile_range(ti)
                    bi = (2, 3, 6, 7)[ctr[0] % 4]
                    ctr[0] += 1
                    pb = P.bank(bi)
                    kpb = "psb%d" % bi
                    for kc in range(16):
                        P.mm(pb[:n, :nc_], h0T[:, kc, t0 - base:t0 - base + n], w[:, kc, :nc_],
                             kc == 0, kc == 15, (kh0T, kw), (kpb,))
                    ev = "act" if ctr[0] % 2 == 0 else "dve"
                    if self.stages == -1 or (self.stages == -2 and gi < 4) or (self.stages == -3 and gi >= 4) or (self.stages == -4 and gi < 12):
                        sf, ksf = stg[ctr[0] % 3]
                        P.copy(ev, sf[:n, :], pb[:n, :], (kpb,), (ksf,))
                        continue
                    if gi < 4:
                        sb, ksb = stgb[ctr[0] % 2]
                        if gi < 2:
                            P.act(sb[:n, :], pb[:n, :], AF.Copy, (kpb,), (ksb,), scale=128.0 ** -0.5)
                        else:
                            P.copy(ev, sb[:n, :], pb[:n, :], (kpb,), (ksb,))
                        tb = 4
                        ptb = P.bank(tb, BF16)
                        for j in range(4):
                            P.tr(ptb[:, j * 128:j * 128 + n], sb[:n, j * 128:(j + 1) * 128], self.ident_b[:n, :n],
                                 (ksb, self.kid[1]), ("psb4",))
                        qt, kqt = qkT[ctr[0] % 2]
                        P.copy("dve" if ev == "act" else "act", qt[:, :, :n],
                               ptb[:, 0:512].rearrange("p (a b) -> p a b", a=4)[:, :, :n], ("psb4",), (kqt,))
                        dst = d[("QT_" if gi < 2 else "KT_") + n_]
                        h0 = (gi % 2) * 4
                        P.dma("sp", dst[h0:h0 + 4, :, t0:t0 + n].rearrange("h p t -> p h t"), qt[:, :, :n], (kqt,), ())
                    elif gi < 6:
                        sb, ksb = stgb[ctr[0] % 2]
                        P.copy(ev, sb[:n, :], pb[:n, :], (kpb,), (ksb,))
                        P.dma("sp", d["V_" + n_][t0:t0 + n, (gi - 4) * 512:(gi - 3) * 512], sb[:n, :], (ksb,), ())
                    elif gi < 12:
                        sf, ksf = stg[ctr[0] % 3]
                        P.copy(ev, sf[:n, :], pb[:n, :], (kpb,), (ksf,))
                        P.dma("sp", d["P_" + n_][t0:t0 + n, (gi - 6) * 512:(gi - 5) * 512], sf[:n, :], (ksf,), ())
                    else:
                        sf, ksf = stg[ctr[0] % 3]
                        P.copy("dve", sf[:n, :192], pb[:n, :192], (kpb,), (ksf,))
                        P.dma("sp", d["P_" + n_][t0:t0 + n, 3072:3264], sf[:n, :192], (ksf,), ())
                        sb, ksb = stgb[ctr[0] % 2]
                        P.act(sb[:n, :256], pb[:n, 192:448], AF.Sigmoid, (kpb,), (ksb,))
                        ptb = P.bank(4, BF16)
                        for j in range(2):
                            P.tr(ptb[:, j * 128:j * 128 + n], sb[:n, j * 128:(j + 1) * 128], self.ident_b[:n, :n],
                                 (ksb, self.kid[1]), ("psb4",))
                        sg, ksg = sgT[ctr[0] % 2]
                        P.copy("dve", sg[:, :, :n], ptb[:, 0:256].rearrange("p (a b) -> p a b", a=2)[:, :, :n],
                               ("psb4",), (ksg,))
                        go, kgo = gst[ctr[0] % 2]
                        for hf in range(2):
                            pg = P.bank(5)
                            kpg = "psb5"
                            for j in range(2):
                                P.mm(pg[:n, :], sg[:, j, :n], g2[:, j, hf * 512:(hf + 1) * 512], j == 0, j == 1,
                                     (ksg, kg2), (kpg,))
                            P.copy("act" if hf == 0 else "dve", go[:n, hf * 512:(hf + 1) * 512], pg[:n, :], (kpg,), (kgo,))
                        P.dma("sp", d["G_" + n_][t0:t0 + n, :], go[:n, :], (kgo,), ())

    def stage2(self, s):
        P = self.P
        d = self.dram
        n_ = s.name
        for dr in (0, 1):
            P.barrier()
            P.reset_arena(self.const_end)
            self._rwkv_dir(s, dr)

    def _rwkv_dir(self, s, dr):
        P = self.P
        d = self.dram
        n_ = s.name
        Pd = d["P_" + n_]
        L = s.L
        mu, kmu = self.bcast_load(d["rwkv_mu"][dr], NSH, "mu")
        w0, kw0 = self.bcast_load(d["rwkv_w0"][dr], DR, "w0")
        a0, ka0 = self.bcast_load(d["rwkv_a0"][dr], DR, "a0")
        kkp, kkkp = self.bcast_load(d["rwkv_k_k"], DR, "k_k")
        kap, kkap = self.bcast_load(d["rwkv_k_a"], DR, "k_a")
        rkp, krkp = self.bcast_load(d["rwkv_r_k"], DR, "r_k")
        w2, kw2 = P.tile([128, DR], F32, "w2")
        a2, ka2 = P.tile([128, DR], F32, "a2")
        P.dma("sp", w2[:96, :], d["rwkv_w2"][dr], (), (kw2,))
        P.dma("sp", a2[:96, :], d["rwkv_a2"][dr], (), (ka2,))
        tri, ktri = P.tile([128, 128], F32, "tri")
        P.dma("sp", tri, d["c_masks"][dr], (), (ktri,))
        ones, kones = P.tile([128, 128], F32, "ones")
        P.memset("pool", ones, 1.0, (kones,))
        m4, km4 = P.tile([128, 4, 128], F32, "mask4")
        for q in range(4):
            P.dma("sp", m4[:, q, :], d["c_masks"][2 + 2 * dr + (q % 2)], (), (km4,))
        suT, ksuT = P.tile([128, 128], F32, "suT")
        P.dma("sp", suT, d["c_masks"][2 + 2 * (1 - dr)], (), (ksuT,))
        if dr == 1:
            lg, klg = self.bcast_load(d["rwkv_lnx_g"], DR, "lnxg")
            lb, klb = self.bcast_load(d["rwkv_lnx_b"], DR, "lnxb")
        Sf, kSf = P.tile([128, 8, 64], F32, "Sf")
        Sb, kSb = P.tile([128, 8, 128], BF16, "Sbd")
        P.memset("pool", Sf, 0.0, (kSf,))
        P.memset("pool", Sb, 0.0, (kSb,))
        kSfh = [[kSf + "/%d/%d" % (j, h) for h in range(2)] for j in range(8)]
        kSbh = [[kSb + "/%d/%d" % (j, h) for h in range(2)] for j in range(8)]
        for j in range(8):
            for h in range(2):
                P.last_w[kSfh[j][h]] = P.last_w[kSf]
                P.last_w[kSbh[j][h]] = P.last_w[kSb]
        pt = [P.tile([128, NSH], F32, "p%d" % i) for i in range(1)]
        pn = [P.tile([128, NSH], F32, "pn%d" % i) for i in range(1)]
        T = lambda nm, dt=F32: P.tile([128, DR], dt, nm)
        sg, ksg = T("sg"); av, kav = T("a"); cum, kcum = T("cum"); e1, ke1 = T("e1"); e2, ke2 = T("e2")
        kk, kkk = T("kk"); kq, kkq = T("kq"); ka_, kka_ = T("ka"); tmp, ktmp = T("tmp"); bon, kbon = T("bon")
        Y, kY = T("Y")
        At, kAt = T("At", BF16); Bt, kBt = T("Bt", BF16); Kt, kKt = T("Kt", BF16); Rt, kRt = T("Rt", BF16)
        BW, kBW = T("BW", BF16); KW, kKW = T("KW", BF16); Vb, kVb = T("Vb", BF16)
        ART, kART = P.tile([128, 8, 256], BF16, "ART")
        BTt, kBTt = P.tile([128, 8, 128], BF16, "BT")
        KTt, kKTt = P.tile([128, 8, 128], BF16, "KT")
        thT, kthT = P.tile([128, 2, 128], F32, "thT")
        th, kth = P.tile([128, 192], F32, "th")
        sm, ksm = P.tile([128, 4, 16], F32, "small")
        wc, kwc = P.tile([128, 8], F32, "wc")
        if dr == 1:
            yf, kyf = e1, ke1
            gt, kgt = e2, ke2
            ob, kob = T("ob", BF16)
            oT, koT = P.tile([128, 8, 128], BF16, "oT")
        M4a, kM4a0 = P.tile([128, 16, 4, 128], BF16, "M4a")
        kM4 = [kM4a0 + "/%d" % i for i in range(16)]
        NTa, kNTa0 = P.tile([128, 16, 128], BF16, "NTa")
        kNT = [kNTa0 + "/%d" % g for g in range(4)]
        Ta = []
        PPa = []
        for i in range(2):
            t_, k_ = P.tile([128, 16, 128], BF16, "Ta%d" % i)
            Ta.append((t_, [k_ + "/%d" % g for g in range(4)]))
            t_, k_ = P.tile([128, 2, 16, 128], BF16, "PPa%d" % i)
            PPa.append((t_, [[k_ + "/%d/%d" % (w, g) for g in range(4)] for w in range(2)]))
        Xa, kXa0 = P.tile([128, 16, 64], BF16, "Xa")
        kXa = [kXa0 + "/%d" % b for b in range(2)]
        Ua, kUa0 = P.tile([128, 16, 64], BF16, "Ua")
        kUa = [kUa0 + "/%d" % b for b in range(2)]
        id4, kid4 = P.tile([128, 4, 128], BF16, "id4")
        for q in range(4):
            P.copy("pool", id4[:, q, :], self.ident_b, (self.kid[1],), (kid4,))
        order = list(range(s.ntile)) if dr == 0 else list(range(s.ntile - 1, -1, -1))
        tctr = [0]
        v3 = lambda ap: ap.rearrange("p (h c) -> p h c", h=16)
        p, kp = pt[0]
        q_, kq_ = pn[0]
        f, kf = q_, kq_
        r_ = f[:, 0:DR]; k_ = f[:, DR:2 * DR]; v_ = f[:, 2 * DR:3 * DR]

        def prep1(ti):
            t0, n = s.tile_range(ti)
            P.dma("sp", p[:n, :], Pd[t0:t0 + n, :], (), (kp,))
            if dr == 0:
                if t0 == 0:
                    P.dma("sp", q_[0:1, :], d["c_zero"], (), (kq_,))
                    P.dma("sp", q_[1:n, :], Pd[0:n - 1, :], (), (kq_,))
                else:
                    P.dma("sp", q_[:n, :], Pd[t0 - 1:t0 - 1 + n, :], (), (kq_,))
            else:
                if t0 + n == L:
                    P.dma("sp", q_[n - 1:n, :], d["c_zero"], (), (kq_,))
                    P.dma("sp", q_[0:n - 1, :], Pd[t0 + 1:t0 + n, :], (), (kq_,))
                else:
                    P.dma("sp", q_[:n, :], Pd[t0 + 1:t0 + 1 + n, :], (), (kq_,))
            yield
            P.tt("dve", q_[:n, :], q_[:n, :], p[:n, :], ALU.subtract, (kq_, kp), (kq_,))
            yield
            P.tt("pool", q_[:n, :], q_[:n, :], mu[:n, :], ALU.mult, (kq_, kmu), (kq_,))
            yield
            P.tt("dve", q_[:n, :], q_[:n, :], p[:n, :], ALU.add, (kq_, kp), (kq_,))
            yield
            P.act(th[:n, 0:96], f[:n, 3 * DR:3 * DR + 96], AF.Tanh, (kf,), (kth,))
            P.copy("pool", th[:n, 96:192], f[:n, 3 * DR + 96:3 * DR + 192], (kf,), (kth,))
            pb0 = P.bank(0)
            for q in range(2):
                P.tr(pb0[:96, q * 128:q * 128 + n], th[:n, q * 96:(q + 1) * 96], self.ident_f[:n, :n],
                     (kth, self.kid[0]), ("psb0",))
            P.copy("act", thT[:96, :, :n], pb0[:96, 0:256].rearrange("p (a b) -> p a b", a=2)[:, :, :n], ("psb0",), (kthT,))
            yield
            for which, (wm, kwm, bias, kbias, dst, kdst) in enumerate(((w2, kw2, w0, kw0, sg, ksg), (a2, ka2, a0, ka0, av, kav))):
                for hf in range(2):
                    pb = P.bank(hf); kpb = "psb%d" % hf
                    P.mm(pb[:n, :], thT[:96, which, :n], wm[:96, hf * 512:(hf + 1) * 512], True, True, (kthT, kwm), (kpb,))
                    P.tt("dve", dst[:n, hf * 512:(hf + 1) * 512], pb[:n, :], bias[:n, hf * 512:(hf + 1) * 512], ALU.add,
                         (kpb, kbias), (kdst,))
                    yield
                P.act(dst[:n, :], dst[:n, :], AF.Sigmoid, (kdst,), (kdst,))
                yield
            for hf in range(2):
                pb = P.bank(hf); kpb = "psb%d" % hf
                P.mm(pb[:n, :], tri[:n, :n], sg[:n, hf * 512:(hf + 1) * 512], True, True, (ktri, ksg), (kpb,))
                P.copy("act", cum[:n, hf * 512:(hf + 1) * 512], pb[:n, :], (kpb,), (kcum,))
                yield
            P.tt("pool", kk[:n, :], k_[:n, :], kkp[:n, :], ALU.mult, (kf, kkkp), (kkk,))
            yield
            P.tt("pool", tmp[:n, :], kk[:n, :], kk[:n, :], ALU.mult, (kkk,), (ktmp,))
            yield
            P.reduce("dve", sm[:n, 0, :], v3(tmp[:n, :]), ALU.add, (ktmp,), (ksm,))
            P.act(sm[:n, 0, :], sm[:n, 0, :], AF.Sqrt, (ksm,), (ksm,))
            P.ts("dve", sm[:n, 0, :], sm[:n, 0, :], 1e-12, None, ALU.max, None, (ksm,), (ksm,))
            P.recip("dve", sm[:n, 0, :], sm[:n, 0, :], (ksm,), (ksm,))
            yield
            P.tt("dve", v3(kk[:n, :]), v3(kk[:n, :]), sm[:n, 0, :].unsqueeze(2).to_broadcast([n, 16, 64]), ALU.mult,
                 (kkk, ksm), (kkk,))
            yield
            P.stt("dve", kq[:n, :], av[:n, :], -1.0, kap[:n, :], ALU.add, ALU.mult, (kav, kkap), (kkq,))
            yield
            P.stt("dve", kq[:n, :], kq[:n, :], 1.0, k_[:n, :], ALU.add, ALU.mult, (kkq, kf), (kkq,))
            yield
            P.tt("pool", ka_[:n, :], kk[:n, :], av[:n, :], ALU.mult, (kkk, kav), (kka_,))
            yield

        def prep2(ti):
            t0, n = s.tile_range(ti)
            for hf in range(2):
                pb = P.bank(2 + hf); kpb = "psb%d" % (2 + hf)
                P.mm(pb[:n, :], ones[:n, :n], sg[:n, hf * 512:(hf + 1) * 512], True, True, (kones, ksg), (kpb,))
                P.tt("dve", e2[:n, hf * 512:(hf + 1) * 512], pb[:n, :], cum[:n, hf * 512:(hf + 1) * 512], ALU.subtract,
                     (kpb, kcum), (ke2,))
            pb0 = P.bank(0)
            for j in range(8):
                P.mm(pb0[:, j:j + 1], sg[:n, j * 128:(j + 1) * 128], ones[:n, 0:1], True, True, (ksg, kones), ("psb0",))
            P.act(wc[:, :], pb0[:, 0:8], AF.Exp, ("psb0",), (kwc,), scale=CDEC)
            P.tt("pool", tmp[:n, :], r_[:n, :], kq[:n, :], ALU.mult, (kf, kkq), (ktmp,))
            P.tt("pool", tmp[:n, :], tmp[:n, :], rkp[:n, :], ALU.mult, (ktmp, krkp), (ktmp,))
            P.reduce("dve", sm[:n, 1, :], v3(tmp[:n, :]), ALU.add, (ktmp,), (ksm,))
            P.tt("dve", v3(bon[:n, :]), v3(v_[:n, :]), sm[:n, 1, :].unsqueeze(2).to_broadcast([n, 16, 64]), ALU.mult,
                 (kf, ksm), (kbon,))
            if dr == 0:
                P.dma("sp", d["BF_" + n_][t0:t0 + n, :], bon[:n, :], (kbon,), ())
            P.act(e1[:n, :], cum[:n, :], AF.Exp, (kcum,), (ke1,), scale=CDEC)
            P.tt("dve", Rt[:n, :], r_[:n, :], e1[:n, :], ALU.mult, (kf, ke1), (kRt,))
            P.act(e1[:n, :], cum[:n, :], AF.Exp, (kcum,), (ke1,), scale=-CDEC)
            P.tt("pool", Bt[:n, :], ka_[:n, :], e1[:n, :], ALU.mult, (kka_, ke1), (kBt,))
            P.tt("dve", Kt[:n, :], kq[:n, :], e1[:n, :], ALU.mult, (kkq, ke1), (kKt,))
            P.act(e2[:n, :], e2[:n, :], AF.Exp, (ke2,), (ke2,), scale=CDEC)
            P.tt("pool", BW[:n, :], ka_[:n, :], e2[:n, :], ALU.mult, (kka_, ke2), (kBW,))
            P.tt("dve", KW[:n, :], kq[:n, :], e2[:n, :], ALU.mult, (kkq, ke2), (kKW,))
            P.tt("dve", cum[:n, :], cum[:n, :], sg[:n, :], ALU.subtract, (kcum, ksg), (kcum,))
            P.act(e1[:n, :], cum[:n, :], AF.Exp, (kcum,), (ke1,), scale=CDEC)
            P.stt("dve", At[:n, :], kk[:n, :], -1.0, e1[:n, :], ALU.mult, ALU.mult, (kkk, ke1), (kAt,))
            P.copy("pool", Vb[:n, :], v_[:n, :], (kf,), (kVb,))
            for xi, (src, ksrc) in enumerate(((At, kAt), (Rt, kRt), (Bt, kBt), (Kt, kKt))):
                bi = tctr[0] % 4
                tctr[0] += 1
                ptb = P.bank(bi, BF16)
                kptb = "psb%d" % bi
                for j in range(8):
                    P.tr(ptb[:, j * 128:j * 128 + n], src[:n, j * 128:(j + 1) * 128], self.ident_b[:n, :n],
                         (ksrc, self.kid[1]), (kptb,))
                srcv = ptb.rearrange("p (a b) -> p a b", a=8)[:, :, :n]
                if xi == 0:
                    P.copy("act", ART[:, :, 0:n], srcv, (kptb,), (kART,))
                elif xi == 1:
                    P.copy("dve", ART[:, :, 128:128 + n], srcv, (kptb,), (kART,))
                elif xi == 2:
                    P.copy("act", BTt[:, :, :n], srcv, (kptb,), (kBTt,))
                else:
                    P.copy("dve", KTt[:, :, :n], srcv, (kptb,), (kKTt,))

        def units(ti, gen):
            t0, n = s.tile_range(ti)

            def pump():
                next(gen, None)
            nsq = int(np.log2(n)) - 1
            HD = [dict(hd=hd, j=hd // 2, hb=64 * (hd % 2), col=hd * 64, g=hd // 4, sl=hd % 4) for hd in range(16)]
            bk = lambda g: P.bank(4 + g)
            kbk = lambda g: "psb%d" % (4 + g)
            for H in HD:
                if H["hd"] % 4 == 0:
                    pump()
                hd, j, hb = H["hd"], H["j"], H["hb"]
                b_ = bk(hd % 4); kb_ = kbk(hd % 4)
                P.mm(b_[:n, 0:256], BTt[hb:hb + 64, j, :n], ART[hb:hb + 64, j, :], True, True, (kBTt, kART), (kb_,))
                P.mm(b_[:n, 256:512], KTt[hb:hb + 64, j, :n], ART[hb:hb + 64, j, :], True, True, (kKTt, kART), (kb_,))
                P.tt("dve", M4a[:n, hd, :, :n], b_[:n, :].rearrange("p (a b) -> p a b", a=4)[:, :, :n], m4[:n, :, :n], ALU.mult,
                     (kb_, km4), (kM4[hd],))
            for H in HD:
                if H["hd"] % 4 == 0:
                    pump()
                hd, j, hb = H["hd"], H["j"], H["hb"]
                nb = (hd % 2) * 2 + hd // 8
                sl = (hd % 8) // 2
                P.mm(bk(nb)[:n, sl * 128:sl * 128 + n], ART[hb:hb + 64, j, 0:n], BTt[hb:hb + 64, j, :n], True, True,
                     (kART, kBTt), (kbk(nb),))
                if sl == 3:
                    base = (hd // 8) * 8 + hd % 2
                    P.tt("dve", NTa[:n, base:base + 7:2, :n], bk(nb)[:n, :].rearrange("p (a b) -> p a b", a=4)[:, :, :n],
                         suT[:n, :n].unsqueeze(1).to_broadcast([n, 4, n]), ALU.mult, (kbk(nb), ksuT), (kNT[nb],))
            T0, kT0 = Ta[0]
            for g in range(4):
                P.tt("pool", T0[:n, g * 4:g * 4 + 4, :n], M4a[:n, g * 4:g * 4 + 4, 0, :n], id4[:n, :, :n], ALU.add,
                     tuple(kM4[g * 4:g * 4 + 4]) + (kid4,), (kT0[g],))
            Pget = lambda hd: (M4a[:, hd, 0, :], kM4[hd])
            PTget = lambda hd: (NTa[:, hd, :], kNT[(hd % 2) * 2 + hd // 8])
            tcur = 0
            for it in range(1, nsq + 1):
                last = it == nsq
                PPn, kPPn = PPa[it % 2]
                for H in HD:
                    if H["hd"] % 4 == 0:
                        pump()
                    hd, g, sl = H["hd"], H["g"], H["sl"]
                    Pc, kPc = Pget(hd); PTc, kPTc = PTget(hd)
                    P.mm(bk(g)[:n, sl * 128:sl * 128 + n], Pc[:n, :n], PTc[:n, :n], True, True, (kPc, kPTc), (kbk(g),))
                    if sl == 3:
                        P.copy("act", PPn[:n, 1, g * 4:g * 4 + 4, :n], bk(g)[:n, :].rearrange("p (a b) -> p a b", a=4)[:, :, :n],
                               (kbk(g),), (kPPn[1][g],))
                if not last:
                    for H in HD:
                        if H["hd"] % 4 == 0:
                            pump()
                        hd, g, sl = H["hd"], H["g"], H["sl"]
                        Pc, kPc = Pget(hd); PTc, kPTc = PTget(hd)
                        P.mm(bk(g)[:n, sl * 128:sl * 128 + n], PTc[:n, :n], Pc[:n, :n], True, True, (kPc, kPTc), (kbk(g),))
                        if sl == 3:
                            P.copy("act" if g % 2 else "dve", PPn[:n, 0, g * 4:g * 4 + 4, :n],
                                   bk(g)[:n, :].rearrange("p (a b) -> p a b", a=4)[:, :, :n], (kbk(g),), (kPPn[0][g],))
                Pget = lambda hd, PPn=PPn, kPPn=kPPn: (PPn[:, 0, hd, :], kPPn[0][hd // 4])
                PTget = lambda hd, PPn=PPn, kPPn=kPPn: (PPn[:, 1, hd, :], kPPn[1][hd // 4])
                Tc, kTc = Ta[tcur]
                Tn, kTn = Ta[1 - tcur]
                for H in HD:
                    if H["hd"] % 4 == 0:
                        pump()
                    hd, g, sl = H["hd"], H["g"], H["sl"]
                    PTc, kPTc = PTget(hd)
                    P.mm(bk(g)[:n, sl * 128:sl * 128 + n], PTc[:n, :n], Tc[:n, hd, :n], True, True, (kPTc, kTc[g]), (kbk(g),))
                    if sl == 3:
                        P.tt("dve", Tn[:n, g * 4:g * 4 + 4, :n], bk(g)[:n, :].rearrange("p (a b) -> p a b", a=4)[:, :, :n],
                             Tc[:n, g * 4:g * 4 + 4, :n], ALU.add, (kbk(g), kTc[g]), (kTn[g],))
                tcur = 1 - tcur
            Tc, kTc = Ta[tcur]
            for j in range(8):
                if j % 2 == 0:
                    pump()
                b = j // 4
                o2 = bk(b)[:n, (j % 4) * 128:(j % 4) * 128 + 128]
                P.mm(o2, ART[:, j, 0:n], Sb[:, j, :], True, False, (kART, kSb), (kbk(b),))
                for h2 in range(2):
                    hd = 2 * j + h2
                    col = hd * 64
                    P.mm(bk(b)[:n, (hd % 8) * 64:(hd % 8) * 64 + 64], M4a[:n, hd, 2, :n], Vb[:n, col:col + 64], False, h2 == 1,
                         (kM4[hd], kVb), (kbk(b),))
                if j % 4 == 3:
                    P.copy("act", Xa[:n, b * 8:b * 8 + 8, :], bk(b)[:n, :].rearrange("p (a b) -> p a b", a=8), (kbk(b),), (kXa[b],))
            for H in HD:
                if H["hd"] % 4 == 0:
                    pump()
                hd = H["hd"]
                b = hd // 8
                P.mm(bk(2 + b)[:n, (hd % 8) * 64:(hd % 8) * 64 + 64], Tc[:n, hd, :n], Xa[:n, hd, :], True, True,
                     (kTc[hd // 4], kXa[b]), (kbk(2 + b),))
                if hd % 8 == 7:
                    P.copy("act", Ua[:n, b * 8:b * 8 + 8, :], bk(2 + b)[:n, :].rearrange("p (a b) -> p a b", a=8), (kbk(2 + b),), (kUa[b],))
            for j in range(8):
                if j % 2 == 0:
                    pump()
                b = j // 4
                o2 = bk(b)[:n, (j % 4) * 128:(j % 4) * 128 + 128]
                P.mm(o2, ART[:, j, 128:128 + n], Sb[:, j, :], True, False, (kART, kSb), (kbk(b),))
                for h2 in range(2):
                    hd = 2 * j + h2
                    col = hd * 64
                    o_ = bk(b)[:n, (hd % 8) * 64:(hd % 8) * 64 + 64]
                    P.mm(o_, M4a[:n, hd, 1, :n], Ua[:n, hd, :], False, False, (kM4[hd], kUa[b]), (kbk(b),))
                    P.mm(o_, M4a[:n, hd, 3, :n], Vb[:n, col:col + 64], False, h2 == 1, (kM4[hd], kVb), (kbk(b),))
                if j % 4 == 3:
                    P.copy("act", Y[:n, b * 512:(b + 1) * 512], bk(b)[:n, :], (kbk(b),), (kY,))
            for j in range(8):
                if j % 2 == 0:
                    pump()
                b = 2 + j // 4
                o2 = bk(b)[:, (j % 4) * 128:(j % 4) * 128 + 128]
                P.mm(o2, BW[:n, j * 128:(j + 1) * 128], Ua[:n, 2 * j:2 * j + 2, :].rearrange("p a b -> p (a b)"), True, False,
                     (kBW, kUa[j // 4]), (kbk(b),))
                P.mm(o2, KW[:n, j * 128:(j + 1) * 128], Vb[:n, j * 128:(j + 1) * 128], False, True, (kKW, kVb), (kbk(b),))
            P.tt("dve", Sf[:, :, :], Sf[:, :, :], wc[:, :].unsqueeze(2).to_broadcast([128, 8, 64]), ALU.mult, (kSf, kwc), (kSf,))
            for bq in range(2):
                pv = bk(2 + bq)[:, :].rearrange("p (j c) -> p j c", j=4)
                for hb in (0, 64):
                    P.tt("dve", Sf[hb:hb + 64, bq * 4:bq * 4 + 4, :], Sf[hb:hb + 64, bq * 4:bq * 4 + 4, :], pv[hb:hb + 64, :, hb:hb + 64],
                         ALU.add, (kSf, kbk(2 + bq)), (kSf,))
            for hb in (0, 64):
                P.copy("pool", Sb[hb:hb + 64, :, hb:hb + 64], Sf[hb:hb + 64, :, :], (kSf,), (kSb,))

        def epilogue(ti):
            t0, n = s.tile_range(ti)
            if dr == 0:
                P.dma("sp", d["YF_" + n_][t0:t0 + n, :], Y[:n, :], (kY,), ())
            else:
                P.dma("sp", yf[:n, :], d["YF_" + n_][t0:t0 + n, :], (), (kyf,))
                P.dma("sp", gt[:n, :], d["G_" + n_][t0:t0 + n, :], (), (kgt,))
                P.tt("dve", Y[:n, :], Y[:n, :], yf[:n, :], ALU.add, (kY, kyf), (kY,))
                P.dma("sp", yf[:n, :], d["BF_" + n_][t0:t0 + n, :], (kY,), (kyf,))
                P.reduce("dve", sm[:n, 2, :], v3(Y[:n, :]), ALU.add, (kY,), (ksm,))
                P.tt("pool", tmp[:n, :], Y[:n, :], Y[:n, :], ALU.mult, (kY,), (ktmp,))
                P.reduce("dve", sm[:n, 3, :], v3(tmp[:n, :]), ALU.add, (ktmp,), (ksm,))
                P.ts("dve", sm[:n, 2, :], sm[:n, 2, :], 1.0 / 64, None, ALU.mult, None, (ksm,), (ksm,))
                P.ts("dve", sm[:n, 3, :], sm[:n, 3, :], 1.0 / 64, None, ALU.mult, None, (ksm,), (ksm,))
                P.tt("dve", sm[:n, 0, :], sm[:n, 2, :], sm[:n, 2, :], ALU.mult, (ksm,), (ksm,))
                P.tt("dve", sm[:n, 3, :], sm[:n, 3, :], sm[:n, 0, :], ALU.subtract, (ksm,), (ksm,))
                P.act(sm[:n, 3, :], sm[:n, 3, :], AF.Sqrt, (ksm,), (ksm,), bias=GN_EPS)
                P.recip("dve", sm[:n, 3, :], sm[:n, 3, :], (ksm,), (ksm,))
                bc = lambda q: sm[:n, q, :].unsqueeze(2).to_broadcast([n, 16, 64])
                P.tt("dve", v3(Y[:n, :]), v3(Y[:n, :]), bc(2), ALU.subtract, (kY, ksm), (kY,))
                P.tt("dve", v3(Y[:n, :]), v3(Y[:n, :]), bc(3), ALU.mult, (kY, ksm), (kY,))
                P.tt("pool", Y[:n, :], Y[:n, :], lg[:n, :], ALU.mult, (kY, klg), (kY,))
                P.tt("pool", Y[:n, :], Y[:n, :], lb[:n, :], ALU.add, (kY, klb), (kY,))
                P.tt("dve", Y[:n, :], Y[:n, :], bon[:n, :], ALU.add, (kY, kbon), (kY,))
                P.tt("dve", Y[:n, :], Y[:n, :], yf[:n, :], ALU.add, (kY, kyf), (kY,))
                P.tt("dve", ob[:n, :], Y[:n, :], gt[:n, :], ALU.mult, (kY, kgt), (kob,))
                if s.sliced:
                    P.dma("sp", d["OTOK_" + n_][t0:t0 + n, DA:D], ob[:n, :], (kob,), ())
                    return
                bi = tctr[0] % 4
                tctr[0] += 1
                ptb = P.bank(bi, BF16)
                kptb = "psb%d" % bi
                for j in range(8):
                    P.tr(ptb[:, j * 128:j * 128 + n], ob[:n, j * 128:(j + 1) * 128], self.ident_b[:n, :n],
                         (kob, self.kid[1]), (kptb,))
                P.copy("act", oT[:, :, :n], ptb.rearrange("p (a b) -> p a b", a=8)[:, :, :n], (kptb,), (koT,))
                P.dma("sp", d["OT_" + n_][8:16, :, t0:t0 + n].rearrange("c p t -> p c t"), oT[:, :, :n], (koT,), ())

        for _ in prep1(order[0]):
            pass
        for idx, ti in enumerate(order):
            prep2(ti)
            gen = prep1(order[idx + 1]) if idx + 1 < len(order) else iter(())
            units(ti, gen)
            for _ in gen:
                pass
            epilogue(ti)


    def stage3(self, s):
        P = self.P
        d = self.dram
        n_ = s.name
        L = s.L
        P.barrier()
        P.reset_arena(self.const_end)
        kth = [P.tile([128, L], BF16, "kth%d" % i) for i in range(2)]
        qth = [P.tile([128, L], BF16, "qth%d" % i) for i in range(2)]
        vh = [P.tile([128, s.ntile, 128], BF16, "vh%d" % i) for i in range(2)]
        bt = [P.tile([128, 5, NKEY], F32, "bias%d" % i) for i in range(2)]
        oTh = [P.tile([128, L], BF16, "oTh%d" % i) for i in range(2)]
        ND = 4
        sc = [P.tile([128, NKEY], F32, "sc%d" % i) for i in range(ND)]
        pr = [P.tile([128, NKEY], BF16, "pr%d" % i) for i in range(ND)]
        pT = [P.tile([128, 6, 128], BF16, "pT%d" % i) for i in range(ND)]
        ob = [P.tile([128, 128], BF16, "ob%d" % i) for i in range(ND)]
        st = [P.tile([128, 4], F32, "st%d" % i) for i in range(2 * ND)]
        uc = [0]

        def mk_unit(kind, rp, h, tiles):
            kt, kkt, qt, kqt, v, kv, b, kb, oh, koh = tiles
            u = uc[0]
            uc[0] += 1
            U = dict(kind=kind, h=h, tiles=tiles)
            U["sc"] = sc[u % ND]; U["pr"] = pr[u % ND]; U["pT"] = pT[u % ND]; U["ob"] = ob[u % ND]; U["st"] = st[u % (2 * ND)]
            ba = (u % ND) * 2
            U["ba"] = ba
            if kind == "meta":
                U.update(nq=NM, q0=0, lo=640, blocks=[(5, NM, 0)])
            else:
                r = 2 * rp
                ws = min(max(r - 4, 0), s.rows - WINR)
                assert ws % 2 == 0
                U.update(nq=128, q0=NM + rp * 128, k0=NM + ws * 64, cls=(r - ws) // 2, lo=0,
                         blocks=[(j, 128, ws // 2 + 1 + j) for j in range(5)] + [(5, NM, 0)])
            return U

        def phA(U):
            kt, kkt, qt, kqt, v, kv, b, kb, oh, koh = U["tiles"]
            s_, ks_ = U["sc"]; st_, kst_ = U["st"]
            ba = U["ba"]; nq = U["nq"]; q0 = U["q0"]; lo = U["lo"]
            pa = P.bank(ba); pbk = P.bank(ba + 1)
            kpa = "psb%d" % ba; kpb = "psb%d" % (ba + 1)
            if U["kind"] == "meta":
                P.mm(pbk[:nq, 128:144], qt[:, 0:NM], kt[:, 0:NM], True, True, (kqt, kkt), (kpb,))
                P.copy("dve", s_[:nq, 640:656], pbk[:nq, 128:144], (kpb,), (ks_,))
            else:
                k0 = U["k0"]; cls = U["cls"]
                P.mm(pa[:, :], qt[:, q0:q0 + 128], kt[:, k0:k0 + 512], True, True, (kqt, kkt), (kpa,))
                P.mm(pbk[:, 0:128], qt[:, q0:q0 + 128], kt[:, k0 + 512:k0 + 640], True, True, (kqt, kkt), (kpb,))
                P.mm(pbk[:, 128:144], qt[:, q0:q0 + 128], kt[:, 0:NM], True, True, (kqt, kkt), (kpb,))
                P.tt("dve", s_[:, 0:512], pa[:, :], b[:, cls, 0:512], ALU.add, (kpa, kb), (ks_,))
                P.tt("dve", s_[:, 512:656], pbk[:, 0:144], b[:, cls, 512:656], ALU.add, (kpb, kb), (ks_,))
            P.op("dve", lambda e, o=st_[:nq, 0:1], i=s_[:nq, lo:656]: e.reduce_max(o, i, AX.X), (ks_,), (kst_,))
            P.ts("dve", st_[:nq, 1:2], st_[:nq, 0:1], -1.0, None, ALU.mult, None, (kst_,), (kst_,))

        def phB(U):
            s_, ks_ = U["sc"]; st_, kst_ = U["st"]; p_, kp_ = U["pr"]
            nq = U["nq"]; lo = U["lo"]
            P.act(p_[:nq, lo:656], s_[:nq, lo:656], AF.Exp, (ks_, kst_), (kp_, kst_), bias=st_[:nq, 1:2], accum=st_[:nq, 2:3])
            P.recip("dve", st_[:nq, 3:4], st_[:nq, 2:3], (kst_,), (kst_,))

        def phC(U):
            p_, kp_ = U["pr"]; pt_, kpt_ = U["pT"]
            ba = U["ba"]; nq = U["nq"]
            ptb = P.bank(ba, BF16)
            kptb = "psb%d" % ba
            for (j, nk, vt) in U["blocks"]:
                P.tr(ptb[:nk, j * 128:j * 128 + nq], p_[:nq, j * 128:j * 128 + nk], self.ident_b[:nq, :nq],
                     (kp_, self.kid[1]), (kptb,))
            if U["kind"] == "meta":
                P.copy("act", pt_[:NM, 5, :nq], ptb[:NM, 640:640 + nq], (kptb,), (kpt_,))
            else:
                P.copy("act", pt_[:, :, :], ptb[:, 0:768].rearrange("p (a b) -> p a b", a=6), (kptb,), (kpt_,))

        def phD(U):
            kt, kkt, qt, kqt, v, kv, b, kb, oh, koh = U["tiles"]
            pt_, kpt_ = U["pT"]; o_, ko_ = U["ob"]; st_, kst_ = U["st"]
            ba = U["ba"]; nq = U["nq"]; q0 = U["q0"]; h = U["h"]
            pbk = P.bank(ba + 1)
            kpb = "psb%d" % (ba + 1)
            blocks = U["blocks"]
            for bi, (j, nk, vt) in enumerate(blocks):
                P.mm(pbk[:nq, 256:384], pt_[:nk, j, :nq], v[:nk, vt, :], bi == 0, bi == len(blocks) - 1, (kpt_, kv), (kpb,))
            P.act(o_[:nq, :], pbk[:nq, 256:384], AF.Copy, (kpb, kst_), (ko_,), scale=st_[:nq, 3:4])
            if s.sliced:
                P.dma("sp", d["OTOK_" + n_][q0:q0 + nq, h * 128:(h + 1) * 128], o_[:nq, :], (ko_,), ())
                return
            pot = pbk.bitcast(BF16)
            P.tr(pot[:, 800:800 + nq], o_[:nq, :], self.ident_b[:nq, :nq], (ko_, self.kid[1]), (kpb,))
            P.copy("dve", oh[:, q0:q0 + nq], pot[:, 800:800 + nq], (kpb,), (koh,))

        for h in range(HA):
            kt, kkt = kth[h % 2]; qt, kqt = qth[h % 2]; v, kv = vh[h % 2]; b, kb = bt[h % 2]; oh, koh = oTh[h % 2]
            P.dma("sp", kt, d["KT_" + n_][h], (), (kkt,))
            P.dma("sp", qt, d["QT_" + n_][h], (), (kqt,))
            P.dma("sp", v[:NM, 0, :], d["V_" + n_][0:NM, h * 128:(h + 1) * 128], (), (kv,))
            for i0 in range(1, s.ntile, 16):
                i1 = min(s.ntile, i0 + 16)
                P.dma("sp", v[:, i0:i1, :],
                      d["V_" + n_][NM + (i0 - 1) * 128:NM + (i1 - 1) * 128, h * 128:(h + 1) * 128].rearrange("(i p) c -> p i c", p=128),
                      (), (kv,))
            P.dma("sp", b, d["bias_tab"][h].rearrange("c q k -> q c k"), (), (kb,))
            tiles = (kt, kkt, qt, kqt, v, kv, b, kb, oh, koh)
            Us = [mk_unit("meta", 0, h, tiles)] + [mk_unit("grid", rp, h, tiles) for rp in range(s.rows // 2)]
            nU = len(Us)
            for step in range(nU + 3):
                if step < nU:
                    phA(Us[step])
                if 0 <= step - 1 < nU:
                    phB(Us[step - 1])
                if 0 <= step - 2 < nU:
                    phC(Us[step - 2])
                if 0 <= step - 3 < nU:
                    phD(Us[step - 3])
            if not s.sliced:
                P.dma("sp", d["OT_" + n_][h], oh, (koh,), ())

    def stage4(self, s):
        P = self.P
        d = self.dram
        n_ = s.name
        P.barrier()
        P.reset_arena(self.const_end)
        g1, kg = self.bcast_load(d["ln1_g"], D, "g1")
        b1, kb = self.bcast_load(d["ln1_b"], D, "b1")
        tmp = self.ln_tmp()
        wo, kwo = P.tile([128, 16, D], BF16, "wo")
        for c0 in range(0, D, 512):
            self.load_w(wo[:, :, c0:c0 + 512], kwo, d["w_out"], D, c0, 512)
        oT = [P.tile([128, 16, 128], BF16, "oT%d" % i) for i in range(2)]
        h0 = [P.tile([128, D], F32, "h0%d" % i) for i in range(2)]
        h1 = [P.tile([128, D], F32, "h1%d" % i) for i in range(2)]
        h1T = [P.tile([128, 16, 128], BF16, "h1T%d" % i) for i in range(2)]
        ctr = [0]
        def mm_phase(ti):
            t0, n = s.tile_range(ti)
            o_, ko_ = oT[ti % 2]; x, kx = h0[ti % 2]
            P.dma("sp", o_[:, :, :n], d["OT_" + n_][:, :, t0:t0 + n].rearrange("c p t -> p c t"), (), (ko_,))
            P.dma("sp", x[:n, :], d["H0_" + n_][t0:t0 + n, :], (), (kx,))
            for hf in range(4):
                pb = P.bank(hf); kpb = "psb%d" % hf
                for kc in range(16):
                    P.mm(pb[:n, :], o_[:, kc, :n], wo[:, kc, hf * 512:(hf + 1) * 512], kc == 0, kc == 15, (ko_, kwo), (kpb,))
                P.stt("dve", x[:n, hf * 512:(hf + 1) * 512], x[:n, hf * 512:(hf + 1) * 512], ALPHA, pb[:n, :], ALU.mult, ALU.add,
                      (kx, kpb), (kx,))

        def ln_phase(ti):
            t0, n = s.tile_range(ti)
            x, kx = h0[ti % 2]; y, ky = h1[ti % 2]; yt, kyt = h1T[ti % 2]
            self.layer_norm(x, kx, n, (g1, b1), (kg, kb), y, ky, LN_EPS, tmp)
            P.dma("sp", d["H1_" + n_][t0:t0 + n, :], y[:n, :], (ky,), ())
            self.transpose_to_fm(y, ky, n, yt, kyt, 0, (4, 5, 6, 7), ctr)
            P.dma("sp", d["H1T_" + n_][:, :, t0:t0 + n].rearrange("c p t -> p c t"), yt[:, :, :n], (kyt,), ())

        mm_phase(0)
        for ti in range(s.ntile):
            if ti + 1 < s.ntile:
                mm_phase(ti + 1)
            ln_phase(ti)

    def stage4_sliced(self, s):
        P = self.P
        d = self.dram
        n_ = s.name
        NG = s.ng
        P.barrier()
        P.reset_arena(self.const_end)
        g0, kg0 = self.bcast_load(d["emb_ln_g"], D, "g0")
        b0, kb0 = self.bcast_load(d["emb_ln_b"], D, "b0")
        g1, kg1 = self.bcast_load(d["ln1_g"], D, "g1")
        b1, kb1 = self.bcast_load(d["ln1_b"], D, "b1")
        tmp = self.ln_tmp()
        wo, kwo = P.tile([128, 16, D], BF16, "wo")
        for c0 in range(0, D, 512):
            self.load_w(wo[:, :, c0:c0 + 512], kwo, d["w_out"], D, c0, 512)
        ms, kms = P.tile([128, NG + 1], F32, "msel")
        P.dma("sp", ms, d["msel_" + n_], (), (kms,))
        mI, kmI = P.tile([128, NG, 128], BF16, "mI")
        for c in range(NG):
            P.ts("dve", mI[:, c, :], self.ident_f, ms[:, c:c + 1], None, ALU.mult, None, (self.kid[0], kms), (kmI,))
        shf, kshf = P.tile([128, 128], F32, "shf")
        P.dma("sp", shf[:NM, :], d["c_shift"], (), (kshf,))
        shI, kshI = P.tile([128, 128], BF16, "shI")
        P.ts("dve", shI[:NM, :], shf[:NM, :], ms[:NM, 0:1], None, ALU.mult, None, (kshf, kms), (kshI,))
        ot = [P.tile([128, D], BF16, "ot%d" % i) for i in range(NG)]
        oTs, koTs = P.tile([128, 16, 128], BF16, "oTs")
        x, kx = P.tile([128, D], F32, "x")
        h0, kh0 = P.tile([128, D], F32, "h0")
        y, ky = P.tile([128, D], F32, "h1")
        yt, kyt = P.tile([128, 16, 128], BF16, "h1T")
        ctr = [0]
        for sl in range(10):
            cands = []
            for c in range(NG):
                ti = 8 * c + sl
                if 0 <= ti < s.ntile:
                    cands.append((c, ti))
            for c, ti in cands:
                t0, n = s.tile_range(ti)
                P.dma("sp", ot[c][0][:n, :], d["OTOK_" + n_][t0:t0 + n, :], (), (ot[c][1],))
            for fc in range(16):
                bi = fc // 4
                pb = P.bank(bi); kpb = "psb%d" % bi
                for ci, (c, ti) in enumerate(cands):
                    t0, n = s.tile_range(ti)
                    rhs = shI[:NM, :] if ti == 0 else mI[:, c, :]
                    krhs = kshI if ti == 0 else kmI
                    P.mm(pb[:, (fc % 4) * 128:(fc % 4 + 1) * 128], ot[c][0][:n, fc * 128:(fc + 1) * 128], rhs,
                         ci == 0, ci == len(cands) - 1, (ot[c][1], krhs), (kpb,))
                if fc % 4 == 3:
                    P.copy("act" if bi % 2 == 0 else "dve", oTs[:, fc - 3:fc + 1, :], pb.rearrange("p (a b) -> p a b", a=4),
                           (kpb,), (koTs,))
            P.dma("sp", x, d["xext_" + n_][sl * 128:(sl + 1) * 128, :], (), (kx,))
            self.layer_norm(x, kx, 128, (g0, b0), (kg0, kb0), h0, kh0, LN_EPS, tmp)
            for hf in range(4):
                pb = P.bank(4 + hf); kpb = "psb%d" % (4 + hf)
                for kc in range(16):
                    P.mm(pb[:, :], oTs[:, kc, :], wo[:, kc, hf * 512:(hf + 1) * 512], kc == 0, kc == 15, (koTs, kwo), (kpb,))
                P.stt("dve", h0[:, hf * 512:(hf + 1) * 512], h0[:, hf * 512:(hf + 1) * 512], ALPHA, pb[:, :], ALU.mult, ALU.add,
                      (kh0, kpb), (kh0,))
            self.layer_norm(h0, kh0, 128, (g1, b1), (kg1, kb1), y, ky, LN_EPS, tmp)
            P.dma("sp", d["H1s_" + n_][sl * 128:(sl + 1) * 128, :], y, (ky,), ())
            self.transpose_to_fm(y, ky, 128, yt, kyt, 0, (0, 1, 2, 3), ctr)
            P.dma("sp", d["H1Ts_" + n_][:, :, sl * 128:(sl + 1) * 128].rearrange("c p t -> p c t"), yt, (kyt,), ())

    def stage5(self, s):
        P = self.P
        d = self.dram
        n_ = s.name
        P.barrier()
        P.reset_arena(self.const_end)
        g2_, kg = self.bcast_load(d["ln2_g"], D, "g2")
        b2_, kb = self.bcast_load(d["ln2_b"], D, "b2")
        tmp = self.ln_tmp()
        cw, kcw = P.tile([128, 44, 4], F32, "convw")
        P.dma("sp", cw, d["ffn_conv_l"], (), (kcw,))
        NB = 1024
        hT, khT = P.tile([128, 16, NB + 2], BF16, "hT")
        acc, kacc0 = P.tile([128, 8, D], F32, "acc")
        kacc = [kacc0 + "/%d" % i for i in range(8)]
        GC = 2
        w1 = [P.tile([128, 16, 2 * GC * 128], BF16, "w1_%d" % i) for i in range(2)]
        w2 = [P.tile([128, GC, D], BF16, "w2_%d" % i) for i in range(2)]
        gT = [P.tile([128, 514], F32, "gT%d" % i) for i in range(2)]
        t1 = [P.tile([128, 512], F32, "t1%d" % i) for i in range(2)]
        aT = [P.tile([128, GC, NB], BF16, "aT%d" % i) for i in range(2)]
        xr = [P.tile([128, D], F32, "xr%d" % i) for i in range(1)]
        yo = [P.tile([128, D], F32, "yo%d" % i) for i in range(1)]
        gi = 0
        hc = 0
        if s.sliced:
            h1t_d = d["H1Ts_" + n_]; h1_d = d["H1s_" + n_]; Ltot = 1280
            blocks = [(128, 0)]
            ms, kms = P.tile([128, s.ng + 1], F32, "msel")
            P.dma("sp", ms, d["msel_" + n_], (), (kms,))
        else:
            h1t_d = d["H1T_" + n_]; h1_d = d["H1_" + n_]; Ltot = s.L
            blocks = [(NM + b0, b0) for b0 in range(0, s.T, NB)]
        for ts0, orow in blocks:
            ntl = NB // 128
            lo = ts0 - 1
            hi = min(Ltot, ts0 + NB + 1)
            P.dma("sp", hT[:, :, 0:hi - lo], h1t_d[:, :, lo:hi].rearrange("c p t -> p c t"), (), (khT,))
            if hi - lo < NB + 2:
                P.memset("pool", hT[:, :, NB + 1:NB + 2], 0.0, (khT,))
            if s.sliced:
                P.ts("pool", hT[:, :, NB + 1:NB + 2], hT[:, :, NB + 1:NB + 2], ms[:, s.ng:s.ng + 1], None, ALU.mult, None,
                     (khT, kms), (khT,))
            for g in range(44 // GC):
                w1_, kw1 = w1[gi % 2]; w2_, kw2 = w2[gi % 2]; a_, ka_ = aT[gi % 2]
                gi += 1
                c0 = g * GC * 128
                self.load_w(w1_[:, :, 0:GC * 128], kw1, d["ffn_w_in"], D, c0, GC * 128)
                self.load_w(w1_[:, :, GC * 128:2 * GC * 128], kw1, d["ffn_w_in"], D, DFF + c0, GC * 128)
                P.dma("pool", w2_, d["ffn_w_out"][c0:c0 + GC * 128, :].rearrange("(c p) n -> p c n", p=128), (), (kw2,))
                for cl in range(GC):
                    fc = g * GC + cl
                    for hh in range(2):
                        g_, kg_ = gT[hc % 2]; t_, kt_ = t1[hc % 2]
                        hc += 1
                        bg = 0 if hh == 0 else 3
                        pg = P.bank(bg); kpg = "psb%d" % bg
                        ph = P.bank(1); pu = P.bank(2)
                        cb = hh * 512
                        for kc in range(16):
                            P.mm(pg[:, :], w1_[:, kc, cl * 128:(cl + 1) * 128], hT[:, kc, cb:cb + 512], kc == 0, kc == 15,
                                 (kw1, khT), (kpg,))
                        for kc in range(16):
                            P.mm(ph[:, 0:2], w1_[:, kc, cl * 128:(cl + 1) * 128], hT[:, kc, cb + 512:cb + 514], kc == 0, kc == 15,
                                 (kw1, khT), ("psb1",))
                        for kc in range(16):
                            P.mm(pu[:, :], w1_[:, kc, (GC + cl) * 128:(GC + cl + 1) * 128], hT[:, kc, cb + 1:cb + 513],
                                 kc == 0, kc == 15, (kw1, khT), ("psb2",))
                        P.copy("act", g_[:, 0:512], pg[:, :], (kpg,), (kg_,))
                        P.copy("act", g_[:, 512:514], ph[:, 0:2], ("psb1",), (kg_,))
                        P.act(t_[:, :], g_[:, 1:513], AF.Identity, (kg_, kcw), (kt_,), bias=cw[:, fc, 3:4], scale=cw[:, fc, 1:2])
                        P.stt("dve", t_[:, :], g_[:, 0:512], cw[:, fc, 0:1], t_[:, :], ALU.mult, ALU.add, (kg_, kcw, kt_), (kt_,))
                        P.stt("dve", t_[:, :], g_[:, 2:514], cw[:, fc, 2:3], t_[:, :], ALU.mult, ALU.add, (kg_, kcw, kt_), (kt_,))
                        P.act(t_[:, :], t_[:, :], AF.Gelu, (kt_,), (kt_,))
                        P.tt("dve", a_[:, cl, cb:cb + 512], t_[:, :], pu[:, :], ALU.mult, (kt_, "psb2"), (ka_,))
                for tl in range(ntl):
                    for hf in range(4):
                        pb = P.bank(4 + hf); kpb = "psb%d" % (4 + hf)
                        for cl in range(GC):
                            P.mm(pb[:, :], a_[:, cl, tl * 128:(tl + 1) * 128], w2_[:, cl, hf * 512:(hf + 1) * 512],
                                 cl == 0, cl == GC - 1, (ka_, kw2), (kpb,))
                        dst = acc[:, tl, hf * 512:(hf + 1) * 512]
                        if g == 0:
                            P.copy("dve" if hf % 2 else "act", dst, pb[:, :], (kpb,), (kacc[tl],))
                        else:
                            P.tt("dve", dst, dst, pb[:, :], ALU.add, (kpb, kacc[tl]), (kacc[tl],))
            for tl in range(ntl):
                x, kx = xr[0]; y, ky = yo[0]
                tq = ts0 + tl * 128
                P.dma("sp", x, h1_d[tq:tq + 128, :], (), (kx,))
                P.stt("dve", x, x, ALPHA, acc[:, tl, :], ALU.mult, ALU.add, (kx, kacc[tl]), (kx,))
                self.layer_norm(x, kx, 128, (g2_, b2_), (kg, kb), y, ky, LN_EPS, tmp)
                P.dma("sp", d["y_" + n_][orow + tl * 128:orow + tl * 128 + 128, :], y, (ky,), ())

    def build(self):
        self.setup_consts()
        for s in self.seqs:
            if 1 in self.run:
                self.stage1(s)
            if 2 in self.run:
                self.stage2(s)
            if 3 in self.run:
                self.stage3(s)
            if 4 in self.run:
                if s.sliced:
                    self.stage4_sliced(s)
                else:
                    self.stage4(s)
            if 5 in self.run:
                self.stage5(s)
        self.P.finish()
        return self.nc


def _bias_table(rpb):
    rpb = np.asarray(rpb, np.float32).reshape(HA, 15, 31)
    tab = np.full((HA, 5, 128, NKEY), NEG, np.float32)
    qc = np.arange(64)
    c0 = np.clip(qc - 8, 0, 48)
    kc = np.arange(64)
    colmask = (kc[None, :] >= c0[:, None]) & (kc[None, :] < c0[:, None] + 16)
    dc = np.clip(kc[None, :] - qc[:, None], -15, 15) + 15
    rel = {0: (0, 0), 1: (0, 0), 2: (0, 1), 3: (2, 2), 4: (2, 2)}
    for c in range(5):
        for qr in range(2):
            r_rel = 2 * c + qr
            r0 = rel[c][qr]
            for j in range(8):
                krow = r0 + j
                dr = krow - r_rel + 7
                blk = rpb[:, dr][:, dc]
                blk = np.where(colmask[None], blk, NEG)
                tab[:, c, qr * 64:(qr + 1) * 64, krow * 64:(krow + 1) * 64] = blk
        tab[:, c, :, WINR * 64:] = 0.0
    return tab


def _consts():
    ident = np.eye(128, dtype=np.float32)
    s = np.arange(128)[:, None]
    t = np.arange(128)[None, :]
    masks = np.stack([(s <= t), (s >= t), (s < t), (s <= t), (s > t), (s >= t)]).astype(np.float32)
    return ident, masks


def _common_inputs(inp):
    ident, masks = _consts()
    f = lambda a: np.ascontiguousarray(np.asarray(a, np.float32))
    m = {
        "meta_tokens": f(inp["meta_tokens"]),
        "emb_ln_g": f(inp["emb_ln_g"]), "emb_ln_b": f(inp["emb_ln_b"]),
        "ln1_g": f(inp["ln1_g"][0]), "ln1_b": f(inp["ln1_b"][0]),
        "ln2_g": f(inp["ln2_g"][0]), "ln2_b": f(inp["ln2_b"][0]),
        "w_in": f(inp["w_in"][0]),
        "bias_tab": _bias_table(inp["attn_rpb"][0]),
        "rwkv_mu": f(inp["rwkv_mu"][0]), "rwkv_w0": f(inp["rwkv_w0"][0]), "rwkv_w2": f(inp["rwkv_w2"][0]),
        "rwkv_a0": f(inp["rwkv_a0"][0]), "rwkv_a2": f(inp["rwkv_a2"][0]), "rwkv_g2": f(inp["rwkv_g2"][0]),
        "rwkv_k_k": f(inp["rwkv_k_k"][0]), "rwkv_k_a": f(inp["rwkv_k_a"][0]),
        "rwkv_r_k": f(inp["rwkv_r_k"][0]).reshape(DR),
        "rwkv_lnx_g": f(inp["rwkv_lnx_g"][0]), "rwkv_lnx_b": f(inp["rwkv_lnx_b"][0]),
        "w_out": f(inp["w_out"][0]), "ffn_w_in": f(inp["ffn_w_in"][0]),
        "ffn_conv_l": np.ascontiguousarray(np.concatenate([f(inp["ffn_conv_w"][0]), f(inp["ffn_conv_b"][0])[None]], 0)
                                           .reshape(4, 44, 128).transpose(2, 1, 0)),
        "ffn_w_out": f(inp["ffn_w_out"][0]),
        "c_ident": ident, "c_masks": masks, "c_zero": np.zeros((1, NSH), np.float32),
        "c_shift": np.eye(NM, 128, 128 - NM, dtype=np.float32),
    }
    return m


def slice_inputs(x, meta, c, ng):
    x = np.asarray(x, np.float32)
    prev = x[1024 * c - 128:1024 * c] if c > 0 else np.concatenate([np.zeros((128 - NM, D), np.float32), np.asarray(meta, np.float32)], 0)
    nxt = x[1024 * c + 1024:1024 * c + 1152] if c < ng - 1 else np.zeros((128, D), np.float32)
    xext = np.ascontiguousarray(np.concatenate([prev, x[1024 * c:1024 * c + 1024], nxt], 0))
    msel = np.zeros((128, ng + 1), np.float32)
    msel[:, c] = 1.0
    msel[:, ng] = 1.0 if c < ng - 1 else 0.0
    return xext, msel


def kernel(**inputs):
    xp = np.asarray(inputs["x_prompt"], np.float32)
    xs = np.asarray(inputs["x_sample"], np.float32)
    b = Builder([xs.shape[1], xp.shape[1]], sliced=[False, True])
    nc = b.build()
    common = _common_inputs(inputs)
    in_maps = []
    for c in range(8):
        m = dict(common)
        m["x_s0"] = np.ascontiguousarray(xs[c])
        m["x_s1"] = np.ascontiguousarray(xp[0])
        m["xext_s1"], m["msel_s1"] = slice_inputs(xp[0], inputs["meta_tokens"], c, 8)
        in_maps.append(m)
    res = run_bass_kernel_spmd(nc, in_maps, core_ids=list(range(8)))
    y_s = np.stack([res.results[c]["y_s0"] for c in range(8)], axis=0)
    y_p = np.concatenate([res.results[c]["y_s1"] for c in range(8)], axis=0)[None]
    return (y_p.astype(np.float32), y_s.astype(np.float32))
```
